# Optimizing a Trainium2 kernel written in Bass

```python
import math
import jax, jax.numpy as jnp
from jax import lax
import numpy as np

D_MODEL = 1024
BATCH = 4
SEQ = 4096
DEPTH = 2
DEC_BATCH = 4
DEC_SEQ = 8192
PAST_LEN = 128

N_MEM = 256
BRANCH_W = 512
N_BRANCH = 5
Q_BLOCK = 128
NEG_INF = -1e30
EPS = 1e-6
A_HEADS = 4
A_DK = 64
A_DV = 128
B_HEADS = 4
B_Q_LORA = 256
B_KV_LORA = 128
B_NOPE = 64
B_ROPE = 32
B_DV = 128
ROPE_THETA = 10000.0
C_QH = 8
C_KVH = 2
C_HD = 64
C_WINDOW = 128
C_BLOCK = 128
D_HEADS = 8
D_HD = 64
DIL_PAIRS = ((128, 1), (512, 4), (2048, 16))
N_DIL = 3
M_HEADS = 4
M_HD = 128

IN_LAYOUT = (
    ("a_q", 2 * A_HEADS * A_DK), ("a_k", 2 * A_HEADS * A_DK), ("a_v", A_HEADS * A_DV),
    ("b_cq", B_Q_LORA), ("b_ckv", B_KV_LORA), ("b_kr", B_ROPE),
    ("c_q", C_QH * C_HD), ("c_k", C_KVH * C_HD), ("c_v", C_KVH * C_HD),
    ("d_q", N_DIL * D_HEADS * D_HD), ("d_k", N_DIL * D_HEADS * D_HD), ("d_v", N_DIL * D_HEADS * D_HD),
    ("m_q", M_HEADS * M_HD),
    ("z", N_BRANCH * BRANCH_W),
    ("g", N_BRANCH * D_MODEL),
)
N_IN = sum(size for _, size in IN_LAYOUT)

kernel_name = "hybrid_gated_bidir_encoder"


def rms_norm(x, g):
    xf = x.astype(jnp.float32)
    y = xf * lax.rsqrt(jnp.mean(xf * xf, axis=-1, keepdims=True) + EPS)
    return (y * g.astype(jnp.float32)).astype(x.dtype)


def alibi_slopes(n):
    return 2.0 ** (-8.0 * jnp.arange(1, n + 1, dtype=jnp.float32) / n)


def rope_tables(S):
    half = B_ROPE // 2
    inv = ROPE_THETA ** (-jnp.arange(half, dtype=jnp.float32) / half)
    ang = jnp.arange(S, dtype=jnp.float32)[:, None] * inv[None, :]
    return jnp.cos(ang), jnp.sin(ang)


def apply_rope(x, cos, sin):
    half = x.shape[-1] // 2
    xf = x.astype(jnp.float32)
    x1, x2 = xf[..., :half], xf[..., half:]
    return jnp.concatenate([x1 * cos - x2 * sin, x2 * cos + x1 * sin], axis=-1).astype(x.dtype)


def to_blocks(t):
    B, S = t.shape[:2]
    return jnp.moveaxis(t.reshape(B, S // Q_BLOCK, Q_BLOCK, *t.shape[2:]), 1, 0)


def from_blocks(t):
    nb, B, qb = t.shape[:3]
    return jnp.moveaxis(t, 0, 1).reshape(B, nb * qb, *t.shape[3:])


def split_columns(proj):
    parts, start = {}, 0
    for name, size in IN_LAYOUT:
        parts[name] = proj[..., start:start + size]
        start += size
    return parts


def diff_attention(q, k, v, lam):
    S = q.shape[1]
    pos = jnp.arange(S)
    slopes = alibi_slopes(q.shape[3])
    scale = q.shape[-1] ** -0.5

    def block(args):
        qb, qp = args
        s = jnp.einsum("bqmhd,bkmhd->bmhqk", qb, k).astype(jnp.float32) * scale
        bias = -slopes[:, None, None] * jnp.abs(qp[:, None] - pos[None, :])
        p = jax.nn.softmax(s + bias, axis=-1)
        a = p[:, 0] - lam * p[:, 1]
        return jnp.einsum("bhqk,bkhd->bqhd", a.astype(v.dtype), v)

    return from_blocks(lax.map(block, (to_blocks(q), pos.reshape(-1, Q_BLOCK))))


def mla_attention(qn, qr, kn, kr, v):
    scale = (qn.shape[-1] + qr.shape[-1]) ** -0.5

    def block(args):
        qnb, qrb = args
        s = (jnp.einsum("bqhd,bkhd->bhqk", qnb, kn)
             + jnp.einsum("bqhr,bkr->bhqk", qrb, kr)).astype(jnp.float32) * scale
        p = jax.nn.softmax(s, axis=-1)
        return jnp.einsum("bhqk,bkhd->bqhd", p.astype(v.dtype), v)

    return from_blocks(lax.map(block, (to_blocks(qn), to_blocks(qr))))


def window_gqa(q, k, v, sink):
    B, S, Hq, hd = q.shape
    Hkv = k.shape[2]
    G = Hq // Hkv
    nb = S // C_BLOCK
    scale = hd ** -0.5
    qb = q.reshape(B, nb, C_BLOCK, Hkv, G, hd)

    def band(t):
        tp = jnp.pad(t, ((0, 0), (C_BLOCK, C_BLOCK), (0, 0), (0, 0))).reshape(B, nb + 2, C_BLOCK, Hkv, hd)
        return jnp.concatenate([tp[:, :-2], tp[:, 1:-1], tp[:, 2:]], axis=2)

    kb, vb = band(k), band(v)
    s = jnp.einsum("bnqhgd,bnkhd->bnhgqk", qb, kb).astype(jnp.float32) * scale
    koff = jnp.arange(3 * C_BLOCK) - C_BLOCK
    rel = koff[None, :] - jnp.arange(C_BLOCK)[:, None]
    kabs = jnp.arange(nb)[:, None] * C_BLOCK + koff[None, :]
    valid = (jnp.abs(rel) <= C_WINDOW)[None] & ((kabs >= 0) & (kabs < S))[:, None, :]
    slopes = alibi_slopes(Hq).reshape(Hkv, G)
    logits = s - slopes[:, :, None, None] * jnp.abs(rel)
    logits = jnp.where(valid[None, :, None, None], logits, NEG_INF)
    sink_l = jnp.broadcast_to(sink.astype(jnp.float32).reshape(Hkv, G)[:, :, None, None],
                              logits.shape[:-1] + (1,))
    p = jax.nn.softmax(jnp.concatenate([logits, sink_l], axis=-1), axis=-1)[..., :-1]
    o = jnp.einsum("bnhgqk,bnkhd->bnqhgd", p.astype(v.dtype), vb)
    return o.reshape(B, S, Hq, hd)


def dilated_attention(q, k, v):
    S, H = q.shape[1], q.shape[3]
    scale = q.shape[-1] ** -0.5
    slopes = alibi_slopes(H)
    pos = jnp.arange(S)
    ks = [k[:, :, g] for g in range(N_DIL)]
    vs = [v[:, :, g] for g in range(N_DIL)]
    offsets = [dil * jnp.arange(-(w // (2 * dil)), w // (2 * dil) + 1) for w, dil in DIL_PAIRS]

    def block(args):
        qb, qp = args
        outs, lses = [], []
        for g in range(N_DIL):
            off = offsets[g]
            idx = qp[:, None] + off[None, :]
            valid = (idx >= 0) & (idx < S)
            idx = jnp.clip(idx, 0, S - 1)
            kg = jnp.take(ks[g], idx, axis=1)
            vg = jnp.take(vs[g], idx, axis=1)
            s = jnp.einsum("bqhd,bqjhd->bhqj", qb[:, :, g], kg).astype(jnp.float32) * scale
            s = s - slopes[:, None, None] * jnp.abs(off)[None, None, :]
            s = jnp.where(valid[None, None], s, NEG_INF)
            m = jnp.max(s, axis=-1, keepdims=True)
            e = jnp.exp(s - m)
            den = jnp.sum(e, axis=-1, keepdims=True)
            outs.append(jnp.einsum("bhqj,bqjhd->bqhd", (e / den).astype(v.dtype), vg))
            lses.append((m + jnp.log(den))[..., 0])
        w = jax.nn.softmax(jnp.stack(lses, axis=0), axis=0)
        w = jnp.swapaxes(w, 2, 3)[..., None]
        return jnp.sum(w * jnp.stack(outs, axis=0).astype(jnp.float32), axis=0).astype(v.dtype)

    return from_blocks(lax.map(block, (to_blocks(q), pos.reshape(-1, Q_BLOCK))))


def memory_attention(q, k, v):
    scale = q.shape[-1] ** -0.5
    s = jnp.einsum("bqhd,bkhd->bhqk", q, k).astype(jnp.float32) * scale
    p = jax.nn.softmax(s, axis=-1)
    return jnp.einsum("bhqk,bkhd->bqhd", p.astype(v.dtype), v)


def encoder_layer(x, mem, layer_idx, p):
    B, S, _ = x.shape
    h = rms_norm(x, p["norm_g"])
    parts = split_columns(h @ p["w_in"])

    aq = rms_norm(parts["a_q"].reshape(B, S, 2, A_HEADS, A_DK), p["a_qn"])
    ak = rms_norm(parts["a_k"].reshape(B, S, 2, A_HEADS, A_DK), p["a_kn"])
    av = parts["a_v"].reshape(B, S, A_HEADS, A_DV)
    lam_init = 0.8 - 0.6 * math.exp(-0.3 * layer_idx)
    lf = p["a_lam"].astype(jnp.float32)
    lam = jnp.exp(jnp.sum(lf[0] * lf[1])) - jnp.exp(jnp.sum(lf[2] * lf[3])) + lam_init
    oa = diff_attention(aq, ak, av, lam)
    oa = (rms_norm(oa, p["a_hn"]) * (1.0 - lam_init)).reshape(B, S, BRANCH_W)

    cq = rms_norm(parts["b_cq"], p["b_cqn"])
    qb = (cq @ p["w_qb"]).reshape(B, S, B_HEADS, B_NOPE + B_ROPE)
    ckv = rms_norm(parts["b_ckv"], p["b_ckvn"])
    kvb = (ckv @ p["w_kvb"]).reshape(B, S, B_HEADS, B_NOPE + B_DV)
    cos, sin = rope_tables(S)
    qn = rms_norm(qb[..., :B_NOPE], p["b_qn"][:B_NOPE])
    qr = apply_rope(rms_norm(qb[..., B_NOPE:], p["b_qn"][B_NOPE:]), cos[:, None], sin[:, None])
    kn = rms_norm(kvb[..., :B_NOPE], p["b_kn"][:B_NOPE])
    kr = apply_rope(rms_norm(parts["b_kr"], p["b_kn"][B_NOPE:]), cos, sin)
    ob = mla_attention(qn, qr, kn, kr, kvb[..., B_NOPE:]).reshape(B, S, BRANCH_W)

    cq_ = rms_norm(parts["c_q"].reshape(B, S, C_QH, C_HD), p["c_qn"])
    ck = rms_norm(parts["c_k"].reshape(B, S, C_KVH, C_HD), p["c_kn"])
    cv = parts["c_v"].reshape(B, S, C_KVH, C_HD)
    oc = window_gqa(cq_, ck, cv, p["c_sink"]).reshape(B, S, BRANCH_W)

    dq = rms_norm(parts["d_q"].reshape(B, S, N_DIL, D_HEADS, D_HD), p["d_qn"])
    dk = rms_norm(parts["d_k"].reshape(B, S, N_DIL, D_HEADS, D_HD), p["d_kn"])
    dv = parts["d_v"].reshape(B, S, N_DIL, D_HEADS, D_HD)
    od = dilated_attention(dq, dk, dv).reshape(B, S, BRANCH_W)

    mn = rms_norm(mem, p["m_norm"])
    mkv = (mn @ p["w_mem_kv"]).reshape(B, mem.shape[1], 2, M_HEADS, M_HD)
    mq = rms_norm(parts["m_q"].reshape(B, S, M_HEADS, M_HD), p["m_qn"])
    mk = rms_norm(mkv[:, :, 0], p["m_kn"])
    om = memory_attention(mq, mk, mkv[:, :, 1]).reshape(B, S, BRANCH_W)

    z, g = parts["z"], parts["g"]
    acc = jnp.zeros_like(x)
    for i, o in enumerate((oa, ob, oc, od, om)):
        zi = jax.nn.silu(z[..., i * BRANCH_W:(i + 1) * BRANCH_W])
        gi = jax.nn.sigmoid(g[..., i * D_MODEL:(i + 1) * D_MODEL] + p["b_gate"][i])
        acc = acc + gi * ((o * zi) @ p["w_br"][i])
    return x + acc @ p["w_out"]


def setup_inputs(seed: int = 0) -> dict:
    key = jax.random.key(seed)
    ks = jax.random.split(key, 32)
    L, D = DEPTH, D_MODEL
    f32 = jnp.float32

    def nrm(k, shape, scale):
        return jax.random.normal(k, shape, f32) * scale

    def gain(k, shape):
        return 1.0 + 0.01 * jax.random.normal(k, shape, f32)

    return {
        "x_prompt": nrm(ks[0], (BATCH, SEQ, D), 1.0),
        "x_sample": nrm(ks[1], (DEC_BATCH, DEC_SEQ, D), 1.0),
        "mem_prompt": nrm(ks[2], (BATCH, N_MEM, D), 1.0),
        "mem_sample": nrm(ks[3], (DEC_BATCH, N_MEM, D), 1.0),
        "norm_g": gain(ks[4], (L, D)),
        "w_in": nrm(ks[5], (L, D, N_IN), D ** -0.5),
        "a_qn": gain(ks[6], (L, A_DK)),
        "a_kn": gain(ks[7], (L, A_DK)),
        "a_lam": nrm(ks[8], (L, 4, A_DK), 0.1),
        "a_hn": gain(ks[9], (L, A_DV)),
        "b_cqn": gain(ks[10], (L, B_Q_LORA)),
        "b_ckvn": gain(ks[11], (L, B_KV_LORA)),
        "w_qb": nrm(ks[12], (L, B_Q_LORA, B_HEADS * (B_NOPE + B_ROPE)), B_Q_LORA ** -0.5),
        "w_kvb": nrm(ks[13], (L, B_KV_LORA, B_HEADS * (B_NOPE + B_DV)), B_KV_LORA ** -0.5),
        "b_qn": gain(ks[14], (L, B_NOPE + B_ROPE)),
        "b_kn": gain(ks[15], (L, B_NOPE + B_ROPE)),
        "c_qn": gain(ks[16], (L, C_HD)),
        "c_kn": gain(ks[17], (L, C_HD)),
        "c_sink": nrm(ks[18], (L, C_QH), 0.5),
        "d_qn": gain(ks[19], (L, D_HD)),
        "d_kn": gain(ks[20], (L, D_HD)),
        "m_norm": gain(ks[21], (L, D)),
        "w_mem_kv": nrm(ks[22], (L, D, 2 * M_HEADS * M_HD), D ** -0.5),
        "m_qn": gain(ks[23], (L, M_HD)),
        "m_kn": gain(ks[24], (L, M_HD)),
        "b_gate": nrm(ks[25], (L, N_BRANCH, D), 0.01),
        "w_br": nrm(ks[26], (L, N_BRANCH, BRANCH_W, D), BRANCH_W ** -0.5),
        "w_out": nrm(ks[27], (L, D, D), D ** -0.5),
    }


def reference(x_prompt, x_sample, mem_prompt, mem_sample, norm_g, w_in, a_qn, a_kn, a_lam, a_hn,
              b_cqn, b_ckvn, w_qb, w_kvb, b_qn, b_kn, c_qn, c_kn, c_sink, d_qn, d_kn,
              m_norm, w_mem_kv, m_qn, m_kn, b_gate, w_br, w_out):
    y_prompt, y_sample = x_prompt, x_sample
    for l in range(DEPTH):
        p = {
            "norm_g": norm_g[l], "w_in": w_in[l],
            "a_qn": a_qn[l], "a_kn": a_kn[l], "a_lam": a_lam[l], "a_hn": a_hn[l],
            "b_cqn": b_cqn[l], "b_ckvn": b_ckvn[l], "w_qb": w_qb[l], "w_kvb": w_kvb[l],
            "b_qn": b_qn[l], "b_kn": b_kn[l],
            "c_qn": c_qn[l], "c_kn": c_kn[l], "c_sink": c_sink[l],
            "d_qn": d_qn[l], "d_kn": d_kn[l],
            "m_norm": m_norm[l], "w_mem_kv": w_mem_kv[l], "m_qn": m_qn[l], "m_kn": m_kn[l],
            "b_gate": b_gate[l], "w_br": w_br[l], "w_out": w_out[l],
        }
        y_prompt = encoder_layer(y_prompt, mem_prompt, l, p)
        y_sample = encoder_layer(y_sample, mem_sample, l, p)
    return (y_prompt, y_sample)
```

```python
import contextlib
import math
import numpy as np
import concourse.bass as bass
import concourse.mybir as mybir
from concourse.bass_utils import run_bass_kernel_spmd

F32 = mybir.dt.float32
BF16 = mybir.dt.bfloat16
AF = mybir.ActivationFunctionType
ALU = mybir.AluOpType

COMPUTE = ("pe", "act", "dve", "pool")
QUEUES = ("pe", "act", "dve", "pool", "sp")

DM = 1024
N_IN = 15520
OFF = dict(a_q=0, a_k=512, a_v=1024, b_cq=1536, b_ckv=1792, b_kr=1920, c_q=1952, c_k=2464, c_v=2592,
           d_q=2720, d_k=4256, d_v=5792, m_q=7328, z=7840, g=10400)
EPS = 1e-6
BIG = 1.0e6
SH_A, SH_B, SH_C, SH_D, SH_M = 8.0, 10.0, 8.0, 8.0, 11.5
SL4 = [2.0 ** (-8.0 * (i + 1) / 4) for i in range(4)]
SL8 = [2.0 ** (-8.0 * (i + 1) / 8) for i in range(8)]
GDIL = (1, 1, 4, 16)
GNKV = (2, 8, 8, 8)
ARN = 44000


class Sem:
    __slots__ = ("h", "total")


class Buf:
    __slots__ = ("name", "ap", "last_w", "readers", "sem")

    def __init__(self, name, ap):
        self.name, self.ap = name, ap
        self.last_w, self.readers, self.sem = None, [], None

    def __getitem__(self, idx):
        return self.ap[idx]


class Op:
    __slots__ = ("q", "fn", "dma", "deps", "inc", "val", "dsem", "dval", "grp")


class Prog:
    def __init__(self, nc, stack):
        self.nc, self.stack = nc, stack
        self.ops, self.bufs, self.nsem = [], [], 0
        self.free_sems, self.live_sems = [], []

    def buf(self, name, ap):
        b = Buf(name, ap)
        self.bufs.append(b)
        return b

    def _newsem(self, name):
        self.nsem += 1
        return self.stack.enter_context(self.nc.semaphore(name))

    def _dsem(self):
        if self.free_sems:
            s = self.free_sems.pop()
        else:
            s = Sem()
            s.h, s.total = self._newsem("d%d" % self.nsem), 0
        self.live_sems.append(s)
        return s

    def op(self, q, fn, reads=(), writes=(), dma=None, grp=None):
        o = Op()
        o.q, o.fn, o.dma, o.grp = q, fn, dma, grp
        o.inc, o.val, o.dsem, o.dval = False, None, None, None
        deps = set()
        me = "dma" if dma is not None else q
        for b in reads:
            w = b.last_w
            if w is not None:
                we = "dma" if w.dma is not None else w.q
                if not (we == me and me == "pe"):
                    deps.add(w)
        for b in writes:
            w = b.last_w
            if w is not None:
                we = "dma" if w.dma is not None else w.q
                if (we != me or me == "dma") and not (grp is not None and w.grp is grp):
                    deps.add(w)
            for r in b.readers:
                re_ = "dma" if r.dma is not None else r.q
                if re_ != me or me == "dma":
                    deps.add(r)
        o.deps = deps
        for d in deps:
            if d.dma is None:
                d.inc = True
        if dma is not None:
            if dma.sem is None:
                dma.sem = self._dsem()
            dma.sem.total += 16
            o.dsem, o.dval = dma.sem, dma.sem.total
        for b in reads:
            b.readers.append(o)
        for b in writes:
            b.last_w = o
            b.readers = []
        self.ops.append(o)
        return o

    def barrier(self):
        last = {}
        for o in self.ops:
            if o.dma is None and o.fn is not None:
                last[o.q] = o
        for q in QUEUES:
            o = Op()
            o.q, o.fn, o.dma, o.grp = q, None, None, None
            o.inc, o.val, o.dsem, o.dval = False, None, None, None
            o.deps = set(v for k, v in last.items() if k != q)
            for d in o.deps:
                d.inc = True
            o.deps |= set(("D", s, s.total) for s in self.live_sems)
            self.ops.append(o)
        self.free_sems.extend(self.live_sems)
        self.live_sems = []
        for b in self.bufs:
            b.last_w, b.readers, b.sem = None, [], None

    def emit(self):
        nc = self.nc
        esem = {q: self._newsem("e_" + q) for q in COMPUTE}
        cnt = {q: 0 for q in COMPUTE}
        for o in self.ops:
            if o.dma is None and o.inc:
                cnt[o.q] += 1
                o.val = cnt[o.q]
        byq = {q: [] for q in QUEUES}
        for o in self.ops:
            byq[o.q].append(o)
        allsems = self.free_sems + self.live_sems
        block = self.stack.enter_context(nc.Block())
        handles = {"pe": block.tensor, "act": block.scalar, "dve": block.vector,
                   "pool": block.gpsimd, "sp": block.sync}

        def run_queue(q, e):
            known = {}
            for o in byq[q]:
                need = {}
                for d in o.deps:
                    if isinstance(d, tuple):
                        _, s, v = d
                        key, sem = ("D", id(s)), s.h
                    elif d.dma is not None:
                        key, sem, v = ("D", id(d.dsem)), d.dsem.h, d.dval
                    else:
                        key, sem, v = ("E", d.q), esem[d.q], d.val
                    if v > known.get(key, 0) and v > need.get(key, (None, 0))[1]:
                        need[key] = (sem, v)
                for key, (sem, v) in need.items():
                    e.wait_ge(sem, v)
                    known[key] = v
                if o.fn is None:
                    continue
                ins = o.fn(e)
                if o.dma is not None:
                    ins.then_inc(o.dsem.h, 16)
                elif o.inc:
                    ins.then_inc(esem[q], 1)
            if q == "sp":
                for s in allsems:
                    if s.total > known.get(("D", id(s)), 0):
                        e.wait_ge(s.h, s.total)

        for q in QUEUES:
            handles[q](lambda e, q=q: run_queue(q, e))


class Arena:
    def __init__(self, P, t, ncols):
        self.P, self.t, self.ncols, self.off, self.n = P, t, ncols, 0, 0

    def reset(self, off=0):
        self.off = off

    def alloc(self, name, shape, dt):
        per = 1
        for s in shape[1:]:
            per *= s
        nb = per * (2 if dt == BF16 else 4)
        n4 = (nb + 3) // 4
        assert self.off + n4 <= self.ncols, ("SBUF arena overflow", name, self.off, n4, self.ncols)
        ap = self.t[0:shape[0], self.off:self.off + n4]
        if dt == BF16:
            ap = ap.bitcast(BF16)[:, 0:per]
        if len(shape) == 3:
            ap = ap.rearrange("p (a b) -> p a b", a=shape[1])
        elif len(shape) == 4:
            ap = ap.rearrange("p (a b c) -> p a b c", a=shape[1], b=shape[2])
        self.off += n4
        self.n += 1
        return self.P.buf(name + "_" + str(self.n), ap)


class Rot:
    def __init__(self, items):
        self.items, self.i = items, 0

    def next(self):
        v = self.items[self.i % len(self.items)]
        self.i += 1
        return v


def const_tables(S):
    c = {}
    c["ident"] = np.eye(128, dtype=np.float32)
    c["ones128"] = np.ones((128, 128), np.float32)
    ii = np.arange(128)
    c["blk64"] = (ii[:, None] // 64 == ii[None, :] // 64).astype(np.float32)
    c["blk32"] = (ii[:, None] // 32 == ii[None, :] // 32).astype(np.float32)
    R = np.zeros((128, 128), np.float32)
    for m in range(128):
        if m % 32 < 16:
            R[m + 16, m] = -1.0
        else:
            R[m - 16, m] = 1.0
    c["rot"] = R
    i = np.arange(128, dtype=np.float32)[:, None]
    j = np.arange(512, dtype=np.float32)[None, :]
    c["tabA_L"] = (j - i).astype(np.float32)
    c["tabA_D"] = np.stack([np.abs(j - i - 128.0 * m) for m in range(4)], axis=1).astype(np.float32)
    q = np.arange(128, dtype=np.float32)[None, :]
    tabs = []
    for m in range(3):
        rel = (m - 1) * 128 + i - q
        tabs.append(np.where(np.abs(rel) <= 128, np.abs(rel), BIG))
    c["tabB128"] = np.concatenate(tabs, axis=1).astype(np.float32)
    tabs = []
    for m in range(3):
        rel = (m * 128 - 64) + i - q
        tabs.append(np.where(np.abs(rel) <= 64, np.abs(rel), BIG))
    c["tabB64"] = np.concatenate(tabs, axis=1).astype(np.float32)
    i64 = np.arange(128, dtype=np.float64)[:, None]
    j64 = np.arange(512, dtype=np.float64)[None, :]
    EA = np.zeros((4, 128, 6, 512), np.float32)
    for h in range(4):
        EA[h, :, 0, :] = np.exp(-SL4[h] * (j64 - i64 + 127.0))
        EA[h, :, 1, :] = np.exp(-SL4[h] * (i64 - j64 + 511.0))
        for m in range(4):
            EA[h, :, 2 + m, :] = np.exp(-SL4[h] * np.abs(j64 - i64 - 128.0 * m))
    c["EA"] = EA
    EB = np.zeros((4, 8, 128, 384), np.float32)
    for G in range(4):
        tab = c["tabB128"] if G == 0 else c["tabB64"]
        for h in range(8):
            EB[G, h] = np.where(tab < BIG, np.exp(-SL8[h] * GDIL[G] * tab.astype(np.float64)), 0.0)
    c["EB"] = EB
    half = 16
    inv = (np.float32(10000.0) ** (-np.arange(half, dtype=np.float32) / np.float32(half))).astype(np.float32)
    ang = (np.arange(S, dtype=np.float32)[:, None] * inv[None, :]).astype(np.float32)
    cos, sin = np.cos(ang).astype(np.float32), np.sin(ang).astype(np.float32)
    idx = np.arange(128) % 16
    c["cosT"] = np.ascontiguousarray(cos[:, idx].T)
    c["sinT"] = np.ascontiguousarray(sin[:, idx].T)
    return c


def const_shapes(S):
    return dict(ident=[128, 128], ones128=[128, 128], blk64=[128, 128], blk32=[128, 128], rot=[128, 128],
                tabA_L=[128, 512], tabA_D=[128, 4, 512], tabB128=[128, 384], tabB64=[128, 384],
                cosT=[128, S], sinT=[128, S], EA=[4, 128, 6, 512], EB=[4, 8, 128, 384])


WSHAPES = dict(norm_g=[2, 1024], w_in=[2, 1024, N_IN], a_qn=[2, 64], a_kn=[2, 64], a_lam=[2, 4, 64],
               a_hn=[2, 128], b_cqn=[2, 256], b_ckvn=[2, 128], w_qb=[2, 256, 384], w_kvb=[2, 128, 768],
               b_qn=[2, 96], b_kn=[2, 96], c_qn=[2, 64], c_kn=[2, 64], c_sink=[2, 8], d_qn=[2, 64],
               d_kn=[2, 64], m_norm=[2, 1024], w_mem_kv=[2, 1024, 1024], m_qn=[2, 128], m_kn=[2, 128],
               b_gate=[2, 5, 1024], w_br=[2, 5, 512, 1024], w_out=[2, 1024, 1024])


def build(S, L=2, dbg=(), phases="PABMCDG"):
    assert S % 2048 == 0
    NT, NQ = S // 128, S // 512
    TH = 2048
    NHALF = S // TH
    TPH = TH // 512
    nc = bass.Bass("TRN2", target_bir_lowering=False)

    def din(name, shape, dt=F32):
        return nc.dram_tensor(name, list(shape), dt, kind="ExternalInput").ap()

    def dscr(name, shape, dt):
        kind = "ExternalOutput" if name in dbg else "Internal"
        return nc.dram_tensor(name, list(shape), dt, kind=kind).ap()

    x_in = din("x", [S, DM])
    mem_in = din("mem", [256, DM])
    valid_in = din("valid", [128, NT])
    C = {k: din(k, v) for k, v in const_shapes(S).items()}
    Wt = {k: din(k, v) for k, v in WSHAPES.items()}
    y_out = nc.dram_tensor("y", [S, DM], F32, kind="ExternalOutput").ap()

    xT = [dscr("xT0", [8, 128, S], F32), dscr("xT1", [8, 128, S], F32)]
    aqT = dscr("aqT", [4, 128, S], BF16)
    akT = dscr("akT", [4, 128, S], BF16)
    avd = dscr("avd", [S, 512], BF16)
    bqT = dscr("bqT", [4, 96, S], BF16)
    bkT = dscr("bkT", [4, 96, S], BF16)
    bvd = dscr("bvd", [S, 512], BF16)
    mqT = dscr("mqT", [4, 128, S], BF16)
    zT = dscr("zT", [20, 128, S], BF16)
    gT = dscr("gT", [40, 128, S], BF16)
    uT = dscr("uT", [5, 4, 128, S], BF16)
    GH = [128 * d for d in GDIL]
    gqT = [dscr("gqT%d" % g, [8, 64, S], BF16) for g in range(4)]
    gkT = [dscr("gkT%d" % g, [GNKV[g], 64, S + 2 * GH[g]], BF16) for g in range(4)]
    gva = [dscr("gva%d" % g, [S + 2 * GH[g], GNKV[g], 65], BF16) for g in range(4)]

    with contextlib.ExitStack() as st:
        P = Prog(nc, st)
        arena_t = st.enter_context(nc.sbuf_tensor("arena", [128, ARN], F32))
        AR = Arena(P, arena_t, ARN)
        pbt = [st.enter_context(nc.psum_tensor("pb%d" % i, [128, 512], F32)) for i in range(8)]
        pb = []

        def new_pb():
            pb[:] = [P.buf("pb%d" % i, pbt[i][:]) for i in range(8)]

        new_pb()

        def dma(out, in_, reads=(), writes=(), own=None, q="sp", grp=None, slow=False):
            if slow:
                P.op(q, lambda e: e.dma_start(out=out, in_=in_, allow_slow_non_contiguous=True),
                     reads=reads, writes=writes, dma=own, grp=grp)
            else:
                P.op(q, lambda e: e.dma_start(out=out, in_=in_), reads=reads, writes=writes, dma=own, grp=grp)

        def load(b, src, sl=None, q="sp", own=None, grp=None, slow=False):
            dma(b.ap if sl is None else sl, src, writes=[b], own=own or b, q=q, grp=grp, slow=slow)

        def store(dst, b, sl=None, q="sp"):
            dma(dst, b.ap if sl is None else sl, reads=[b], own=b, q=q)

        def mm(ps, out, lhsT, rhs, start, stop, reads):
            P.op("pe", lambda e: e.matmul(out, lhsT=lhsT, rhs=rhs, start=start, stop=stop), reads=reads, writes=[ps])

        def act(func, out, in_, reads, writes, bias=0.0, scale=1.0):
            P.op("act", lambda e: e.activation(out=out, in_=in_, func=func, bias=bias, scale=scale),
                 reads=reads, writes=writes)

        def acopy(out, in_, reads, writes):
            P.op("act", lambda e: e.copy(out=out, in_=in_), reads=reads, writes=writes)

        def vcopy(out, in_, reads, writes, q="dve"):
            P.op(q, lambda e: e.tensor_copy(out=out, in_=in_), reads=reads, writes=writes)

        def tt(q, out, in0, in1, op, reads, writes):
            P.op(q, lambda e: e.tensor_tensor(out=out, in0=in0, in1=in1, op=op), reads=reads, writes=writes)

        def ts(q, out, in0, s1, op0, reads, writes):
            P.op(q, lambda e: e.tensor_scalar(out=out, in0=in0, scalar1=s1, scalar2=None, op0=op0),
                 reads=reads, writes=writes)

        def stt(out, in0, scalar, in1, op0, op1, reads, writes):
            P.op("dve", lambda e: e.scalar_tensor_tensor(out=out, in0=in0, scalar=scalar, in1=in1, op0=op0, op1=op1),
                 reads=reads, writes=writes)

        def recip(out, in_, reads, writes):
            P.op("dve", lambda e: e.reciprocal(out=out, in_=in_), reads=reads, writes=writes)

        def memset(q, b, val, sl=None):
            ap = b.ap if sl is None else sl
            P.op(q, lambda e: e.memset(ap, val), writes=[b])

        tmp_rot = {}

        def tmp(name, shape, dt, n=2):
            key = (name, tuple(shape), dt)
            if key not in tmp_rot:
                tmp_rot[key] = Rot([AR.alloc(name, shape, dt) for _ in range(n)])
            return tmp_rot[key].next()

        def phase_reset(off):
            AR.reset(off)
            tmp_rot.clear()

        parsem = AR.alloc("parsem", [128, 1], F32)
        eps_c = AR.alloc("eps", [128, 1], F32)
        memset("dve", eps_c, EPS)
        ident = AR.alloc("ident", [128, 128], F32)
        load(ident, C["ident"], own=parsem)
        valid = AR.alloc("valid", [128, NT], F32)
        load(valid, valid_in, own=parsem)
        cb = {}
        stg0 = AR.off
        for nm in ("ones128", "blk64", "blk32", "rot"):
            cb[nm] = AR.alloc(nm, [128, 128], BF16)
        CONST_END = AR.off
        stgs = {}
        for nm in ("ones128", "blk64", "blk32", "rot"):
            stgs[nm] = AR.alloc(nm + "f", [128, 128], F32)
            load(stgs[nm], C[nm], own=parsem)
        P.barrier()
        new_pb()
        for nm in ("ones128", "blk64", "blk32", "rot"):
            vcopy(cb[nm].ap, stgs[nm].ap, [stgs[nm]], [cb[nm]])
        ones128, rotm = cb["ones128"], cb["rot"]
        blk = {128: cb["ones128"], 64: cb["blk64"], 32: cb["blk32"]}
        zero_b = AR.alloc("zero", [128, 2048], BF16)
        memset("pool", zero_b, 0.0)
        for g in range(4):
            H = GH[g]
            for kv in range(GNKV[g]):
                for base in (0, H + S):
                    for o in range(0, H, 2048):
                        n = min(2048, H - o)
                        store(gkT[g][kv, :, base + o:base + o + n], zero_b, zero_b[0:64, 0:n])
            for base in (0, H + S):
                for o in range(0, H, 128):
                    store(gva[g][base + o:base + o + 128].rearrange("p k c -> p (k c)"), zero_b,
                          zero_b[:, 0:GNKV[g] * 65])

        def transpose_in():
            xt_r = Rot([AR.alloc("xt", [128, 4, DM], F32) for _ in range(2)])
            xo_r = Rot([AR.alloc("xo", [128, 8, 512], F32) for _ in range(2)])
            pr = Rot(pb[0:4])
            for t0 in range(0, S, 512):
                xt = xt_r.next()
                load(xt, x_in[t0:t0 + 512, :].rearrange("(k p) f -> p k f", p=128))
                xo = xo_r.next()
                for c in range(8):
                    ps = pr.next()
                    for k in range(4):
                        P.op("pe", lambda e, ps=ps, k=k, c=c, xt=xt: e.transpose(
                            out=ps[:, k * 128:(k + 1) * 128], in_=xt[:, k, c * 128:(c + 1) * 128], identity=ident.ap),
                            reads=[xt, ident], writes=[ps])
                    acopy(xo[:, c, :], ps.ap, [ps], [xo])
                store(xT[0][:, :, t0:t0 + 512].rearrange("c p j -> p c j"), xo)

        transpose_in()
        P.barrier()
        new_pb()

        def rms1(ps, n, w=512):
            qf = tmp("qf", [128, 512], F32, 3)
            sq = tmp("sq", [128, 512], BF16, 3)
            vcopy(qf[0:n, 0:w], ps[0:n, 0:w], [ps], [qf])
            tt("pool", sq[0:n, 0:w], qf[0:n, 0:w], qf[0:n, 0:w], ALU.mult, [qf], [sq])
            return qf, sq

        def rms2(qfs, sqs, n, group, gains, gbufs, outs, ps2, w=512, total=None):
            for i, sq in enumerate(sqs):
                mm(ps2, ps2[0:n, 0:w], blk[group][0:n, 0:n], sq[0:n, 0:w], i == 0, i == len(sqs) - 1, [sq, blk[group]])
            rt = tmp("rt", [128, 512], F32, 2)
            act(AF.Ln, rt[0:n, 0:w], ps2[0:n, 0:w], [ps2, eps_c], [rt], bias=eps_c[0:n, :],
                scale=1.0 / (total if total else group))
            rr = tmp("rr", [128, 512], F32, 2)
            act(AF.Exp, rr[0:n, 0:w], rt[0:n, 0:w], [rt], [rr], scale=-0.5)
            for q_, g_, gb, (ob, oap) in zip(qfs, gains, gbufs, outs):
                stt(oap, q_[0:n, 0:w], g_, rr[0:n, 0:w], ALU.mult, ALU.mult, [q_, rr, gb], [ob])

        def rope(xb, n, t0, ps, dests):
            cs = tmp("cs", [128, 2, 512], F32, 2)
            load(cs, C["cosT"][0:n, t0:t0 + 512], sl=cs[0:n, 0, :], grp=cs)
            load(cs, C["sinT"][0:n, t0:t0 + 512], sl=cs[0:n, 1, :], grp=cs)
            mm(ps, ps[0:n, :], rotm[0:n, 0:n], xb[0:n, :], True, True, [rotm, xb])
            t1 = tmp("rp1", [128, 512], F32, 2)
            t2 = tmp("rp2", [128, 512], F32, 2)
            tt("dve", t1[0:n, :], xb[0:n, :], cs[0:n, 0, :], ALU.mult, [xb, cs], [t1])
            tt("dve", t2[0:n, :], ps[0:n, :], cs[0:n, 1, :], ALU.mult, [ps, cs], [t2])
            ro = tmp("ro", [128, 512], BF16, 2)
            tt("pool", ro[0:n, :], t1[0:n, :], t2[0:n, :], ALU.add, [t1, t2], [ro])
            for (dt_, h, r0) in dests:
                store(dt_[h, 64:96, t0:t0 + 512], ro, ro[r0:r0 + 32, :])

        for l in range(L):
            phase_reset(CONST_END)
            lam_init = 0.8 - 0.6 * math.exp(-0.3 * l)

            def colvec(name, vec_ap, n, reps, mul=None):
                b = AR.alloc(name, [128, 1], F32)
                for r in range(reps):
                    dma(b[r * n:(r + 1) * n, :], vec_ap.rearrange("(p o) -> p o", o=1), writes=[b], own=parsem, grp=b, slow=True)
                return (b, n * reps, mul)

            def colmat(name, vec_ap, ncol):
                b = AR.alloc(name, [128, ncol], F32)
                load(b, vec_ap.rearrange("(c p) -> p c", p=128), own=parsem, slow=True)
                return b

            cols = {}
            cols["a_qn"] = colvec("a_qn", Wt["a_qn"][l], 64, 2, 64 ** -0.5)
            cols["a_kn"] = colvec("a_kn", Wt["a_kn"][l], 64, 2)
            cols["a_hn"] = colvec("a_hn", Wt["a_hn"][l], 128, 1, 1.0 - lam_init)
            cols["b_ckvn"] = colvec("b_ckvn", Wt["b_ckvn"][l], 128, 1)
            cols["b_qn_n"] = colvec("b_qn_n", Wt["b_qn"][l, 0:64], 64, 2, 96 ** -0.5)
            cols["b_qn_r"] = colvec("b_qn_r", Wt["b_qn"][l, 64:96], 32, 4, 96 ** -0.5)
            cols["b_kn_n"] = colvec("b_kn_n", Wt["b_kn"][l, 0:64], 64, 2)
            cols["b_kn_r"] = colvec("b_kn_r", Wt["b_kn"][l, 64:96], 32, 1)
            cols["c_qn"] = colvec("c_qn", Wt["c_qn"][l], 64, 2, 64 ** -0.5)
            cols["c_kn"] = colvec("c_kn", Wt["c_kn"][l], 64, 2)
            cols["d_qn"] = colvec("d_qn", Wt["d_qn"][l], 64, 2, 64 ** -0.5)
            cols["d_kn"] = colvec("d_kn", Wt["d_kn"][l], 64, 2)
            cols["m_qn"] = colvec("m_qn", Wt["m_qn"][l], 128, 1, 128 ** -0.5)
            cols["m_kn"] = colvec("m_kn", Wt["m_kn"][l], 128, 1)
            gcol = colmat("gcol", Wt["norm_g"][l], 8)
            b_cqn = colmat("b_cqn", Wt["b_cqn"][l], 2)
            mncol = colmat("mncol", Wt["m_norm"][l], 8)
            bgate = AR.alloc("bgate", [128, 5, 8], F32)
            for i in range(5):
                dma(bgate[:, i, :], Wt["b_gate"][l, i].rearrange("(c p) -> p c", p=128), writes=[bgate], own=parsem, grp=bgate, slow=True)
            lamt = AR.alloc("lamt", [128, 4, 64], F32)
            load(lamt, Wt["a_lam"][l].rearrange("a d -> (a d)").partition_broadcast(128), sl=lamt.ap.rearrange("p a d -> p (a d)"),
                 own=parsem)
            sinkt = AR.alloc("sinkt", [128, 8], F32)
            load(sinkt, Wt["c_sink"][l].partition_broadcast(128), own=parsem)
            P.barrier()
            new_pb()
            for k, (b, n, mul) in cols.items():
                if mul is not None:
                    ts("pool", b[0:n, :], b[0:n, :], float(mul), ALU.mult, [b], [b])
            col = {k: v[0] for k, v in cols.items()}
            lp = AR.alloc("lp", [128, 2, 64], F32)
            tt("dve", lp[:, 0, :], lamt[:, 0, :], lamt[:, 1, :], ALU.mult, [lamt], [lp])
            tt("dve", lp[:, 1, :], lamt[:, 2, :], lamt[:, 3, :], ALU.mult, [lamt], [lp])
            ls = AR.alloc("ls", [128, 2], F32)
            P.op("dve", lambda e, ls=ls, lp=lp: e.reduce_sum(out=ls.ap, in_=lp.ap, axis=mybir.AxisListType.X),
                 reads=[lp], writes=[ls])
            le = AR.alloc("le", [128, 2], F32)
            act(AF.Exp, le.ap, ls.ap, [ls], [le])
            neglam = AR.alloc("neglam", [128, 1], F32)
            tt("dve", neglam.ap, le[:, 1:2], le[:, 0:1], ALU.subtract, [le], [neglam])
            ts("dve", neglam.ap, neglam.ap, -lam_init, ALU.add, [neglam], [neglam])
            esink = AR.alloc("esink", [128, 8], F32)
            act(AF.Exp, esink.ap, sinkt.ap, [sinkt], [esink], bias=-SH_C)
            LAYER_END = AR.off

            if "P" in phases:
                wqbf = AR.alloc("wqbf", [128, 2, 384], F32)
                for h in range(4):
                    dma(wqbf[:, :, h * 64:(h + 1) * 64],
                        Wt["w_qb"][l][:, h * 96:h * 96 + 64].rearrange("(k p) n -> p k n", p=128),
                        writes=[wqbf], own=wqbf, grp=wqbf)
                    dma(wqbf[:, :, 256 + h * 32:256 + (h + 1) * 32],
                        Wt["w_qb"][l][:, h * 96 + 64:h * 96 + 96].rearrange("(k p) n -> p k n", p=128),
                        writes=[wqbf], own=wqbf, grp=wqbf)
                wqb = AR.alloc("wqb", [128, 2, 384], BF16)
                vcopy(wqb.ap, wqbf.ap, [wqbf], [wqb], q="pool")
                wkvf = AR.alloc("wkvf", [128, 768], F32)
                for h in range(4):
                    dma(wkvf[:, h * 64:(h + 1) * 64], Wt["w_kvb"][l][:, h * 192:h * 192 + 64],
                        writes=[wkvf], own=wkvf, grp=wkvf)
                    dma(wkvf[:, 256 + h * 128:256 + (h + 1) * 128], Wt["w_kvb"][l][:, h * 192 + 64:h * 192 + 192],
                        writes=[wkvf], own=wkvf, grp=wkvf)
                wkv = AR.alloc("wkv", [128, 768], BF16)
                vcopy(wkv.ap, wkvf.ap, [wkvf], [wkv], q="pool")
                ones_f = AR.alloc("ones_f", [128, 8], F32)
                memset("pool", ones_f, 1.0)
                xn_all = AR.alloc("xn", [128, 8, TH], BF16)
                xn_b = [P.buf("xnb%d" % i, xn_all[:, :, i * 512:(i + 1) * 512]) for i in range(TPH)]
                PH_BASE = AR.off
                w_in = Wt["w_in"][l]

                for half in range(NHALF):
                    phase_reset(PH_BASE)
                    xt_r = Rot([AR.alloc("xTt", [128, 8, 512], F32) for _ in range(2)])
                    sq8_r = Rot([AR.alloc("sq8", [128, 8, 512], BF16) for _ in range(1)])
                    for s in range(TPH):
                        t0 = half * TH + s * 512
                        xt_ = xt_r.next()
                        sq8 = sq8_r.next()
                        load(xt_, xT[l][:, :, t0:t0 + 512].rearrange("c p j -> p c j"))
                        tt("pool", sq8.ap, xt_.ap, xt_.ap, ALU.mult, [xt_], [sq8])
                        ps = pb[s % 2]
                        for c in range(8):
                            mm(ps, ps.ap, ones128.ap, sq8[:, c, :], c == 0, c == 7, [sq8, ones128])
                        rt = tmp("rt", [128, 512], F32, 2)
                        act(AF.Sqrt, rt.ap, ps.ap, [ps, eps_c], [rt], bias=eps_c.ap, scale=1.0 / DM)
                        rr = tmp("rr", [128, 512], F32, 2)
                        recip(rr.ap, rt.ap, [rt], [rr])
                        for c in range(8):
                            stt(xn_b[s][:, c, :], xt_[:, c, :], gcol[:, c:c + 1], rr.ap, ALU.mult, ALU.mult,
                                [xt_, rr, gcol], [xn_b[s]])
                    P.barrier()
                    new_pb()
                    phase_reset(PH_BASE)
                    wb_r = Rot([AR.alloc("wb", [128, 8, 128], BF16) for _ in range(4)])
                    wtb_r = Rot([AR.alloc("wtb", [128, 8, 512], BF16) for _ in range(2)])
                    stage_r = Rot([AR.alloc("stg", [128, 2048], BF16) for _ in range(3)])
                    main_rot = Rot(pb[0:3])
                    pss_rot = Rot(pb[3:5])
                    aux_rot = Rot(pb[5:8])

                    def load_w(segs, ncols):
                        wb = wb_r.next()
                        o = 0
                        for (c0, n) in segs:
                            dma(wb[:, :, o:o + n], w_in[:, c0:c0 + n].rearrange("(k p) n -> p k n", p=128),
                                writes=[wb], own=wb, grp=wb, q="pool")
                            o += n
                        return wb

                    def proj_fm(wb, ncols, s):
                        ps = main_rot.next()
                        for c in range(8):
                            mm(ps, ps[0:ncols, :], wb[:, c, 0:ncols], xn_b[s][:, c, :], c == 0, c == 7, [wb, xn_b[s]])
                        return ps

                    units = []

                    def fm_norm_unit(segs, group, gname, dests):
                        ncols = sum(n for _, n in segs)
                        units.append((lambda: load_w(segs, ncols),
                                      lambda wb: fm_norm_run(wb, ncols, group, gname, dests)))

                    def fm_norm_run(wb, ncols, group, gname, dests):
                        gb = col[gname]
                        pend = None
                        sg_new = None
                        for s in range(TPH + 1):
                            cur = None
                            if s < TPH:
                                if s % 4 == 0:
                                    sg_new = stage_r.next()
                                ps = proj_fm(wb, ncols, s)
                                qf, sq = rms1(ps, ncols)
                                cur = (qf, sq, s, sg_new)
                            if pend is not None:
                                qf_, sq_, s_, sg_ = pend
                                o = (s_ % 4) * 512
                                rms2([qf_], [sq_], ncols, group, [gb[0:ncols, :]], [gb],
                                     [(sg_, sg_[0:ncols, o:o + 512])], pss_rot.next())
                                if s_ % 4 == 3:
                                    t0 = half * TH + (s_ - 3) * 512
                                    for (dfn, r0, nr) in dests:
                                        store(dfn(t0, t0 + 2048), sg_, sg_[r0:r0 + nr, :])
                            pend = cur

                    def fm_act_unit(c0, func, bias_ap, bias_b, dst):
                        units.append((lambda: load_w([(c0, 128)], 128),
                                      lambda wb: fm_act_run(wb, func, bias_ap, bias_b, dst)))

                    def fm_act_run(wb, func, bias_ap, bias_b, dst):
                        sg_ = None
                        for s in range(TPH):
                            if s % 4 == 0:
                                sg_ = stage_r.next()
                            ps = proj_fm(wb, 128, s)
                            o = (s % 4) * 512
                            if bias_ap is None:
                                act(func, sg_[:, o:o + 512], ps.ap, [ps], [sg_])
                            else:
                                act(func, sg_[:, o:o + 512], ps.ap, [ps, bias_b], [sg_], bias=bias_ap)
                            if s % 4 == 3:
                                t0 = half * TH + (s - 3) * 512
                                store(dst[:, t0:t0 + 2048], sg_)

                    def load_wt(c0, ncols):
                        wtb = wtb_r.next()
                        dma(wtb[:, :, 0:ncols], w_in[:, c0:c0 + ncols].rearrange("(k p) n -> p k n", p=128),
                            writes=[wtb], own=wtb, q="pool")
                        return wtb

                    def tm_unit(c0, ncols, nh, aug, dst_fn):
                        units.append((lambda: load_wt(c0, ncols), lambda wtb: tm_run(wtb, ncols, nh, aug, dst_fn)))

                    def tm_run(wtb, ncols, nh, aug, dst_fn):
                        for s in range(TPH):
                            for k in range(4):
                                tok = half * TH + s * 512 + k * 128
                                ps = aux_rot.next()
                                for c in range(8):
                                    mm(ps, ps[:, 0:ncols], xn_b[s][:, c, k * 128:(k + 1) * 128], wtb[:, c, 0:ncols],
                                       c == 0, c == 7, [xn_b[s], wtb])
                                vcol = valid[:, tok // 128:tok // 128 + 1]
                                if not aug:
                                    vb = tmp("vb", [128, 512], BF16, 3)
                                    ts("dve", vb[:, 0:ncols], ps[:, 0:ncols], vcol, ALU.mult, [ps, valid], [vb])
                                    store(dst_fn(tok), vb, vb[:, 0:ncols])
                                else:
                                    va = tmp("va", [128, 8, 65], BF16, 3)
                                    ts("dve", va[:, 0:nh, 0:64], ps[:, 0:ncols].rearrange("p (h d) -> p h d", h=nh), vcol,
                                       ALU.mult, [ps, valid], [va])
                                    ts("dve", va[:, 0:nh, 64:65], ones_f[:, 0:nh].rearrange("p (h o) -> p h o", o=1), vcol,
                                       ALU.mult, [ones_f, valid], [va])
                                    store(dst_fn(tok), va, va[:, 0:nh, :])

                    for h in range(4):
                        fm_norm_unit([(OFF["a_q"] + h * 64, 64), (OFF["a_q"] + 256 + h * 64, 64)], 64, "a_qn",
                                     [(lambda a, b, h=h: aqT[h, :, a:b], 0, 128)])
                        fm_norm_unit([(OFF["a_k"] + h * 64, 64), (OFF["a_k"] + 256 + h * 64, 64)], 64, "a_kn",
                                     [(lambda a, b, h=h: akT[h, :, a:b], 0, 128)])
                    tm_unit(OFF["a_v"], 512, 0, False, lambda tok: avd[tok:tok + 128, :])
                    for b4 in range(4):
                        fm_norm_unit([(OFF["c_q"] + b4 * 128, 128)], 64, "c_qn",
                                     [(lambda a, b, hh=2 * b4: gqT[0][hh, :, a:b], 0, 64),
                                      (lambda a, b, hh=2 * b4 + 1: gqT[0][hh, :, a:b], 64, 64)])
                    fm_norm_unit([(OFF["c_k"], 128)], 64, "c_kn",
                                 [(lambda a, b: gkT[0][0, :, GH[0] + a:GH[0] + b], 0, 64),
                                  (lambda a, b: gkT[0][1, :, GH[0] + a:GH[0] + b], 64, 64)])
                    tm_unit(OFF["c_v"], 128, 2, True, lambda tok: gva[0][GH[0] + tok:GH[0] + tok + 128, :, :])
                    for g in range(3):
                        G = g + 1
                        for b4 in range(4):
                            fm_norm_unit([(OFF["d_q"] + g * 512 + b4 * 128, 128)], 64, "d_qn",
                                         [(lambda a, b, G=G, hh=2 * b4: gqT[G][hh, :, a:b], 0, 64),
                                          (lambda a, b, G=G, hh=2 * b4 + 1: gqT[G][hh, :, a:b], 64, 64)])
                            fm_norm_unit([(OFF["d_k"] + g * 512 + b4 * 128, 128)], 64, "d_kn",
                                         [(lambda a, b, G=G, hh=2 * b4: gkT[G][hh, :, GH[G] + a:GH[G] + b], 0, 64),
                                          (lambda a, b, G=G, hh=2 * b4 + 1: gkT[G][hh, :, GH[G] + a:GH[G] + b], 64, 64)])
                        tm_unit(OFF["d_v"] + g * 512, 512, 8, True,
                                lambda tok, G=G: gva[G][GH[G] + tok:GH[G] + tok + 128, :, :])
                    for h in range(4):
                        fm_norm_unit([(OFF["m_q"] + h * 128, 128)], 128, "m_qn", [(lambda a, b, h=h: mqT[h, :, a:b], 0, 128)])
                    for b20 in range(20):
                        fm_act_unit(OFF["z"] + b20 * 128, AF.Silu, None, None, zT[b20])
                    for b40 in range(40):
                        fm_act_unit(OFF["g"] + b40 * 128, AF.Sigmoid, bgate[:, b40 // 8, b40 % 8:b40 % 8 + 1], bgate, gT[b40])
                    wq = [units[0][0](), units[1][0]()]
                    for ui in range(len(units)):
                        if ui + 2 < len(units):
                            wq.append(units[ui + 2][0]())
                        units[ui][1](wq.pop(0))
                    wcq = [load_w([(OFF["b_cq"] + i * 128, 128)], 128) for i in range(2)]
                    for s in range(TPH):
                        t0 = half * TH + s * 512
                        qfs, sqs = [], []
                        for i in range(2):
                            ps = proj_fm(wcq[i], 128, s)
                            qf, sq = rms1(ps, 128)
                            qfs.append(qf)
                            sqs.append(sq)
                        cqn = tmp("cqn", [128, 2, 512], BF16, 2)
                        rms2(qfs, sqs, 128, 128, [b_cqn[:, 0:1], b_cqn[:, 1:2]], [b_cqn, b_cqn],
                             [(cqn, cqn[:, 0, :]), (cqn, cqn[:, 1, :])], pss_rot.next(), total=256)
                        for ob in range(3):
                            ps = aux_rot.next()
                            for i in range(2):
                                mm(ps, ps.ap, wqb[:, i, ob * 128:(ob + 1) * 128], cqn[:, i, :], i == 0, i == 1, [wqb, cqn])
                            qf, sq = rms1(ps, 128)
                            qo = tmp("qo", [128, 512], BF16, 3)
                            if ob < 2:
                                rms2([qf], [sq], 128, 64, [col["b_qn_n"].ap], [col["b_qn_n"]], [(qo, qo.ap)], pss_rot.next())
                                for hh in range(2):
                                    store(bqT[2 * ob + hh, 0:64, t0:t0 + 512], qo, qo[hh * 64:(hh + 1) * 64, :])
                            else:
                                rms2([qf], [sq], 128, 32, [col["b_qn_r"].ap], [col["b_qn_r"]], [(qo, qo.ap)], pss_rot.next())
                                rope(qo, 128, t0, aux_rot.next(), [(bqT, h, h * 32) for h in range(4)])
                    wckv = load_w([(OFF["b_ckv"], 128)], 128)
                    wkr = load_w([(OFF["b_kr"], 32)], 32)
                    for s in range(TPH):
                        t0 = half * TH + s * 512
                        ps = proj_fm(wckv, 128, s)
                        qf, sq = rms1(ps, 128)
                        ckvn = tmp("ckvn", [128, 512], BF16, 2)
                        rms2([qf], [sq], 128, 128, [col["b_ckvn"].ap], [col["b_ckvn"]], [(ckvn, ckvn.ap)], pss_rot.next())
                        for ob in range(2):
                            ps = aux_rot.next()
                            mm(ps, ps.ap, wkv[:, ob * 128:(ob + 1) * 128], ckvn.ap, True, True, [wkv, ckvn])
                            qf, sq = rms1(ps, 128)
                            ko = tmp("qo", [128, 512], BF16, 3)
                            rms2([qf], [sq], 128, 64, [col["b_kn_n"].ap], [col["b_kn_n"]], [(ko, ko.ap)], pss_rot.next())
                            for hh in range(2):
                                store(bkT[2 * ob + hh, 0:64, t0:t0 + 512], ko, ko[hh * 64:(hh + 1) * 64, :])
                        for k in range(4):
                            tok = t0 + k * 128
                            ps = aux_rot.next()
                            mm(ps, ps.ap, ckvn[:, k * 128:(k + 1) * 128], wkv[:, 256:768], True, True, [ckvn, wkv])
                            vb = tmp("vb", [128, 512], BF16, 3)
                            ts("dve", vb.ap, ps.ap, valid[:, tok // 128:tok // 128 + 1], ALU.mult, [ps, valid], [vb])
                            store(bvd[tok:tok + 128, :], vb)
                        ps = proj_fm(wkr, 32, s)
                        qf, sq = rms1(ps, 32)
                        ko = tmp("qo", [128, 512], BF16, 3)
                        rms2([qf], [sq], 32, 32, [col["b_kn_r"][0:32, :]], [col["b_kn_r"]], [(ko, ko[0:32, :])], pss_rot.next())
                        rope(ko, 32, t0, aux_rot.next(), [(bkT, h, 0) for h in range(4)])
                    P.barrier()
                    new_pb()

            def attn_core(maps, kts, vfn, vreads, onesfn, onesreads, bias, shift, po, pd, sps):
                def qk(kt):
                    res = []
                    for mi, m in enumerate(maps):
                        ps = sps[mi].next()
                        mm(ps, ps.ap, m["k"](kt), m["q"], True, True, m["reads"])
                        res.append(ps)
                    return res
                if isinstance(kts, int):
                    kts = list(range(kts))
                cur = qk(kts[0])
                for ki, kt in enumerate(kts):
                    nxt = qk(kts[ki + 1]) if ki + 1 < len(kts) else None
                    for mi in range(len(maps)):
                        ps = cur[mi]
                        pT = tmp("pT", [128, 512], BF16, 4)
                        if bias is None:
                            act(AF.Exp, pT.ap, ps.ap, [ps], [pT], bias=-shift)
                        else:
                            e_ap, e_buf, const = bias(mi, kt)
                            p0 = tmp("p0", [128, 512], BF16, 4)
                            act(AF.Exp, p0.ap, ps.ap, [ps], [p0], bias=const - shift)
                            tt("dve" if mi == 0 else "pool", pT.ap, p0.ap, e_ap, ALU.mult, [p0, e_buf], [pT])
                        mm(po[mi], po[mi].ap, vfn(mi, kt), pT.ap, ki == 0, ki == len(kts) - 1, [pT] + vreads)
                        mm(pd[mi], pd[mi].ap, onesfn(kt), pT.ap, ki == 0, ki == len(kts) - 1, [pT] + onesreads)
                    cur = nxt

            phase_reset(LAYER_END)
            onesv = AR.alloc("onesv", [128, NT, 128], BF16)
            ones_ff = AR.alloc("ones_ff", [128, 128], F32)
            memset("pool", ones_ff, 1.0)
            for t in range(NT):
                ts("pool", onesv[:, t, :], ones_ff.ap, valid[:, t:t + 1], ALU.mult, [ones_ff, valid], [onesv])
            P.barrier()
            new_pb()
            ATT_BASE = AR.off

            if "A" in phases:
                phase_reset(ATT_BASE)
                sets = Rot([(AR.alloc("aK", [128, S], BF16), AR.alloc("aQ", [128, S], BF16),
                             AR.alloc("aV", [128, NT, 128], BF16)) for _ in range(2 if S <= 4096 else 1)])
                ea_rot = Rot([AR.alloc("EAh", [128, 6, 512], BF16) for _ in range(2)])
                for h in range(4):
                    Kb, Qb, Vb = sets.next()
                    load(Kb, akT[h])
                    load(Qb, aqT[h])
                    load(Vb, avd[:, h * 128:(h + 1) * 128].rearrange("(t p) d -> p t d", p=128))
                    sl = SL4[h]
                    EAh = ea_rot.next()
                    load(EAh, C["EA"][h], q="pool")
                    for qt in range(NQ):
                        q0 = qt * 512
                        zt = tmp("zt", [128, 512], BF16, 2)
                        load(zt, zT[h][:, q0:q0 + 512])

                        def bias(mi, kt, q0=q0, sl=sl, EAh=EAh):
                            k0 = kt * 128
                            if k0 < q0:
                                return EAh[:, 0, :], EAh, -sl * (q0 - k0 - 127)
                            if k0 < q0 + 512:
                                return EAh[:, 2 + (k0 - q0) // 128, :], EAh, 0.0
                            return EAh[:, 1, :], EAh, -sl * (k0 - q0 - 511)

                        maps = [dict(k=lambda kt, m=m, Kb=Kb: Kb[64 * m:64 * m + 64, kt * 128:(kt + 1) * 128],
                                     q=Qb[64 * m:64 * m + 64, q0:q0 + 512], reads=[Kb, Qb]) for m in range(2)]
                        po, pd = [pb[4], pb[6]], [pb[5], pb[7]]
                        kts = [kt for kt in range(NT)
                               if sl * max(0, kt * 128 - (q0 + 511), q0 - (kt * 128 + 127)) < 140.0]
                        attn_core(maps, kts, lambda mi, kt, Vb=Vb: Vb[:, kt, :], [Vb], lambda kt: onesv[:, kt, :], [onesv],
                                  bias, SH_A, po, pd, [Rot([pb[0], pb[1]]), Rot([pb[2], pb[3]])])
                        tn = []
                        for m in range(2):
                            rd = tmp("rd", [128, 512], F32, 2)
                            ts("dve", rd.ap, pd[m].ap, 1e-30, ALU.max, [pd[m]], [rd])
                            recip(rd.ap, rd.ap, [rd], [rd])
                            t_ = tmp("tn", [128, 512], F32, 2)
                            tt("dve", t_.ap, po[m].ap, rd.ap, ALU.mult, [po[m], rd], [t_])
                            tn.append(t_)
                        o_ = tmp("ao", [128, 512], F32, 2)
                        stt(o_.ap, tn[1].ap, neglam.ap, tn[0].ap, ALU.mult, ALU.add, [tn[0], tn[1], neglam], [o_])
                        sq = tmp("sq", [128, 512], BF16, 2)
                        tt("pool", sq.ap, o_.ap, o_.ap, ALU.mult, [o_], [sq])
                        on = tmp("on", [128, 512], F32, 2)
                        rms2([o_], [sq], 128, 128, [col["a_hn"].ap], [col["a_hn"]], [(on, on.ap)], pb[0])
                        u_ = tmp("u", [128, 512], BF16, 2)
                        tt("pool", u_.ap, on.ap, zt.ap, ALU.mult, [on, zt], [u_])
                        store(uT[0, h, :, q0:q0 + 512], u_)

            if "B" in phases:
                P.barrier()
                new_pb()
                phase_reset(ATT_BASE)
                sets = Rot([(AR.alloc("bK", [128, S], BF16), AR.alloc("bQ", [128, S], BF16),
                             AR.alloc("bV", [128, NT, 128], BF16)) for _ in range(2 if S <= 4096 else 1)])
                porot = Rot([(pb[4], pb[5]), (pb[6], pb[7])])
                for h in range(4):
                    Kb, Qb, Vb = sets.next()
                    load(Kb, bkT[h], sl=Kb[0:96, :])
                    load(Qb, bqT[h], sl=Qb[0:96, :])
                    load(Vb, bvd[:, h * 128:(h + 1) * 128].rearrange("(t p) d -> p t d", p=128))
                    for qt in range(NQ):
                        q0 = qt * 512
                        zt = tmp("zt", [128, 512], BF16, 2)
                        load(zt, zT[4 + h][:, q0:q0 + 512])
                        maps = [dict(k=lambda kt, Kb=Kb: Kb[0:96, kt * 128:(kt + 1) * 128], q=Qb[0:96, q0:q0 + 512],
                                     reads=[Kb, Qb])]
                        po_, pd_ = porot.next()
                        attn_core(maps, NT, lambda mi, kt, Vb=Vb: Vb[:, kt, :], [Vb], lambda kt: onesv[:, kt, :], [onesv],
                                  None, SH_B, [po_], [pd_], [Rot([pb[0], pb[1], pb[2], pb[3]])])
                        rd = tmp("rd", [128, 512], F32, 2)
                        ts("dve", rd.ap, pd_.ap, 1e-30, ALU.max, [pd_], [rd])
                        recip(rd.ap, rd.ap, [rd], [rd])
                        t_ = tmp("tn", [128, 512], F32, 2)
                        tt("dve", t_.ap, po_.ap, rd.ap, ALU.mult, [po_, rd], [t_])
                        u_ = tmp("u", [128, 512], BF16, 2)
                        tt("pool", u_.ap, t_.ap, zt.ap, ALU.mult, [t_, zt], [u_])
                        store(uT[1, h, :, q0:q0 + 512], u_)

            if "M" in phases:
                P.barrier()
                new_pb()
                phase_reset(ATT_BASE)
                mt = AR.alloc("mt", [128, 2, DM], F32)
                load(mt, mem_in.rearrange("(k p) f -> p k f", p=128))
                memT = AR.alloc("memT", [128, 8, 256], F32)
                for c in range(8):
                    ps = pb[c % 4]
                    for k in range(2):
                        P.op("pe", lambda e, ps=ps, k=k, c=c: e.transpose(
                            out=ps[:, k * 128:(k + 1) * 128], in_=mt[:, k, c * 128:(c + 1) * 128], identity=ident.ap),
                            reads=[mt, ident], writes=[ps])
                    acopy(memT[:, c, :], ps[:, 0:256], [ps], [memT])
                sqm = AR.alloc("sqm", [128, 8, 256], BF16)
                tt("pool", sqm.ap, memT.ap, memT.ap, ALU.mult, [memT], [sqm])
                ps = pb[4]
                for c in range(8):
                    mm(ps, ps[:, 0:256], ones128.ap, sqm[:, c, :], c == 0, c == 7, [sqm, ones128])
                rtm = AR.alloc("rtm", [128, 256], F32)
                act(AF.Sqrt, rtm.ap, ps[:, 0:256], [ps, eps_c], [rtm], bias=eps_c.ap, scale=1.0 / DM)
                rrm = AR.alloc("rrm", [128, 256], F32)
                recip(rrm.ap, rtm.ap, [rtm], [rrm])
                mnT = AR.alloc("mnT", [128, 8, 256], BF16)
                for c in range(8):
                    stt(mnT[:, c, :], memT[:, c, :], mncol[:, c:c + 1], rrm.ap, ALU.mult, ALU.mult, [memT, rrm, mncol], [mnT])
                wmf = AR.alloc("wmf", [128, 8, 512], F32)
                wmk = AR.alloc("wmk", [128, 8, 512], BF16)
                wmv = AR.alloc("wmv", [128, 8, 512], BF16)
                load(wmf, Wt["w_mem_kv"][l][:, 0:512].rearrange("(k p) n -> p k n", p=128))
                vcopy(wmk.ap, wmf.ap, [wmf], [wmk], q="pool")
                load(wmf, Wt["w_mem_kv"][l][:, 512:1024].rearrange("(k p) n -> p k n", p=128))
                vcopy(wmv.ap, wmf.ap, [wmf], [wmv], q="pool")
                mkT = AR.alloc("mkT", [128, 4, 256], BF16)
                mv = AR.alloc("mv", [128, 2, 512], BF16)
                for h in range(4):
                    ps = pb[h % 2]
                    for c in range(8):
                        mm(ps, ps[:, 0:256], wmk[:, c, h * 128:(h + 1) * 128], mnT[:, c, :], c == 0, c == 7, [wmk, mnT])
                    qf, sq = rms1(ps, 128, w=256)
                    rms2([qf], [sq], 128, 128, [col["m_kn"].ap], [col["m_kn"]], [(mkT, mkT[:, h, :])], pb[2 + h % 2], w=256)
                for k in range(2):
                    ps = pb[4 + k]
                    for c in range(8):
                        mm(ps, ps.ap, mnT[:, c, k * 128:(k + 1) * 128], wmv[:, c, :], c == 0, c == 7, [mnT, wmv])
                    vcopy(mv[:, k, :], ps.ap, [ps], [mv])
                porot = Rot([(pb[4], pb[5]), (pb[6], pb[7])])
                srot = Rot([pb[0], pb[1], pb[2], pb[3]])
                for qt in range(NQ):
                    q0 = qt * 512
                    mq = tmp("mq", [128, 4, 512], BF16, 2)
                    load(mq, mqT[:, :, q0:q0 + 512].rearrange("h p j -> p h j"))
                    zt4 = tmp("zt4", [128, 4, 512], BF16, 2)
                    load(zt4, zT[16:20, :, q0:q0 + 512].rearrange("h p j -> p h j"))
                    for h in range(4):
                        maps = [dict(k=lambda kt, h=h: mkT[:, h, kt * 128:(kt + 1) * 128], q=mq[:, h, :], reads=[mkT, mq])]
                        po_, pd_ = porot.next()
                        attn_core(maps, 2, lambda mi, kt, h=h: mv[:, kt, h * 128:(h + 1) * 128], [mv],
                                  lambda kt: ones128.ap, [ones128], None, SH_M, [po_], [pd_], [srot])
                        rd = tmp("rd", [128, 512], F32, 2)
                        ts("dve", rd.ap, pd_.ap, 1e-30, ALU.max, [pd_], [rd])
                        recip(rd.ap, rd.ap, [rd], [rd])
                        t_ = tmp("tn", [128, 512], F32, 2)
                        tt("dve", t_.ap, po_.ap, rd.ap, ALU.mult, [po_, rd], [t_])
                        u_ = tmp("u", [128, 512], BF16, 2)
                        tt("pool", u_.ap, t_.ap, zt4[:, h, :], ALU.mult, [t_, zt4], [u_])
                        store(uT[4, h, :, q0:q0 + 512], u_)

            def banded(groups, branch, use_sink, shift):
                P.barrier()
                new_pb()
                phase_reset(LAYER_END)
                UNIT = 2048
                accO = AR.alloc("accO", [64, 4, UNIT], F32)
                accD = AR.alloc("accD", [64, 4, UNIT], F32)
                Qb = AR.alloc("gQ", [64, 4, UNIT], BF16)
                maxspan = max(128 * GDIL[G] for G in groups)
                nk = 1 if groups == [0] else 4
                Kb = AR.alloc("gK", [64, nk, UNIT + 2 * maxspan], BF16)
                porot = Rot([(pb[4], pb[5]), (pb[6], pb[7])])
                srot = Rot([pb[0], pb[1], pb[2], pb[3]])
                eb_rot = Rot([AR.alloc("EBt", [128, 4, 384], BF16) for _ in range(2)])
                for u0 in range(0, S, UNIT):
                    for hh in range(2):
                        for gi, G in enumerate(groups):
                            dil, H = GDIL[G], GH[G]
                            span = 128 * dil
                            EBt = eb_rot.next()
                            load(EBt, C["EB"][G, hh * 4:hh * 4 + 4].rearrange("h p c -> p h c"), q="pool")
                            for j in range(4):
                                load(Qb, gqT[G][hh * 4 + j, :, u0:u0 + UNIT], sl=Qb[:, j, :], grp=Qb)
                            for j in range(nk):
                                kvh = hh if G == 0 else hh * 4 + j
                                load(Kb, gkT[G][kvh, :, H + u0 - span:H + u0 + UNIT + span],
                                     sl=Kb[:, j, 0:UNIT + 2 * span], grp=Kb)
                            NTL = 3 if G == 0 else 2
                            TW = NTL * 128
                            pend = []

                            def push(fn):
                                pend.append(fn)
                                if len(pend) > 3:
                                    pend.pop(0)()

                            for blk_i in range(UNIT // 128):
                                sp_i, r = blk_i // dil, blk_i % dil
                                qoff = sp_i * span + r
                                koffs = [sp_i * span + r + ((m * 128) if G == 0 else (64 + m * 128)) * dil for m in range(NTL)]
                                Vt = tmp("Vt", [128, 3, 4, 65], BF16, 3)
                                ov = tmp("ov", [128, 3, 64], BF16, 3)
                                for m in range(NTL):
                                    r0 = u0 + koffs[m]
                                    if G == 0:
                                        load(Vt, gva[G][r0:r0 + 127 * dil + 1:dil, hh:hh + 1, :], sl=Vt[:, m, 0:1, :], grp=Vt)
                                    else:
                                        load(Vt, gva[G][r0:r0 + 127 * dil + 1:dil, hh * 4:hh * 4 + 4, :], sl=Vt[:, m, :, :], grp=Vt)
                                vcopy(ov[:, 0:NTL, :], Vt[:, 0:NTL, 0, 64:65].to_broadcast([128, NTL, 64]), [Vt], [ov], q="pool")
                                po_, pd_ = porot.next()
                                for j in range(4):
                                    h = hh * 4 + j
                                    kj = 0 if G == 0 else j
                                    vj = 0 if G == 0 else j
                                    ps = srot.next()
                                    for m in range(NTL):
                                        mm(ps, ps[:, m * 128:(m + 1) * 128], Kb[:, kj, koffs[m]:koffs[m] + 127 * dil + 1:dil],
                                           Qb[:, j, qoff:qoff + 127 * dil + 1:dil], True, True, [Kb, Qb])
                                    p0 = tmp("p03", [128, 384], BF16, 5)
                                    act(AF.Exp, p0[:, 0:TW], ps[:, 0:TW], [ps], [p0], bias=-shift)
                                    pT = tmp("pT3", [128, 384], BF16, 5)
                                    tt("pool", pT[:, 0:TW], p0[:, 0:TW], EBt[:, j, 0:TW], ALU.mult, [p0, EBt], [pT])

                                    def stage2(j=j, vj=vj, pT=pT, Vt=Vt, ov=ov, po_=po_, pd_=pd_, qoff=qoff, NTL=NTL, gi=gi, dil=dil):
                                        for m in range(NTL):
                                            mm(po_, po_[0:64, j * 128:(j + 1) * 128], Vt[:, m, vj, 0:64], pT[:, m * 128:(m + 1) * 128],
                                               m == 0, m == NTL - 1, [Vt, pT])
                                        for m in range(NTL):
                                            mm(pd_, pd_[0:64, j * 128:(j + 1) * 128], ov[:, m, :], pT[:, m * 128:(m + 1) * 128],
                                               m == 0, m == NTL - 1, [ov, pT])
                                        if j == 3:
                                            aO = accO[:, :, qoff:qoff + 127 * dil + 1:dil]
                                            aD = accD[:, :, qoff:qoff + 127 * dil + 1:dil]
                                            pov = po_[0:64, :].rearrange("p (j q) -> p j q", j=4)
                                            pdv = pd_[0:64, :].rearrange("p (j q) -> p j q", j=4)
                                            if gi == 0:
                                                vcopy(aO, pov, [po_], [accO])
                                                acopy(aD, pdv, [pd_], [accD])
                                            else:
                                                tt("dve", aO, pov, aO, ALU.add, [po_, accO], [accO])
                                                tt("dve", aD, pdv, aD, ALU.add, [pd_, accD], [accD])
                                    push(stage2)
                            while pend:
                                pend.pop(0)()
                        if use_sink:
                            for j in range(4):
                                ts("dve", accD[:, j, :], accD[:, j, :], esink[0:64, hh * 4 + j:hh * 4 + j + 1], ALU.add,
                                   [accD, esink], [accD])
                        ts("dve", accD.ap, accD.ap, 1e-30, ALU.max, [accD], [accD])
                        recip(accD.ap, accD.ap, [accD], [accD])
                        tt("dve", accO.ap, accO.ap, accD.ap, ALU.mult, [accO, accD], [accO])
                        for c0 in range(0, UNIT, 512):
                            zt = tmp("gz", [64, 4, 512], BF16, 2)
                            for j in range(4):
                                h = hh * 4 + j
                                load(zt, zT[branch * 4 + h // 2, (h % 2) * 64:(h % 2) * 64 + 64, u0 + c0:u0 + c0 + 512],
                                     sl=zt[:, j, :], grp=zt)
                            u_ = tmp("gu", [64, 4, 512], BF16, 2)
                            tt("pool", u_.ap, accO[:, :, c0:c0 + 512], zt.ap, ALU.mult, [accO, zt], [u_])
                            for j in range(4):
                                h = hh * 4 + j
                                store(uT[branch, h // 2, (h % 2) * 64:(h % 2) * 64 + 64, u0 + c0:u0 + c0 + 512], u_, u_[:, j, :])

            if "C" in phases:
                banded([0], 2, True, SH_C)
            if "D" in phases:
                banded([1, 2, 3], 3, False, SH_D)
            P.barrier()
            new_pb()

            if "G" in phases:
                phase_reset(LAYER_END)
                wbr = AR.alloc("wbr", [128, 5, 4, DM], BF16)
                wout = AR.alloc("wout", [128, 8, DM], BF16)
                WEND = AR.off
                wsf = AR.alloc("wsf", [128, 4, DM], F32)
                for i in range(5):
                    load(wsf, Wt["w_br"][l, i].rearrange("(c p) n -> p c n", p=128))
                    vcopy(wbr[:, i, :, :], wsf.ap, [wsf], [wbr], q="pool")
                for hf in range(2):
                    load(wsf, Wt["w_out"][l][hf * 512:(hf + 1) * 512, :].rearrange("(c p) n -> p c n", p=128))
                    vcopy(wout[:, hf * 4:(hf + 1) * 4, :], wsf.ap, [wsf], [wout], q="pool")
                P.barrier()
                new_pb()
                phase_reset(WEND)
                acc = AR.alloc("acc", [128, 8, 512], F32)
                accb = AR.alloc("accb", [128, 8, 512], BF16)
                mrot = Rot(pb[0:6])
                for tq in range(NQ):
                    t0 = tq * 512
                    xt_ = tmp("xTt", [128, 8, 512], F32, 2)
                    load(xt_, xT[l][:, :, t0:t0 + 512].rearrange("c p j -> p c j"))
                    for i in range(5):
                        ut = tmp("ut", [128, 4, 512], BF16, 3)
                        load(ut, uT[i, :, :, t0:t0 + 512].rearrange("c p j -> p c j"))
                        gt = tmp("gt", [128, 8, 512], BF16, 3)
                        load(gt, gT[i * 8:(i + 1) * 8, :, t0:t0 + 512].rearrange("b p j -> p b j"))
                        for j in range(8):
                            ps = mrot.next()
                            for c in range(4):
                                mm(ps, ps.ap, wbr[:, i, c, j * 128:(j + 1) * 128], ut[:, c, :], c == 0, c == 3, [wbr, ut])
                            if i == 0:
                                tt("dve", acc[:, j, :], ps.ap, gt[:, j, :], ALU.mult, [ps, gt], [acc])
                            else:
                                tm_ = tmp("tm", [128, 512], F32, 3)
                                tt("dve", tm_.ap, ps.ap, gt[:, j, :], ALU.mult, [ps, gt], [tm_])
                                tt("pool", acc[:, j, :], acc[:, j, :], tm_.ap, ALU.add, [acc, tm_], [acc])
                    acopy(accb.ap, acc.ap, [acc], [accb])
                    for j in range(8):
                        ps = mrot.next()
                        for c in range(8):
                            mm(ps, ps.ap, wout[:, c, j * 128:(j + 1) * 128], accb[:, c, :], c == 0, c == 7, [wout, accb])
                        tt("dve", xt_[:, j, :], ps.ap, xt_[:, j, :], ALU.add, [ps, xt_], [xt_])
                    if l < L - 1:
                        store(xT[l + 1][:, :, t0:t0 + 512].rearrange("c p j -> p c j"), xt_)
                    else:
                        for k in range(4):
                            yt = tmp("yt", [128, DM], F32, 2)
                            for hf in range(2):
                                ps = pb[6 + hf]
                                for c4 in range(4):
                                    c = hf * 4 + c4
                                    P.op("pe", lambda e, ps=ps, c=c, c4=c4, k=k, xt_=xt_: e.transpose(
                                        out=ps[:, c4 * 128:(c4 + 1) * 128], in_=xt_[:, c, k * 128:(k + 1) * 128],
                                        identity=ident.ap), reads=[xt_, ident], writes=[ps])
                                acopy(yt[:, hf * 512:(hf + 1) * 512], ps.ap, [ps], [yt])
                            store(y_out[t0 + k * 128:t0 + (k + 1) * 128, :], yt)
                P.barrier()
                new_pb()
        P.emit()
        print("ops", len(P.ops), "sems", P.nsem)
    return nc


_NC_CACHE = {}


def _core_inputs(xs, mems, valids, weights, S):
    consts = const_tables(S)
    maps = []
    for x, m, v in zip(xs, mems, valids):
        d = {"x": x, "mem": m, "valid": v}
        d.update(consts)
        d.update(weights)
        maps.append(d)
    return maps


def kernel(x_prompt, x_sample, mem_prompt, mem_sample, **w):
    S = 8192
    x_prompt = np.asarray(x_prompt, np.float32)
    x_sample = np.asarray(x_sample, np.float32)
    weights = {k: np.ascontiguousarray(np.asarray(v, np.float32)) for k, v in w.items()}
    xs, mems, valids = [], [], []
    for b in range(4):
        xp = np.zeros((S, DM), np.float32)
        xp[:x_prompt.shape[1]] = x_prompt[b]
        xs.append(xp)
        mems.append(np.ascontiguousarray(np.asarray(mem_prompt[b], np.float32)))
        v = np.zeros((S,), np.float32)
        v[:x_prompt.shape[1]] = 1.0
        valids.append(np.ascontiguousarray(v.reshape(S // 128, 128).T))
    for b in range(4):
        xs.append(np.ascontiguousarray(x_sample[b]))
        mems.append(np.ascontiguousarray(np.asarray(mem_sample[b], np.float32)))
        valids.append(np.ones((128, S // 128), np.float32))
    if S not in _NC_CACHE:
        _NC_CACHE[S] = build(S)
    nc = _NC_CACHE[S]
    in_maps = _core_inputs(xs, mems, valids, weights, S)
    res = run_bass_kernel_spmd(nc, in_maps, core_ids=list(range(8)))
    SP = x_prompt.shape[1]
    y_prompt = np.stack([np.asarray(res.results[b]["y"], np.float32)[:SP] for b in range(4)], axis=0)
    y_sample = np.stack([np.asarray(res.results[4 + b]["y"], np.float32) for b in range(4)], axis=0)
    return (y_prompt, y_sample)
```

```python
import contextlib
import math
import numpy as np
import concourse.bass as bass
import concourse.mybir as mybir
from concourse.bass_utils import run_bass_kernel_spmd

F32 = mybir.dt.float32
BF16 = mybir.dt.bfloat16
AF = mybir.ActivationFunctionType
ALU = mybir.AluOpType

COMPUTE = ("pe", "act", "dve", "pool")
QUEUES = ("pe", "act", "dve", "pool", "sp")

DM = 1024
N_IN = 15520
OFF = dict(a_q=0, a_k=512, a_v=1024, b_cq=1536, b_ckv=1792, b_kr=1920, c_q=1952, c_k=2464, c_v=2592,
           d_q=2720, d_k=4256, d_v=5792, m_q=7328, z=7840, g=10400)
EPS = 1e-6
BIG = 1.0e6
SH_A, SH_B, SH_C, SH_D, SH_M = 8.0, 10.0, 8.0, 8.0, 11.5
SL4 = [2.0 ** (-8.0 * (i + 1) / 4) for i in range(4)]
SL8 = [2.0 ** (-8.0 * (i + 1) / 8) for i in range(8)]
GDIL = (1, 1, 4, 16)
GNKV = (2, 8, 8, 8)
ARN = 44000


class Sem:
    __slots__ = ("h", "total")


class Buf:
    __slots__ = ("name", "ap", "last_w", "readers", "sem")

    def __init__(self, name, ap):
        self.name, self.ap = name, ap
        self.last_w, self.readers, self.sem = None, [], None

    def __getitem__(self, idx):
        return self.ap[idx]


class Op:
    __slots__ = ("q", "fn", "dma", "deps", "inc", "val", "dsem", "dval", "grp")


class Prog:
    def __init__(self, nc, stack):
        self.nc, self.stack = nc, stack
        self.ops, self.bufs, self.nsem = [], [], 0
        self.free_sems, self.live_sems = [], []

    def buf(self, name, ap):
        b = Buf(name, ap)
        self.bufs.append(b)
        return b

    def _newsem(self, name):
        self.nsem += 1
        return self.stack.enter_context(self.nc.semaphore(name))

    def _dsem(self):
        if self.free_sems:
            s = self.free_sems.pop()
        else:
            s = Sem()
            s.h, s.total = self._newsem("d%d" % self.nsem), 0
        self.live_sems.append(s)
        return s

    def op(self, q, fn, reads=(), writes=(), dma=None, grp=None):
        o = Op()
        o.q, o.fn, o.dma, o.grp = q, fn, dma, grp
        o.inc, o.val, o.dsem, o.dval = False, None, None, None
        deps = set()
        me = "dma" if dma is not None else q
        for b in reads:
            w = b.last_w
            if w is not None:
                we = "dma" if w.dma is not None else w.q
                if not (we == me and me == "pe"):
                    deps.add(w)
        for b in writes:
            w = b.last_w
            if w is not None:
                we = "dma" if w.dma is not None else w.q
                if (we != me or me == "dma") and not (grp is not None and w.grp is grp):
                    deps.add(w)
            for r in b.readers:
                re_ = "dma" if r.dma is not None else r.q
                if re_ != me or me == "dma":
                    deps.add(r)
        o.deps = deps
        for d in deps:
            if d.dma is None:
                d.inc = True
        if dma is not None:
            if dma.sem is None:
                dma.sem = self._dsem()
            dma.sem.total += 16
            o.dsem, o.dval = dma.sem, dma.sem.total
        for b in reads:
            b.readers.append(o)
        for b in writes:
            b.last_w = o
            b.readers = []
        self.ops.append(o)
        return o

    def barrier(self):
        last = {}
        for o in self.ops:
            if o.dma is None and o.fn is not None:
                last[o.q] = o
        for q in QUEUES:
            o = Op()
            o.q, o.fn, o.dma, o.grp = q, None, None, None
            o.inc, o.val, o.dsem, o.dval = False, None, None, None
            o.deps = set(v for k, v in last.items() if k != q)
            for d in o.deps:
                d.inc = True
            o.deps |= set(("D", s, s.total) for s in self.live_sems)
            self.ops.append(o)
        self.free_sems.extend(self.live_sems)
        self.live_sems = []
        for b in self.bufs:
            b.last_w, b.readers, b.sem = None, [], None

    def emit(self):
        nc = self.nc
        esem = {q: self._newsem("e_" + q) for q in COMPUTE}
        cnt = {q: 0 for q in COMPUTE}
        for o in self.ops:
            if o.dma is None and o.inc:
                cnt[o.q] += 1
                o.val = cnt[o.q]
        byq = {q: [] for q in QUEUES}
        for o in self.ops:
            byq[o.q].append(o)
        allsems = self.free_sems + self.live_sems
        block = self.stack.enter_context(nc.Block())
        handles = {"pe": block.tensor, "act": block.scalar, "dve": block.vector,
                   "pool": block.gpsimd, "sp": block.sync}

        def run_queue(q, e):
            known = {}
            for o in byq[q]:
                need = {}
                for d in o.deps:
                    if isinstance(d, tuple):
                        _, s, v = d
                        key, sem = ("D", id(s)), s.h
                    elif d.dma is not None:
                        key, sem, v = ("D", id(d.dsem)), d.dsem.h, d.dval
                    else:
                        key, sem, v = ("E", d.q), esem[d.q], d.val
                    if v > known.get(key, 0) and v > need.get(key, (None, 0))[1]:
                        need[key] = (sem, v)
                for key, (sem, v) in need.items():
                    e.wait_ge(sem, v)
                    known[key] = v
                if o.fn is None:
                    continue
                ins = o.fn(e)
                if o.dma is not None:
                    ins.then_inc(o.dsem.h, 16)
                elif o.inc:
                    ins.then_inc(esem[q], 1)
            if q == "sp":
                for s in allsems:
                    if s.total > known.get(("D", id(s)), 0):
                        e.wait_ge(s.h, s.total)

        for q in QUEUES:
            handles[q](lambda e, q=q: run_queue(q, e))


class Arena:
    def __init__(self, P, t, ncols):
        self.P, self.t, self.ncols, self.off, self.n = P, t, ncols, 0, 0

    def reset(self, off=0):
        self.off = off

    def alloc(self, name, shape, dt):
        per = 1
        for s in shape[1:]:
            per *= s
        nb = per * (2 if dt == BF16 else 4)
        n4 = (nb + 3) // 4
        assert self.off + n4 <= self.ncols, ("SBUF arena overflow", name, self.off, n4, self.ncols)
        ap = self.t[0:shape[0], self.off:self.off + n4]
        if dt == BF16:
            ap = ap.bitcast(BF16)[:, 0:per]
        if len(shape) == 3:
            ap = ap.rearrange("p (a b) -> p a b", a=shape[1])
        elif len(shape) == 4:
            ap = ap.rearrange("p (a b c) -> p a b c", a=shape[1], b=shape[2])
        self.off += n4
        self.n += 1
        return self.P.buf(name + "_" + str(self.n), ap)


class Rot:
    def __init__(self, items):
        self.items, self.i = items, 0

    def next(self):
        v = self.items[self.i % len(self.items)]
        self.i += 1
        return v


def const_tables(S):
    c = {}
    c["ident"] = np.eye(128, dtype=np.float32)
    c["ones128"] = np.ones((128, 128), np.float32)
    ii = np.arange(128)
    c["blk64"] = (ii[:, None] // 64 == ii[None, :] // 64).astype(np.float32)
    c["blk32"] = (ii[:, None] // 32 == ii[None, :] // 32).astype(np.float32)
    R = np.zeros((128, 128), np.float32)
    for m in range(128):
        if m % 32 < 16:
            R[m + 16, m] = -1.0
        else:
            R[m - 16, m] = 1.0
    c["rot"] = R
    i = np.arange(128, dtype=np.float32)[:, None]
    j = np.arange(512, dtype=np.float32)[None, :]
    c["tabA_L"] = (j - i).astype(np.float32)
    c["tabA_D"] = np.stack([np.abs(j - i - 128.0 * m) for m in range(4)], axis=1).astype(np.float32)
    q = np.arange(128, dtype=np.float32)[None, :]
    tabs = []
    for m in range(3):
        rel = (m - 1) * 128 + i - q
        tabs.append(np.where(np.abs(rel) <= 128, np.abs(rel), BIG))
    c["tabB128"] = np.concatenate(tabs, axis=1).astype(np.float32)
    tabs = []
    for m in range(3):
        rel = (m * 128 - 64) + i - q
        tabs.append(np.where(np.abs(rel) <= 64, np.abs(rel), BIG))
    c["tabB64"] = np.concatenate(tabs, axis=1).astype(np.float32)
    i64 = np.arange(128, dtype=np.float64)[:, None]
    j64 = np.arange(512, dtype=np.float64)[None, :]
    EA = np.zeros((4, 128, 6, 512), np.float32)
    for h in range(4):
        EA[h, :, 0, :] = np.exp(-SL4[h] * (j64 - i64 + 127.0))
        EA[h, :, 1, :] = np.exp(-SL4[h] * (i64 - j64 + 511.0))
        for m in range(4):
            EA[h, :, 2 + m, :] = np.exp(-SL4[h] * np.abs(j64 - i64 - 128.0 * m))
    c["EA"] = EA
    EB = np.zeros((4, 8, 128, 384), np.float32)
    for G in range(4):
        tab = c["tabB128"] if G == 0 else c["tabB64"]
        for h in range(8):
            EB[G, h] = np.where(tab < BIG, np.exp(-SL8[h] * GDIL[G] * tab.astype(np.float64)), 0.0)
    c["EB"] = EB
    half = 16
    inv = (np.float32(10000.0) ** (-np.arange(half, dtype=np.float32) / np.float32(half))).astype(np.float32)
    ang = (np.arange(S, dtype=np.float32)[:, None] * inv[None, :]).astype(np.float32)
    cos, sin = np.cos(ang).astype(np.float32), np.sin(ang).astype(np.float32)
    idx = np.arange(128) % 16
    c["cosT"] = np.ascontiguousarray(cos[:, idx].T)
    c["sinT"] = np.ascontiguousarray(sin[:, idx].T)
    return c


def const_shapes(S):
    return dict(ident=[128, 128], ones128=[128, 128], blk64=[128, 128], blk32=[128, 128], rot=[128, 128],
                tabA_L=[128, 512], tabA_D=[128, 4, 512], tabB128=[128, 384], tabB64=[128, 384],
                cosT=[128, S], sinT=[128, S], EA=[4, 128, 6, 512], EB=[4, 8, 128, 384])


WSHAPES = dict(norm_g=[2, 1024], w_in=[2, 1024, N_IN], a_qn=[2, 64], a_kn=[2, 64], a_lam=[2, 4, 64],
               a_hn=[2, 128], b_cqn=[2, 256], b_ckvn=[2, 128], w_qb=[2, 256, 384], w_kvb=[2, 128, 768],
               b_qn=[2, 96], b_kn=[2, 96], c_qn=[2, 64], c_kn=[2, 64], c_sink=[2, 8], d_qn=[2, 64],
               d_kn=[2, 64], m_norm=[2, 1024], w_mem_kv=[2, 1024, 1024], m_qn=[2, 128], m_kn=[2, 128],
               b_gate=[2, 5, 1024], w_br=[2, 5, 512, 1024], w_out=[2, 1024, 1024])


def build(S, L=2, dbg=(), phases="PABMCDG"):
    assert S % 2048 == 0
    NT, NQ = S // 128, S // 512
    TH = 2048
    NHALF = S // TH
    TPH = TH // 512
    nc = bass.Bass("TRN2", target_bir_lowering=False)

    def din(name, shape, dt=F32):
        return nc.dram_tensor(name, list(shape), dt, kind="ExternalInput").ap()

    def dscr(name, shape, dt):
        kind = "ExternalOutput" if name in dbg else "Internal"
        return nc.dram_tensor(name, list(shape), dt, kind=kind).ap()

    x_in = din("x", [S, DM])
    mem_in = din("mem", [256, DM])
    valid_in = din("valid", [128, NT])
    C = {k: din(k, v) for k, v in const_shapes(S).items()}
    Wt = {k: din(k, v) for k, v in WSHAPES.items()}
    y_out = nc.dram_tensor("y", [S, DM], F32, kind="ExternalOutput").ap()

    xT = [dscr("xT0", [8, 128, S], F32), dscr("xT1", [8, 128, S], F32)]
    aqT = dscr("aqT", [4, 128, S], BF16)
    akT = dscr("akT", [4, 128, S], BF16)
    avd = dscr("avd", [S, 512], BF16)
    bqT = dscr("bqT", [4, 96, S], BF16)
    bkT = dscr("bkT", [4, 96, S], BF16)
    bvd = dscr("bvd", [S, 512], BF16)
    mqT = dscr("mqT", [4, 128, S], BF16)
    zT = dscr("zT", [20, 128, S], BF16)
    gT = dscr("gT", [40, 128, S], BF16)
    uT = dscr("uT", [5, 4, 128, S], BF16)
    GH = [128 * d for d in GDIL]
    gqT = [dscr("gqT%d" % g, [8, 64, S], BF16) for g in range(4)]
    gkT = [dscr("gkT%d" % g, [GNKV[g], 64, S + 2 * GH[g]], BF16) for g in range(4)]
    gva = [dscr("gva%d" % g, [S + 2 * GH[g], GNKV[g], 65], BF16) for g in range(4)]

    with contextlib.ExitStack() as st:
        P = Prog(nc, st)
        arena_t = st.enter_context(nc.sbuf_tensor("arena", [128, ARN], F32))
        AR = Arena(P, arena_t, ARN)
        pbt = [st.enter_context(nc.psum_tensor("pb%d" % i, [128, 512], F32)) for i in range(8)]
        pb = []

        def new_pb():
            pb[:] = [P.buf("pb%d" % i, pbt[i][:]) for i in range(8)]

        new_pb()

        def dma(out, in_, reads=(), writes=(), own=None, q="sp", grp=None, slow=False):
            if slow:
                P.op(q, lambda e: e.dma_start(out=out, in_=in_, allow_slow_non_contiguous=True),
                     reads=reads, writes=writes, dma=own, grp=grp)
            else:
                P.op(q, lambda e: e.dma_start(out=out, in_=in_), reads=reads, writes=writes, dma=own, grp=grp)

        def load(b, src, sl=None, q="sp", own=None, grp=None, slow=False):
            dma(b.ap if sl is None else sl, src, writes=[b], own=own or b, q=q, grp=grp, slow=slow)

        def store(dst, b, sl=None, q="sp"):
            dma(dst, b.ap if sl is None else sl, reads=[b], own=b, q=q)

        def mm(ps, out, lhsT, rhs, start, stop, reads):
            P.op("pe", lambda e: e.matmul(out, lhsT=lhsT, rhs=rhs, start=start, stop=stop), reads=reads, writes=[ps])

        def act(func, out, in_, reads, writes, bias=0.0, scale=1.0):
            P.op("act", lambda e: e.activation(out=out, in_=in_, func=func, bias=bias, scale=scale),
                 reads=reads, writes=writes)

        def acopy(out, in_, reads, writes):
            P.op("act", lambda e: e.copy(out=out, in_=in_), reads=reads, writes=writes)

        def vcopy(out, in_, reads, writes, q="dve"):
            P.op(q, lambda e: e.tensor_copy(out=out, in_=in_), reads=reads, writes=writes)

        def tt(q, out, in0, in1, op, reads, writes):
            P.op(q, lambda e: e.tensor_tensor(out=out, in0=in0, in1=in1, op=op), reads=reads, writes=writes)

        def ts(q, out, in0, s1, op0, reads, writes):
            P.op(q, lambda e: e.tensor_scalar(out=out, in0=in0, scalar1=s1, scalar2=None, op0=op0),
                 reads=reads, writes=writes)

        def stt(out, in0, scalar, in1, op0, op1, reads, writes):
            P.op("dve", lambda e: e.scalar_tensor_tensor(out=out, in0=in0, scalar=scalar, in1=in1, op0=op0, op1=op1),
                 reads=reads, writes=writes)

        def recip(out, in_, reads, writes):
            P.op("dve", lambda e: e.reciprocal(out=out, in_=in_), reads=reads, writes=writes)

        def srecip(out_b, out_ap, in_ap, in_bufs):
            ts("dve", out_ap, in_ap, 1e-18, ALU.max, in_bufs, [out_b])
            act(AF.Ln, out_ap, out_ap, [out_b], [out_b])
            act(AF.Exp, out_ap, out_ap, [out_b], [out_b], scale=-1.0)

        def memset(q, b, val, sl=None):
            ap = b.ap if sl is None else sl
            P.op(q, lambda e: e.memset(ap, val), writes=[b])

        tmp_rot = {}

        def tmp(name, shape, dt, n=2):
            key = (name, tuple(shape), dt)
            if key not in tmp_rot:
                tmp_rot[key] = Rot([AR.alloc(name, shape, dt) for _ in range(n)])
            return tmp_rot[key].next()

        def phase_reset(off):
            AR.reset(off)
            tmp_rot.clear()

        parsem = AR.alloc("parsem", [128, 1], F32)
        eps_c = AR.alloc("eps", [128, 1], F32)
        memset("dve", eps_c, EPS)
        ident = AR.alloc("ident", [128, 128], F32)
        load(ident, C["ident"], own=parsem)
        valid = AR.alloc("valid", [128, NT], F32)
        load(valid, valid_in, own=parsem)
        cb = {}
        stg0 = AR.off
        for nm in ("ones128", "blk64", "blk32", "rot"):
            cb[nm] = AR.alloc(nm, [128, 128], BF16)
        CONST_END = AR.off
        stgs = {}
        for nm in ("ones128", "blk64", "blk32", "rot"):
            stgs[nm] = AR.alloc(nm + "f", [128, 128], F32)
            load(stgs[nm], C[nm], own=parsem)
        P.barrier()
        new_pb()
        for nm in ("ones128", "blk64", "blk32", "rot"):
            vcopy(cb[nm].ap, stgs[nm].ap, [stgs[nm]], [cb[nm]])
        ones128, rotm = cb["ones128"], cb["rot"]
        blk = {128: cb["ones128"], 64: cb["blk64"], 32: cb["blk32"]}
        zero_b = AR.alloc("zero", [128, 2048], BF16)
        memset("pool", zero_b, 0.0)
        for g in range(4):
            H = GH[g]
            for kv in range(GNKV[g]):
                for base in (0, H + S):
                    for o in range(0, H, 2048):
                        n = min(2048, H - o)
                        store(gkT[g][kv, :, base + o:base + o + n], zero_b, zero_b[0:64, 0:n])
            for base in (0, H + S):
                for o in range(0, H, 128):
                    store(gva[g][base + o:base + o + 128].rearrange("p k c -> p (k c)"), zero_b,
                          zero_b[:, 0:GNKV[g] * 65])

        def transpose_in():
            xt_r = Rot([AR.alloc("xt", [128, 4, DM], F32) for _ in range(2)])
            xo_r = Rot([AR.alloc("xo", [128, 8, 512], F32) for _ in range(2)])
            pr = Rot(pb[0:4])
            for t0 in range(0, S, 512):
                xt = xt_r.next()
                load(xt, x_in[t0:t0 + 512, :].rearrange("(k p) f -> p k f", p=128))
                xo = xo_r.next()
                for c in range(8):
                    ps = pr.next()
                    for k in range(4):
                        P.op("pe", lambda e, ps=ps, k=k, c=c, xt=xt: e.transpose(
                            out=ps[:, k * 128:(k + 1) * 128], in_=xt[:, k, c * 128:(c + 1) * 128], identity=ident.ap),
                            reads=[xt, ident], writes=[ps])
                    acopy(xo[:, c, :], ps.ap, [ps], [xo])
                store(xT[0][:, :, t0:t0 + 512].rearrange("c p j -> p c j"), xo)

        transpose_in()
        P.barrier()
        new_pb()

        def rms1(ps, n, w=512):
            qf = tmp("qf", [128, 512], F32, 3)
            sq = tmp("sq", [128, 512], BF16, 3)
            vcopy(qf[0:n, 0:w], ps[0:n, 0:w], [ps], [qf])
            tt("pool", sq[0:n, 0:w], qf[0:n, 0:w], qf[0:n, 0:w], ALU.mult, [qf], [sq])
            return qf, sq

        def rms2(qfs, sqs, n, group, gains, gbufs, outs, ps2, w=512, total=None):
            for i, sq in enumerate(sqs):
                mm(ps2, ps2[0:n, 0:w], blk[group][0:n, 0:n], sq[0:n, 0:w], i == 0, i == len(sqs) - 1, [sq, blk[group]])
            rt = tmp("rt", [128, 512], F32, 2)
            act(AF.Ln, rt[0:n, 0:w], ps2[0:n, 0:w], [ps2, eps_c], [rt], bias=eps_c[0:n, :],
                scale=1.0 / (total if total else group))
            rr = tmp("rr", [128, 512], F32, 2)
            act(AF.Exp, rr[0:n, 0:w], rt[0:n, 0:w], [rt], [rr], scale=-0.5)
            for q_, g_, gb, (ob, oap) in zip(qfs, gains, gbufs, outs):
                stt(oap, q_[0:n, 0:w], g_, rr[0:n, 0:w], ALU.mult, ALU.mult, [q_, rr, gb], [ob])

        def rope(xb, n, t0, ps, dests):
            cs = tmp("cs", [128, 2, 512], F32, 2)
            load(cs, C["cosT"][0:n, t0:t0 + 512], sl=cs[0:n, 0, :], grp=cs)
            load(cs, C["sinT"][0:n, t0:t0 + 512], sl=cs[0:n, 1, :], grp=cs)
            mm(ps, ps[0:n, :], rotm[0:n, 0:n], xb[0:n, :], True, True, [rotm, xb])
            t1 = tmp("rp1", [128, 512], F32, 2)
            t2 = tmp("rp2", [128, 512], F32, 2)
            tt("dve", t1[0:n, :], xb[0:n, :], cs[0:n, 0, :], ALU.mult, [xb, cs], [t1])
            tt("dve", t2[0:n, :], ps[0:n, :], cs[0:n, 1, :], ALU.mult, [ps, cs], [t2])
            ro = tmp("ro", [128, 512], BF16, 2)
            tt("pool", ro[0:n, :], t1[0:n, :], t2[0:n, :], ALU.add, [t1, t2], [ro])
            for (dt_, h, r0) in dests:
                store(dt_[h, 64:96, t0:t0 + 512], ro, ro[r0:r0 + 32, :])

        for l in range(L):
            phase_reset(CONST_END)
            lam_init = 0.8 - 0.6 * math.exp(-0.3 * l)

            def colvec(name, vec_ap, n, reps, mul=None):
                b = AR.alloc(name, [128, 1], F32)
                for r in range(reps):
                    dma(b[r * n:(r + 1) * n, :], vec_ap.rearrange("(p o) -> p o", o=1), writes=[b], own=parsem, grp=b, slow=True)
                return (b, n * reps, mul)

            def colmat(name, vec_ap, ncol):
                b = AR.alloc(name, [128, ncol], F32)
                load(b, vec_ap.rearrange("(c p) -> p c", p=128), own=parsem, slow=True)
                return b

            cols = {}
            cols["a_qn"] = colvec("a_qn", Wt["a_qn"][l], 64, 2, 64 ** -0.5)
            cols["a_kn"] = colvec("a_kn", Wt["a_kn"][l], 64, 2)
            cols["a_hn"] = colvec("a_hn", Wt["a_hn"][l], 128, 1, 1.0 - lam_init)
            cols["b_ckvn"] = colvec("b_ckvn", Wt["b_ckvn"][l], 128, 1)
            cols["b_qn_n"] = colvec("b_qn_n", Wt["b_qn"][l, 0:64], 64, 2, 96 ** -0.5)
            cols["b_qn_r"] = colvec("b_qn_r", Wt["b_qn"][l, 64:96], 32, 4, 96 ** -0.5)
            cols["b_kn_n"] = colvec("b_kn_n", Wt["b_kn"][l, 0:64], 64, 2)
            cols["b_kn_r"] = colvec("b_kn_r", Wt["b_kn"][l, 64:96], 32, 1)
            cols["c_qn"] = colvec("c_qn", Wt["c_qn"][l], 64, 2, 64 ** -0.5)
            cols["c_kn"] = colvec("c_kn", Wt["c_kn"][l], 64, 2)
            cols["d_qn"] = colvec("d_qn", Wt["d_qn"][l], 64, 2, 64 ** -0.5)
            cols["d_kn"] = colvec("d_kn", Wt["d_kn"][l], 64, 2)
            cols["m_qn"] = colvec("m_qn", Wt["m_qn"][l], 128, 1, 128 ** -0.5)
            cols["m_kn"] = colvec("m_kn", Wt["m_kn"][l], 128, 1)
            gcol = colmat("gcol", Wt["norm_g"][l], 8)
            b_cqn = colmat("b_cqn", Wt["b_cqn"][l], 2)
            mncol = colmat("mncol", Wt["m_norm"][l], 8)
            bgate = AR.alloc("bgate", [128, 5, 8], F32)
            for i in range(5):
                dma(bgate[:, i, :], Wt["b_gate"][l, i].rearrange("(c p) -> p c", p=128), writes=[bgate], own=parsem, grp=bgate, slow=True)
            lamt = AR.alloc("lamt", [128, 4, 64], F32)
            load(lamt, Wt["a_lam"][l].rearrange("a d -> (a d)").partition_broadcast(128), sl=lamt.ap.rearrange("p a d -> p (a d)"),
                 own=parsem)
            sinkt = AR.alloc("sinkt", [128, 8], F32)
            load(sinkt, Wt["c_sink"][l].partition_broadcast(128), own=parsem)
            P.barrier()
            new_pb()
            for k, (b, n, mul) in cols.items():
                if mul is not None:
                    ts("pool", b[0:n, :], b[0:n, :], float(mul), ALU.mult, [b], [b])
            col = {k: v[0] for k, v in cols.items()}
            lp = AR.alloc("lp", [128, 2, 64], F32)
            tt("dve", lp[:, 0, :], lamt[:, 0, :], lamt[:, 1, :], ALU.mult, [lamt], [lp])
            tt("dve", lp[:, 1, :], lamt[:, 2, :], lamt[:, 3, :], ALU.mult, [lamt], [lp])
            ls = AR.alloc("ls", [128, 2], F32)
            P.op("dve", lambda e, ls=ls, lp=lp: e.reduce_sum(out=ls.ap, in_=lp.ap, axis=mybir.AxisListType.X),
                 reads=[lp], writes=[ls])
            le = AR.alloc("le", [128, 2], F32)
            act(AF.Exp, le.ap, ls.ap, [ls], [le])
            neglam = AR.alloc("neglam", [128, 1], F32)
            tt("dve", neglam.ap, le[:, 1:2], le[:, 0:1], ALU.subtract, [le], [neglam])
            ts("dve", neglam.ap, neglam.ap, -lam_init, ALU.add, [neglam], [neglam])
            esink = AR.alloc("esink", [128, 8], F32)
            act(AF.Exp, esink.ap, sinkt.ap, [sinkt], [esink], bias=-SH_C)
            LAYER_END = AR.off

            if "P" in phases:
                wqbf = AR.alloc("wqbf", [128, 2, 384], F32)
                for h in range(4):
                    dma(wqbf[:, :, h * 64:(h + 1) * 64],
                        Wt["w_qb"][l][:, h * 96:h * 96 + 64].rearrange("(k p) n -> p k n", p=128),
                        writes=[wqbf], own=wqbf, grp=wqbf)
                    dma(wqbf[:, :, 256 + h * 32:256 + (h + 1) * 32],
                        Wt["w_qb"][l][:, h * 96 + 64:h * 96 + 96].rearrange("(k p) n -> p k n", p=128),
                        writes=[wqbf], own=wqbf, grp=wqbf)
                wqb = AR.alloc("wqb", [128, 2, 384], BF16)
                vcopy(wqb.ap, wqbf.ap, [wqbf], [wqb], q="pool")
                wkvf = AR.alloc("wkvf", [128, 768], F32)
                for h in range(4):
                    dma(wkvf[:, h * 64:(h + 1) * 64], Wt["w_kvb"][l][:, h * 192:h * 192 + 64],
                        writes=[wkvf], own=wkvf, grp=wkvf)
                    dma(wkvf[:, 256 + h * 128:256 + (h + 1) * 128], Wt["w_kvb"][l][:, h * 192 + 64:h * 192 + 192],
                        writes=[wkvf], own=wkvf, grp=wkvf)
                wkv = AR.alloc("wkv", [128, 768], BF16)
                vcopy(wkv.ap, wkvf.ap, [wkvf], [wkv], q="pool")
                ones_f = AR.alloc("ones_f", [128, 8], F32)
                memset("pool", ones_f, 1.0)
                xn_all = AR.alloc("xn", [128, 8, TH], BF16)
                xn_b = [P.buf("xnb%d" % i, xn_all[:, :, i * 512:(i + 1) * 512]) for i in range(TPH)]
                PH_BASE = AR.off
                w_in = Wt["w_in"][l]

                for half in range(NHALF):
                    phase_reset(PH_BASE)
                    xt_r = Rot([AR.alloc("xTt", [128, 8, 512], F32) for _ in range(2)])
                    sq8_r = Rot([AR.alloc("sq8", [128, 8, 512], BF16) for _ in range(1)])
                    for s in range(TPH):
                        t0 = half * TH + s * 512
                        xt_ = xt_r.next()
                        sq8 = sq8_r.next()
                        load(xt_, xT[l][:, :, t0:t0 + 512].rearrange("c p j -> p c j"))
                        tt("pool", sq8.ap, xt_.ap, xt_.ap, ALU.mult, [xt_], [sq8])
                        ps = pb[s % 2]
                        for c in range(8):
                            mm(ps, ps.ap, ones128.ap, sq8[:, c, :], c == 0, c == 7, [sq8, ones128])
                        rt = tmp("rt", [128, 512], F32, 2)
                        act(AF.Ln, rt.ap, ps.ap, [ps, eps_c], [rt], bias=eps_c.ap, scale=1.0 / DM)
                        rr = tmp("rr", [128, 512], F32, 2)
                        act(AF.Exp, rr.ap, rt.ap, [rt], [rr], scale=-0.5)
                        for c in range(8):
                            stt(xn_b[s][:, c, :], xt_[:, c, :], gcol[:, c:c + 1], rr.ap, ALU.mult, ALU.mult,
                                [xt_, rr, gcol], [xn_b[s]])
                    P.barrier()
                    new_pb()
                    phase_reset(PH_BASE)
                    wb_r = Rot([AR.alloc("wb", [128, 8, 128], BF16) for _ in range(4)])
                    wtb_r = Rot([AR.alloc("wtb", [128, 8, 512], BF16) for _ in range(2)])
                    stage_r = Rot([AR.alloc("stg", [128, 2048], BF16) for _ in range(3)])
                    main_rot = Rot(pb[0:3])
                    pss_rot = Rot(pb[3:5])
                    aux_rot = Rot(pb[5:8])

                    def load_w(segs, ncols):
                        wb = wb_r.next()
                        o = 0
                        for (c0, n) in segs:
                            dma(wb[:, :, o:o + n], w_in[:, c0:c0 + n].rearrange("(k p) n -> p k n", p=128),
                                writes=[wb], own=wb, grp=wb, q="pool")
                            o += n
                        return wb

                    def proj_fm(wb, ncols, s):
                        ps = main_rot.next()
                        for c in range(8):
                            mm(ps, ps[0:ncols, :], wb[:, c, 0:ncols], xn_b[s][:, c, :], c == 0, c == 7, [wb, xn_b[s]])
                        return ps

                    units = []

                    def fm_norm_unit(segs, group, gname, dests):
                        ncols = sum(n for _, n in segs)
                        units.append((lambda: load_w(segs, ncols),
                                      lambda wb: fm_norm_run(wb, ncols, group, gname, dests)))

                    def fm_norm_run(wb, ncols, group, gname, dests):
                        gb = col[gname]
                        pend = None
                        sg_new = None
                        for s in range(TPH + 1):
                            cur = None
                            if s < TPH:
                                if s % 4 == 0:
                                    sg_new = stage_r.next()
                                ps = proj_fm(wb, ncols, s)
                                qf, sq = rms1(ps, ncols)
                                cur = (qf, sq, s, sg_new)
                            if pend is not None:
                                qf_, sq_, s_, sg_ = pend
                                o = (s_ % 4) * 512
                                rms2([qf_], [sq_], ncols, group, [gb[0:ncols, :]], [gb],
                                     [(sg_, sg_[0:ncols, o:o + 512])], pss_rot.next())
                                if s_ % 4 == 3:
                                    t0 = half * TH + (s_ - 3) * 512
                                    for (dfn, r0, nr) in dests:
                                        store(dfn(t0, t0 + 2048), sg_, sg_[r0:r0 + nr, :])
                            pend = cur

                    def fm_act_unit(c0, func, bias_ap, bias_b, dst):
                        units.append((lambda: load_w([(c0, 128)], 128),
                                      lambda wb: fm_act_run(wb, func, bias_ap, bias_b, dst)))

                    def fm_act_run(wb, func, bias_ap, bias_b, dst):
                        sg_ = None
                        for s in range(TPH):
                            if s % 4 == 0:
                                sg_ = stage_r.next()
                            ps = proj_fm(wb, 128, s)
                            o = (s % 4) * 512
                            if bias_ap is None:
                                act(func, sg_[:, o:o + 512], ps.ap, [ps], [sg_])
                            else:
                                act(func, sg_[:, o:o + 512], ps.ap, [ps, bias_b], [sg_], bias=bias_ap)
                            if s % 4 == 3:
                                t0 = half * TH + (s - 3) * 512
                                store(dst[:, t0:t0 + 2048], sg_)

                    def load_wt(c0, ncols):
                        wtb = wtb_r.next()
                        dma(wtb[:, :, 0:ncols], w_in[:, c0:c0 + ncols].rearrange("(k p) n -> p k n", p=128),
                            writes=[wtb], own=wtb, q="pool")
                        return wtb

                    def tm_unit(c0, ncols, nh, aug, dst_fn):
                        units.append((lambda: load_wt(c0, ncols), lambda wtb: tm_run(wtb, ncols, nh, aug, dst_fn)))

                    def tm_run(wtb, ncols, nh, aug, dst_fn):
                        for s in range(TPH):
                            for k in range(4):
                                tok = half * TH + s * 512 + k * 128
                                ps = aux_rot.next()
                                for c in range(8):
                                    mm(ps, ps[:, 0:ncols], xn_b[s][:, c, k * 128:(k + 1) * 128], wtb[:, c, 0:ncols],
                                       c == 0, c == 7, [xn_b[s], wtb])
                                vcol = valid[:, tok // 128:tok // 128 + 1]
                                if not aug:
                                    vb = tmp("vb", [128, 512], BF16, 3)
                                    ts("dve", vb[:, 0:ncols], ps[:, 0:ncols], vcol, ALU.mult, [ps, valid], [vb])
                                    store(dst_fn(tok), vb, vb[:, 0:ncols])
                                else:
                                    va = tmp("va", [128, 8, 65], BF16, 3)
                                    ts("dve", va[:, 0:nh, 0:64], ps[:, 0:ncols].rearrange("p (h d) -> p h d", h=nh), vcol,
                                       ALU.mult, [ps, valid], [va])
                                    ts("dve", va[:, 0:nh, 64:65], ones_f[:, 0:nh].rearrange("p (h o) -> p h o", o=1), vcol,
                                       ALU.mult, [ones_f, valid], [va])
                                    store(dst_fn(tok), va, va[:, 0:nh, :])

                    for h in range(4):
                        fm_norm_unit([(OFF["a_q"] + h * 64, 64), (OFF["a_q"] + 256 + h * 64, 64)], 64, "a_qn",
                                     [(lambda a, b, h=h: aqT[h, :, a:b], 0, 128)])
                        fm_norm_unit([(OFF["a_k"] + h * 64, 64), (OFF["a_k"] + 256 + h * 64, 64)], 64, "a_kn",
                                     [(lambda a, b, h=h: akT[h, :, a:b], 0, 128)])
                    tm_unit(OFF["a_v"], 512, 0, False, lambda tok: avd[tok:tok + 128, :])
                    for b4 in range(4):
                        fm_norm_unit([(OFF["c_q"] + b4 * 128, 128)], 64, "c_qn",
                                     [(lambda a, b, hh=2 * b4: gqT[0][hh, :, a:b], 0, 64),
                                      (lambda a, b, hh=2 * b4 + 1: gqT[0][hh, :, a:b], 64, 64)])
                    fm_norm_unit([(OFF["c_k"], 128)], 64, "c_kn",
                                 [(lambda a, b: gkT[0][0, :, GH[0] + a:GH[0] + b], 0, 64),
                                  (lambda a, b: gkT[0][1, :, GH[0] + a:GH[0] + b], 64, 64)])
                    tm_unit(OFF["c_v"], 128, 2, True, lambda tok: gva[0][GH[0] + tok:GH[0] + tok + 128, :, :])
                    for g in range(3):
                        G = g + 1
                        for b4 in range(4):
                            fm_norm_unit([(OFF["d_q"] + g * 512 + b4 * 128, 128)], 64, "d_qn",
                                         [(lambda a, b, G=G, hh=2 * b4: gqT[G][hh, :, a:b], 0, 64),
                                          (lambda a, b, G=G, hh=2 * b4 + 1: gqT[G][hh, :, a:b], 64, 64)])
                            fm_norm_unit([(OFF["d_k"] + g * 512 + b4 * 128, 128)], 64, "d_kn",
                                         [(lambda a, b, G=G, hh=2 * b4: gkT[G][hh, :, GH[G] + a:GH[G] + b], 0, 64),
                                          (lambda a, b, G=G, hh=2 * b4 + 1: gkT[G][hh, :, GH[G] + a:GH[G] + b], 64, 64)])
                        tm_unit(OFF["d_v"] + g * 512, 512, 8, True,
                                lambda tok, G=G: gva[G][GH[G] + tok:GH[G] + tok + 128, :, :])
                    for h in range(4):
                        fm_norm_unit([(OFF["m_q"] + h * 128, 128)], 128, "m_qn", [(lambda a, b, h=h: mqT[h, :, a:b], 0, 128)])
                    for b20 in range(20):
                        fm_act_unit(OFF["z"] + b20 * 128, AF.Silu, None, None, zT[b20])
                    for b40 in range(40):
                        fm_act_unit(OFF["g"] + b40 * 128, AF.Sigmoid, bgate[:, b40 // 8, b40 % 8:b40 % 8 + 1], bgate, gT[b40])
                    wq = [units[0][0](), units[1][0]()]
                    for ui in range(len(units)):
                        if ui + 2 < len(units):
                            wq.append(units[ui + 2][0]())
                        units[ui][1](wq.pop(0))
                    wcq = [load_w([(OFF["b_cq"] + i * 128, 128)], 128) for i in range(2)]
                    for s in range(TPH):
                        t0 = half * TH + s * 512
                        qfs, sqs = [], []
                        for i in range(2):
                            ps = proj_fm(wcq[i], 128, s)
                            qf, sq = rms1(ps, 128)
                            qfs.append(qf)
                            sqs.append(sq)
                        cqn = tmp("cqn", [128, 2, 512], BF16, 2)
                        rms2(qfs, sqs, 128, 128, [b_cqn[:, 0:1], b_cqn[:, 1:2]], [b_cqn, b_cqn],
                             [(cqn, cqn[:, 0, :]), (cqn, cqn[:, 1, :])], pss_rot.next(), total=256)
                        for ob in range(3):
                            ps = aux_rot.next()
                            for i in range(2):
                                mm(ps, ps.ap, wqb[:, i, ob * 128:(ob + 1) * 128], cqn[:, i, :], i == 0, i == 1, [wqb, cqn])
                            qf, sq = rms1(ps, 128)
                            qo = tmp("qo", [128, 512], BF16, 3)
                            if ob < 2:
                                rms2([qf], [sq], 128, 64, [col["b_qn_n"].ap], [col["b_qn_n"]], [(qo, qo.ap)], pss_rot.next())
                                for hh in range(2):
                                    store(bqT[2 * ob + hh, 0:64, t0:t0 + 512], qo, qo[hh * 64:(hh + 1) * 64, :])
                            else:
                                rms2([qf], [sq], 128, 32, [col["b_qn_r"].ap], [col["b_qn_r"]], [(qo, qo.ap)], pss_rot.next())
                                rope(qo, 128, t0, aux_rot.next(), [(bqT, h, h * 32) for h in range(4)])
                    wckv = load_w([(OFF["b_ckv"], 128)], 128)
                    wkr = load_w([(OFF["b_kr"], 32)], 32)
                    for s in range(TPH):
                        t0 = half * TH + s * 512
                        ps = proj_fm(wckv, 128, s)
                        qf, sq = rms1(ps, 128)
                        ckvn = tmp("ckvn", [128, 512], BF16, 2)
                        rms2([qf], [sq], 128, 128, [col["b_ckvn"].ap], [col["b_ckvn"]], [(ckvn, ckvn.ap)], pss_rot.next())
                        for ob in range(2):
                            ps = aux_rot.next()
                            mm(ps, ps.ap, wkv[:, ob * 128:(ob + 1) * 128], ckvn.ap, True, True, [wkv, ckvn])
                            qf, sq = rms1(ps, 128)
                            ko = tmp("qo", [128, 512], BF16, 3)
                            rms2([qf], [sq], 128, 64, [col["b_kn_n"].ap], [col["b_kn_n"]], [(ko, ko.ap)], pss_rot.next())
                            for hh in range(2):
                                store(bkT[2 * ob + hh, 0:64, t0:t0 + 512], ko, ko[hh * 64:(hh + 1) * 64, :])
                        for k in range(4):
                            tok = t0 + k * 128
                            ps = aux_rot.next()
                            mm(ps, ps.ap, ckvn[:, k * 128:(k + 1) * 128], wkv[:, 256:768], True, True, [ckvn, wkv])
                            vb = tmp("vb", [128, 512], BF16, 3)
                            ts("dve", vb.ap, ps.ap, valid[:, tok // 128:tok // 128 + 1], ALU.mult, [ps, valid], [vb])
                            store(bvd[tok:tok + 128, :], vb)
                        ps = proj_fm(wkr, 32, s)
                        qf, sq = rms1(ps, 32)
                        ko = tmp("qo", [128, 512], BF16, 3)
                        rms2([qf], [sq], 32, 32, [col["b_kn_r"][0:32, :]], [col["b_kn_r"]], [(ko, ko[0:32, :])], pss_rot.next())
                        rope(ko, 32, t0, aux_rot.next(), [(bkT, h, 0) for h in range(4)])
                    P.barrier()
                    new_pb()

            def attn_core(maps, kts, vfn, vreads, onesfn, onesreads, bias, shift, po, pd, sps):
                def qk(kt):
                    res = []
                    for mi, m in enumerate(maps):
                        ps = sps[mi].next()
                        mm(ps, ps.ap, m["k"](kt), m["q"], True, True, m["reads"])
                        res.append(ps)
                    return res
                if isinstance(kts, int):
                    kts = list(range(kts))
                cur = qk(kts[0])
                for ki, kt in enumerate(kts):
                    nxt = qk(kts[ki + 1]) if ki + 1 < len(kts) else None
                    for mi in range(len(maps)):
                        ps = cur[mi]
                        pT = tmp("pT", [128, 512], BF16, 4)
                        if bias is None:
                            act(AF.Exp, pT.ap, ps.ap, [ps], [pT], bias=-shift)
                        else:
                            e_ap, e_buf, const = bias(mi, kt)
                            p0 = tmp("p0", [128, 512], BF16, 4)
                            act(AF.Exp, p0.ap, ps.ap, [ps], [p0], bias=const - shift)
                            tt("dve", pT.ap, p0.ap, e_ap, ALU.mult, [p0, e_buf], [pT])
                        mm(po[mi], po[mi].ap, vfn(mi, kt), pT.ap, ki == 0, ki == len(kts) - 1, [pT] + vreads)
                        mm(pd[mi], pd[mi].ap, onesfn(kt), pT.ap, ki == 0, ki == len(kts) - 1, [pT] + onesreads)
                    cur = nxt

            phase_reset(LAYER_END)
            onesv = AR.alloc("onesv", [128, NT, 128], BF16)
            ones_ff = AR.alloc("ones_ff", [128, 128], F32)
            memset("pool", ones_ff, 1.0)
            for t in range(NT):
                ts("pool", onesv[:, t, :], ones_ff.ap, valid[:, t:t + 1], ALU.mult, [ones_ff, valid], [onesv])
            P.barrier()
            new_pb()
            ATT_BASE = AR.off

            if "A" in phases:
                phase_reset(ATT_BASE)
                sets = Rot([(AR.alloc("aK", [128, S], BF16), AR.alloc("aQ", [128, S], BF16),
                             AR.alloc("aV", [128, NT, 128], BF16)) for _ in range(2 if S <= 4096 else 1)])
                ea_rot = Rot([AR.alloc("EAh", [128, 6, 512], BF16) for _ in range(2)])
                for h in range(4):
                    Kb, Qb, Vb = sets.next()
                    load(Kb, akT[h])
                    load(Qb, aqT[h])
                    load(Vb, avd[:, h * 128:(h + 1) * 128].rearrange("(t p) d -> p t d", p=128))
                    sl = SL4[h]
                    EAh = ea_rot.next()
                    load(EAh, C["EA"][h], q="pool")
                    for qt in range(NQ):
                        q0 = qt * 512
                        zt = tmp("zt", [128, 512], BF16, 2)
                        load(zt, zT[h][:, q0:q0 + 512])

                        def bias(mi, kt, q0=q0, sl=sl, EAh=EAh):
                            k0 = kt * 128
                            if k0 < q0:
                                return EAh[:, 0, :], EAh, -sl * (q0 - k0 - 127)
                            if k0 < q0 + 512:
                                return EAh[:, 2 + (k0 - q0) // 128, :], EAh, 0.0
                            return EAh[:, 1, :], EAh, -sl * (k0 - q0 - 511)

                        maps = [dict(k=lambda kt, m=m, Kb=Kb: Kb[64 * m:64 * m + 64, kt * 128:(kt + 1) * 128],
                                     q=Qb[64 * m:64 * m + 64, q0:q0 + 512], reads=[Kb, Qb]) for m in range(2)]
                        po, pd = [pb[4], pb[6]], [pb[5], pb[7]]
                        kts = [kt for kt in range(NT)
                               if sl * max(0, kt * 128 - (q0 + 511), q0 - (kt * 128 + 127)) < 140.0]
                        attn_core(maps, kts, lambda mi, kt, Vb=Vb: Vb[:, kt, :], [Vb], lambda kt: onesv[:, kt, :], [onesv],
                                  bias, SH_A, po, pd, [Rot([pb[0], pb[1]]), Rot([pb[2], pb[3]])])
                        tn = []
                        for m in range(2):
                            rd = tmp("rd", [128, 512], F32, 2)
                            srecip(rd, rd.ap, pd[m].ap, [pd[m]])
                            t_ = tmp("tn", [128, 512], F32, 2)
                            tt("dve", t_.ap, po[m].ap, rd.ap, ALU.mult, [po[m], rd], [t_])
                            tn.append(t_)
                        o_ = tmp("ao", [128, 512], F32, 2)
                        stt(o_.ap, tn[1].ap, neglam.ap, tn[0].ap, ALU.mult, ALU.add, [tn[0], tn[1], neglam], [o_])
                        sq = tmp("sq", [128, 512], BF16, 2)
                        tt("pool", sq.ap, o_.ap, o_.ap, ALU.mult, [o_], [sq])
                        on = tmp("on", [128, 512], F32, 2)
                        rms2([o_], [sq], 128, 128, [col["a_hn"].ap], [col["a_hn"]], [(on, on.ap)], pb[0])
                        u_ = tmp("u", [128, 512], BF16, 2)
                        tt("pool", u_.ap, on.ap, zt.ap, ALU.mult, [on, zt], [u_])
                        store(uT[0, h, :, q0:q0 + 512], u_)

            if "B" in phases:
                P.barrier()
                new_pb()
                phase_reset(ATT_BASE)
                sets = Rot([(AR.alloc("bK", [128, S], BF16), AR.alloc("bQ", [128, S], BF16),
                             AR.alloc("bV", [128, NT, 128], BF16)) for _ in range(2 if S <= 4096 else 1)])
                porot = Rot([(pb[4], pb[5]), (pb[6], pb[7])])
                for h in range(4):
                    Kb, Qb, Vb = sets.next()
                    load(Kb, bkT[h], sl=Kb[0:96, :])
                    load(Qb, bqT[h], sl=Qb[0:96, :])
                    load(Vb, bvd[:, h * 128:(h + 1) * 128].rearrange("(t p) d -> p t d", p=128))
                    for qt in range(NQ):
                        q0 = qt * 512
                        zt = tmp("zt", [128, 512], BF16, 2)
                        load(zt, zT[4 + h][:, q0:q0 + 512])
                        maps = [dict(k=lambda kt, Kb=Kb: Kb[0:96, kt * 128:(kt + 1) * 128], q=Qb[0:96, q0:q0 + 512],
                                     reads=[Kb, Qb])]
                        po_, pd_ = porot.next()
                        attn_core(maps, NT, lambda mi, kt, Vb=Vb: Vb[:, kt, :], [Vb], lambda kt: onesv[:, kt, :], [onesv],
                                  None, SH_B, [po_], [pd_], [Rot([pb[0], pb[1], pb[2], pb[3]])])
                        rd = tmp("rd", [128, 512], F32, 2)
                        srecip(rd, rd.ap, pd_.ap, [pd_])
                        t_ = tmp("tn", [128, 512], F32, 2)
                        tt("dve", t_.ap, po_.ap, rd.ap, ALU.mult, [po_, rd], [t_])
                        u_ = tmp("u", [128, 512], BF16, 2)
                        tt("pool", u_.ap, t_.ap, zt.ap, ALU.mult, [t_, zt], [u_])
                        store(uT[1, h, :, q0:q0 + 512], u_)

            if "M" in phases:
                P.barrier()
                new_pb()
                phase_reset(ATT_BASE)
                mt = AR.alloc("mt", [128, 2, DM], F32)
                load(mt, mem_in.rearrange("(k p) f -> p k f", p=128))
                memT = AR.alloc("memT", [128, 8, 256], F32)
                for c in range(8):
                    ps = pb[c % 4]
                    for k in range(2):
                        P.op("pe", lambda e, ps=ps, k=k, c=c: e.transpose(
                            out=ps[:, k * 128:(k + 1) * 128], in_=mt[:, k, c * 128:(c + 1) * 128], identity=ident.ap),
                            reads=[mt, ident], writes=[ps])
                    acopy(memT[:, c, :], ps[:, 0:256], [ps], [memT])
                sqm = AR.alloc("sqm", [128, 8, 256], BF16)
                tt("pool", sqm.ap, memT.ap, memT.ap, ALU.mult, [memT], [sqm])
                ps = pb[4]
                for c in range(8):
                    mm(ps, ps[:, 0:256], ones128.ap, sqm[:, c, :], c == 0, c == 7, [sqm, ones128])
                rtm = AR.alloc("rtm", [128, 256], F32)
                act(AF.Sqrt, rtm.ap, ps[:, 0:256], [ps, eps_c], [rtm], bias=eps_c.ap, scale=1.0 / DM)
                rrm = AR.alloc("rrm", [128, 256], F32)
                recip(rrm.ap, rtm.ap, [rtm], [rrm])
                mnT = AR.alloc("mnT", [128, 8, 256], BF16)
                for c in range(8):
                    stt(mnT[:, c, :], memT[:, c, :], mncol[:, c:c + 1], rrm.ap, ALU.mult, ALU.mult, [memT, rrm, mncol], [mnT])
                wmf = AR.alloc("wmf", [128, 8, 512], F32)
                wmk = AR.alloc("wmk", [128, 8, 512], BF16)
                wmv = AR.alloc("wmv", [128, 8, 512], BF16)
                load(wmf, Wt["w_mem_kv"][l][:, 0:512].rearrange("(k p) n -> p k n", p=128))
                vcopy(wmk.ap, wmf.ap, [wmf], [wmk], q="pool")
                load(wmf, Wt["w_mem_kv"][l][:, 512:1024].rearrange("(k p) n -> p k n", p=128))
                vcopy(wmv.ap, wmf.ap, [wmf], [wmv], q="pool")
                mkT = AR.alloc("mkT", [128, 4, 256], BF16)
                mv = AR.alloc("mv", [128, 2, 512], BF16)
                for h in range(4):
                    ps = pb[h % 2]
                    for c in range(8):
                        mm(ps, ps[:, 0:256], wmk[:, c, h * 128:(h + 1) * 128], mnT[:, c, :], c == 0, c == 7, [wmk, mnT])
                    qf, sq = rms1(ps, 128, w=256)
                    rms2([qf], [sq], 128, 128, [col["m_kn"].ap], [col["m_kn"]], [(mkT, mkT[:, h, :])], pb[2 + h % 2], w=256)
                for k in range(2):
                    ps = pb[4 + k]
                    for c in range(8):
                        mm(ps, ps.ap, mnT[:, c, k * 128:(k + 1) * 128], wmv[:, c, :], c == 0, c == 7, [mnT, wmv])
                    vcopy(mv[:, k, :], ps.ap, [ps], [mv])
                porot = Rot([(pb[4], pb[5]), (pb[6], pb[7])])
                srot = Rot([pb[0], pb[1], pb[2], pb[3]])
                for qt in range(NQ):
                    q0 = qt * 512
                    mq = tmp("mq", [128, 4, 512], BF16, 2)
                    load(mq, mqT[:, :, q0:q0 + 512].rearrange("h p j -> p h j"))
                    zt4 = tmp("zt4", [128, 4, 512], BF16, 2)
                    load(zt4, zT[16:20, :, q0:q0 + 512].rearrange("h p j -> p h j"))
                    for h in range(4):
                        maps = [dict(k=lambda kt, h=h: mkT[:, h, kt * 128:(kt + 1) * 128], q=mq[:, h, :], reads=[mkT, mq])]
                        po_, pd_ = porot.next()
                        attn_core(maps, 2, lambda mi, kt, h=h: mv[:, kt, h * 128:(h + 1) * 128], [mv],
                                  lambda kt: ones128.ap, [ones128], None, SH_M, [po_], [pd_], [srot])
                        rd = tmp("rd", [128, 512], F32, 2)
                        srecip(rd, rd.ap, pd_.ap, [pd_])
                        t_ = tmp("tn", [128, 512], F32, 2)
                        tt("dve", t_.ap, po_.ap, rd.ap, ALU.mult, [po_, rd], [t_])
                        u_ = tmp("u", [128, 512], BF16, 2)
                        tt("pool", u_.ap, t_.ap, zt4[:, h, :], ALU.mult, [t_, zt4], [u_])
                        store(uT[4, h, :, q0:q0 + 512], u_)

            def banded(groups, branch, use_sink, shift):
                P.barrier()
                new_pb()
                phase_reset(LAYER_END)
                UNIT = 2048
                accO = AR.alloc("accO", [64, 4, UNIT], F32)
                accD = AR.alloc("accD", [64, 4, UNIT], F32)
                Qb = AR.alloc("gQ", [64, 4, UNIT], BF16)
                maxspan = max(128 * GDIL[G] for G in groups)
                nk = 1 if groups == [0] else 4
                Kb = AR.alloc("gK", [64, nk, UNIT + 2 * maxspan], BF16)
                porot = Rot([(pb[4], pb[5]), (pb[6], pb[7])])
                srot = Rot([pb[0], pb[1], pb[2], pb[3]])
                eb_rot = Rot([AR.alloc("EBt", [128, 4, 384], BF16) for _ in range(2)])
                for u0 in range(0, S, UNIT):
                    for hh in range(2):
                        for gi, G in enumerate(groups):
                            dil, H = GDIL[G], GH[G]
                            span = 128 * dil
                            EBt = eb_rot.next()
                            load(EBt, C["EB"][G, hh * 4:hh * 4 + 4].rearrange("h p c -> p h c"), q="pool")
                            for j in range(4):
                                load(Qb, gqT[G][hh * 4 + j, :, u0:u0 + UNIT], sl=Qb[:, j, :], grp=Qb)
                            for j in range(nk):
                                kvh = hh if G == 0 else hh * 4 + j
                                load(Kb, gkT[G][kvh, :, H + u0 - span:H + u0 + UNIT + span],
                                     sl=Kb[:, j, 0:UNIT + 2 * span], grp=Kb)
                            NTL = 3 if G == 0 else 2
                            TW = NTL * 128
                            pend = []

                            def push(fn):
                                pend.append(fn)
                                if len(pend) > 3:
                                    pend.pop(0)()

                            for blk_i in range(UNIT // 128):
                                sp_i, r = blk_i // dil, blk_i % dil
                                qoff = sp_i * span + r
                                koffs = [sp_i * span + r + ((m * 128) if G == 0 else (64 + m * 128)) * dil for m in range(NTL)]
                                Vt = tmp("Vt", [128, 3, 4, 65], BF16, 3)
                                ov = tmp("ov", [128, 3, 64], BF16, 3)
                                for m in range(NTL):
                                    r0 = u0 + koffs[m]
                                    if G == 0:
                                        load(Vt, gva[G][r0:r0 + 127 * dil + 1:dil, hh:hh + 1, :], sl=Vt[:, m, 0:1, :], grp=Vt)
                                    else:
                                        load(Vt, gva[G][r0:r0 + 127 * dil + 1:dil, hh * 4:hh * 4 + 4, :], sl=Vt[:, m, :, :], grp=Vt)
                                vcopy(ov[:, 0:NTL, :], Vt[:, 0:NTL, 0, 64:65].to_broadcast([128, NTL, 64]), [Vt], [ov], q="pool")
                                po_, pd_ = porot.next()
                                for j in range(4):
                                    h = hh * 4 + j
                                    kj = 0 if G == 0 else j
                                    vj = 0 if G == 0 else j
                                    ps = srot.next()
                                    for m in range(NTL):
                                        mm(ps, ps[:, m * 128:(m + 1) * 128], Kb[:, kj, koffs[m]:koffs[m] + 127 * dil + 1:dil],
                                           Qb[:, j, qoff:qoff + 127 * dil + 1:dil], True, True, [Kb, Qb])
                                    p0 = tmp("p03", [128, 384], BF16, 5)
                                    act(AF.Exp, p0[:, 0:TW], ps[:, 0:TW], [ps], [p0], bias=-shift)
                                    pT = tmp("pT3", [128, 384], BF16, 5)
                                    tt("pool", pT[:, 0:TW], p0[:, 0:TW], EBt[:, j, 0:TW], ALU.mult, [p0, EBt], [pT])

                                    def stage2(j=j, vj=vj, pT=pT, Vt=Vt, ov=ov, po_=po_, pd_=pd_, qoff=qoff, NTL=NTL, gi=gi, dil=dil):
                                        for m in range(NTL):
                                            mm(po_, po_[0:64, j * 128:(j + 1) * 128], Vt[:, m, vj, 0:64], pT[:, m * 128:(m + 1) * 128],
                                               m == 0, m == NTL - 1, [Vt, pT])
                                        for m in range(NTL):
                                            mm(pd_, pd_[0:64, j * 128:(j + 1) * 128], ov[:, m, :], pT[:, m * 128:(m + 1) * 128],
                                               m == 0, m == NTL - 1, [ov, pT])
                                        if j == 3:
                                            aO = accO[:, :, qoff:qoff + 127 * dil + 1:dil]
                                            aD = accD[:, :, qoff:qoff + 127 * dil + 1:dil]
                                            pov = po_[0:64, :].rearrange("p (j q) -> p j q", j=4)
                                            pdv = pd_[0:64, :].rearrange("p (j q) -> p j q", j=4)
                                            if gi == 0:
                                                vcopy(aO, pov, [po_], [accO])
                                                acopy(aD, pdv, [pd_], [accD])
                                            else:
                                                tt("dve", aO, pov, aO, ALU.add, [po_, accO], [accO])
                                                tt("dve", aD, pdv, aD, ALU.add, [pd_, accD], [accD])
                                    push(stage2)
                            while pend:
                                pend.pop(0)()
                        if use_sink:
                            for j in range(4):
                                ts("dve", accD[:, j, :], accD[:, j, :], esink[0:64, hh * 4 + j:hh * 4 + j + 1], ALU.add,
                                   [accD, esink], [accD])
                        srecip(accD, accD.ap, accD.ap, [accD])
                        tt("dve", accO.ap, accO.ap, accD.ap, ALU.mult, [accO, accD], [accO])
                        for c0 in range(0, UNIT, 512):
                            zt = tmp("gz", [64, 4, 512], BF16, 2)
                            for j in range(4):
                                h = hh * 4 + j
                                load(zt, zT[branch * 4 + h // 2, (h % 2) * 64:(h % 2) * 64 + 64, u0 + c0:u0 + c0 + 512],
                                     sl=zt[:, j, :], grp=zt)
                            u_ = tmp("gu", [64, 4, 512], BF16, 2)
                            tt("pool", u_.ap, accO[:, :, c0:c0 + 512], zt.ap, ALU.mult, [accO, zt], [u_])
                            for j in range(4):
                                h = hh * 4 + j
                                store(uT[branch, h // 2, (h % 2) * 64:(h % 2) * 64 + 64, u0 + c0:u0 + c0 + 512], u_, u_[:, j, :])

            if "C" in phases:
                banded([0], 2, True, SH_C)
            if "D" in phases:
                banded([1, 2, 3], 3, False, SH_D)
            P.barrier()
            new_pb()

            if "G" in phases:
                phase_reset(LAYER_END)
                wbr = AR.alloc("wbr", [128, 5, 4, DM], BF16)
                wout = AR.alloc("wout", [128, 8, DM], BF16)
                WEND = AR.off
                wsf = AR.alloc("wsf", [128, 4, DM], F32)
                for i in range(5):
                    load(wsf, Wt["w_br"][l, i].rearrange("(c p) n -> p c n", p=128))
                    vcopy(wbr[:, i, :, :], wsf.ap, [wsf], [wbr], q="pool")
                for hf in range(2):
                    load(wsf, Wt["w_out"][l][hf * 512:(hf + 1) * 512, :].rearrange("(c p) n -> p c n", p=128))
                    vcopy(wout[:, hf * 4:(hf + 1) * 4, :], wsf.ap, [wsf], [wout], q="pool")
                P.barrier()
                new_pb()
                phase_reset(WEND)
                acc = AR.alloc("acc", [128, 8, 512], F32)
                accb = AR.alloc("accb", [128, 8, 512], BF16)
                mrot = Rot(pb[0:6])
                for tq in range(NQ):
                    t0 = tq * 512
                    xt_ = tmp("xTt", [128, 8, 512], F32, 2)
                    load(xt_, xT[l][:, :, t0:t0 + 512].rearrange("c p j -> p c j"))
                    for i in range(5):
                        ut = tmp("ut", [128, 4, 512], BF16, 3)
                        load(ut, uT[i, :, :, t0:t0 + 512].rearrange("c p j -> p c j"))
                        gt = tmp("gt", [128, 8, 512], BF16, 3)
                        load(gt, gT[i * 8:(i + 1) * 8, :, t0:t0 + 512].rearrange("b p j -> p b j"))
                        for j in range(8):
                            ps = mrot.next()
                            for c in range(4):
                                mm(ps, ps.ap, wbr[:, i, c, j * 128:(j + 1) * 128], ut[:, c, :], c == 0, c == 3, [wbr, ut])
                            if i == 0:
                                tt("dve", acc[:, j, :], ps.ap, gt[:, j, :], ALU.mult, [ps, gt], [acc])
                            else:
                                tm_ = tmp("tm", [128, 512], F32, 3)
                                tt("dve", tm_.ap, ps.ap, gt[:, j, :], ALU.mult, [ps, gt], [tm_])
                                tt("pool", acc[:, j, :], acc[:, j, :], tm_.ap, ALU.add, [acc, tm_], [acc])
                    acopy(accb.ap, acc.ap, [acc], [accb])
                    for j in range(8):
                        ps = mrot.next()
                        for c in range(8):
                            mm(ps, ps.ap, wout[:, c, j * 128:(j + 1) * 128], accb[:, c, :], c == 0, c == 7, [wout, accb])
                        tt("dve", xt_[:, j, :], ps.ap, xt_[:, j, :], ALU.add, [ps, xt_], [xt_])
                    if l < L - 1:
                        store(xT[l + 1][:, :, t0:t0 + 512].rearrange("c p j -> p c j"), xt_)
                    else:
                        for k in range(4):
                            yt = tmp("yt", [128, DM], F32, 2)
                            for hf in range(2):
                                ps = pb[6 + hf]
                                for c4 in range(4):
                                    c = hf * 4 + c4
                                    P.op("pe", lambda e, ps=ps, c=c, c4=c4, k=k, xt_=xt_: e.transpose(
                                        out=ps[:, c4 * 128:(c4 + 1) * 128], in_=xt_[:, c, k * 128:(k + 1) * 128],
                                        identity=ident.ap), reads=[xt_, ident], writes=[ps])
                                acopy(yt[:, hf * 512:(hf + 1) * 512], ps.ap, [ps], [yt])
                            store(y_out[t0 + k * 128:t0 + (k + 1) * 128, :], yt)
                P.barrier()
                new_pb()
        P.emit()
        print("ops", len(P.ops), "sems", P.nsem)
    return nc


_NC_CACHE = {}


def _core_inputs(xs, mems, valids, weights, S):
    consts = const_tables(S)
    maps = []
    for x, m, v in zip(xs, mems, valids):
        d = {"x": x, "mem": m, "valid": v}
        d.update(consts)
        d.update(weights)
        maps.append(d)
    return maps


def kernel(x_prompt, x_sample, mem_prompt, mem_sample, **w):
    S = 8192
    x_prompt = np.asarray(x_prompt, np.float32)
    x_sample = np.asarray(x_sample, np.float32)
    weights = {k: np.ascontiguousarray(np.asarray(v, np.float32)) for k, v in w.items()}
    xs, mems, valids = [], [], []
    for b in range(4):
        xp = np.zeros((S, DM), np.float32)
        xp[:x_prompt.shape[1]] = x_prompt[b]
        xs.append(xp)
        mems.append(np.ascontiguousarray(np.asarray(mem_prompt[b], np.float32)))
        v = np.zeros((S,), np.float32)
        v[:x_prompt.shape[1]] = 1.0
        valids.append(np.ascontiguousarray(v.reshape(S // 128, 128).T))
    for b in range(4):
        xs.append(np.ascontiguousarray(x_sample[b]))
        mems.append(np.ascontiguousarray(np.asarray(mem_sample[b], np.float32)))
        valids.append(np.ones((128, S // 128), np.float32))
    if S not in _NC_CACHE:
        _NC_CACHE[S] = build(S)
    nc = _NC_CACHE[S]
    in_maps = _core_inputs(xs, mems, valids, weights, S)
    res = run_bass_kernel_spmd(nc, in_maps, core_ids=list(range(8)))
    SP = x_prompt.shape[1]
    y_prompt = np.stack([np.asarray(res.results[b]["y"], np.float32)[:SP] for b in range(4)], axis=0)
    y_sample = np.stack([np.asarray(res.results[4 + b]["y"], np.float32) for b in range(4)], axis=0)
    return (y_prompt, y_sample)
```

```python
import contextlib
import math
import numpy as np
import concourse.bass as bass
import concourse.mybir as mybir
from concourse.bass_utils import run_bass_kernel_spmd

F32 = mybir.dt.float32
BF16 = mybir.dt.bfloat16
AF = mybir.ActivationFunctionType
ALU = mybir.AluOpType

COMPUTE = ("pe", "act", "dve", "pool")
QUEUES = ("pe", "act", "dve", "pool", "sp")

DM = 1024
N_IN = 15520
OFF = dict(a_q=0, a_k=512, a_v=1024, b_cq=1536, b_ckv=1792, b_kr=1920, c_q=1952, c_k=2464, c_v=2592,
           d_q=2720, d_k=4256, d_v=5792, m_q=7328, z=7840, g=10400)
EPS = 1e-6
BIG = 1.0e6
SH_A, SH_B, SH_C, SH_D, SH_M = 8.0, 10.0, 8.0, 8.0, 11.5
SL4 = [2.0 ** (-8.0 * (i + 1) / 4) for i in range(4)]
SL8 = [2.0 ** (-8.0 * (i + 1) / 8) for i in range(8)]
GDIL = (1, 1, 4, 16)
GNKV = (2, 8, 8, 8)
ARN = 44000


class Sem:
    __slots__ = ("h", "total")


class Buf:
    __slots__ = ("name", "ap", "last_w", "readers", "sem")

    def __init__(self, name, ap):
        self.name, self.ap = name, ap
        self.last_w, self.readers, self.sem = None, [], None

    def __getitem__(self, idx):
        return self.ap[idx]


class Op:
    __slots__ = ("q", "fn", "dma", "deps", "inc", "val", "dsem", "dval", "grp")


class Prog:
    def __init__(self, nc, stack):
        self.nc, self.stack = nc, stack
        self.ops, self.bufs, self.nsem = [], [], 0
        self.free_sems, self.live_sems = [], []

    def buf(self, name, ap):
        b = Buf(name, ap)
        self.bufs.append(b)
        return b

    def _newsem(self, name):
        self.nsem += 1
        return self.stack.enter_context(self.nc.semaphore(name))

    def _dsem(self):
        if self.free_sems:
            s = self.free_sems.pop()
        else:
            s = Sem()
            s.h, s.total = self._newsem("d%d" % self.nsem), 0
        self.live_sems.append(s)
        return s

    def op(self, q, fn, reads=(), writes=(), dma=None, grp=None):
        o = Op()
        o.q, o.fn, o.dma, o.grp = q, fn, dma, grp
        o.inc, o.val, o.dsem, o.dval = False, None, None, None
        deps = set()
        me = "dma" if dma is not None else q
        for b in reads:
            w = b.last_w
            if w is not None:
                we = "dma" if w.dma is not None else w.q
                if not (we == me and me == "pe"):
                    deps.add(w)
        for b in writes:
            w = b.last_w
            if w is not None:
                we = "dma" if w.dma is not None else w.q
                if (we != me or me == "dma") and not (grp is not None and w.grp is grp):
                    deps.add(w)
            for r in b.readers:
                re_ = "dma" if r.dma is not None else r.q
                if re_ != me or me == "dma":
                    deps.add(r)
        o.deps = deps
        for d in deps:
            if d.dma is None:
                d.inc = True
        if dma is not None:
            if dma.sem is None:
                dma.sem = self._dsem()
            dma.sem.total += 16
            o.dsem, o.dval = dma.sem, dma.sem.total
        for b in reads:
            b.readers.append(o)
        for b in writes:
            b.last_w = o
            b.readers = []
        self.ops.append(o)
        return o

    def barrier(self):
        last = {}
        for o in self.ops:
            if o.dma is None and o.fn is not None:
                last[o.q] = o
        for q in QUEUES:
            o = Op()
            o.q, o.fn, o.dma, o.grp = q, None, None, None
            o.inc, o.val, o.dsem, o.dval = False, None, None, None
            o.deps = set(v for k, v in last.items() if k != q)
            for d in o.deps:
                d.inc = True
            o.deps |= set(("D", s, s.total) for s in self.live_sems)
            self.ops.append(o)
        self.free_sems.extend(self.live_sems)
        self.live_sems = []
        for b in self.bufs:
            b.last_w, b.readers, b.sem = None, [], None

    def emit(self):
        nc = self.nc
        esem = {q: self._newsem("e_" + q) for q in COMPUTE}
        cnt = {q: 0 for q in COMPUTE}
        for o in self.ops:
            if o.dma is None and o.inc:
                cnt[o.q] += 1
                o.val = cnt[o.q]
        byq = {q: [] for q in QUEUES}
        for o in self.ops:
            byq[o.q].append(o)
        allsems = self.free_sems + self.live_sems
        block = self.stack.enter_context(nc.Block())
        handles = {"pe": block.tensor, "act": block.scalar, "dve": block.vector,
                   "pool": block.gpsimd, "sp": block.sync}

        def run_queue(q, e):
            known = {}
            for o in byq[q]:
                need = {}
                for d in o.deps:
                    if isinstance(d, tuple):
                        _, s, v = d
                        key, sem = ("D", id(s)), s.h
                    elif d.dma is not None:
                        key, sem, v = ("D", id(d.dsem)), d.dsem.h, d.dval
                    else:
                        key, sem, v = ("E", d.q), esem[d.q], d.val
                    if v > known.get(key, 0) and v > need.get(key, (None, 0))[1]:
                        need[key] = (sem, v)
                for key, (sem, v) in need.items():
                    e.wait_ge(sem, v)
                    known[key] = v
                if o.fn is None:
                    continue
                ins = o.fn(e)
                if o.dma is not None:
                    ins.then_inc(o.dsem.h, 16)
                elif o.inc:
                    ins.then_inc(esem[q], 1)
            if q == "sp":
                for s in allsems:
                    if s.total > known.get(("D", id(s)), 0):
                        e.wait_ge(s.h, s.total)

        for q in QUEUES:
            handles[q](lambda e, q=q: run_queue(q, e))


class Arena:
    def __init__(self, P, t, ncols):
        self.P, self.t, self.ncols, self.off, self.n = P, t, ncols, 0, 0

    def reset(self, off=0):
        self.off = off

    def alloc(self, name, shape, dt):
        per = 1
        for s in shape[1:]:
            per *= s
        nb = per * (2 if dt == BF16 else 4)
        n4 = (nb + 3) // 4
        assert self.off + n4 <= self.ncols, ("SBUF arena overflow", name, self.off, n4, self.ncols)
        ap = self.t[0:shape[0], self.off:self.off + n4]
        if dt == BF16:
            ap = ap.bitcast(BF16)[:, 0:per]
        if len(shape) == 3:
            ap = ap.rearrange("p (a b) -> p a b", a=shape[1])
        elif len(shape) == 4:
            ap = ap.rearrange("p (a b c) -> p a b c", a=shape[1], b=shape[2])
        self.off += n4
        self.n += 1
        return self.P.buf(name + "_" + str(self.n), ap)


class Rot:
    def __init__(self, items):
        self.items, self.i = items, 0

    def next(self):
        v = self.items[self.i % len(self.items)]
        self.i += 1
        return v


def const_tables(S):
    c = {}
    c["ident"] = np.eye(128, dtype=np.float32)
    c["ones128"] = np.ones((128, 128), np.float32)
    ii = np.arange(128)
    c["blk64"] = (ii[:, None] // 64 == ii[None, :] // 64).astype(np.float32)
    c["blk32"] = (ii[:, None] // 32 == ii[None, :] // 32).astype(np.float32)
    R = np.zeros((128, 128), np.float32)
    for m in range(128):
        if m % 32 < 16:
            R[m + 16, m] = -1.0
        else:
            R[m - 16, m] = 1.0
    c["rot"] = R
    i = np.arange(128, dtype=np.float32)[:, None]
    j = np.arange(512, dtype=np.float32)[None, :]
    c["tabA_L"] = (j - i).astype(np.float32)
    c["tabA_D"] = np.stack([np.abs(j - i - 128.0 * m) for m in range(4)], axis=1).astype(np.float32)
    q = np.arange(128, dtype=np.float32)[None, :]
    tabs = []
    for m in range(3):
        rel = (m - 1) * 128 + i - q
        tabs.append(np.where(np.abs(rel) <= 128, np.abs(rel), BIG))
    c["tabB128"] = np.concatenate(tabs, axis=1).astype(np.float32)
    tabs = []
    for m in range(3):
        rel = (m * 128 - 64) + i - q
        tabs.append(np.where(np.abs(rel) <= 64, np.abs(rel), BIG))
    c["tabB64"] = np.concatenate(tabs, axis=1).astype(np.float32)
    i64 = np.arange(128, dtype=np.float64)[:, None]
    j64 = np.arange(512, dtype=np.float64)[None, :]
    EA = np.zeros((4, 128, 6, 512), np.float32)
    for h in range(4):
        EA[h, :, 0, :] = np.exp(-SL4[h] * (j64 - i64 + 127.0))
        EA[h, :, 1, :] = np.exp(-SL4[h] * (i64 - j64 + 511.0))
        for m in range(4):
            EA[h, :, 2 + m, :] = np.exp(-SL4[h] * np.abs(j64 - i64 - 128.0 * m))
    c["EA"] = EA
    EB = np.zeros((4, 8, 128, 384), np.float32)
    for G in range(4):
        tab = c["tabB128"] if G == 0 else c["tabB64"]
        for h in range(8):
            EB[G, h] = np.where(tab < BIG, np.exp(-SL8[h] * GDIL[G] * tab.astype(np.float64)), 0.0)
    c["EB"] = EB
    half = 16
    inv = (np.float32(10000.0) ** (-np.arange(half, dtype=np.float32) / np.float32(half))).astype(np.float32)
    ang = (np.arange(S, dtype=np.float32)[:, None] * inv[None, :]).astype(np.float32)
    cos, sin = np.cos(ang).astype(np.float32), np.sin(ang).astype(np.float32)
    idx = np.arange(128) % 16
    c["cosT"] = np.ascontiguousarray(cos[:, idx].T)
    c["sinT"] = np.ascontiguousarray(sin[:, idx].T)
    return c


def const_shapes(S):
    return dict(ident=[128, 128], ones128=[128, 128], blk64=[128, 128], blk32=[128, 128], rot=[128, 128],
                tabA_L=[128, 512], tabA_D=[128, 4, 512], tabB128=[128, 384], tabB64=[128, 384],
                cosT=[128, S], sinT=[128, S], EA=[4, 128, 6, 512], EB=[4, 8, 128, 384])


WSHAPES = dict(norm_g=[2, 1024], w_in=[2, 1024, N_IN], a_qn=[2, 64], a_kn=[2, 64], a_lam=[2, 4, 64],
               a_hn=[2, 128], b_cqn=[2, 256], b_ckvn=[2, 128], w_qb=[2, 256, 384], w_kvb=[2, 128, 768],
               b_qn=[2, 96], b_kn=[2, 96], c_qn=[2, 64], c_kn=[2, 64], c_sink=[2, 8], d_qn=[2, 64],
               d_kn=[2, 64], m_norm=[2, 1024], w_mem_kv=[2, 1024, 1024], m_qn=[2, 128], m_kn=[2, 128],
               b_gate=[2, 5, 1024], w_br=[2, 5, 512, 1024], w_out=[2, 1024, 1024])


def build(S, L=2, dbg=(), phases="PABMCDG"):
    assert S % 2048 == 0
    NT, NQ = S // 128, S // 512
    TH = 2048
    NHALF = S // TH
    TPH = TH // 512
    nc = bass.Bass("TRN2", target_bir_lowering=False)

    def din(name, shape, dt=F32):
        return nc.dram_tensor(name, list(shape), dt, kind="ExternalInput").ap()

    def dscr(name, shape, dt):
        kind = "ExternalOutput" if name in dbg else "Internal"
        return nc.dram_tensor(name, list(shape), dt, kind=kind).ap()

    x_in = din("x", [S, DM])
    mem_in = din("mem", [256, DM])
    valid_in = din("valid", [128, NT])
    C = {k: din(k, v) for k, v in const_shapes(S).items()}
    Wt = {k: din(k, v) for k, v in WSHAPES.items()}
    y_out = nc.dram_tensor("y", [S, DM], F32, kind="ExternalOutput").ap()

    xT = [dscr("xT0", [8, 128, S], F32), dscr("xT1", [8, 128, S], F32)]
    aqT = dscr("aqT", [4, 128, S], BF16)
    akT = dscr("akT", [4, 128, S], BF16)
    avd = dscr("avd", [S, 512], BF16)
    bqT = dscr("bqT", [4, 96, S], BF16)
    bkT = dscr("bkT", [4, 96, S], BF16)
    bvd = dscr("bvd", [S, 512], BF16)
    mqT = dscr("mqT", [4, 128, S], BF16)
    zT = dscr("zT", [20, 128, S], BF16)
    gT = dscr("gT", [40, 128, S], BF16)
    uT = dscr("uT", [5, 4, 128, S], BF16)
    GH = [128 * d for d in GDIL]
    gqT = [dscr("gqT%d" % g, [8, 64, S], BF16) for g in range(4)]
    gkT = [dscr("gkT%d" % g, [GNKV[g], 64, S + 2 * GH[g]], BF16) for g in range(4)]
    gva = [dscr("gva%d" % g, [S + 2 * GH[g], GNKV[g], 65], BF16) for g in range(4)]

    with contextlib.ExitStack() as st:
        P = Prog(nc, st)
        arena_t = st.enter_context(nc.sbuf_tensor("arena", [128, ARN], F32))
        AR = Arena(P, arena_t, ARN)
        pbt = [st.enter_context(nc.psum_tensor("pb%d" % i, [128, 512], F32)) for i in range(8)]
        pb = []

        def new_pb():
            pb[:] = [P.buf("pb%d" % i, pbt[i][:]) for i in range(8)]

        new_pb()

        def dma(out, in_, reads=(), writes=(), own=None, q="sp", grp=None, slow=False):
            if slow:
                P.op(q, lambda e: e.dma_start(out=out, in_=in_, allow_slow_non_contiguous=True),
                     reads=reads, writes=writes, dma=own, grp=grp)
            else:
                P.op(q, lambda e: e.dma_start(out=out, in_=in_), reads=reads, writes=writes, dma=own, grp=grp)

        def load(b, src, sl=None, q="sp", own=None, grp=None, slow=False):
            dma(b.ap if sl is None else sl, src, writes=[b], own=own or b, q=q, grp=grp, slow=slow)

        def store(dst, b, sl=None, q="sp"):
            dma(dst, b.ap if sl is None else sl, reads=[b], own=b, q=q)

        def mm(ps, out, lhsT, rhs, start, stop, reads):
            P.op("pe", lambda e: e.matmul(out, lhsT=lhsT, rhs=rhs, start=start, stop=stop), reads=reads, writes=[ps])

        def act(func, out, in_, reads, writes, bias=0.0, scale=1.0):
            P.op("act", lambda e: e.activation(out=out, in_=in_, func=func, bias=bias, scale=scale),
                 reads=reads, writes=writes)

        def acopy(out, in_, reads, writes):
            P.op("act", lambda e: e.copy(out=out, in_=in_), reads=reads, writes=writes)

        def vcopy(out, in_, reads, writes, q="dve"):
            P.op(q, lambda e: e.tensor_copy(out=out, in_=in_), reads=reads, writes=writes)

        def tt(q, out, in0, in1, op, reads, writes):
            P.op(q, lambda e: e.tensor_tensor(out=out, in0=in0, in1=in1, op=op), reads=reads, writes=writes)

        def ts(q, out, in0, s1, op0, reads, writes):
            P.op(q, lambda e: e.tensor_scalar(out=out, in0=in0, scalar1=s1, scalar2=None, op0=op0),
                 reads=reads, writes=writes)

        def stt(out, in0, scalar, in1, op0, op1, reads, writes):
            P.op("dve", lambda e: e.scalar_tensor_tensor(out=out, in0=in0, scalar=scalar, in1=in1, op0=op0, op1=op1),
                 reads=reads, writes=writes)

        def recip(out, in_, reads, writes):
            P.op("dve", lambda e: e.reciprocal(out=out, in_=in_), reads=reads, writes=writes)

        def srecip(out_b, out_ap, in_ap, in_bufs):
            ts("dve", out_ap, in_ap, 1e-18, ALU.max, in_bufs, [out_b])
            act(AF.Ln, out_ap, out_ap, [out_b], [out_b])
            act(AF.Exp, out_ap, out_ap, [out_b], [out_b], scale=-1.0)

        def memset(q, b, val, sl=None):
            ap = b.ap if sl is None else sl
            P.op(q, lambda e: e.memset(ap, val), writes=[b])

        tmp_rot = {}

        def tmp(name, shape, dt, n=2):
            key = (name, tuple(shape), dt)
            if key not in tmp_rot:
                tmp_rot[key] = Rot([AR.alloc(name, shape, dt) for _ in range(n)])
            return tmp_rot[key].next()

        def phase_reset(off):
            AR.reset(off)
            tmp_rot.clear()

        parsem = AR.alloc("parsem", [128, 1], F32)
        eps_c = AR.alloc("eps", [128, 1], F32)
        memset("dve", eps_c, EPS)
        ident = AR.alloc("ident", [128, 128], F32)
        load(ident, C["ident"], own=parsem)
        valid = AR.alloc("valid", [128, NT], F32)
        load(valid, valid_in, own=parsem)
        cb = {}
        stg0 = AR.off
        for nm in ("ones128", "blk64", "blk32", "rot"):
            cb[nm] = AR.alloc(nm, [128, 128], BF16)
        CONST_END = AR.off
        stgs = {}
        for nm in ("ones128", "blk64", "blk32", "rot"):
            stgs[nm] = AR.alloc(nm + "f", [128, 128], F32)
            load(stgs[nm], C[nm], own=parsem)
        P.barrier()
        new_pb()
        for nm in ("ones128", "blk64", "blk32", "rot"):
            vcopy(cb[nm].ap, stgs[nm].ap, [stgs[nm]], [cb[nm]])
        ones128, rotm = cb["ones128"], cb["rot"]
        blk = {128: cb["ones128"], 64: cb["blk64"], 32: cb["blk32"]}
        zero_b = AR.alloc("zero", [128, 2048], BF16)
        memset("pool", zero_b, 0.0)
        for g in range(4):
            H = GH[g]
            for kv in range(GNKV[g]):
                for base in (0, H + S):
                    for o in range(0, H, 2048):
                        n = min(2048, H - o)
                        store(gkT[g][kv, :, base + o:base + o + n], zero_b, zero_b[0:64, 0:n])
            for base in (0, H + S):
                for o in range(0, H, 128):
                    store(gva[g][base + o:base + o + 128].rearrange("p k c -> p (k c)"), zero_b,
                          zero_b[:, 0:GNKV[g] * 65])

        def transpose_in():
            xt_r = Rot([AR.alloc("xt", [128, 4, DM], F32) for _ in range(2)])
            xo_r = Rot([AR.alloc("xo", [128, 8, 512], F32) for _ in range(2)])
            pr = Rot(pb[0:4])
            for t0 in range(0, S, 512):
                xt = xt_r.next()
                load(xt, x_in[t0:t0 + 512, :].rearrange("(k p) f -> p k f", p=128))
                xo = xo_r.next()
                for c in range(8):
                    ps = pr.next()
                    for k in range(4):
                        P.op("pe", lambda e, ps=ps, k=k, c=c, xt=xt: e.transpose(
                            out=ps[:, k * 128:(k + 1) * 128], in_=xt[:, k, c * 128:(c + 1) * 128], identity=ident.ap),
                            reads=[xt, ident], writes=[ps])
                    acopy(xo[:, c, :], ps.ap, [ps], [xo])
                store(xT[0][:, :, t0:t0 + 512].rearrange("c p j -> p c j"), xo)

        transpose_in()
        P.barrier()
        new_pb()

        def rms1(ps, n, w=512):
            qf = tmp("qf", [128, 512], F32, 3)
            sq = tmp("sq", [128, 512], BF16, 3)
            vcopy(qf[0:n, 0:w], ps[0:n, 0:w], [ps], [qf])
            tt("pool", sq[0:n, 0:w], qf[0:n, 0:w], qf[0:n, 0:w], ALU.mult, [qf], [sq])
            return qf, sq

        def rms2(qfs, sqs, n, group, gains, gbufs, outs, ps2, w=512, total=None):
            for i, sq in enumerate(sqs):
                mm(ps2, ps2[0:n, 0:w], blk[group][0:n, 0:n], sq[0:n, 0:w], i == 0, i == len(sqs) - 1, [sq, blk[group]])
            rt = tmp("rt", [128, 512], F32, 2)
            act(AF.Ln, rt[0:n, 0:w], ps2[0:n, 0:w], [ps2, eps_c], [rt], bias=eps_c[0:n, :],
                scale=1.0 / (total if total else group))
            rr = tmp("rr", [128, 512], F32, 2)
            act(AF.Exp, rr[0:n, 0:w], rt[0:n, 0:w], [rt], [rr], scale=-0.5)
            for q_, g_, gb, (ob, oap) in zip(qfs, gains, gbufs, outs):
                stt(oap, q_[0:n, 0:w], g_, rr[0:n, 0:w], ALU.mult, ALU.mult, [q_, rr, gb], [ob])

        def rope(xb, n, t0, ps, dests):
            cs = tmp("cs", [128, 2, 512], F32, 2)
            load(cs, C["cosT"][0:n, t0:t0 + 512], sl=cs[0:n, 0, :], grp=cs)
            load(cs, C["sinT"][0:n, t0:t0 + 512], sl=cs[0:n, 1, :], grp=cs)
            mm(ps, ps[0:n, :], rotm[0:n, 0:n], xb[0:n, :], True, True, [rotm, xb])
            t1 = tmp("rp1", [128, 512], F32, 2)
            t2 = tmp("rp2", [128, 512], F32, 2)
            tt("dve", t1[0:n, :], xb[0:n, :], cs[0:n, 0, :], ALU.mult, [xb, cs], [t1])
            tt("dve", t2[0:n, :], ps[0:n, :], cs[0:n, 1, :], ALU.mult, [ps, cs], [t2])
            ro = tmp("ro", [128, 512], BF16, 2)
            tt("pool", ro[0:n, :], t1[0:n, :], t2[0:n, :], ALU.add, [t1, t2], [ro])
            for (dt_, h, r0) in dests:
                store(dt_[h, 64:96, t0:t0 + 512], ro, ro[r0:r0 + 32, :])

        for l in range(L):
            phase_reset(CONST_END)
            lam_init = 0.8 - 0.6 * math.exp(-0.3 * l)

            def colvec(name, vec_ap, n, reps, mul=None):
                b = AR.alloc(name, [128, 1], F32)
                for r in range(reps):
                    dma(b[r * n:(r + 1) * n, :], vec_ap.rearrange("(p o) -> p o", o=1), writes=[b], own=parsem, grp=b, slow=True)
                return (b, n * reps, mul)

            def colmat(name, vec_ap, ncol):
                b = AR.alloc(name, [128, ncol], F32)
                load(b, vec_ap.rearrange("(c p) -> p c", p=128), own=parsem, slow=True)
                return b

            cols = {}
            cols["a_qn"] = colvec("a_qn", Wt["a_qn"][l], 64, 2, 64 ** -0.5)
            cols["a_kn"] = colvec("a_kn", Wt["a_kn"][l], 64, 2)
            cols["a_hn"] = colvec("a_hn", Wt["a_hn"][l], 128, 1, 1.0 - lam_init)
            cols["b_ckvn"] = colvec("b_ckvn", Wt["b_ckvn"][l], 128, 1)
            cols["b_qn_n"] = colvec("b_qn_n", Wt["b_qn"][l, 0:64], 64, 2, 96 ** -0.5)
            cols["b_qn_r"] = colvec("b_qn_r", Wt["b_qn"][l, 64:96], 32, 4, 96 ** -0.5)
            cols["b_kn_n"] = colvec("b_kn_n", Wt["b_kn"][l, 0:64], 64, 2)
            cols["b_kn_r"] = colvec("b_kn_r", Wt["b_kn"][l, 64:96], 32, 1)
            cols["c_qn"] = colvec("c_qn", Wt["c_qn"][l], 64, 2, 64 ** -0.5)
            cols["c_kn"] = colvec("c_kn", Wt["c_kn"][l], 64, 2)
            cols["d_qn"] = colvec("d_qn", Wt["d_qn"][l], 64, 2, 64 ** -0.5)
            cols["d_kn"] = colvec("d_kn", Wt["d_kn"][l], 64, 2)
            cols["m_qn"] = colvec("m_qn", Wt["m_qn"][l], 128, 1, 128 ** -0.5)
            cols["m_kn"] = colvec("m_kn", Wt["m_kn"][l], 128, 1)
            gcol = colmat("gcol", Wt["norm_g"][l], 8)
            b_cqn = colmat("b_cqn", Wt["b_cqn"][l], 2)
            mncol = colmat("mncol", Wt["m_norm"][l], 8)
            bgate = AR.alloc("bgate", [128, 5, 8], F32)
            for i in range(5):
                dma(bgate[:, i, :], Wt["b_gate"][l, i].rearrange("(c p) -> p c", p=128), writes=[bgate], own=parsem, grp=bgate, slow=True)
            lamt = AR.alloc("lamt", [128, 4, 64], F32)
            load(lamt, Wt["a_lam"][l].rearrange("a d -> (a d)").partition_broadcast(128), sl=lamt.ap.rearrange("p a d -> p (a d)"),
                 own=parsem)
            sinkt = AR.alloc("sinkt", [128, 8], F32)
            load(sinkt, Wt["c_sink"][l].partition_broadcast(128), own=parsem)
            P.barrier()
            new_pb()
            for k, (b, n, mul) in cols.items():
                if mul is not None:
                    ts("pool", b[0:n, :], b[0:n, :], float(mul), ALU.mult, [b], [b])
            col = {k: v[0] for k, v in cols.items()}
            lp = AR.alloc("lp", [128, 2, 64], F32)
            tt("dve", lp[:, 0, :], lamt[:, 0, :], lamt[:, 1, :], ALU.mult, [lamt], [lp])
            tt("dve", lp[:, 1, :], lamt[:, 2, :], lamt[:, 3, :], ALU.mult, [lamt], [lp])
            ls = AR.alloc("ls", [128, 2], F32)
            P.op("dve", lambda e, ls=ls, lp=lp: e.reduce_sum(out=ls.ap, in_=lp.ap, axis=mybir.AxisListType.X),
                 reads=[lp], writes=[ls])
            le = AR.alloc("le", [128, 2], F32)
            act(AF.Exp, le.ap, ls.ap, [ls], [le])
            neglam = AR.alloc("neglam", [128, 1], F32)
            tt("dve", neglam.ap, le[:, 1:2], le[:, 0:1], ALU.subtract, [le], [neglam])
            ts("dve", neglam.ap, neglam.ap, -lam_init, ALU.add, [neglam], [neglam])
            esink = AR.alloc("esink", [128, 8], F32)
            act(AF.Exp, esink.ap, sinkt.ap, [sinkt], [esink], bias=-SH_C)
            LAYER_END = AR.off

            if "P" in phases:
                wqbf = AR.alloc("wqbf", [128, 2, 384], F32)
                for h in range(4):
                    dma(wqbf[:, :, h * 64:(h + 1) * 64],
                        Wt["w_qb"][l][:, h * 96:h * 96 + 64].rearrange("(k p) n -> p k n", p=128),
                        writes=[wqbf], own=wqbf, grp=wqbf)
                    dma(wqbf[:, :, 256 + h * 32:256 + (h + 1) * 32],
                        Wt["w_qb"][l][:, h * 96 + 64:h * 96 + 96].rearrange("(k p) n -> p k n", p=128),
                        writes=[wqbf], own=wqbf, grp=wqbf)
                wqb = AR.alloc("wqb", [128, 2, 384], BF16)
                vcopy(wqb.ap, wqbf.ap, [wqbf], [wqb], q="pool")
                wkvf = AR.alloc("wkvf", [128, 768], F32)
                for h in range(4):
                    dma(wkvf[:, h * 64:(h + 1) * 64], Wt["w_kvb"][l][:, h * 192:h * 192 + 64],
                        writes=[wkvf], own=wkvf, grp=wkvf)
                    dma(wkvf[:, 256 + h * 128:256 + (h + 1) * 128], Wt["w_kvb"][l][:, h * 192 + 64:h * 192 + 192],
                        writes=[wkvf], own=wkvf, grp=wkvf)
                wkv = AR.alloc("wkv", [128, 768], BF16)
                vcopy(wkv.ap, wkvf.ap, [wkvf], [wkv], q="pool")
                ones_f = AR.alloc("ones_f", [128, 8], F32)
                memset("pool", ones_f, 1.0)
                xn_all = AR.alloc("xn", [128, 8, TH], BF16)
                xn_b = [P.buf("xnb%d" % i, xn_all[:, :, i * 512:(i + 1) * 512]) for i in range(TPH)]
                PH_BASE = AR.off
                w_in = Wt["w_in"][l]

                for half in range(NHALF):
                    phase_reset(PH_BASE)
                    xt_r = Rot([AR.alloc("xTt", [128, 8, 512], F32) for _ in range(2)])
                    sq8_r = Rot([AR.alloc("sq8", [128, 8, 512], BF16) for _ in range(1)])
                    for s in range(TPH):
                        t0 = half * TH + s * 512
                        xt_ = xt_r.next()
                        sq8 = sq8_r.next()
                        load(xt_, xT[l][:, :, t0:t0 + 512].rearrange("c p j -> p c j"))
                        tt("pool", sq8.ap, xt_.ap, xt_.ap, ALU.mult, [xt_], [sq8])
                        ps = pb[s % 2]
                        for c in range(8):
                            mm(ps, ps.ap, ones128.ap, sq8[:, c, :], c == 0, c == 7, [sq8, ones128])
                        rt = tmp("rt", [128, 512], F32, 2)
                        act(AF.Ln, rt.ap, ps.ap, [ps, eps_c], [rt], bias=eps_c.ap, scale=1.0 / DM)
                        rr = tmp("rr", [128, 512], F32, 2)
                        act(AF.Exp, rr.ap, rt.ap, [rt], [rr], scale=-0.5)
                        for c in range(8):
                            stt(xn_b[s][:, c, :], xt_[:, c, :], gcol[:, c:c + 1], rr.ap, ALU.mult, ALU.mult,
                                [xt_, rr, gcol], [xn_b[s]])
                    P.barrier()
                    new_pb()
                    phase_reset(PH_BASE)
                    wb_r = Rot([AR.alloc("wb", [128, 8, 128], BF16) for _ in range(4)])
                    wtb_r = Rot([AR.alloc("wtb", [128, 8, 512], BF16) for _ in range(2)])
                    stage_r = Rot([AR.alloc("stg", [128, 2048], BF16) for _ in range(3)])
                    main_rot = Rot(pb[0:3])
                    pss_rot = Rot(pb[3:5])
                    aux_rot = Rot(pb[5:8])

                    def load_w(segs, ncols):
                        wb = wb_r.next()
                        o = 0
                        for (c0, n) in segs:
                            dma(wb[:, :, o:o + n], w_in[:, c0:c0 + n].rearrange("(k p) n -> p k n", p=128),
                                writes=[wb], own=wb, grp=wb, q="pool")
                            o += n
                        return wb

                    def proj_fm(wb, ncols, s):
                        ps = main_rot.next()
                        for c in range(8):
                            mm(ps, ps[0:ncols, :], wb[:, c, 0:ncols], xn_b[s][:, c, :], c == 0, c == 7, [wb, xn_b[s]])
                        return ps

                    units = []

                    def fm_norm_unit(segs, group, gname, dests):
                        ncols = sum(n for _, n in segs)
                        units.append((lambda: load_w(segs, ncols),
                                      lambda wb: fm_norm_run(wb, ncols, group, gname, dests)))

                    def fm_norm_run(wb, ncols, group, gname, dests):
                        gb = col[gname]
                        pend = None
                        sg_new = None
                        for s in range(TPH + 1):
                            cur = None
                            if s < TPH:
                                if s % 4 == 0:
                                    sg_new = stage_r.next()
                                ps = proj_fm(wb, ncols, s)
                                qf, sq = rms1(ps, ncols)
                                cur = (qf, sq, s, sg_new)
                            if pend is not None:
                                qf_, sq_, s_, sg_ = pend
                                o = (s_ % 4) * 512
                                rms2([qf_], [sq_], ncols, group, [gb[0:ncols, :]], [gb],
                                     [(sg_, sg_[0:ncols, o:o + 512])], pss_rot.next())
                                if s_ % 4 == 3:
                                    t0 = half * TH + (s_ - 3) * 512
                                    for (dfn, r0, nr) in dests:
                                        store(dfn(t0, t0 + 2048), sg_, sg_[r0:r0 + nr, :])
                            pend = cur

                    def fm_act_unit(c0, func, bias_ap, bias_b, dst):
                        units.append((lambda: load_w([(c0, 128)], 128),
                                      lambda wb: fm_act_run(wb, func, bias_ap, bias_b, dst)))

                    def fm_act_run(wb, func, bias_ap, bias_b, dst):
                        sg_ = None
                        for s in range(TPH):
                            if s % 4 == 0:
                                sg_ = stage_r.next()
                            ps = proj_fm(wb, 128, s)
                            o = (s % 4) * 512
                            if bias_ap is None:
                                act(func, sg_[:, o:o + 512], ps.ap, [ps], [sg_])
                            else:
                                act(func, sg_[:, o:o + 512], ps.ap, [ps, bias_b], [sg_], bias=bias_ap)
                            if s % 4 == 3:
                                t0 = half * TH + (s - 3) * 512
                                store(dst[:, t0:t0 + 2048], sg_)

                    def load_wt(c0, ncols):
                        wtb = wtb_r.next()
                        dma(wtb[:, :, 0:ncols], w_in[:, c0:c0 + ncols].rearrange("(k p) n -> p k n", p=128),
                            writes=[wtb], own=wtb, q="pool")
                        return wtb

                    def tm_unit(c0, ncols, nh, aug, dst_fn):
                        units.append((lambda: load_wt(c0, ncols), lambda wtb: tm_run(wtb, ncols, nh, aug, dst_fn)))

                    def tm_run(wtb, ncols, nh, aug, dst_fn):
                        for s in range(TPH):
                            for k in range(4):
                                tok = half * TH + s * 512 + k * 128
                                ps = aux_rot.next()
                                for c in range(8):
                                    mm(ps, ps[:, 0:ncols], xn_b[s][:, c, k * 128:(k + 1) * 128], wtb[:, c, 0:ncols],
                                       c == 0, c == 7, [xn_b[s], wtb])
                                vcol = valid[:, tok // 128:tok // 128 + 1]
                                if not aug:
                                    vb = tmp("vb", [128, 512], BF16, 3)
                                    ts("dve", vb[:, 0:ncols], ps[:, 0:ncols], vcol, ALU.mult, [ps, valid], [vb])
                                    store(dst_fn(tok), vb, vb[:, 0:ncols])
                                else:
                                    va = tmp("va", [128, 8, 65], BF16, 3)
                                    ts("dve", va[:, 0:nh, 0:64], ps[:, 0:ncols].rearrange("p (h d) -> p h d", h=nh), vcol,
                                       ALU.mult, [ps, valid], [va])
                                    ts("dve", va[:, 0:nh, 64:65], ones_f[:, 0:nh].rearrange("p (h o) -> p h o", o=1), vcol,
                                       ALU.mult, [ones_f, valid], [va])
                                    store(dst_fn(tok), va, va[:, 0:nh, :])

                    for h in range(4):
                        fm_norm_unit([(OFF["a_q"] + h * 64, 64), (OFF["a_q"] + 256 + h * 64, 64)], 64, "a_qn",
                                     [(lambda a, b, h=h: aqT[h, :, a:b], 0, 128)])
                        fm_norm_unit([(OFF["a_k"] + h * 64, 64), (OFF["a_k"] + 256 + h * 64, 64)], 64, "a_kn",
                                     [(lambda a, b, h=h: akT[h, :, a:b], 0, 128)])
                    tm_unit(OFF["a_v"], 512, 0, False, lambda tok: avd[tok:tok + 128, :])
                    for b4 in range(4):
                        fm_norm_unit([(OFF["c_q"] + b4 * 128, 128)], 64, "c_qn",
                                     [(lambda a, b, hh=2 * b4: gqT[0][hh, :, a:b], 0, 64),
                                      (lambda a, b, hh=2 * b4 + 1: gqT[0][hh, :, a:b], 64, 64)])
                    fm_norm_unit([(OFF["c_k"], 128)], 64, "c_kn",
                                 [(lambda a, b: gkT[0][0, :, GH[0] + a:GH[0] + b], 0, 64),
                                  (lambda a, b: gkT[0][1, :, GH[0] + a:GH[0] + b], 64, 64)])
                    tm_unit(OFF["c_v"], 128, 2, True, lambda tok: gva[0][GH[0] + tok:GH[0] + tok + 128, :, :])
                    for g in range(3):
                        G = g + 1
                        for b4 in range(4):
                            fm_norm_unit([(OFF["d_q"] + g * 512 + b4 * 128, 128)], 64, "d_qn",
                                         [(lambda a, b, G=G, hh=2 * b4: gqT[G][hh, :, a:b], 0, 64),
                                          (lambda a, b, G=G, hh=2 * b4 + 1: gqT[G][hh, :, a:b], 64, 64)])
                            fm_norm_unit([(OFF["d_k"] + g * 512 + b4 * 128, 128)], 64, "d_kn",
                                         [(lambda a, b, G=G, hh=2 * b4: gkT[G][hh, :, GH[G] + a:GH[G] + b], 0, 64),
                                          (lambda a, b, G=G, hh=2 * b4 + 1: gkT[G][hh, :, GH[G] + a:GH[G] + b], 64, 64)])
                        tm_unit(OFF["d_v"] + g * 512, 512, 8, True,
                                lambda tok, G=G: gva[G][GH[G] + tok:GH[G] + tok + 128, :, :])
                    for h in range(4):
                        fm_norm_unit([(OFF["m_q"] + h * 128, 128)], 128, "m_qn", [(lambda a, b, h=h: mqT[h, :, a:b], 0, 128)])
                    for b20 in range(20):
                        fm_act_unit(OFF["z"] + b20 * 128, AF.Silu, None, None, zT[b20])
                    for b40 in range(40):
                        fm_act_unit(OFF["g"] + b40 * 128, AF.Sigmoid, bgate[:, b40 // 8, b40 % 8:b40 % 8 + 1], bgate, gT[b40])
                    wq = [units[0][0](), units[1][0]()]
                    for ui in range(len(units)):
                        if ui + 2 < len(units):
                            wq.append(units[ui + 2][0]())
                        units[ui][1](wq.pop(0))
                    wcq = [load_w([(OFF["b_cq"] + i * 128, 128)], 128) for i in range(2)]
                    for s in range(TPH):
                        t0 = half * TH + s * 512
                        qfs, sqs = [], []
                        for i in range(2):
                            ps = proj_fm(wcq[i], 128, s)
                            qf, sq = rms1(ps, 128)
                            qfs.append(qf)
                            sqs.append(sq)
                        cqn = tmp("cqn", [128, 2, 512], BF16, 2)
                        rms2(qfs, sqs, 128, 128, [b_cqn[:, 0:1], b_cqn[:, 1:2]], [b_cqn, b_cqn],
                             [(cqn, cqn[:, 0, :]), (cqn, cqn[:, 1, :])], pss_rot.next(), total=256)
                        for ob in range(3):
                            ps = aux_rot.next()
                            for i in range(2):
                                mm(ps, ps.ap, wqb[:, i, ob * 128:(ob + 1) * 128], cqn[:, i, :], i == 0, i == 1, [wqb, cqn])
                            qf, sq = rms1(ps, 128)
                            qo = tmp("qo", [128, 512], BF16, 3)
                            if ob < 2:
                                rms2([qf], [sq], 128, 64, [col["b_qn_n"].ap], [col["b_qn_n"]], [(qo, qo.ap)], pss_rot.next())
                                for hh in range(2):
                                    store(bqT[2 * ob + hh, 0:64, t0:t0 + 512], qo, qo[hh * 64:(hh + 1) * 64, :])
                            else:
                                rms2([qf], [sq], 128, 32, [col["b_qn_r"].ap], [col["b_qn_r"]], [(qo, qo.ap)], pss_rot.next())
                                rope(qo, 128, t0, aux_rot.next(), [(bqT, h, h * 32) for h in range(4)])
                    wckv = load_w([(OFF["b_ckv"], 128)], 128)
                    wkr = load_w([(OFF["b_kr"], 32)], 32)
                    for s in range(TPH):
                        t0 = half * TH + s * 512
                        ps = proj_fm(wckv, 128, s)
                        qf, sq = rms1(ps, 128)
                        ckvn = tmp("ckvn", [128, 512], BF16, 2)
                        rms2([qf], [sq], 128, 128, [col["b_ckvn"].ap], [col["b_ckvn"]], [(ckvn, ckvn.ap)], pss_rot.next())
                        for ob in range(2):
                            ps = aux_rot.next()
                            mm(ps, ps.ap, wkv[:, ob * 128:(ob + 1) * 128], ckvn.ap, True, True, [wkv, ckvn])
                            qf, sq = rms1(ps, 128)
                            ko = tmp("qo", [128, 512], BF16, 3)
                            rms2([qf], [sq], 128, 64, [col["b_kn_n"].ap], [col["b_kn_n"]], [(ko, ko.ap)], pss_rot.next())
                            for hh in range(2):
                                store(bkT[2 * ob + hh, 0:64, t0:t0 + 512], ko, ko[hh * 64:(hh + 1) * 64, :])
                        for k in range(4):
                            tok = t0 + k * 128
                            ps = aux_rot.next()
                            mm(ps, ps.ap, ckvn[:, k * 128:(k + 1) * 128], wkv[:, 256:768], True, True, [ckvn, wkv])
                            vb = tmp("vb", [128, 512], BF16, 3)
                            ts("dve", vb.ap, ps.ap, valid[:, tok // 128:tok // 128 + 1], ALU.mult, [ps, valid], [vb])
                            store(bvd[tok:tok + 128, :], vb)
                        ps = proj_fm(wkr, 32, s)
                        qf, sq = rms1(ps, 32)
                        ko = tmp("qo", [128, 512], BF16, 3)
                        rms2([qf], [sq], 32, 32, [col["b_kn_r"][0:32, :]], [col["b_kn_r"]], [(ko, ko[0:32, :])], pss_rot.next())
                        rope(ko, 32, t0, aux_rot.next(), [(bkT, h, 0) for h in range(4)])
                    P.barrier()
                    new_pb()

            def attn_core(maps, kts, vfn, vreads, onesfn, onesreads, bias, shift, po, pd, sps, dacc=None):
                def qk(kt):
                    res = []
                    for mi, m in enumerate(maps):
                        ps = sps[mi].next()
                        mm(ps, ps.ap, m["k"](kt), m["q"], True, True, m["reads"])
                        res.append(ps)
                    return res
                if isinstance(kts, int):
                    kts = list(range(kts))
                cur = qk(kts[0])
                for ki, kt in enumerate(kts):
                    nxt = qk(kts[ki + 1]) if ki + 1 < len(kts) else None
                    for mi in range(len(maps)):
                        ps = cur[mi]
                        pT = tmp("pT", [128, 512], BF16, 4)
                        if bias is None:
                            act(AF.Exp, pT.ap, ps.ap, [ps], [pT], bias=-shift)
                        else:
                            e_ap, e_buf, const = bias(mi, kt)
                            p0 = tmp("p0", [128, 512], BF16, 4)
                            act(AF.Exp, p0.ap, ps.ap, [ps], [p0], bias=const - shift)
                            tt("dve", pT.ap, p0.ap, e_ap, ALU.mult, [p0, e_buf], [pT])
                        mm(po[mi], po[mi].ap, vfn(mi, kt), pT.ap, ki == 0, ki == len(kts) - 1, [pT] + vreads)
                        if dacc is None:
                            mm(pd[mi], pd[mi].ap, onesfn(kt), pT.ap, ki == 0, ki == len(kts) - 1, [pT] + onesreads)
                        elif ki % 3 == 2:
                            mm(pd[mi], pd[mi].ap, onesfn(kt), pT.ap, ki == 2, False, [pT] + onesreads)
                        elif ki == 0:
                            ts("dve", dacc.ap, pT.ap, valid[:, kt:kt + 1], ALU.mult, [pT, valid], [dacc])
                        else:
                            stt(dacc.ap, pT.ap, valid[:, kt:kt + 1], dacc.ap, ALU.mult, ALU.add, [pT, valid, dacc], [dacc])
                    cur = nxt
                if dacc is not None:
                    assert len(kts) >= 3 and len(maps) == 1
                    mm(pd[0], pd[0].ap, ones_ff.ap, dacc.ap, False, True, [ones_ff, dacc])

            phase_reset(LAYER_END)
            onesv = AR.alloc("onesv", [128, NT, 128], BF16)
            ones_ff = AR.alloc("ones_ff", [128, 128], F32)
            memset("pool", ones_ff, 1.0)
            for t in range(NT):
                ts("pool", onesv[:, t, :], ones_ff.ap, valid[:, t:t + 1], ALU.mult, [ones_ff, valid], [onesv])
            P.barrier()
            new_pb()
            ATT_BASE = AR.off

            if "A" in phases:
                phase_reset(ATT_BASE)
                sets = Rot([(AR.alloc("aK", [128, S], BF16), AR.alloc("aQ", [128, S], BF16),
                             AR.alloc("aV", [128, NT, 128], BF16)) for _ in range(2 if S <= 4096 else 1)])
                ea_rot = Rot([AR.alloc("EAh", [128, 6, 512], BF16) for _ in range(2)])
                for h in range(4):
                    Kb, Qb, Vb = sets.next()
                    load(Kb, akT[h])
                    load(Qb, aqT[h])
                    load(Vb, avd[:, h * 128:(h + 1) * 128].rearrange("(t p) d -> p t d", p=128))
                    sl = SL4[h]
                    EAh = ea_rot.next()
                    load(EAh, C["EA"][h], q="pool")
                    for qt in range(NQ):
                        q0 = qt * 512
                        zt = tmp("zt", [128, 512], BF16, 2)
                        load(zt, zT[h][:, q0:q0 + 512])

                        def bias(mi, kt, q0=q0, sl=sl, EAh=EAh):
                            k0 = kt * 128
                            if k0 < q0:
                                return EAh[:, 0, :], EAh, -sl * (q0 - k0 - 127)
                            if k0 < q0 + 512:
                                return EAh[:, 2 + (k0 - q0) // 128, :], EAh, 0.0
                            return EAh[:, 1, :], EAh, -sl * (k0 - q0 - 511)

                        maps = [dict(k=lambda kt, m=m, Kb=Kb: Kb[64 * m:64 * m + 64, kt * 128:(kt + 1) * 128],
                                     q=Qb[64 * m:64 * m + 64, q0:q0 + 512], reads=[Kb, Qb]) for m in range(2)]
                        po, pd = [pb[4], pb[6]], [pb[5], pb[7]]
                        kts = [kt for kt in range(NT)
                               if sl * max(0, kt * 128 - (q0 + 511), q0 - (kt * 128 + 127)) < 140.0]
                        attn_core(maps, kts, lambda mi, kt, Vb=Vb: Vb[:, kt, :], [Vb], lambda kt: onesv[:, kt, :], [onesv],
                                  bias, SH_A, po, pd, [Rot([pb[0], pb[1]]), Rot([pb[2], pb[3]])])
                        tn = []
                        for m in range(2):
                            rd = tmp("rd", [128, 512], F32, 2)
                            srecip(rd, rd.ap, pd[m].ap, [pd[m]])
                            t_ = tmp("tn", [128, 512], F32, 2)
                            tt("dve", t_.ap, po[m].ap, rd.ap, ALU.mult, [po[m], rd], [t_])
                            tn.append(t_)
                        o_ = tmp("ao", [128, 512], F32, 2)
                        stt(o_.ap, tn[1].ap, neglam.ap, tn[0].ap, ALU.mult, ALU.add, [tn[0], tn[1], neglam], [o_])
                        sq = tmp("sq", [128, 512], BF16, 2)
                        tt("pool", sq.ap, o_.ap, o_.ap, ALU.mult, [o_], [sq])
                        on = tmp("on", [128, 512], F32, 2)
                        rms2([o_], [sq], 128, 128, [col["a_hn"].ap], [col["a_hn"]], [(on, on.ap)], pb[0])
                        u_ = tmp("u", [128, 512], BF16, 2)
                        tt("pool", u_.ap, on.ap, zt.ap, ALU.mult, [on, zt], [u_])
                        store(uT[0, h, :, q0:q0 + 512], u_)

            if "B" in phases:
                P.barrier()
                new_pb()
                phase_reset(ATT_BASE)
                sets = Rot([(AR.alloc("bK", [128, S], BF16), AR.alloc("bQ", [128, S], BF16),
                             AR.alloc("bV", [128, NT, 128], BF16)) for _ in range(2 if S <= 4096 else 1)])
                porot = Rot([(pb[4], pb[5]), (pb[6], pb[7])])
                for h in range(4):
                    Kb, Qb, Vb = sets.next()
                    load(Kb, bkT[h], sl=Kb[0:96, :])
                    load(Qb, bqT[h], sl=Qb[0:96, :])
                    load(Vb, bvd[:, h * 128:(h + 1) * 128].rearrange("(t p) d -> p t d", p=128))
                    for qt in range(NQ):
                        q0 = qt * 512
                        zt = tmp("zt", [128, 512], BF16, 2)
                        load(zt, zT[4 + h][:, q0:q0 + 512])
                        maps = [dict(k=lambda kt, Kb=Kb: Kb[0:96, kt * 128:(kt + 1) * 128], q=Qb[0:96, q0:q0 + 512],
                                     reads=[Kb, Qb])]
                        po_, pd_ = porot.next()
                        attn_core(maps, NT, lambda mi, kt, Vb=Vb: Vb[:, kt, :], [Vb], lambda kt: onesv[:, kt, :], [onesv],
                                  None, SH_B, [po_], [pd_], [Rot([pb[0], pb[1], pb[2], pb[3]])],
                                  dacc=tmp("dacc", [128, 512], F32, 2))
                        rd = tmp("rd", [128, 512], F32, 2)
                        srecip(rd, rd.ap, pd_.ap, [pd_])
                        t_ = tmp("tn", [128, 512], F32, 2)
                        tt("dve", t_.ap, po_.ap, rd.ap, ALU.mult, [po_, rd], [t_])
                        u_ = tmp("u", [128, 512], BF16, 2)
                        tt("pool", u_.ap, t_.ap, zt.ap, ALU.mult, [t_, zt], [u_])
                        store(uT[1, h, :, q0:q0 + 512], u_)

            if "M" in phases:
                P.barrier()
                new_pb()
                phase_reset(ATT_BASE)
                mt = AR.alloc("mt", [128, 2, DM], F32)
                load(mt, mem_in.rearrange("(k p) f -> p k f", p=128))
                memT = AR.alloc("memT", [128, 8, 256], F32)
                for c in range(8):
                    ps = pb[c % 4]
                    for k in range(2):
                        P.op("pe", lambda e, ps=ps, k=k, c=c: e.transpose(
                            out=ps[:, k * 128:(k + 1) * 128], in_=mt[:, k, c * 128:(c + 1) * 128], identity=ident.ap),
                            reads=[mt, ident], writes=[ps])
                    acopy(memT[:, c, :], ps[:, 0:256], [ps], [memT])
                sqm = AR.alloc("sqm", [128, 8, 256], BF16)
                tt("pool", sqm.ap, memT.ap, memT.ap, ALU.mult, [memT], [sqm])
                ps = pb[4]
                for c in range(8):
                    mm(ps, ps[:, 0:256], ones128.ap, sqm[:, c, :], c == 0, c == 7, [sqm, ones128])
                rtm = AR.alloc("rtm", [128, 256], F32)
                act(AF.Sqrt, rtm.ap, ps[:, 0:256], [ps, eps_c], [rtm], bias=eps_c.ap, scale=1.0 / DM)
                rrm = AR.alloc("rrm", [128, 256], F32)
                recip(rrm.ap, rtm.ap, [rtm], [rrm])
                mnT = AR.alloc("mnT", [128, 8, 256], BF16)
                for c in range(8):
                    stt(mnT[:, c, :], memT[:, c, :], mncol[:, c:c + 1], rrm.ap, ALU.mult, ALU.mult, [memT, rrm, mncol], [mnT])
                wmf = AR.alloc("wmf", [128, 8, 512], F32)
                wmk = AR.alloc("wmk", [128, 8, 512], BF16)
                wmv = AR.alloc("wmv", [128, 8, 512], BF16)
                load(wmf, Wt["w_mem_kv"][l][:, 0:512].rearrange("(k p) n -> p k n", p=128))
                vcopy(wmk.ap, wmf.ap, [wmf], [wmk], q="pool")
                load(wmf, Wt["w_mem_kv"][l][:, 512:1024].rearrange("(k p) n -> p k n", p=128))
                vcopy(wmv.ap, wmf.ap, [wmf], [wmv], q="pool")
                mkT = AR.alloc("mkT", [128, 4, 256], BF16)
                mv = AR.alloc("mv", [128, 2, 512], BF16)
                for h in range(4):
                    ps = pb[h % 2]
                    for c in range(8):
                        mm(ps, ps[:, 0:256], wmk[:, c, h * 128:(h + 1) * 128], mnT[:, c, :], c == 0, c == 7, [wmk, mnT])
                    qf, sq = rms1(ps, 128, w=256)
                    rms2([qf], [sq], 128, 128, [col["m_kn"].ap], [col["m_kn"]], [(mkT, mkT[:, h, :])], pb[2 + h % 2], w=256)
                for k in range(2):
                    ps = pb[4 + k]
                    for c in range(8):
                        mm(ps, ps.ap, mnT[:, c, k * 128:(k + 1) * 128], wmv[:, c, :], c == 0, c == 7, [mnT, wmv])
                    vcopy(mv[:, k, :], ps.ap, [ps], [mv])
                porot = Rot([(pb[4], pb[5]), (pb[6], pb[7])])
                srot = Rot([pb[0], pb[1], pb[2], pb[3]])
                for qt in range(NQ):
                    q0 = qt * 512
                    mq = tmp("mq", [128, 4, 512], BF16, 2)
                    load(mq, mqT[:, :, q0:q0 + 512].rearrange("h p j -> p h j"))
                    zt4 = tmp("zt4", [128, 4, 512], BF16, 2)
                    load(zt4, zT[16:20, :, q0:q0 + 512].rearrange("h p j -> p h j"))
                    for h in range(4):
                        maps = [dict(k=lambda kt, h=h: mkT[:, h, kt * 128:(kt + 1) * 128], q=mq[:, h, :], reads=[mkT, mq])]
                        po_, pd_ = porot.next()
                        attn_core(maps, 2, lambda mi, kt, h=h: mv[:, kt, h * 128:(h + 1) * 128], [mv],
                                  lambda kt: ones128.ap, [ones128], None, SH_M, [po_], [pd_], [srot])
                        rd = tmp("rd", [128, 512], F32, 2)
                        srecip(rd, rd.ap, pd_.ap, [pd_])
                        t_ = tmp("tn", [128, 512], F32, 2)
                        tt("dve", t_.ap, po_.ap, rd.ap, ALU.mult, [po_, rd], [t_])
                        u_ = tmp("u", [128, 512], BF16, 2)
                        tt("pool", u_.ap, t_.ap, zt4[:, h, :], ALU.mult, [t_, zt4], [u_])
                        store(uT[4, h, :, q0:q0 + 512], u_)

            def banded(groups, branch, use_sink, shift):
                P.barrier()
                new_pb()
                phase_reset(LAYER_END)
                UNIT = 2048
                accO = AR.alloc("accO", [64, 4, UNIT], F32)
                accD = AR.alloc("accD", [64, 4, UNIT], F32)
                Qb = AR.alloc("gQ", [64, 4, UNIT], BF16)
                maxspan = max(128 * GDIL[G] for G in groups)
                nk = 1 if groups == [0] else 4
                Kb = AR.alloc("gK", [64, nk, UNIT + 2 * maxspan], BF16)
                porot = Rot([(pb[4], pb[5]), (pb[6], pb[7])])
                srot = Rot([pb[0], pb[1], pb[2], pb[3]])
                eb_rot = Rot([AR.alloc("EBt", [128, 4, 384], BF16) for _ in range(2)])
                for u0 in range(0, S, UNIT):
                    for hh in range(2):
                        for gi, G in enumerate(groups):
                            dil, H = GDIL[G], GH[G]
                            span = 128 * dil
                            EBt = eb_rot.next()
                            load(EBt, C["EB"][G, hh * 4:hh * 4 + 4].rearrange("h p c -> p h c"), q="pool")
                            for j in range(4):
                                load(Qb, gqT[G][hh * 4 + j, :, u0:u0 + UNIT], sl=Qb[:, j, :], grp=Qb)
                            for j in range(nk):
                                kvh = hh if G == 0 else hh * 4 + j
                                load(Kb, gkT[G][kvh, :, H + u0 - span:H + u0 + UNIT + span],
                                     sl=Kb[:, j, 0:UNIT + 2 * span], grp=Kb)
                            NTL = 3 if G == 0 else 2
                            TW = NTL * 128
                            pend = []

                            def push(fn):
                                pend.append(fn)
                                if len(pend) > 3:
                                    pend.pop(0)()

                            for blk_i in range(UNIT // 128):
                                sp_i, r = blk_i // dil, blk_i % dil
                                qoff = sp_i * span + r
                                koffs = [sp_i * span + r + ((m * 128) if G == 0 else (64 + m * 128)) * dil for m in range(NTL)]
                                Vt = tmp("Vt", [128, 3, 4, 65], BF16, 3)
                                ov = tmp("ov", [128, 3, 64], BF16, 3)
                                for m in range(NTL):
                                    r0 = u0 + koffs[m]
                                    if G == 0:
                                        load(Vt, gva[G][r0:r0 + 127 * dil + 1:dil, hh:hh + 1, :], sl=Vt[:, m, 0:1, :], grp=Vt)
                                    else:
                                        load(Vt, gva[G][r0:r0 + 127 * dil + 1:dil, hh * 4:hh * 4 + 4, :], sl=Vt[:, m, :, :], grp=Vt)
                                vcopy(ov[:, 0:NTL, :], Vt[:, 0:NTL, 0, 64:65].to_broadcast([128, NTL, 64]), [Vt], [ov], q="pool")
                                po_, pd_ = porot.next()
                                for j in range(4):
                                    h = hh * 4 + j
                                    kj = 0 if G == 0 else j
                                    vj = 0 if G == 0 else j
                                    ps = srot.next()
                                    for m in range(NTL):
                                        mm(ps, ps[:, m * 128:(m + 1) * 128], Kb[:, kj, koffs[m]:koffs[m] + 127 * dil + 1:dil],
                                           Qb[:, j, qoff:qoff + 127 * dil + 1:dil], True, True, [Kb, Qb])
                                    p0 = tmp("p03", [128, 384], BF16, 5)
                                    act(AF.Exp, p0[:, 0:TW], ps[:, 0:TW], [ps], [p0], bias=-shift)
                                    pT = tmp("pT3", [128, 384], BF16, 5)
                                    tt("pool", pT[:, 0:TW], p0[:, 0:TW], EBt[:, j, 0:TW], ALU.mult, [p0, EBt], [pT])

                                    def stage2(j=j, vj=vj, pT=pT, Vt=Vt, ov=ov, po_=po_, pd_=pd_, qoff=qoff, NTL=NTL, gi=gi, dil=dil):
                                        for m in range(NTL):
                                            mm(po_, po_[0:64, j * 128:(j + 1) * 128], Vt[:, m, vj, 0:64], pT[:, m * 128:(m + 1) * 128],
                                               m == 0, m == NTL - 1, [Vt, pT])
                                        for m in range(NTL):
                                            mm(pd_, pd_[0:64, j * 128:(j + 1) * 128], ov[:, m, :], pT[:, m * 128:(m + 1) * 128],
                                               m == 0, m == NTL - 1, [ov, pT])
                                        if j == 3:
                                            aO = accO[:, :, qoff:qoff + 127 * dil + 1:dil]
                                            aD = accD[:, :, qoff:qoff + 127 * dil + 1:dil]
                                            pov = po_[0:64, :].rearrange("p (j q) -> p j q", j=4)
                                            pdv = pd_[0:64, :].rearrange("p (j q) -> p j q", j=4)
                                            if gi == 0:
                                                vcopy(aO, pov, [po_], [accO])
                                                acopy(aD, pdv, [pd_], [accD])
                                            else:
                                                tt("dve", aO, pov, aO, ALU.add, [po_, accO], [accO])
                                                tt("dve", aD, pdv, aD, ALU.add, [pd_, accD], [accD])
                                    push(stage2)
                            while pend:
                                pend.pop(0)()
                        if use_sink:
                            for j in range(4):
                                ts("dve", accD[:, j, :], accD[:, j, :], esink[0:64, hh * 4 + j:hh * 4 + j + 1], ALU.add,
                                   [accD, esink], [accD])
                        srecip(accD, accD.ap, accD.ap, [accD])
                        tt("dve", accO.ap, accO.ap, accD.ap, ALU.mult, [accO, accD], [accO])
                        for c0 in range(0, UNIT, 512):
                            zt = tmp("gz", [64, 4, 512], BF16, 2)
                            for j in range(4):
                                h = hh * 4 + j
                                load(zt, zT[branch * 4 + h // 2, (h % 2) * 64:(h % 2) * 64 + 64, u0 + c0:u0 + c0 + 512],
                                     sl=zt[:, j, :], grp=zt)
                            u_ = tmp("gu", [64, 4, 512], BF16, 2)
                            tt("pool", u_.ap, accO[:, :, c0:c0 + 512], zt.ap, ALU.mult, [accO, zt], [u_])
                            for j in range(4):
                                h = hh * 4 + j
                                store(uT[branch, h // 2, (h % 2) * 64:(h % 2) * 64 + 64, u0 + c0:u0 + c0 + 512], u_, u_[:, j, :])

            if "C" in phases:
                banded([0], 2, True, SH_C)
            if "D" in phases:
                banded([1, 2, 3], 3, False, SH_D)
            P.barrier()
            new_pb()

            if "G" in phases:
                phase_reset(LAYER_END)
                wbr = AR.alloc("wbr", [128, 5, 4, DM], BF16)
                wout = AR.alloc("wout", [128, 8, DM], BF16)
                WEND = AR.off
                wsf = AR.alloc("wsf", [128, 4, DM], F32)
                for i in range(5):
                    load(wsf, Wt["w_br"][l, i].rearrange("(c p) n -> p c n", p=128))
                    vcopy(wbr[:, i, :, :], wsf.ap, [wsf], [wbr], q="pool")
                for hf in range(2):
                    load(wsf, Wt["w_out"][l][hf * 512:(hf + 1) * 512, :].rearrange("(c p) n -> p c n", p=128))
                    vcopy(wout[:, hf * 4:(hf + 1) * 4, :], wsf.ap, [wsf], [wout], q="pool")
                P.barrier()
                new_pb()
                phase_reset(WEND)
                acc = AR.alloc("acc", [128, 8, 512], F32)
                accb = AR.alloc("accb", [128, 8, 512], BF16)
                mrot = Rot(pb[0:6])
                def ld_x(tq):
                    b_ = tmp("xTt", [128, 8, 512], F32, 2)
                    load(b_, xT[l][:, :, tq * 512:(tq + 1) * 512].rearrange("c p j -> p c j"))
                    return b_

                def ld_ug(k):
                    tq_, i_ = k // 5, k % 5
                    ut_ = tmp("ut", [128, 4, 512], BF16, 3)
                    load(ut_, uT[i_, :, :, tq_ * 512:(tq_ + 1) * 512].rearrange("c p j -> p c j"))
                    gt_ = tmp("gt", [128, 8, 512], BF16, 3)
                    load(gt_, gT[i_ * 8:(i_ + 1) * 8, :, tq_ * 512:(tq_ + 1) * 512].rearrange("b p j -> p b j"))
                    return ut_, gt_

                xq = [ld_x(0)]
                ugq = [ld_ug(0)]
                for tq in range(NQ):
                    t0 = tq * 512
                    xt_ = xq.pop(0)
                    for i in range(5):
                        if tq * 5 + i + 1 < NQ * 5:
                            ugq.append(ld_ug(tq * 5 + i + 1))
                        if i == 1 and tq + 1 < NQ:
                            xq.append(ld_x(tq + 1))
                        ut, gt = ugq.pop(0)
                        for j in range(8):
                            ps = mrot.next()
                            for c in range(4):
                                mm(ps, ps.ap, wbr[:, i, c, j * 128:(j + 1) * 128], ut[:, c, :], c == 0, c == 3, [wbr, ut])
                            if i == 0:
                                tt("dve", acc[:, j, :], ps.ap, gt[:, j, :], ALU.mult, [ps, gt], [acc])
                            else:
                                tm_ = tmp("tm", [128, 512], F32, 3)
                                tt("dve", tm_.ap, ps.ap, gt[:, j, :], ALU.mult, [ps, gt], [tm_])
                                tt("pool", acc[:, j, :], acc[:, j, :], tm_.ap, ALU.add, [acc, tm_], [acc])
                    acopy(accb.ap, acc.ap, [acc], [accb])
                    for j in range(8):
                        ps = mrot.next()
                        for c in range(8):
                            mm(ps, ps.ap, wout[:, c, j * 128:(j + 1) * 128], accb[:, c, :], c == 0, c == 7, [wout, accb])
                        tt("dve", xt_[:, j, :], ps.ap, xt_[:, j, :], ALU.add, [ps, xt_], [xt_])
                    if l < L - 1:
                        store(xT[l + 1][:, :, t0:t0 + 512].rearrange("c p j -> p c j"), xt_)
                    else:
                        for k in range(4):
                            yt = tmp("yt", [128, DM], F32, 2)
                            for hf in range(2):
                                ps = pb[6 + hf]
                                for c4 in range(4):
                                    c = hf * 4 + c4
                                    P.op("pe", lambda e, ps=ps, c=c, c4=c4, k=k, xt_=xt_: e.transpose(
                                        out=ps[:, c4 * 128:(c4 + 1) * 128], in_=xt_[:, c, k * 128:(k + 1) * 128],
                                        identity=ident.ap), reads=[xt_, ident], writes=[ps])
                                acopy(yt[:, hf * 512:(hf + 1) * 512], ps.ap, [ps], [yt])
                            store(y_out[t0 + k * 128:t0 + (k + 1) * 128, :], yt)
                P.barrier()
                new_pb()
        P.emit()
        print("ops", len(P.ops), "sems", P.nsem)
    return nc


_NC_CACHE = {}


def _core_inputs(xs, mems, valids, weights, S):
    consts = const_tables(S)
    maps = []
    for x, m, v in zip(xs, mems, valids):
        d = {"x": x, "mem": m, "valid": v}
        d.update(consts)
        d.update(weights)
        maps.append(d)
    return maps


def kernel(x_prompt, x_sample, mem_prompt, mem_sample, **w):
    S = 8192
    x_prompt = np.asarray(x_prompt, np.float32)
    x_sample = np.asarray(x_sample, np.float32)
    weights = {k: np.ascontiguousarray(np.asarray(v, np.float32)) for k, v in w.items()}
    xs, mems, valids = [], [], []
    for b in range(4):
        xp = np.zeros((S, DM), np.float32)
        xp[:x_prompt.shape[1]] = x_prompt[b]
        xs.append(xp)
        mems.append(np.ascontiguousarray(np.asarray(mem_prompt[b], np.float32)))
        v = np.zeros((S,), np.float32)
        v[:x_prompt.shape[1]] = 1.0
        valids.append(np.ascontiguousarray(v.reshape(S // 128, 128).T))
    for b in range(4):
        xs.append(np.ascontiguousarray(x_sample[b]))
        mems.append(np.ascontiguousarray(np.asarray(mem_sample[b], np.float32)))
        valids.append(np.ones((128, S // 128), np.float32))
    if S not in _NC_CACHE:
        _NC_CACHE[S] = build(S)
    nc = _NC_CACHE[S]
    in_maps = _core_inputs(xs, mems, valids, weights, S)
    res = run_bass_kernel_spmd(nc, in_maps, core_ids=list(range(8)))
    SP = x_prompt.shape[1]
    y_prompt = np.stack([np.asarray(res.results[b]["y"], np.float32)[:SP] for b in range(4)], axis=0)
    y_sample = np.stack([np.asarray(res.results[4 + b]["y"], np.float32) for b in range(4)], axis=0)
    return (y_prompt, y_sample)
```

```python
import contextlib
import math
import numpy as np
import concourse.bass as bass
import concourse.mybir as mybir
from concourse.bass_utils import run_bass_kernel_spmd

F32 = mybir.dt.float32
BF16 = mybir.dt.bfloat16
AF = mybir.ActivationFunctionType
ALU = mybir.AluOpType

COMPUTE = ("pe", "act", "dve", "pool")
QUEUES = ("pe", "act", "dve", "pool", "sp")

DM = 1024
N_IN = 15520
OFF = dict(a_q=0, a_k=512, a_v=1024, b_cq=1536, b_ckv=1792, b_kr=1920, c_q=1952, c_k=2464, c_v=2592,
           d_q=2720, d_k=4256, d_v=5792, m_q=7328, z=7840, g=10400)
EPS = 1e-6
BIG = 1.0e6
SH_A, SH_B, SH_C, SH_D, SH_M = 8.0, 10.0, 8.0, 8.0, 11.5
SL4 = [2.0 ** (-8.0 * (i + 1) / 4) for i in range(4)]
SL8 = [2.0 ** (-8.0 * (i + 1) / 8) for i in range(8)]
GDIL = (1, 1, 4, 16)
GNKV = (2, 8, 8, 8)
ARN = 44000


class Sem:
    __slots__ = ("h", "total")


class Buf:
    __slots__ = ("name", "ap", "last_w", "readers", "sem")

    def __init__(self, name, ap):
        self.name, self.ap = name, ap
        self.last_w, self.readers, self.sem = None, [], None

    def __getitem__(self, idx):
        return self.ap[idx]


class Op:
    __slots__ = ("q", "fn", "dma", "deps", "inc", "val", "dsem", "dval", "grp")


class Prog:
    def __init__(self, nc, stack):
        self.nc, self.stack = nc, stack
        self.ops, self.bufs, self.nsem = [], [], 0
        self.free_sems, self.live_sems = [], []

    def buf(self, name, ap):
        b = Buf(name, ap)
        self.bufs.append(b)
        return b

    def _newsem(self, name):
        self.nsem += 1
        return self.stack.enter_context(self.nc.semaphore(name))

    def _dsem(self):
        if self.free_sems:
            s = self.free_sems.pop()
        else:
            s = Sem()
            s.h, s.total = self._newsem("d%d" % self.nsem), 0
        self.live_sems.append(s)
        return s

    def op(self, q, fn, reads=(), writes=(), dma=None, grp=None):
        o = Op()
        o.q, o.fn, o.dma, o.grp = q, fn, dma, grp
        o.inc, o.val, o.dsem, o.dval = False, None, None, None
        deps = set()
        me = "dma" if dma is not None else q
        for b in reads:
            w = b.last_w
            if w is not None:
                we = "dma" if w.dma is not None else w.q
                if not (we == me and me == "pe"):
                    deps.add(w)
        for b in writes:
            w = b.last_w
            if w is not None:
                we = "dma" if w.dma is not None else w.q
                if (we != me or me == "dma") and not (grp is not None and w.grp is grp):
                    deps.add(w)
            for r in b.readers:
                re_ = "dma" if r.dma is not None else r.q
                if re_ != me or me == "dma":
                    deps.add(r)
        o.deps = deps
        for d in deps:
            if d.dma is None:
                d.inc = True
        if dma is not None:
            if dma.sem is None:
                dma.sem = self._dsem()
            dma.sem.total += 16
            o.dsem, o.dval = dma.sem, dma.sem.total
        for b in reads:
            b.readers.append(o)
        for b in writes:
            b.last_w = o
            b.readers = []
        self.ops.append(o)
        return o

    def barrier(self):
        last = {}
        for o in self.ops:
            if o.dma is None and o.fn is not None:
                last[o.q] = o
        for q in QUEUES:
            o = Op()
            o.q, o.fn, o.dma, o.grp = q, None, None, None
            o.inc, o.val, o.dsem, o.dval = False, None, None, None
            o.deps = set(v for k, v in last.items() if k != q)
            for d in o.deps:
                d.inc = True
            o.deps |= set(("D", s, s.total) for s in self.live_sems)
            self.ops.append(o)
        self.free_sems.extend(self.live_sems)
        self.live_sems = []
        for b in self.bufs:
            b.last_w, b.readers, b.sem = None, [], None

    def emit(self):
        nc = self.nc
        esem = {q: self._newsem("e_" + q) for q in COMPUTE}
        cnt = {q: 0 for q in COMPUTE}
        for o in self.ops:
            if o.dma is None and o.inc:
                cnt[o.q] += 1
                o.val = cnt[o.q]
        byq = {q: [] for q in QUEUES}
        for o in self.ops:
            byq[o.q].append(o)
        allsems = self.free_sems + self.live_sems
        block = self.stack.enter_context(nc.Block())
        handles = {"pe": block.tensor, "act": block.scalar, "dve": block.vector,
                   "pool": block.gpsimd, "sp": block.sync}

        def run_queue(q, e):
            known = {}
            for o in byq[q]:
                need = {}
                for d in o.deps:
                    if isinstance(d, tuple):
                        _, s, v = d
                        key, sem = ("D", id(s)), s.h
                    elif d.dma is not None:
                        key, sem, v = ("D", id(d.dsem)), d.dsem.h, d.dval
                    else:
                        key, sem, v = ("E", d.q), esem[d.q], d.val
                    if v > known.get(key, 0) and v > need.get(key, (None, 0))[1]:
                        need[key] = (sem, v)
                for key, (sem, v) in need.items():
                    e.wait_ge(sem, v)
                    known[key] = v
                if o.fn is None:
                    continue
                ins = o.fn(e)
                if o.dma is not None:
                    ins.then_inc(o.dsem.h, 16)
                elif o.inc:
                    ins.then_inc(esem[q], 1)
            if q == "sp":
                for s in allsems:
                    if s.total > known.get(("D", id(s)), 0):
                        e.wait_ge(s.h, s.total)

        for q in QUEUES:
            handles[q](lambda e, q=q: run_queue(q, e))


class Arena:
    def __init__(self, P, t, ncols):
        self.P, self.t, self.ncols, self.off, self.n = P, t, ncols, 0, 0

    def reset(self, off=0):
        self.off = off

    def alloc(self, name, shape, dt):
        per = 1
        for s in shape[1:]:
            per *= s
        nb = per * (2 if dt == BF16 else 4)
        n4 = (nb + 3) // 4
        assert self.off + n4 <= self.ncols, ("SBUF arena overflow", name, self.off, n4, self.ncols)
        ap = self.t[0:shape[0], self.off:self.off + n4]
        if dt == BF16:
            ap = ap.bitcast(BF16)[:, 0:per]
        if len(shape) == 3:
            ap = ap.rearrange("p (a b) -> p a b", a=shape[1])
        elif len(shape) == 4:
            ap = ap.rearrange("p (a b c) -> p a b c", a=shape[1], b=shape[2])
        self.off += n4
        self.n += 1
        return self.P.buf(name + "_" + str(self.n), ap)


class Rot:
    def __init__(self, items):
        self.items, self.i = items, 0

    def next(self):
        v = self.items[self.i % len(self.items)]
        self.i += 1
        return v


def const_tables(S):
    c = {}
    c["ident"] = np.eye(128, dtype=np.float32)
    c["ones128"] = np.ones((128, 128), np.float32)
    ii = np.arange(128)
    c["blk64"] = (ii[:, None] // 64 == ii[None, :] // 64).astype(np.float32)
    c["blk32"] = (ii[:, None] // 32 == ii[None, :] // 32).astype(np.float32)
    R = np.zeros((128, 128), np.float32)
    for m in range(128):
        if m % 32 < 16:
            R[m + 16, m] = -1.0
        else:
            R[m - 16, m] = 1.0
    c["rot"] = R
    i = np.arange(128, dtype=np.float32)[:, None]
    j = np.arange(512, dtype=np.float32)[None, :]
    c["tabA_L"] = (j - i).astype(np.float32)
    c["tabA_D"] = np.stack([np.abs(j - i - 128.0 * m) for m in range(4)], axis=1).astype(np.float32)
    q = np.arange(128, dtype=np.float32)[None, :]
    tabs = []
    for m in range(3):
        rel = (m - 1) * 128 + i - q
        tabs.append(np.where(np.abs(rel) <= 128, np.abs(rel), BIG))
    c["tabB128"] = np.concatenate(tabs, axis=1).astype(np.float32)
    tabs = []
    for m in range(3):
        rel = (m * 128 - 64) + i - q
        tabs.append(np.where(np.abs(rel) <= 64, np.abs(rel), BIG))
    c["tabB64"] = np.concatenate(tabs, axis=1).astype(np.float32)
    i64 = np.arange(128, dtype=np.float64)[:, None]
    j64 = np.arange(512, dtype=np.float64)[None, :]
    EA = np.zeros((4, 128, 6, 512), np.float32)
    for h in range(4):
        EA[h, :, 0, :] = np.exp(-SL4[h] * (j64 - i64 + 127.0))
        EA[h, :, 1, :] = np.exp(-SL4[h] * (i64 - j64 + 511.0))
        for m in range(4):
            EA[h, :, 2 + m, :] = np.exp(-SL4[h] * np.abs(j64 - i64 - 128.0 * m))
    c["EA"] = EA
    EB = np.zeros((4, 8, 128, 384), np.float32)
    for G in range(4):
        tab = c["tabB128"] if G == 0 else c["tabB64"]
        for h in range(8):
            EB[G, h] = np.where(tab < BIG, np.exp(-SL8[h] * GDIL[G] * tab.astype(np.float64)), 0.0)
    c["EB"] = EB
    half = 16
    inv = (np.float32(10000.0) ** (-np.arange(half, dtype=np.float32) / np.float32(half))).astype(np.float32)
    ang = (np.arange(S, dtype=np.float32)[:, None] * inv[None, :]).astype(np.float32)
    cos, sin = np.cos(ang).astype(np.float32), np.sin(ang).astype(np.float32)
    idx = np.arange(128) % 16
    c["cosT"] = np.ascontiguousarray(cos[:, idx].T)
    c["sinT"] = np.ascontiguousarray(sin[:, idx].T)
    return c


def const_shapes(S):
    return dict(ident=[128, 128], ones128=[128, 128], blk64=[128, 128], blk32=[128, 128], rot=[128, 128],
                tabA_L=[128, 512], tabA_D=[128, 4, 512], tabB128=[128, 384], tabB64=[128, 384],
                cosT=[128, S], sinT=[128, S], EA=[4, 128, 6, 512], EB=[4, 8, 128, 384])


WSHAPES = dict(norm_g=[2, 1024], w_in=[2, 1024, N_IN], a_qn=[2, 64], a_kn=[2, 64], a_lam=[2, 4, 64],
               a_hn=[2, 128], b_cqn=[2, 256], b_ckvn=[2, 128], w_qb=[2, 256, 384], w_kvb=[2, 128, 768],
               b_qn=[2, 96], b_kn=[2, 96], c_qn=[2, 64], c_kn=[2, 64], c_sink=[2, 8], d_qn=[2, 64],
               d_kn=[2, 64], m_norm=[2, 1024], w_mem_kv=[2, 1024, 1024], m_qn=[2, 128], m_kn=[2, 128],
               b_gate=[2, 5, 1024], w_br=[2, 5, 512, 1024], w_out=[2, 1024, 1024])


def build(S, L=2, dbg=(), phases="PABMCDG"):
    assert S % 2048 == 0
    NT, NQ = S // 128, S // 512
    TH = 2048
    NHALF = S // TH
    TPH = TH // 512
    nc = bass.Bass("TRN2", target_bir_lowering=False)

    def din(name, shape, dt=F32):
        return nc.dram_tensor(name, list(shape), dt, kind="ExternalInput").ap()

    def dscr(name, shape, dt):
        kind = "ExternalOutput" if name in dbg else "Internal"
        return nc.dram_tensor(name, list(shape), dt, kind=kind).ap()

    x_in = din("x", [S, DM])
    mem_in = din("mem", [256, DM])
    valid_in = din("valid", [128, NT])
    C = {k: din(k, v) for k, v in const_shapes(S).items()}
    Wt = {k: din(k, v) for k, v in WSHAPES.items()}
    y_out = nc.dram_tensor("y", [S, DM], F32, kind="ExternalOutput").ap()

    xT = [dscr("xT0", [8, 128, S], F32), dscr("xT1", [8, 128, S], F32)]
    aqT = dscr("aqT", [4, 128, S], BF16)
    akT = dscr("akT", [4, 128, S], BF16)
    avd = dscr("avd", [S, 512], BF16)
    bqT = dscr("bqT", [4, 96, S], BF16)
    bkT = dscr("bkT", [4, 96, S], BF16)
    bvd = dscr("bvd", [S, 512], BF16)
    mqT = dscr("mqT", [4, 128, S], BF16)
    zT = dscr("zT", [20, 128, S], BF16)
    gT = dscr("gT", [40, 128, S], BF16)
    uT = dscr("uT", [5, 4, 128, S], BF16)
    GH = [128 * d for d in GDIL]
    gqT = [dscr("gqT%d" % g, [8, 64, S], BF16) for g in range(4)]
    gkT = [dscr("gkT%d" % g, [GNKV[g], 64, S + 2 * GH[g]], BF16) for g in range(4)]
    gva = [dscr("gva%d" % g, [S + 2 * GH[g], GNKV[g], 65], BF16) for g in range(4)]

    with contextlib.ExitStack() as st:
        P = Prog(nc, st)
        arena_t = st.enter_context(nc.sbuf_tensor("arena", [128, ARN], F32))
        AR = Arena(P, arena_t, ARN)
        pbt = [st.enter_context(nc.psum_tensor("pb%d" % i, [128, 512], F32)) for i in range(8)]
        pb = []

        def new_pb():
            pb[:] = [P.buf("pb%d" % i, pbt[i][:]) for i in range(8)]

        new_pb()

        def dma(out, in_, reads=(), writes=(), own=None, q="sp", grp=None, slow=False):
            if slow:
                P.op(q, lambda e: e.dma_start(out=out, in_=in_, allow_slow_non_contiguous=True),
                     reads=reads, writes=writes, dma=own, grp=grp)
            else:
                P.op(q, lambda e: e.dma_start(out=out, in_=in_), reads=reads, writes=writes, dma=own, grp=grp)

        def load(b, src, sl=None, q="sp", own=None, grp=None, slow=False):
            dma(b.ap if sl is None else sl, src, writes=[b], own=own or b, q=q, grp=grp, slow=slow)

        def store(dst, b, sl=None, q="sp"):
            dma(dst, b.ap if sl is None else sl, reads=[b], own=b, q=q)

        def mm(ps, out, lhsT, rhs, start, stop, reads):
            P.op("pe", lambda e: e.matmul(out, lhsT=lhsT, rhs=rhs, start=start, stop=stop), reads=reads, writes=[ps])

        def act(func, out, in_, reads, writes, bias=0.0, scale=1.0):
            P.op("act", lambda e: e.activation(out=out, in_=in_, func=func, bias=bias, scale=scale),
                 reads=reads, writes=writes)

        def acopy(out, in_, reads, writes):
            P.op("act", lambda e: e.copy(out=out, in_=in_), reads=reads, writes=writes)

        def vcopy(out, in_, reads, writes, q="dve"):
            P.op(q, lambda e: e.tensor_copy(out=out, in_=in_), reads=reads, writes=writes)

        def tt(q, out, in0, in1, op, reads, writes):
            P.op(q, lambda e: e.tensor_tensor(out=out, in0=in0, in1=in1, op=op), reads=reads, writes=writes)

        def ts(q, out, in0, s1, op0, reads, writes):
            P.op(q, lambda e: e.tensor_scalar(out=out, in0=in0, scalar1=s1, scalar2=None, op0=op0),
                 reads=reads, writes=writes)

        def stt(out, in0, scalar, in1, op0, op1, reads, writes):
            P.op("dve", lambda e: e.scalar_tensor_tensor(out=out, in0=in0, scalar=scalar, in1=in1, op0=op0, op1=op1),
                 reads=reads, writes=writes)

        def recip(out, in_, reads, writes):
            P.op("dve", lambda e: e.reciprocal(out=out, in_=in_), reads=reads, writes=writes)

        def srecip(out_b, out_ap, in_ap, in_bufs):
            ts("dve", out_ap, in_ap, 1e-18, ALU.max, in_bufs, [out_b])
            act(AF.Ln, out_ap, out_ap, [out_b], [out_b])
            act(AF.Exp, out_ap, out_ap, [out_b], [out_b], scale=-1.0)

        def memset(q, b, val, sl=None):
            ap = b.ap if sl is None else sl
            P.op(q, lambda e: e.memset(ap, val), writes=[b])

        tmp_rot = {}

        def tmp(name, shape, dt, n=2):
            key = (name, tuple(shape), dt)
            if key not in tmp_rot:
                tmp_rot[key] = Rot([AR.alloc(name, shape, dt) for _ in range(n)])
            return tmp_rot[key].next()

        def phase_reset(off):
            AR.reset(off)
            tmp_rot.clear()

        parsem = AR.alloc("parsem", [128, 1], F32)
        eps_c = AR.alloc("eps", [128, 1], F32)
        memset("dve", eps_c, EPS)
        ident = AR.alloc("ident", [128, 128], F32)
        load(ident, C["ident"], own=parsem)
        valid = AR.alloc("valid", [128, NT], F32)
        load(valid, valid_in, own=parsem)
        cb = {}
        stg0 = AR.off
        for nm in ("ones128", "blk64", "blk32", "rot"):
            cb[nm] = AR.alloc(nm, [128, 128], BF16)
        CONST_END = AR.off
        stgs = {}
        for nm in ("ones128", "blk64", "blk32", "rot"):
            stgs[nm] = AR.alloc(nm + "f", [128, 128], F32)
            load(stgs[nm], C[nm], own=parsem)
        P.barrier()
        new_pb()
        for nm in ("ones128", "blk64", "blk32", "rot"):
            vcopy(cb[nm].ap, stgs[nm].ap, [stgs[nm]], [cb[nm]])
        ones128, rotm = cb["ones128"], cb["rot"]
        blk = {128: cb["ones128"], 64: cb["blk64"], 32: cb["blk32"]}
        zero_b = AR.alloc("zero", [128, 2048], BF16)
        memset("pool", zero_b, 0.0)
        for g in range(4):
            H = GH[g]
            for kv in range(GNKV[g]):
                for base in (0, H + S):
                    for o in range(0, H, 2048):
                        n = min(2048, H - o)
                        store(gkT[g][kv, :, base + o:base + o + n], zero_b, zero_b[0:64, 0:n])
            for base in (0, H + S):
                for o in range(0, H, 128):
                    store(gva[g][base + o:base + o + 128].rearrange("p k c -> p (k c)"), zero_b,
                          zero_b[:, 0:GNKV[g] * 65])

        def transpose_in():
            xt_r = Rot([AR.alloc("xt", [128, 4, DM], F32) for _ in range(2)])
            xo_r = Rot([AR.alloc("xo", [128, 8, 512], F32) for _ in range(2)])
            pr = Rot(pb[0:4])
            for t0 in range(0, S, 512):
                xt = xt_r.next()
                load(xt, x_in[t0:t0 + 512, :].rearrange("(k p) f -> p k f", p=128))
                xo = xo_r.next()
                for c in range(8):
                    ps = pr.next()
                    for k in range(4):
                        P.op("pe", lambda e, ps=ps, k=k, c=c, xt=xt: e.transpose(
                            out=ps[:, k * 128:(k + 1) * 128], in_=xt[:, k, c * 128:(c + 1) * 128], identity=ident.ap),
                            reads=[xt, ident], writes=[ps])
                    acopy(xo[:, c, :], ps.ap, [ps], [xo])
                store(xT[0][:, :, t0:t0 + 512].rearrange("c p j -> p c j"), xo)

        transpose_in()
        P.barrier()
        new_pb()

        def rms1(ps, n, w=512):
            qf = tmp("qf", [128, 512], F32, 3)
            sq = tmp("sq", [128, 512], BF16, 3)
            vcopy(qf[0:n, 0:w], ps[0:n, 0:w], [ps], [qf])
            tt("pool", sq[0:n, 0:w], qf[0:n, 0:w], qf[0:n, 0:w], ALU.mult, [qf], [sq])
            return qf, sq

        def rms2(qfs, sqs, n, group, gains, gbufs, outs, ps2, w=512, total=None):
            for i, sq in enumerate(sqs):
                mm(ps2, ps2[0:n, 0:w], blk[group][0:n, 0:n], sq[0:n, 0:w], i == 0, i == len(sqs) - 1, [sq, blk[group]])
            rt = tmp("rt", [128, 512], F32, 2)
            act(AF.Ln, rt[0:n, 0:w], ps2[0:n, 0:w], [ps2, eps_c], [rt], bias=eps_c[0:n, :],
                scale=1.0 / (total if total else group))
            rr = tmp("rr", [128, 512], F32, 2)
            act(AF.Exp, rr[0:n, 0:w], rt[0:n, 0:w], [rt], [rr], scale=-0.5)
            for q_, g_, gb, (ob, oap) in zip(qfs, gains, gbufs, outs):
                stt(oap, q_[0:n, 0:w], g_, rr[0:n, 0:w], ALU.mult, ALU.mult, [q_, rr, gb], [ob])

        def rope(xb, n, t0, ps, dests):
            cs = tmp("cs", [128, 2, 512], F32, 2)
            load(cs, C["cosT"][0:n, t0:t0 + 512], sl=cs[0:n, 0, :], grp=cs)
            load(cs, C["sinT"][0:n, t0:t0 + 512], sl=cs[0:n, 1, :], grp=cs)
            mm(ps, ps[0:n, :], rotm[0:n, 0:n], xb[0:n, :], True, True, [rotm, xb])
            t1 = tmp("rp1", [128, 512], F32, 2)
            t2 = tmp("rp2", [128, 512], F32, 2)
            tt("dve", t1[0:n, :], xb[0:n, :], cs[0:n, 0, :], ALU.mult, [xb, cs], [t1])
            tt("dve", t2[0:n, :], ps[0:n, :], cs[0:n, 1, :], ALU.mult, [ps, cs], [t2])
            ro = tmp("ro", [128, 512], BF16, 2)
            tt("pool", ro[0:n, :], t1[0:n, :], t2[0:n, :], ALU.add, [t1, t2], [ro])
            for (dt_, h, r0) in dests:
                store(dt_[h, 64:96, t0:t0 + 512], ro, ro[r0:r0 + 32, :])

        for l in range(L):
            phase_reset(CONST_END)
            lam_init = 0.8 - 0.6 * math.exp(-0.3 * l)

            def colvec(name, vec_ap, n, reps, mul=None):
                b = AR.alloc(name, [128, 1], F32)
                for r in range(reps):
                    dma(b[r * n:(r + 1) * n, :], vec_ap.rearrange("(p o) -> p o", o=1), writes=[b], own=parsem, grp=b, slow=True)
                return (b, n * reps, mul)

            def colmat(name, vec_ap, ncol):
                b = AR.alloc(name, [128, ncol], F32)
                load(b, vec_ap.rearrange("(c p) -> p c", p=128), own=parsem, slow=True)
                return b

            cols = {}
            cols["a_qn"] = colvec("a_qn", Wt["a_qn"][l], 64, 2, 64 ** -0.5)
            cols["a_kn"] = colvec("a_kn", Wt["a_kn"][l], 64, 2)
            cols["a_hn"] = colvec("a_hn", Wt["a_hn"][l], 128, 1, 1.0 - lam_init)
            cols["b_ckvn"] = colvec("b_ckvn", Wt["b_ckvn"][l], 128, 1)
            cols["b_qn_n"] = colvec("b_qn_n", Wt["b_qn"][l, 0:64], 64, 2, 96 ** -0.5)
            cols["b_qn_r"] = colvec("b_qn_r", Wt["b_qn"][l, 64:96], 32, 4, 96 ** -0.5)
            cols["b_kn_n"] = colvec("b_kn_n", Wt["b_kn"][l, 0:64], 64, 2)
            cols["b_kn_r"] = colvec("b_kn_r", Wt["b_kn"][l, 64:96], 32, 1)
            cols["c_qn"] = colvec("c_qn", Wt["c_qn"][l], 64, 2, 64 ** -0.5)
            cols["c_kn"] = colvec("c_kn", Wt["c_kn"][l], 64, 2)
            cols["d_qn"] = colvec("d_qn", Wt["d_qn"][l], 64, 2, 64 ** -0.5)
            cols["d_kn"] = colvec("d_kn", Wt["d_kn"][l], 64, 2)
            cols["m_qn"] = colvec("m_qn", Wt["m_qn"][l], 128, 1, 128 ** -0.5)
            cols["m_kn"] = colvec("m_kn", Wt["m_kn"][l], 128, 1)
            gcol = colmat("gcol", Wt["norm_g"][l], 8)
            b_cqn = colmat("b_cqn", Wt["b_cqn"][l], 2)
            mncol = colmat("mncol", Wt["m_norm"][l], 8)
            bgate = AR.alloc("bgate", [128, 5, 8], F32)
            for i in range(5):
                dma(bgate[:, i, :], Wt["b_gate"][l, i].rearrange("(c p) -> p c", p=128), writes=[bgate], own=parsem, grp=bgate, slow=True)
            lamt = AR.alloc("lamt", [128, 4, 64], F32)
            load(lamt, Wt["a_lam"][l].rearrange("a d -> (a d)").partition_broadcast(128), sl=lamt.ap.rearrange("p a d -> p (a d)"),
                 own=parsem)
            sinkt = AR.alloc("sinkt", [128, 8], F32)
            load(sinkt, Wt["c_sink"][l].partition_broadcast(128), own=parsem)
            P.barrier()
            new_pb()
            for k, (b, n, mul) in cols.items():
                if mul is not None:
                    ts("pool", b[0:n, :], b[0:n, :], float(mul), ALU.mult, [b], [b])
            col = {k: v[0] for k, v in cols.items()}
            lp = AR.alloc("lp", [128, 2, 64], F32)
            tt("dve", lp[:, 0, :], lamt[:, 0, :], lamt[:, 1, :], ALU.mult, [lamt], [lp])
            tt("dve", lp[:, 1, :], lamt[:, 2, :], lamt[:, 3, :], ALU.mult, [lamt], [lp])
            ls = AR.alloc("ls", [128, 2], F32)
            P.op("dve", lambda e, ls=ls, lp=lp: e.reduce_sum(out=ls.ap, in_=lp.ap, axis=mybir.AxisListType.X),
                 reads=[lp], writes=[ls])
            le = AR.alloc("le", [128, 2], F32)
            act(AF.Exp, le.ap, ls.ap, [ls], [le])
            neglam = AR.alloc("neglam", [128, 1], F32)
            tt("dve", neglam.ap, le[:, 1:2], le[:, 0:1], ALU.subtract, [le], [neglam])
            ts("dve", neglam.ap, neglam.ap, -lam_init, ALU.add, [neglam], [neglam])
            esink = AR.alloc("esink", [128, 8], F32)
            act(AF.Exp, esink.ap, sinkt.ap, [sinkt], [esink], bias=-SH_C)
            LAYER_END = AR.off

            if "P" in phases:
                wqbf = AR.alloc("wqbf", [128, 2, 384], F32)
                for h in range(4):
                    dma(wqbf[:, :, h * 64:(h + 1) * 64],
                        Wt["w_qb"][l][:, h * 96:h * 96 + 64].rearrange("(k p) n -> p k n", p=128),
                        writes=[wqbf], own=wqbf, grp=wqbf)
                    dma(wqbf[:, :, 256 + h * 32:256 + (h + 1) * 32],
                        Wt["w_qb"][l][:, h * 96 + 64:h * 96 + 96].rearrange("(k p) n -> p k n", p=128),
                        writes=[wqbf], own=wqbf, grp=wqbf)
                wqb = AR.alloc("wqb", [128, 2, 384], BF16)
                vcopy(wqb.ap, wqbf.ap, [wqbf], [wqb], q="pool")
                wkvf = AR.alloc("wkvf", [128, 768], F32)
                for h in range(4):
                    dma(wkvf[:, h * 64:(h + 1) * 64], Wt["w_kvb"][l][:, h * 192:h * 192 + 64],
                        writes=[wkvf], own=wkvf, grp=wkvf)
                    dma(wkvf[:, 256 + h * 128:256 + (h + 1) * 128], Wt["w_kvb"][l][:, h * 192 + 64:h * 192 + 192],
                        writes=[wkvf], own=wkvf, grp=wkvf)
                wkv = AR.alloc("wkv", [128, 768], BF16)
                vcopy(wkv.ap, wkvf.ap, [wkvf], [wkv], q="pool")
                ones_f = AR.alloc("ones_f", [128, 8], F32)
                memset("pool", ones_f, 1.0)
                xn_all = AR.alloc("xn", [128, 8, TH], BF16)
                xn_b = [P.buf("xnb%d" % i, xn_all[:, :, i * 512:(i + 1) * 512]) for i in range(TPH)]
                PH_BASE = AR.off
                w_in = Wt["w_in"][l]

                for half in range(NHALF):
                    phase_reset(PH_BASE)
                    xt_r = Rot([AR.alloc("xTt", [128, 8, 512], F32) for _ in range(2)])
                    sq8_r = Rot([AR.alloc("sq8", [128, 8, 512], BF16) for _ in range(1)])
                    for s in range(TPH):
                        t0 = half * TH + s * 512
                        xt_ = xt_r.next()
                        sq8 = sq8_r.next()
                        load(xt_, xT[l][:, :, t0:t0 + 512].rearrange("c p j -> p c j"))
                        tt("pool", sq8.ap, xt_.ap, xt_.ap, ALU.mult, [xt_], [sq8])
                        ps = pb[s % 2]
                        for c in range(8):
                            mm(ps, ps.ap, ones128.ap, sq8[:, c, :], c == 0, c == 7, [sq8, ones128])
                        rt = tmp("rt", [128, 512], F32, 2)
                        act(AF.Ln, rt.ap, ps.ap, [ps, eps_c], [rt], bias=eps_c.ap, scale=1.0 / DM)
                        rr = tmp("rr", [128, 512], F32, 2)
                        act(AF.Exp, rr.ap, rt.ap, [rt], [rr], scale=-0.5)
                        for c in range(8):
                            stt(xn_b[s][:, c, :], xt_[:, c, :], gcol[:, c:c + 1], rr.ap, ALU.mult, ALU.mult,
                                [xt_, rr, gcol], [xn_b[s]])
                    P.barrier()
                    new_pb()
                    phase_reset(PH_BASE)
                    wb_r = Rot([AR.alloc("wb", [128, 8, 128], BF16) for _ in range(4)])
                    wtb_r = Rot([AR.alloc("wtb", [128, 8, 512], BF16) for _ in range(2)])
                    stage_r = Rot([AR.alloc("stg", [128, 2048], BF16) for _ in range(3)])
                    main_rot = Rot(pb[0:3])
                    pss_rot = Rot(pb[3:5])
                    aux_rot = Rot(pb[5:8])

                    def load_w(segs, ncols):
                        wb = wb_r.next()
                        o = 0
                        for (c0, n) in segs:
                            dma(wb[:, :, o:o + n], w_in[:, c0:c0 + n].rearrange("(k p) n -> p k n", p=128),
                                writes=[wb], own=wb, grp=wb, q="pool")
                            o += n
                        return wb

                    def proj_fm(wb, ncols, s):
                        ps = main_rot.next()
                        for c in range(8):
                            mm(ps, ps[0:ncols, :], wb[:, c, 0:ncols], xn_b[s][:, c, :], c == 0, c == 7, [wb, xn_b[s]])
                        return ps

                    units = []

                    def fm_norm_unit(segs, group, gname, dests):
                        ncols = sum(n for _, n in segs)
                        units.append((lambda: load_w(segs, ncols),
                                      lambda wb: fm_norm_run(wb, ncols, group, gname, dests)))

                    def fm_norm_run(wb, ncols, group, gname, dests):
                        gb = col[gname]
                        pend = None
                        sg_new = None
                        for s in range(TPH + 1):
                            cur = None
                            if s < TPH:
                                if s % 4 == 0:
                                    sg_new = stage_r.next()
                                ps = proj_fm(wb, ncols, s)
                                qf, sq = rms1(ps, ncols)
                                cur = (qf, sq, s, sg_new)
                            if pend is not None:
                                qf_, sq_, s_, sg_ = pend
                                o = (s_ % 4) * 512
                                rms2([qf_], [sq_], ncols, group, [gb[0:ncols, :]], [gb],
                                     [(sg_, sg_[0:ncols, o:o + 512])], pss_rot.next())
                                if s_ % 4 == 3:
                                    t0 = half * TH + (s_ - 3) * 512
                                    for (dfn, r0, nr) in dests:
                                        store(dfn(t0, t0 + 2048), sg_, sg_[r0:r0 + nr, :])
                            pend = cur

                    def fm_act_unit(c0, func, bias_ap, bias_b, dst):
                        units.append((lambda: load_w([(c0, 128)], 128),
                                      lambda wb: fm_act_run(wb, func, bias_ap, bias_b, dst)))

                    def fm_act_run(wb, func, bias_ap, bias_b, dst):
                        sg_ = None
                        for s in range(TPH):
                            if s % 4 == 0:
                                sg_ = stage_r.next()
                            ps = proj_fm(wb, 128, s)
                            o = (s % 4) * 512
                            if bias_ap is None:
                                act(func, sg_[:, o:o + 512], ps.ap, [ps], [sg_])
                            else:
                                act(func, sg_[:, o:o + 512], ps.ap, [ps, bias_b], [sg_], bias=bias_ap)
                            if s % 4 == 3:
                                t0 = half * TH + (s - 3) * 512
                                store(dst[:, t0:t0 + 2048], sg_)

                    def load_wt(c0, ncols):
                        wtb = wtb_r.next()
                        dma(wtb[:, :, 0:ncols], w_in[:, c0:c0 + ncols].rearrange("(k p) n -> p k n", p=128),
                            writes=[wtb], own=wtb, q="pool")
                        return wtb

                    def tm_unit(c0, ncols, nh, aug, dst_fn):
                        units.append((lambda: load_wt(c0, ncols), lambda wtb: tm_run(wtb, ncols, nh, aug, dst_fn)))

                    def tm_run(wtb, ncols, nh, aug, dst_fn):
                        for s in range(TPH):
                            for k in range(4):
                                tok = half * TH + s * 512 + k * 128
                                ps = aux_rot.next()
                                for c in range(8):
                                    mm(ps, ps[:, 0:ncols], xn_b[s][:, c, k * 128:(k + 1) * 128], wtb[:, c, 0:ncols],
                                       c == 0, c == 7, [xn_b[s], wtb])
                                vcol = valid[:, tok // 128:tok // 128 + 1]
                                if not aug:
                                    vb = tmp("vb", [128, 512], BF16, 3)
                                    ts("dve", vb[:, 0:ncols], ps[:, 0:ncols], vcol, ALU.mult, [ps, valid], [vb])
                                    store(dst_fn(tok), vb, vb[:, 0:ncols])
                                else:
                                    va = tmp("va", [128, 8, 65], BF16, 3)
                                    ts("dve", va[:, 0:nh, 0:64], ps[:, 0:ncols].rearrange("p (h d) -> p h d", h=nh), vcol,
                                       ALU.mult, [ps, valid], [va])
                                    ts("dve", va[:, 0:nh, 64:65], ones_f[:, 0:nh].rearrange("p (h o) -> p h o", o=1), vcol,
                                       ALU.mult, [ones_f, valid], [va])
                                    store(dst_fn(tok), va, va[:, 0:nh, :])

                    for h in range(4):
                        fm_norm_unit([(OFF["a_q"] + h * 64, 64), (OFF["a_q"] + 256 + h * 64, 64)], 64, "a_qn",
                                     [(lambda a, b, h=h: aqT[h, :, a:b], 0, 128)])
                        fm_norm_unit([(OFF["a_k"] + h * 64, 64), (OFF["a_k"] + 256 + h * 64, 64)], 64, "a_kn",
                                     [(lambda a, b, h=h: akT[h, :, a:b], 0, 128)])
                    tm_unit(OFF["a_v"], 512, 0, False, lambda tok: avd[tok:tok + 128, :])
                    for b4 in range(4):
                        fm_norm_unit([(OFF["c_q"] + b4 * 128, 128)], 64, "c_qn",
                                     [(lambda a, b, hh=2 * b4: gqT[0][hh, :, a:b], 0, 64),
                                      (lambda a, b, hh=2 * b4 + 1: gqT[0][hh, :, a:b], 64, 64)])
                    fm_norm_unit([(OFF["c_k"], 128)], 64, "c_kn",
                                 [(lambda a, b: gkT[0][0, :, GH[0] + a:GH[0] + b], 0, 64),
                                  (lambda a, b: gkT[0][1, :, GH[0] + a:GH[0] + b], 64, 64)])
                    tm_unit(OFF["c_v"], 128, 2, True, lambda tok: gva[0][GH[0] + tok:GH[0] + tok + 128, :, :])
                    for g in range(3):
                        G = g + 1
                        for b4 in range(4):
                            fm_norm_unit([(OFF["d_q"] + g * 512 + b4 * 128, 128)], 64, "d_qn",
                                         [(lambda a, b, G=G, hh=2 * b4: gqT[G][hh, :, a:b], 0, 64),
                                          (lambda a, b, G=G, hh=2 * b4 + 1: gqT[G][hh, :, a:b], 64, 64)])
                            fm_norm_unit([(OFF["d_k"] + g * 512 + b4 * 128, 128)], 64, "d_kn",
                                         [(lambda a, b, G=G, hh=2 * b4: gkT[G][hh, :, GH[G] + a:GH[G] + b], 0, 64),
                                          (lambda a, b, G=G, hh=2 * b4 + 1: gkT[G][hh, :, GH[G] + a:GH[G] + b], 64, 64)])
                        tm_unit(OFF["d_v"] + g * 512, 512, 8, True,
                                lambda tok, G=G: gva[G][GH[G] + tok:GH[G] + tok + 128, :, :])
                    for h in range(4):
                        fm_norm_unit([(OFF["m_q"] + h * 128, 128)], 128, "m_qn", [(lambda a, b, h=h: mqT[h, :, a:b], 0, 128)])
                    for b20 in range(20):
                        fm_act_unit(OFF["z"] + b20 * 128, AF.Silu, None, None, zT[b20])
                    for b40 in range(40):
                        fm_act_unit(OFF["g"] + b40 * 128, AF.Sigmoid, bgate[:, b40 // 8, b40 % 8:b40 % 8 + 1], bgate, gT[b40])
                    wq = [units[0][0](), units[1][0]()]
                    for ui in range(len(units)):
                        if ui + 2 < len(units):
                            wq.append(units[ui + 2][0]())
                        units[ui][1](wq.pop(0))
                    wcq = [load_w([(OFF["b_cq"] + i * 128, 128)], 128) for i in range(2)]
                    for s in range(TPH):
                        t0 = half * TH + s * 512
                        qfs, sqs = [], []
                        for i in range(2):
                            ps = proj_fm(wcq[i], 128, s)
                            qf, sq = rms1(ps, 128)
                            qfs.append(qf)
                            sqs.append(sq)
                        cqn = tmp("cqn", [128, 2, 512], BF16, 2)
                        rms2(qfs, sqs, 128, 128, [b_cqn[:, 0:1], b_cqn[:, 1:2]], [b_cqn, b_cqn],
                             [(cqn, cqn[:, 0, :]), (cqn, cqn[:, 1, :])], pss_rot.next(), total=256)
                        for ob in range(3):
                            ps = aux_rot.next()
                            for i in range(2):
                                mm(ps, ps.ap, wqb[:, i, ob * 128:(ob + 1) * 128], cqn[:, i, :], i == 0, i == 1, [wqb, cqn])
                            qf, sq = rms1(ps, 128)
                            qo = tmp("qo", [128, 512], BF16, 3)
                            if ob < 2:
                                rms2([qf], [sq], 128, 64, [col["b_qn_n"].ap], [col["b_qn_n"]], [(qo, qo.ap)], pss_rot.next())
                                for hh in range(2):
                                    store(bqT[2 * ob + hh, 0:64, t0:t0 + 512], qo, qo[hh * 64:(hh + 1) * 64, :])
                            else:
                                rms2([qf], [sq], 128, 32, [col["b_qn_r"].ap], [col["b_qn_r"]], [(qo, qo.ap)], pss_rot.next())
                                rope(qo, 128, t0, aux_rot.next(), [(bqT, h, h * 32) for h in range(4)])
                    wckv = load_w([(OFF["b_ckv"], 128)], 128)
                    wkr = load_w([(OFF["b_kr"], 32)], 32)
                    for s in range(TPH):
                        t0 = half * TH + s * 512
                        ps = proj_fm(wckv, 128, s)
                        qf, sq = rms1(ps, 128)
                        ckvn = tmp("ckvn", [128, 512], BF16, 2)
                        rms2([qf], [sq], 128, 128, [col["b_ckvn"].ap], [col["b_ckvn"]], [(ckvn, ckvn.ap)], pss_rot.next())
                        for ob in range(2):
                            ps = aux_rot.next()
                            mm(ps, ps.ap, wkv[:, ob * 128:(ob + 1) * 128], ckvn.ap, True, True, [wkv, ckvn])
                            qf, sq = rms1(ps, 128)
                            ko = tmp("qo", [128, 512], BF16, 3)
                            rms2([qf], [sq], 128, 64, [col["b_kn_n"].ap], [col["b_kn_n"]], [(ko, ko.ap)], pss_rot.next())
                            for hh in range(2):
                                store(bkT[2 * ob + hh, 0:64, t0:t0 + 512], ko, ko[hh * 64:(hh + 1) * 64, :])
                        for k in range(4):
                            tok = t0 + k * 128
                            ps = aux_rot.next()
                            mm(ps, ps.ap, ckvn[:, k * 128:(k + 1) * 128], wkv[:, 256:768], True, True, [ckvn, wkv])
                            vb = tmp("vb", [128, 512], BF16, 3)
                            ts("dve", vb.ap, ps.ap, valid[:, tok // 128:tok // 128 + 1], ALU.mult, [ps, valid], [vb])
                            store(bvd[tok:tok + 128, :], vb)
                        ps = proj_fm(wkr, 32, s)
                        qf, sq = rms1(ps, 32)
                        ko = tmp("qo", [128, 512], BF16, 3)
                        rms2([qf], [sq], 32, 32, [col["b_kn_r"][0:32, :]], [col["b_kn_r"]], [(ko, ko[0:32, :])], pss_rot.next())
                        rope(ko, 32, t0, aux_rot.next(), [(bkT, h, 0) for h in range(4)])
                    P.barrier()
                    new_pb()

            def attn_core(maps, kts, vfn, vreads, onesfn, onesreads, bias, shift, po, pd, sps, dacc=None):
                def qk(kt):
                    res = []
                    for mi, m in enumerate(maps):
                        ps = sps[mi].next()
                        mm(ps, ps.ap, m["k"](kt), m["q"], True, True, m["reads"])
                        res.append(ps)
                    return res
                if isinstance(kts, int):
                    kts = list(range(kts))
                depth = max(1, min(len(sps[0].items) - 2, 2))
                qq = [qk(kts[i]) for i in range(min(depth, len(kts)))]
                for ki, kt in enumerate(kts):
                    if ki + depth < len(kts):
                        qq.append(qk(kts[ki + depth]))
                    cur = qq.pop(0)
                    for mi in range(len(maps)):
                        ps = cur[mi]
                        pT = tmp("pT", [128, 512], BF16, 4)
                        if bias is None:
                            act(AF.Exp, pT.ap, ps.ap, [ps], [pT], bias=-shift)
                        else:
                            e_ap, e_buf, const = bias(mi, kt)
                            p0 = tmp("p0", [128, 512], BF16, 4)
                            act(AF.Exp, p0.ap, ps.ap, [ps], [p0], bias=const - shift)
                            tt("dve", pT.ap, p0.ap, e_ap, ALU.mult, [p0, e_buf], [pT])
                        mm(po[mi], po[mi].ap, vfn(mi, kt), pT.ap, ki == 0, ki == len(kts) - 1, [pT] + vreads)
                        if dacc is None:
                            mm(pd[mi], pd[mi].ap, onesfn(kt), pT.ap, ki == 0, ki == len(kts) - 1, [pT] + onesreads)
                        elif ki % 3 == 2:
                            mm(pd[mi], pd[mi].ap, onesfn(kt), pT.ap, ki == 2, False, [pT] + onesreads)
                        elif ki == 0:
                            ts("dve", dacc.ap, pT.ap, valid[:, kt:kt + 1], ALU.mult, [pT, valid], [dacc])
                        else:
                            stt(dacc.ap, pT.ap, valid[:, kt:kt + 1], dacc.ap, ALU.mult, ALU.add, [pT, valid, dacc], [dacc])
                if dacc is not None:
                    assert len(kts) >= 3 and len(maps) == 1
                    mm(pd[0], pd[0].ap, ones_ff.ap, dacc.ap, False, True, [ones_ff, dacc])

            phase_reset(LAYER_END)
            onesv = AR.alloc("onesv", [128, NT, 128], BF16)
            ones_ff = AR.alloc("ones_ff", [128, 128], F32)
            memset("pool", ones_ff, 1.0)
            for t in range(NT):
                ts("pool", onesv[:, t, :], ones_ff.ap, valid[:, t:t + 1], ALU.mult, [ones_ff, valid], [onesv])
            P.barrier()
            new_pb()
            ATT_BASE = AR.off

            if "A" in phases:
                phase_reset(ATT_BASE)
                sets = Rot([(AR.alloc("aK", [128, S], BF16), AR.alloc("aQ", [128, S], BF16),
                             AR.alloc("aV", [128, NT, 128], BF16)) for _ in range(2 if S <= 4096 else 1)])
                ea_rot = Rot([AR.alloc("EAh", [128, 6, 512], BF16) for _ in range(2)])
                for h in range(4):
                    Kb, Qb, Vb = sets.next()
                    load(Kb, akT[h])
                    load(Qb, aqT[h])
                    load(Vb, avd[:, h * 128:(h + 1) * 128].rearrange("(t p) d -> p t d", p=128))
                    sl = SL4[h]
                    EAh = ea_rot.next()
                    load(EAh, C["EA"][h], q="pool")
                    for qt in range(NQ):
                        q0 = qt * 512
                        zt = tmp("zt", [128, 512], BF16, 2)
                        load(zt, zT[h][:, q0:q0 + 512])

                        def bias(mi, kt, q0=q0, sl=sl, EAh=EAh):
                            k0 = kt * 128
                            if k0 < q0:
                                return EAh[:, 0, :], EAh, -sl * (q0 - k0 - 127)
                            if k0 < q0 + 512:
                                return EAh[:, 2 + (k0 - q0) // 128, :], EAh, 0.0
                            return EAh[:, 1, :], EAh, -sl * (k0 - q0 - 511)

                        maps = [dict(k=lambda kt, m=m, Kb=Kb: Kb[64 * m:64 * m + 64, kt * 128:(kt + 1) * 128],
                                     q=Qb[64 * m:64 * m + 64, q0:q0 + 512], reads=[Kb, Qb]) for m in range(2)]
                        po, pd = [pb[4], pb[6]], [pb[5], pb[7]]
                        kts = [kt for kt in range(NT)
                               if sl * max(0, kt * 128 - (q0 + 511), q0 - (kt * 128 + 127)) < 140.0]
                        attn_core(maps, kts, lambda mi, kt, Vb=Vb: Vb[:, kt, :], [Vb], lambda kt: onesv[:, kt, :], [onesv],
                                  bias, SH_A, po, pd, [Rot([pb[0], pb[1]]), Rot([pb[2], pb[3]])])
                        tn = []
                        for m in range(2):
                            rd = tmp("rd", [128, 512], F32, 2)
                            srecip(rd, rd.ap, pd[m].ap, [pd[m]])
                            t_ = tmp("tn", [128, 512], F32, 2)
                            tt("dve", t_.ap, po[m].ap, rd.ap, ALU.mult, [po[m], rd], [t_])
                            tn.append(t_)
                        o_ = tmp("ao", [128, 512], F32, 2)
                        stt(o_.ap, tn[1].ap, neglam.ap, tn[0].ap, ALU.mult, ALU.add, [tn[0], tn[1], neglam], [o_])
                        sq = tmp("sq", [128, 512], BF16, 2)
                        tt("pool", sq.ap, o_.ap, o_.ap, ALU.mult, [o_], [sq])
                        on = tmp("on", [128, 512], F32, 2)
                        rms2([o_], [sq], 128, 128, [col["a_hn"].ap], [col["a_hn"]], [(on, on.ap)], pb[0])
                        u_ = tmp("u", [128, 512], BF16, 2)
                        tt("pool", u_.ap, on.ap, zt.ap, ALU.mult, [on, zt], [u_])
                        store(uT[0, h, :, q0:q0 + 512], u_)

            if "B" in phases:
                P.barrier()
                new_pb()
                phase_reset(ATT_BASE)
                sets = Rot([(AR.alloc("bK", [128, S], BF16), AR.alloc("bQ", [128, S], BF16),
                             AR.alloc("bV", [128, NT, 128], BF16)) for _ in range(2 if S <= 4096 else 1)])
                porot = Rot([(pb[4], pb[5]), (pb[6], pb[7])])
                for h in range(4):
                    Kb, Qb, Vb = sets.next()
                    load(Kb, bkT[h], sl=Kb[0:96, :])
                    load(Qb, bqT[h], sl=Qb[0:96, :])
                    load(Vb, bvd[:, h * 128:(h + 1) * 128].rearrange("(t p) d -> p t d", p=128))
                    for qt in range(NQ):
                        q0 = qt * 512
                        zt = tmp("zt", [128, 512], BF16, 2)
                        load(zt, zT[4 + h][:, q0:q0 + 512])
                        maps = [dict(k=lambda kt, Kb=Kb: Kb[0:96, kt * 128:(kt + 1) * 128], q=Qb[0:96, q0:q0 + 512],
                                     reads=[Kb, Qb])]
                        po_, pd_ = porot.next()
                        attn_core(maps, NT, lambda mi, kt, Vb=Vb: Vb[:, kt, :], [Vb], lambda kt: onesv[:, kt, :], [onesv],
                                  None, SH_B, [po_], [pd_], [Rot([pb[0], pb[1], pb[2], pb[3]])],
                                  dacc=tmp("dacc", [128, 512], F32, 2))
                        rd = tmp("rd", [128, 512], F32, 2)
                        srecip(rd, rd.ap, pd_.ap, [pd_])
                        t_ = tmp("tn", [128, 512], F32, 2)
                        tt("dve", t_.ap, po_.ap, rd.ap, ALU.mult, [po_, rd], [t_])
                        u_ = tmp("u", [128, 512], BF16, 2)
                        tt("pool", u_.ap, t_.ap, zt.ap, ALU.mult, [t_, zt], [u_])
                        store(uT[1, h, :, q0:q0 + 512], u_)

            if "M" in phases:
                P.barrier()
                new_pb()
                phase_reset(ATT_BASE)
                mt = AR.alloc("mt", [128, 2, DM], F32)
                load(mt, mem_in.rearrange("(k p) f -> p k f", p=128))
                memT = AR.alloc("memT", [128, 8, 256], F32)
                for c in range(8):
                    ps = pb[c % 4]
                    for k in range(2):
                        P.op("pe", lambda e, ps=ps, k=k, c=c: e.transpose(
                            out=ps[:, k * 128:(k + 1) * 128], in_=mt[:, k, c * 128:(c + 1) * 128], identity=ident.ap),
                            reads=[mt, ident], writes=[ps])
                    acopy(memT[:, c, :], ps[:, 0:256], [ps], [memT])
                sqm = AR.alloc("sqm", [128, 8, 256], BF16)
                tt("pool", sqm.ap, memT.ap, memT.ap, ALU.mult, [memT], [sqm])
                ps = pb[4]
                for c in range(8):
                    mm(ps, ps[:, 0:256], ones128.ap, sqm[:, c, :], c == 0, c == 7, [sqm, ones128])
                rtm = AR.alloc("rtm", [128, 256], F32)
                act(AF.Sqrt, rtm.ap, ps[:, 0:256], [ps, eps_c], [rtm], bias=eps_c.ap, scale=1.0 / DM)
                rrm = AR.alloc("rrm", [128, 256], F32)
                recip(rrm.ap, rtm.ap, [rtm], [rrm])
                mnT = AR.alloc("mnT", [128, 8, 256], BF16)
                for c in range(8):
                    stt(mnT[:, c, :], memT[:, c, :], mncol[:, c:c + 1], rrm.ap, ALU.mult, ALU.mult, [memT, rrm, mncol], [mnT])
                wmf = AR.alloc("wmf", [128, 8, 512], F32)
                wmk = AR.alloc("wmk", [128, 8, 512], BF16)
                wmv = AR.alloc("wmv", [128, 8, 512], BF16)
                load(wmf, Wt["w_mem_kv"][l][:, 0:512].rearrange("(k p) n -> p k n", p=128))
                vcopy(wmk.ap, wmf.ap, [wmf], [wmk], q="pool")
                load(wmf, Wt["w_mem_kv"][l][:, 512:1024].rearrange("(k p) n -> p k n", p=128))
                vcopy(wmv.ap, wmf.ap, [wmf], [wmv], q="pool")
                mkT = AR.alloc("mkT", [128, 4, 256], BF16)
                mv = AR.alloc("mv", [128, 2, 512], BF16)
                for h in range(4):
                    ps = pb[h % 2]
                    for c in range(8):
                        mm(ps, ps[:, 0:256], wmk[:, c, h * 128:(h + 1) * 128], mnT[:, c, :], c == 0, c == 7, [wmk, mnT])
                    qf, sq = rms1(ps, 128, w=256)
                    rms2([qf], [sq], 128, 128, [col["m_kn"].ap], [col["m_kn"]], [(mkT, mkT[:, h, :])], pb[2 + h % 2], w=256)
                for k in range(2):
                    ps = pb[4 + k]
                    for c in range(8):
                        mm(ps, ps.ap, mnT[:, c, k * 128:(k + 1) * 128], wmv[:, c, :], c == 0, c == 7, [mnT, wmv])
                    vcopy(mv[:, k, :], ps.ap, [ps], [mv])
                porot = Rot([(pb[4], pb[5]), (pb[6], pb[7])])
                srot = Rot([pb[0], pb[1], pb[2], pb[3]])
                for qt in range(NQ):
                    q0 = qt * 512
                    mq = tmp("mq", [128, 4, 512], BF16, 2)
                    load(mq, mqT[:, :, q0:q0 + 512].rearrange("h p j -> p h j"))
                    zt4 = tmp("zt4", [128, 4, 512], BF16, 2)
                    load(zt4, zT[16:20, :, q0:q0 + 512].rearrange("h p j -> p h j"))
                    for h in range(4):
                        maps = [dict(k=lambda kt, h=h: mkT[:, h, kt * 128:(kt + 1) * 128], q=mq[:, h, :], reads=[mkT, mq])]
                        po_, pd_ = porot.next()
                        attn_core(maps, 2, lambda mi, kt, h=h: mv[:, kt, h * 128:(h + 1) * 128], [mv],
                                  lambda kt: ones128.ap, [ones128], None, SH_M, [po_], [pd_], [srot])
                        rd = tmp("rd", [128, 512], F32, 2)
                        srecip(rd, rd.ap, pd_.ap, [pd_])
                        t_ = tmp("tn", [128, 512], F32, 2)
                        tt("dve", t_.ap, po_.ap, rd.ap, ALU.mult, [po_, rd], [t_])
                        u_ = tmp("u", [128, 512], BF16, 2)
                        tt("pool", u_.ap, t_.ap, zt4[:, h, :], ALU.mult, [t_, zt4], [u_])
                        store(uT[4, h, :, q0:q0 + 512], u_)

            def banded(groups, branch, use_sink, shift):
                P.barrier()
                new_pb()
                phase_reset(LAYER_END)
                UNIT = 2048
                accO = AR.alloc("accO", [64, 4, UNIT], F32)
                accD = AR.alloc("accD", [64, 4, UNIT], F32)
                Qb = AR.alloc("gQ", [64, 4, UNIT], BF16)
                maxspan = max(128 * GDIL[G] for G in groups)
                nk = 1 if groups == [0] else 4
                Kb = AR.alloc("gK", [64, nk, UNIT + 2 * maxspan], BF16)
                porot = Rot([(pb[4], pb[5]), (pb[6], pb[7])])
                srot = Rot([pb[0], pb[1], pb[2], pb[3]])
                eb_rot = Rot([AR.alloc("EBt", [128, 4, 384], BF16) for _ in range(2)])
                for u0 in range(0, S, UNIT):
                    for hh in range(2):
                        for gi, G in enumerate(groups):
                            dil, H = GDIL[G], GH[G]
                            span = 128 * dil
                            EBt = eb_rot.next()
                            load(EBt, C["EB"][G, hh * 4:hh * 4 + 4].rearrange("h p c -> p h c"), q="pool")
                            for j in range(4):
                                load(Qb, gqT[G][hh * 4 + j, :, u0:u0 + UNIT], sl=Qb[:, j, :], grp=Qb)
                            for j in range(nk):
                                kvh = hh if G == 0 else hh * 4 + j
                                load(Kb, gkT[G][kvh, :, H + u0 - span:H + u0 + UNIT + span],
                                     sl=Kb[:, j, 0:UNIT + 2 * span], grp=Kb)
                            NTL = 3 if G == 0 else 2
                            TW = NTL * 128
                            pend = []

                            def push(fn):
                                pend.append(fn)
                                if len(pend) > 3:
                                    pend.pop(0)()

                            for blk_i in range(UNIT // 128):
                                sp_i, r = blk_i // dil, blk_i % dil
                                qoff = sp_i * span + r
                                koffs = [sp_i * span + r + ((m * 128) if G == 0 else (64 + m * 128)) * dil for m in range(NTL)]
                                Vt = tmp("Vt", [128, 3, 4, 65], BF16, 3)
                                ov = tmp("ov", [128, 3, 64], BF16, 3)
                                for m in range(NTL):
                                    r0 = u0 + koffs[m]
                                    if G == 0:
                                        load(Vt, gva[G][r0:r0 + 127 * dil + 1:dil, hh:hh + 1, :], sl=Vt[:, m, 0:1, :], grp=Vt)
                                    else:
                                        load(Vt, gva[G][r0:r0 + 127 * dil + 1:dil, hh * 4:hh * 4 + 4, :], sl=Vt[:, m, :, :], grp=Vt)
                                vcopy(ov[:, 0:NTL, :], Vt[:, 0:NTL, 0, 64:65].to_broadcast([128, NTL, 64]), [Vt], [ov], q="pool")
                                po_, pd_ = porot.next()
                                for j in range(4):
                                    h = hh * 4 + j
                                    kj = 0 if G == 0 else j
                                    vj = 0 if G == 0 else j
                                    ps = srot.next()
                                    for m in range(NTL):
                                        mm(ps, ps[:, m * 128:(m + 1) * 128], Kb[:, kj, koffs[m]:koffs[m] + 127 * dil + 1:dil],
                                           Qb[:, j, qoff:qoff + 127 * dil + 1:dil], True, True, [Kb, Qb])
                                    p0 = tmp("p03", [128, 384], BF16, 5)
                                    act(AF.Exp, p0[:, 0:TW], ps[:, 0:TW], [ps], [p0], bias=-shift)
                                    pT = tmp("pT3", [128, 384], BF16, 5)
                                    tt("pool", pT[:, 0:TW], p0[:, 0:TW], EBt[:, j, 0:TW], ALU.mult, [p0, EBt], [pT])

                                    def stage2(j=j, vj=vj, pT=pT, Vt=Vt, ov=ov, po_=po_, pd_=pd_, qoff=qoff, NTL=NTL, gi=gi, dil=dil):
                                        for m in range(NTL):
                                            mm(po_, po_[0:64, j * 128:(j + 1) * 128], Vt[:, m, vj, 0:64], pT[:, m * 128:(m + 1) * 128],
                                               m == 0, m == NTL - 1, [Vt, pT])
                                        for m in range(NTL):
                                            mm(pd_, pd_[0:64, j * 128:(j + 1) * 128], ov[:, m, :], pT[:, m * 128:(m + 1) * 128],
                                               m == 0, m == NTL - 1, [ov, pT])
                                        if j == 3:
                                            aO = accO[:, :, qoff:qoff + 127 * dil + 1:dil]
                                            aD = accD[:, :, qoff:qoff + 127 * dil + 1:dil]
                                            pov = po_[0:64, :].rearrange("p (j q) -> p j q", j=4)
                                            pdv = pd_[0:64, :].rearrange("p (j q) -> p j q", j=4)
                                            if gi == 0:
                                                vcopy(aO, pov, [po_], [accO])
                                                acopy(aD, pdv, [pd_], [accD])
                                            else:
                                                tt("dve", aO, pov, aO, ALU.add, [po_, accO], [accO])
                                                tt("dve", aD, pdv, aD, ALU.add, [pd_, accD], [accD])
                                    push(stage2)
                            while pend:
                                pend.pop(0)()
                        if use_sink:
                            for j in range(4):
                                ts("dve", accD[:, j, :], accD[:, j, :], esink[0:64, hh * 4 + j:hh * 4 + j + 1], ALU.add,
                                   [accD, esink], [accD])
                        srecip(accD, accD.ap, accD.ap, [accD])
                        tt("dve", accO.ap, accO.ap, accD.ap, ALU.mult, [accO, accD], [accO])
                        for c0 in range(0, UNIT, 512):
                            zt = tmp("gz", [64, 4, 512], BF16, 2)
                            for j in range(4):
                                h = hh * 4 + j
                                load(zt, zT[branch * 4 + h // 2, (h % 2) * 64:(h % 2) * 64 + 64, u0 + c0:u0 + c0 + 512],
                                     sl=zt[:, j, :], grp=zt)
                            u_ = tmp("gu", [64, 4, 512], BF16, 2)
                            tt("pool", u_.ap, accO[:, :, c0:c0 + 512], zt.ap, ALU.mult, [accO, zt], [u_])
                            for j in range(4):
                                h = hh * 4 + j
                                store(uT[branch, h // 2, (h % 2) * 64:(h % 2) * 64 + 64, u0 + c0:u0 + c0 + 512], u_, u_[:, j, :])

            if "C" in phases:
                banded([0], 2, True, SH_C)
            if "D" in phases:
                banded([1, 2, 3], 3, False, SH_D)
            P.barrier()
            new_pb()

            if "G" in phases:
                phase_reset(LAYER_END)
                wbr = AR.alloc("wbr", [128, 5, 4, DM], BF16)
                wout = AR.alloc("wout", [128, 8, DM], BF16)
                WEND = AR.off
                wsf = AR.alloc("wsf", [128, 4, DM], F32)
                for i in range(5):
                    load(wsf, Wt["w_br"][l, i].rearrange("(c p) n -> p c n", p=128))
                    vcopy(wbr[:, i, :, :], wsf.ap, [wsf], [wbr], q="pool")
                for hf in range(2):
                    load(wsf, Wt["w_out"][l][hf * 512:(hf + 1) * 512, :].rearrange("(c p) n -> p c n", p=128))
                    vcopy(wout[:, hf * 4:(hf + 1) * 4, :], wsf.ap, [wsf], [wout], q="pool")
                P.barrier()
                new_pb()
                phase_reset(WEND)
                acc = AR.alloc("acc", [128, 8, 512], F32)
                accb = AR.alloc("accb", [128, 8, 512], BF16)
                mrot = Rot(pb[0:6])
                def ld_x(tq):
                    b_ = tmp("xTt", [128, 8, 512], F32, 2)
                    load(b_, xT[l][:, :, tq * 512:(tq + 1) * 512].rearrange("c p j -> p c j"))
                    return b_

                def ld_ug(k):
                    tq_, i_ = k // 5, k % 5
                    ut_ = tmp("ut", [128, 4, 512], BF16, 3)
                    load(ut_, uT[i_, :, :, tq_ * 512:(tq_ + 1) * 512].rearrange("c p j -> p c j"))
                    gt_ = tmp("gt", [128, 8, 512], BF16, 3)
                    load(gt_, gT[i_ * 8:(i_ + 1) * 8, :, tq_ * 512:(tq_ + 1) * 512].rearrange("b p j -> p b j"))
                    return ut_, gt_

                xq = [ld_x(0)]
                ugq = [ld_ug(0)]
                for tq in range(NQ):
                    t0 = tq * 512
                    xt_ = xq.pop(0)
                    for i in range(5):
                        if tq * 5 + i + 1 < NQ * 5:
                            ugq.append(ld_ug(tq * 5 + i + 1))
                        if i == 1 and tq + 1 < NQ:
                            xq.append(ld_x(tq + 1))
                        ut, gt = ugq.pop(0)
                        for j in range(8):
                            ps = mrot.next()
                            for c in range(4):
                                mm(ps, ps.ap, wbr[:, i, c, j * 128:(j + 1) * 128], ut[:, c, :], c == 0, c == 3, [wbr, ut])
                            if i == 0:
                                tt("dve", acc[:, j, :], ps.ap, gt[:, j, :], ALU.mult, [ps, gt], [acc])
                            else:
                                tm_ = tmp("tm", [128, 512], F32, 3)
                                tt("dve", tm_.ap, ps.ap, gt[:, j, :], ALU.mult, [ps, gt], [tm_])
                                tt("pool", acc[:, j, :], acc[:, j, :], tm_.ap, ALU.add, [acc, tm_], [acc])
                    acopy(accb.ap, acc.ap, [acc], [accb])
                    for j in range(8):
                        ps = mrot.next()
                        for c in range(8):
                            mm(ps, ps.ap, wout[:, c, j * 128:(j + 1) * 128], accb[:, c, :], c == 0, c == 7, [wout, accb])
                        tt("dve", xt_[:, j, :], ps.ap, xt_[:, j, :], ALU.add, [ps, xt_], [xt_])
                    if l < L - 1:
                        store(xT[l + 1][:, :, t0:t0 + 512].rearrange("c p j -> p c j"), xt_)
                    else:
                        for k in range(4):
                            yt = tmp("yt", [128, DM], F32, 2)
                            for hf in range(2):
                                ps = pb[6 + hf]
                                for c4 in range(4):
                                    c = hf * 4 + c4
                                    P.op("pe", lambda e, ps=ps, c=c, c4=c4, k=k, xt_=xt_: e.transpose(
                                        out=ps[:, c4 * 128:(c4 + 1) * 128], in_=xt_[:, c, k * 128:(k + 1) * 128],
                                        identity=ident.ap), reads=[xt_, ident], writes=[ps])
                                acopy(yt[:, hf * 512:(hf + 1) * 512], ps.ap, [ps], [yt])
                            store(y_out[t0 + k * 128:t0 + (k + 1) * 128, :], yt)
                P.barrier()
                new_pb()
        P.emit()
        print("ops", len(P.ops), "sems", P.nsem)
    return nc


_NC_CACHE = {}


def _core_inputs(xs, mems, valids, weights, S):
    consts = const_tables(S)
    maps = []
    for x, m, v in zip(xs, mems, valids):
        d = {"x": x, "mem": m, "valid": v}
        d.update(consts)
        d.update(weights)
        maps.append(d)
    return maps


def kernel(x_prompt, x_sample, mem_prompt, mem_sample, **w):
    S = 8192
    x_prompt = np.asarray(x_prompt, np.float32)
    x_sample = np.asarray(x_sample, np.float32)
    weights = {k: np.ascontiguousarray(np.asarray(v, np.float32)) for k, v in w.items()}
    xs, mems, valids = [], [], []
    for b in range(4):
        xp = np.zeros((S, DM), np.float32)
        xp[:x_prompt.shape[1]] = x_prompt[b]
        xs.append(xp)
        mems.append(np.ascontiguousarray(np.asarray(mem_prompt[b], np.float32)))
        v = np.zeros((S,), np.float32)
        v[:x_prompt.shape[1]] = 1.0
        valids.append(np.ascontiguousarray(v.reshape(S // 128, 128).T))
    for b in range(4):
        xs.append(np.ascontiguousarray(x_sample[b]))
        mems.append(np.ascontiguousarray(np.asarray(mem_sample[b], np.float32)))
        valids.append(np.ones((128, S // 128), np.float32))
    if S not in _NC_CACHE:
        _NC_CACHE[S] = build(S)
    nc = _NC_CACHE[S]
    in_maps = _core_inputs(xs, mems, valids, weights, S)
    res = run_bass_kernel_spmd(nc, in_maps, core_ids=list(range(8)))
    SP = x_prompt.shape[1]
    y_prompt = np.stack([np.asarray(res.results[b]["y"], np.float32)[:SP] for b in range(4)], axis=0)
    y_sample = np.stack([np.asarray(res.results[4 + b]["y"], np.float32) for b in range(4)], axis=0)
    return (y_prompt, y_sample)
```

```python
import contextlib
import math
import numpy as np
import concourse.bass as bass
import concourse.mybir as mybir
from concourse.bass_utils import run_bass_kernel_spmd

F32 = mybir.dt.float32
BF16 = mybir.dt.bfloat16
AF = mybir.ActivationFunctionType
ALU = mybir.AluOpType

COMPUTE = ("pe", "act", "dve", "pool")
QUEUES = ("pe", "act", "dve", "pool", "sp")

DM = 1024
N_IN = 15520
OFF = dict(a_q=0, a_k=512, a_v=1024, b_cq=1536, b_ckv=1792, b_kr=1920, c_q=1952, c_k=2464, c_v=2592,
           d_q=2720, d_k=4256, d_v=5792, m_q=7328, z=7840, g=10400)
EPS = 1e-6
BIG = 1.0e6
SH_A, SH_B, SH_C, SH_D, SH_M = 8.0, 10.0, 8.0, 8.0, 11.5
SL4 = [2.0 ** (-8.0 * (i + 1) / 4) for i in range(4)]
SL8 = [2.0 ** (-8.0 * (i + 1) / 8) for i in range(8)]
GDIL = (1, 1, 4, 16)
GNKV = (2, 8, 8, 8)
ARN = 44000


class Sem:
    __slots__ = ("h", "total")


class Buf:
    __slots__ = ("name", "ap", "last_w", "readers", "sem")

    def __init__(self, name, ap):
        self.name, self.ap = name, ap
        self.last_w, self.readers, self.sem = None, [], None

    def __getitem__(self, idx):
        return self.ap[idx]


class Op:
    __slots__ = ("q", "fn", "dma", "deps", "inc", "val", "dsem", "dval", "grp")


class Prog:
    def __init__(self, nc, stack):
        self.nc, self.stack = nc, stack
        self.ops, self.bufs, self.nsem = [], [], 0
        self.free_sems, self.live_sems = [], []

    def buf(self, name, ap):
        b = Buf(name, ap)
        self.bufs.append(b)
        return b

    def _newsem(self, name):
        self.nsem += 1
        return self.stack.enter_context(self.nc.semaphore(name))

    def _dsem(self):
        if self.free_sems:
            s = self.free_sems.pop()
        else:
            s = Sem()
            s.h, s.total = self._newsem("d%d" % self.nsem), 0
        self.live_sems.append(s)
        return s

    def op(self, q, fn, reads=(), writes=(), dma=None, grp=None):
        o = Op()
        o.q, o.fn, o.dma, o.grp = q, fn, dma, grp
        o.inc, o.val, o.dsem, o.dval = False, None, None, None
        deps = set()
        me = "dma" if dma is not None else q
        for b in reads:
            w = b.last_w
            if w is not None:
                we = "dma" if w.dma is not None else w.q
                if not (we == me and me == "pe"):
                    deps.add(w)
        for b in writes:
            w = b.last_w
            if w is not None:
                we = "dma" if w.dma is not None else w.q
                if (we != me or me == "dma") and not (grp is not None and w.grp is grp):
                    deps.add(w)
            for r in b.readers:
                re_ = "dma" if r.dma is not None else r.q
                if re_ != me or me == "dma":
                    deps.add(r)
        o.deps = deps
        for d in deps:
            if d.dma is None:
                d.inc = True
        if dma is not None:
            if dma.sem is None:
                dma.sem = self._dsem()
            dma.sem.total += 16
            o.dsem, o.dval = dma.sem, dma.sem.total
        for b in reads:
            b.readers.append(o)
        for b in writes:
            b.last_w = o
            b.readers = []
        self.ops.append(o)
        return o

    def barrier(self):
        last = {}
        for o in self.ops:
            if o.dma is None and o.fn is not None:
                last[o.q] = o
        for q in QUEUES:
            o = Op()
            o.q, o.fn, o.dma, o.grp = q, None, None, None
            o.inc, o.val, o.dsem, o.dval = False, None, None, None
            o.deps = set(v for k, v in last.items() if k != q)
            for d in o.deps:
                d.inc = True
            o.deps |= set(("D", s, s.total) for s in self.live_sems)
            self.ops.append(o)
        self.free_sems.extend(self.live_sems)
        self.live_sems = []
        for b in self.bufs:
            b.last_w, b.readers, b.sem = None, [], None

    def emit(self):
        nc = self.nc
        esem = {q: self._newsem("e_" + q) for q in COMPUTE}
        cnt = {q: 0 for q in COMPUTE}
        for o in self.ops:
            if o.dma is None and o.inc:
                cnt[o.q] += 1
                o.val = cnt[o.q]
        byq = {q: [] for q in QUEUES}
        for o in self.ops:
            byq[o.q].append(o)
        allsems = self.free_sems + self.live_sems
        block = self.stack.enter_context(nc.Block())
        handles = {"pe": block.tensor, "act": block.scalar, "dve": block.vector,
                   "pool": block.gpsimd, "sp": block.sync}

        def run_queue(q, e):
            known = {}
            for o in byq[q]:
                need = {}
                for d in o.deps:
                    if isinstance(d, tuple):
                        _, s, v = d
                        key, sem = ("D", id(s)), s.h
                    elif d.dma is not None:
                        key, sem, v = ("D", id(d.dsem)), d.dsem.h, d.dval
                    else:
                        key, sem, v = ("E", d.q), esem[d.q], d.val
                    if v > known.get(key, 0) and v > need.get(key, (None, 0))[1]:
                        need[key] = (sem, v)
                for key, (sem, v) in need.items():
                    e.wait_ge(sem, v)
                    known[key] = v
                if o.fn is None:
                    continue
                ins = o.fn(e)
                if o.dma is not None:
                    ins.then_inc(o.dsem.h, 16)
                elif o.inc:
                    ins.then_inc(esem[q], 1)
            if q == "sp":
                for s in allsems:
                    if s.total > known.get(("D", id(s)), 0):
                        e.wait_ge(s.h, s.total)

        for q in QUEUES:
            handles[q](lambda e, q=q: run_queue(q, e))


class Arena:
    def __init__(self, P, t, ncols):
        self.P, self.t, self.ncols, self.off, self.n = P, t, ncols, 0, 0

    def reset(self, off=0):
        self.off = off

    def alloc(self, name, shape, dt):
        per = 1
        for s in shape[1:]:
            per *= s
        nb = per * (2 if dt == BF16 else 4)
        n4 = (nb + 3) // 4
        assert self.off + n4 <= self.ncols, ("SBUF arena overflow", name, self.off, n4, self.ncols)
        ap = self.t[0:shape[0], self.off:self.off + n4]
        if dt == BF16:
            ap = ap.bitcast(BF16)[:, 0:per]
        if len(shape) == 3:
            ap = ap.rearrange("p (a b) -> p a b", a=shape[1])
        elif len(shape) == 4:
            ap = ap.rearrange("p (a b c) -> p a b c", a=shape[1], b=shape[2])
        self.off += n4
        self.n += 1
        return self.P.buf(name + "_" + str(self.n), ap)


class Rot:
    def __init__(self, items):
        self.items, self.i = items, 0

    def next(self):
        v = self.items[self.i % len(self.items)]
        self.i += 1
        return v


def const_tables(S):
    c = {}
    c["ident"] = np.eye(128, dtype=np.float32)
    c["ones128"] = np.ones((128, 128), np.float32)
    ii = np.arange(128)
    c["blk64"] = (ii[:, None] // 64 == ii[None, :] // 64).astype(np.float32)
    c["blk32"] = (ii[:, None] // 32 == ii[None, :] // 32).astype(np.float32)
    R = np.zeros((128, 128), np.float32)
    for m in range(128):
        if m % 32 < 16:
            R[m + 16, m] = -1.0
        else:
            R[m - 16, m] = 1.0
    c["rot"] = R
    i = np.arange(128, dtype=np.float32)[:, None]
    j = np.arange(512, dtype=np.float32)[None, :]
    c["tabA_L"] = (j - i).astype(np.float32)
    c["tabA_D"] = np.stack([np.abs(j - i - 128.0 * m) for m in range(4)], axis=1).astype(np.float32)
    q = np.arange(128, dtype=np.float32)[None, :]
    tabs = []
    for m in range(3):
        rel = (m - 1) * 128 + i - q
        tabs.append(np.where(np.abs(rel) <= 128, np.abs(rel), BIG))
    c["tabB128"] = np.concatenate(tabs, axis=1).astype(np.float32)
    tabs = []
    for m in range(3):
        rel = (m * 128 - 64) + i - q
        tabs.append(np.where(np.abs(rel) <= 64, np.abs(rel), BIG))
    c["tabB64"] = np.concatenate(tabs, axis=1).astype(np.float32)
    i64 = np.arange(128, dtype=np.float64)[:, None]
    j64 = np.arange(512, dtype=np.float64)[None, :]
    EA = np.zeros((4, 128, 6, 512), np.float32)
    for h in range(4):
        EA[h, :, 0, :] = np.exp(-SL4[h] * (j64 - i64 + 127.0))
        EA[h, :, 1, :] = np.exp(-SL4[h] * (i64 - j64 + 511.0))
        for m in range(4):
            EA[h, :, 2 + m, :] = np.exp(-SL4[h] * np.abs(j64 - i64 - 128.0 * m))
    c["EA"] = EA
    EB = np.zeros((4, 8, 128, 384), np.float32)
    for G in range(4):
        tab = c["tabB128"] if G == 0 else c["tabB64"]
        for h in range(8):
            EB[G, h] = np.where(tab < BIG, np.exp(-SL8[h] * GDIL[G] * tab.astype(np.float64)), 0.0)
    c["EB"] = EB
    half = 16
    inv = (np.float32(10000.0) ** (-np.arange(half, dtype=np.float32) / np.float32(half))).astype(np.float32)
    ang = (np.arange(S, dtype=np.float32)[:, None] * inv[None, :]).astype(np.float32)
    cos, sin = np.cos(ang).astype(np.float32), np.sin(ang).astype(np.float32)
    idx = np.arange(128) % 16
    c["cosT"] = np.ascontiguousarray(cos[:, idx].T)
    c["sinT"] = np.ascontiguousarray(sin[:, idx].T)
    return c


def const_shapes(S):
    return dict(ident=[128, 128], ones128=[128, 128], blk64=[128, 128], blk32=[128, 128], rot=[128, 128],
                tabA_L=[128, 512], tabA_D=[128, 4, 512], tabB128=[128, 384], tabB64=[128, 384],
                cosT=[128, S], sinT=[128, S], EA=[4, 128, 6, 512], EB=[4, 8, 128, 384])


WSHAPES = dict(norm_g=[2, 1024], w_in=[2, 1024, N_IN], a_qn=[2, 64], a_kn=[2, 64], a_lam=[2, 4, 64],
               a_hn=[2, 128], b_cqn=[2, 256], b_ckvn=[2, 128], w_qb=[2, 256, 384], w_kvb=[2, 128, 768],
               b_qn=[2, 96], b_kn=[2, 96], c_qn=[2, 64], c_kn=[2, 64], c_sink=[2, 8], d_qn=[2, 64],
               d_kn=[2, 64], m_norm=[2, 1024], w_mem_kv=[2, 1024, 1024], m_qn=[2, 128], m_kn=[2, 128],
               b_gate=[2, 5, 1024], w_br=[2, 5, 512, 1024], w_out=[2, 1024, 1024])


def build(S, L=2, dbg=(), phases="PABMCDG"):
    assert S % 2048 == 0
    NT, NQ = S // 128, S // 512
    TH = min(S, 4096)
    NHALF = S // TH
    TPH = TH // 512
    nc = bass.Bass("TRN2", target_bir_lowering=False)

    def din(name, shape, dt=F32):
        return nc.dram_tensor(name, list(shape), dt, kind="ExternalInput").ap()

    def dscr(name, shape, dt):
        kind = "ExternalOutput" if name in dbg else "Internal"
        return nc.dram_tensor(name, list(shape), dt, kind=kind).ap()

    x_in = din("x", [S, DM])
    mem_in = din("mem", [256, DM])
    valid_in = din("valid", [128, NT])
    C = {k: din(k, v) for k, v in const_shapes(S).items()}
    Wt = {k: din(k, v) for k, v in WSHAPES.items()}
    y_out = nc.dram_tensor("y", [S, DM], F32, kind="ExternalOutput").ap()

    xT = [dscr("xT0", [8, 128, S], F32), dscr("xT1", [8, 128, S], F32)]
    aqT = dscr("aqT", [4, 128, S], BF16)
    akT = dscr("akT", [4, 128, S], BF16)
    avd = dscr("avd", [S, 512], BF16)
    bqT = dscr("bqT", [4, 96, S], BF16)
    bkT = dscr("bkT", [4, 96, S], BF16)
    bvd = dscr("bvd", [S, 512], BF16)
    mqT = dscr("mqT", [4, 128, S], BF16)
    zT = dscr("zT", [20, 128, S], BF16)
    gT = dscr("gT", [40, 128, S], BF16)
    uT = dscr("uT", [5, 4, 128, S], BF16)
    GH = [128 * d for d in GDIL]
    gqT = [dscr("gqT%d" % g, [8, 64, S], BF16) for g in range(4)]
    gkT = [dscr("gkT%d" % g, [GNKV[g], 64, S + 2 * GH[g]], BF16) for g in range(4)]
    gva = [dscr("gva%d" % g, [S + 2 * GH[g], GNKV[g], 65], BF16) for g in range(4)]

    with contextlib.ExitStack() as st:
        P = Prog(nc, st)
        arena_t = st.enter_context(nc.sbuf_tensor("arena", [128, ARN], F32))
        AR = Arena(P, arena_t, ARN)
        pbt = [st.enter_context(nc.psum_tensor("pb%d" % i, [128, 512], F32)) for i in range(8)]
        pb = []

        def new_pb():
            pb[:] = [P.buf("pb%d" % i, pbt[i][:]) for i in range(8)]

        new_pb()

        def dma(out, in_, reads=(), writes=(), own=None, q="sp", grp=None, slow=False):
            if slow:
                P.op(q, lambda e: e.dma_start(out=out, in_=in_, allow_slow_non_contiguous=True),
                     reads=reads, writes=writes, dma=own, grp=grp)
            else:
                P.op(q, lambda e: e.dma_start(out=out, in_=in_), reads=reads, writes=writes, dma=own, grp=grp)

        def load(b, src, sl=None, q="sp", own=None, grp=None, slow=False):
            dma(b.ap if sl is None else sl, src, writes=[b], own=own or b, q=q, grp=grp, slow=slow)

        def store(dst, b, sl=None, q="sp"):
            dma(dst, b.ap if sl is None else sl, reads=[b], own=b, q=q)

        def mm(ps, out, lhsT, rhs, start, stop, reads):
            P.op("pe", lambda e: e.matmul(out, lhsT=lhsT, rhs=rhs, start=start, stop=stop), reads=reads, writes=[ps])

        def act(func, out, in_, reads, writes, bias=0.0, scale=1.0):
            P.op("act", lambda e: e.activation(out=out, in_=in_, func=func, bias=bias, scale=scale),
                 reads=reads, writes=writes)

        def acopy(out, in_, reads, writes):
            P.op("act", lambda e: e.copy(out=out, in_=in_), reads=reads, writes=writes)

        def vcopy(out, in_, reads, writes, q="dve"):
            P.op(q, lambda e: e.tensor_copy(out=out, in_=in_), reads=reads, writes=writes)

        def tt(q, out, in0, in1, op, reads, writes):
            P.op(q, lambda e: e.tensor_tensor(out=out, in0=in0, in1=in1, op=op), reads=reads, writes=writes)

        def ts(q, out, in0, s1, op0, reads, writes):
            P.op(q, lambda e: e.tensor_scalar(out=out, in0=in0, scalar1=s1, scalar2=None, op0=op0),
                 reads=reads, writes=writes)

        def stt(out, in0, scalar, in1, op0, op1, reads, writes):
            P.op("dve", lambda e: e.scalar_tensor_tensor(out=out, in0=in0, scalar=scalar, in1=in1, op0=op0, op1=op1),
                 reads=reads, writes=writes)

        def recip(out, in_, reads, writes):
            P.op("dve", lambda e: e.reciprocal(out=out, in_=in_), reads=reads, writes=writes)

        def srecip(out_b, out_ap, in_ap, in_bufs):
            ts("dve", out_ap, in_ap, 1e-18, ALU.max, in_bufs, [out_b])
            act(AF.Ln, out_ap, out_ap, [out_b], [out_b])
            act(AF.Exp, out_ap, out_ap, [out_b], [out_b], scale=-1.0)

        def memset(q, b, val, sl=None):
            ap = b.ap if sl is None else sl
            P.op(q, lambda e: e.memset(ap, val), writes=[b])

        tmp_rot = {}

        def tmp(name, shape, dt, n=2):
            key = (name, tuple(shape), dt)
            if key not in tmp_rot:
                tmp_rot[key] = Rot([AR.alloc(name, shape, dt) for _ in range(n)])
            return tmp_rot[key].next()

        def phase_reset(off):
            AR.reset(off)
            tmp_rot.clear()

        parsem = AR.alloc("parsem", [128, 1], F32)
        eps_c = AR.alloc("eps", [128, 1], F32)
        memset("dve", eps_c, EPS)
        ident = AR.alloc("ident", [128, 128], F32)
        load(ident, C["ident"], own=parsem)
        valid = AR.alloc("valid", [128, NT], F32)
        load(valid, valid_in, own=parsem)
        cb = {}
        stg0 = AR.off
        for nm in ("ones128", "blk64", "blk32", "rot"):
            cb[nm] = AR.alloc(nm, [128, 128], BF16)
        CONST_END = AR.off
        stgs = {}
        for nm in ("ones128", "blk64", "blk32", "rot"):
            stgs[nm] = AR.alloc(nm + "f", [128, 128], F32)
            load(stgs[nm], C[nm], own=parsem)
        P.barrier()
        new_pb()
        for nm in ("ones128", "blk64", "blk32", "rot"):
            vcopy(cb[nm].ap, stgs[nm].ap, [stgs[nm]], [cb[nm]])
        ones128, rotm = cb["ones128"], cb["rot"]
        blk = {128: cb["ones128"], 64: cb["blk64"], 32: cb["blk32"]}
        zero_b = AR.alloc("zero", [128, 2048], BF16)
        memset("pool", zero_b, 0.0)
        for g in range(4):
            H = GH[g]
            for kv in range(GNKV[g]):
                for base in (0, H + S):
                    for o in range(0, H, 2048):
                        n = min(2048, H - o)
                        store(gkT[g][kv, :, base + o:base + o + n], zero_b, zero_b[0:64, 0:n])
            for base in (0, H + S):
                for o in range(0, H, 128):
                    store(gva[g][base + o:base + o + 128].rearrange("p k c -> p (k c)"), zero_b,
                          zero_b[:, 0:GNKV[g] * 65])

        def transpose_in():
            xt_r = Rot([AR.alloc("xt", [128, 4, DM], F32) for _ in range(2)])
            xo_r = Rot([AR.alloc("xo", [128, 8, 512], F32) for _ in range(2)])
            pr = Rot(pb[0:4])
            for t0 in range(0, S, 512):
                xt = xt_r.next()
                load(xt, x_in[t0:t0 + 512, :].rearrange("(k p) f -> p k f", p=128))
                xo = xo_r.next()
                for c in range(8):
                    ps = pr.next()
                    for k in range(4):
                        P.op("pe", lambda e, ps=ps, k=k, c=c, xt=xt: e.transpose(
                            out=ps[:, k * 128:(k + 1) * 128], in_=xt[:, k, c * 128:(c + 1) * 128], identity=ident.ap),
                            reads=[xt, ident], writes=[ps])
                    acopy(xo[:, c, :], ps.ap, [ps], [xo])
                store(xT[0][:, :, t0:t0 + 512].rearrange("c p j -> p c j"), xo)

        transpose_in()
        P.barrier()
        new_pb()

        def rms1(ps, n, w=512):
            qf = tmp("qf", [128, 512], F32, 3)
            sq = tmp("sq", [128, 512], BF16, 3)
            vcopy(qf[0:n, 0:w], ps[0:n, 0:w], [ps], [qf])
            tt("pool", sq[0:n, 0:w], qf[0:n, 0:w], qf[0:n, 0:w], ALU.mult, [qf], [sq])
            return qf, sq

        def rms2(qfs, sqs, n, group, gains, gbufs, outs, ps2, w=512, total=None):
            for i, sq in enumerate(sqs):
                mm(ps2, ps2[0:n, 0:w], blk[group][0:n, 0:n], sq[0:n, 0:w], i == 0, i == len(sqs) - 1, [sq, blk[group]])
            rt = tmp("rt", [128, 512], F32, 2)
            act(AF.Ln, rt[0:n, 0:w], ps2[0:n, 0:w], [ps2, eps_c], [rt], bias=eps_c[0:n, :],
                scale=1.0 / (total if total else group))
            rr = tmp("rr", [128, 512], F32, 2)
            act(AF.Exp, rr[0:n, 0:w], rt[0:n, 0:w], [rt], [rr], scale=-0.5)
            for q_, g_, gb, (ob, oap) in zip(qfs, gains, gbufs, outs):
                stt(oap, q_[0:n, 0:w], g_, rr[0:n, 0:w], ALU.mult, ALU.mult, [q_, rr, gb], [ob])

        def rope(xb, n, t0, ps, dests):
            cs = tmp("cs", [128, 2, 512], F32, 2)
            load(cs, C["cosT"][0:n, t0:t0 + 512], sl=cs[0:n, 0, :], grp=cs)
            load(cs, C["sinT"][0:n, t0:t0 + 512], sl=cs[0:n, 1, :], grp=cs)
            mm(ps, ps[0:n, :], rotm[0:n, 0:n], xb[0:n, :], True, True, [rotm, xb])
            t1 = tmp("rp1", [128, 512], F32, 2)
            t2 = tmp("rp2", [128, 512], F32, 2)
            tt("dve", t1[0:n, :], xb[0:n, :], cs[0:n, 0, :], ALU.mult, [xb, cs], [t1])
            tt("dve", t2[0:n, :], ps[0:n, :], cs[0:n, 1, :], ALU.mult, [ps, cs], [t2])
            ro = tmp("ro", [128, 512], BF16, 2)
            tt("pool", ro[0:n, :], t1[0:n, :], t2[0:n, :], ALU.add, [t1, t2], [ro])
            for (dt_, h, r0) in dests:
                store(dt_[h, 64:96, t0:t0 + 512], ro, ro[r0:r0 + 32, :])

        for l in range(L):
            phase_reset(CONST_END)
            lam_init = 0.8 - 0.6 * math.exp(-0.3 * l)

            def colvec(name, vec_ap, n, reps, mul=None):
                b = AR.alloc(name, [128, 1], F32)
                for r in range(reps):
                    dma(b[r * n:(r + 1) * n, :], vec_ap.rearrange("(p o) -> p o", o=1), writes=[b], own=parsem, grp=b, slow=True)
                return (b, n * reps, mul)

            def colmat(name, vec_ap, ncol):
                b = AR.alloc(name, [128, ncol], F32)
                load(b, vec_ap.rearrange("(c p) -> p c", p=128), own=parsem, slow=True)
                return b

            cols = {}
            cols["a_qn"] = colvec("a_qn", Wt["a_qn"][l], 64, 2, 64 ** -0.5)
            cols["a_kn"] = colvec("a_kn", Wt["a_kn"][l], 64, 2)
            cols["a_hn"] = colvec("a_hn", Wt["a_hn"][l], 128, 1, 1.0 - lam_init)
            cols["b_ckvn"] = colvec("b_ckvn", Wt["b_ckvn"][l], 128, 1)
            cols["b_qn_n"] = colvec("b_qn_n", Wt["b_qn"][l, 0:64], 64, 2, 96 ** -0.5)
            cols["b_qn_r"] = colvec("b_qn_r", Wt["b_qn"][l, 64:96], 32, 4, 96 ** -0.5)
            cols["b_kn_n"] = colvec("b_kn_n", Wt["b_kn"][l, 0:64], 64, 2)
            cols["b_kn_r"] = colvec("b_kn_r", Wt["b_kn"][l, 64:96], 32, 1)
            cols["c_qn"] = colvec("c_qn", Wt["c_qn"][l], 64, 2, 64 ** -0.5)
            cols["c_kn"] = colvec("c_kn", Wt["c_kn"][l], 64, 2)
            cols["d_qn"] = colvec("d_qn", Wt["d_qn"][l], 64, 2, 64 ** -0.5)
            cols["d_kn"] = colvec("d_kn", Wt["d_kn"][l], 64, 2)
            cols["m_qn"] = colvec("m_qn", Wt["m_qn"][l], 128, 1, 128 ** -0.5)
            cols["m_kn"] = colvec("m_kn", Wt["m_kn"][l], 128, 1)
            gcol = colmat("gcol", Wt["norm_g"][l], 8)
            b_cqn = colmat("b_cqn", Wt["b_cqn"][l], 2)
            mncol = colmat("mncol", Wt["m_norm"][l], 8)
            bgate = AR.alloc("bgate", [128, 5, 8], F32)
            for i in range(5):
                dma(bgate[:, i, :], Wt["b_gate"][l, i].rearrange("(c p) -> p c", p=128), writes=[bgate], own=parsem, grp=bgate, slow=True)
            lamt = AR.alloc("lamt", [128, 4, 64], F32)
            load(lamt, Wt["a_lam"][l].rearrange("a d -> (a d)").partition_broadcast(128), sl=lamt.ap.rearrange("p a d -> p (a d)"),
                 own=parsem)
            sinkt = AR.alloc("sinkt", [128, 8], F32)
            load(sinkt, Wt["c_sink"][l].partition_broadcast(128), own=parsem)
            P.barrier()
            new_pb()
            for k, (b, n, mul) in cols.items():
                if mul is not None:
                    ts("pool", b[0:n, :], b[0:n, :], float(mul), ALU.mult, [b], [b])
            col = {k: v[0] for k, v in cols.items()}
            lp = AR.alloc("lp", [128, 2, 64], F32)
            tt("dve", lp[:, 0, :], lamt[:, 0, :], lamt[:, 1, :], ALU.mult, [lamt], [lp])
            tt("dve", lp[:, 1, :], lamt[:, 2, :], lamt[:, 3, :], ALU.mult, [lamt], [lp])
            ls = AR.alloc("ls", [128, 2], F32)
            P.op("dve", lambda e, ls=ls, lp=lp: e.reduce_sum(out=ls.ap, in_=lp.ap, axis=mybir.AxisListType.X),
                 reads=[lp], writes=[ls])
            le = AR.alloc("le", [128, 2], F32)
            act(AF.Exp, le.ap, ls.ap, [ls], [le])
            neglam = AR.alloc("neglam", [128, 1], F32)
            tt("dve", neglam.ap, le[:, 1:2], le[:, 0:1], ALU.subtract, [le], [neglam])
            ts("dve", neglam.ap, neglam.ap, -lam_init, ALU.add, [neglam], [neglam])
            esink = AR.alloc("esink", [128, 8], F32)
            act(AF.Exp, esink.ap, sinkt.ap, [sinkt], [esink], bias=-SH_C)
            LAYER_END = AR.off

            if "P" in phases:
                wqbf = AR.alloc("wqbf", [128, 2, 384], F32)
                for h in range(4):
                    dma(wqbf[:, :, h * 64:(h + 1) * 64],
                        Wt["w_qb"][l][:, h * 96:h * 96 + 64].rearrange("(k p) n -> p k n", p=128),
                        writes=[wqbf], own=wqbf, grp=wqbf)
                    dma(wqbf[:, :, 256 + h * 32:256 + (h + 1) * 32],
                        Wt["w_qb"][l][:, h * 96 + 64:h * 96 + 96].rearrange("(k p) n -> p k n", p=128),
                        writes=[wqbf], own=wqbf, grp=wqbf)
                wqb = AR.alloc("wqb", [128, 2, 384], BF16)
                vcopy(wqb.ap, wqbf.ap, [wqbf], [wqb], q="pool")
                wkvf = AR.alloc("wkvf", [128, 768], F32)
                for h in range(4):
                    dma(wkvf[:, h * 64:(h + 1) * 64], Wt["w_kvb"][l][:, h * 192:h * 192 + 64],
                        writes=[wkvf], own=wkvf, grp=wkvf)
                    dma(wkvf[:, 256 + h * 128:256 + (h + 1) * 128], Wt["w_kvb"][l][:, h * 192 + 64:h * 192 + 192],
                        writes=[wkvf], own=wkvf, grp=wkvf)
                wkv = AR.alloc("wkv", [128, 768], BF16)
                vcopy(wkv.ap, wkvf.ap, [wkvf], [wkv], q="pool")
                ones_f = AR.alloc("ones_f", [128, 8], F32)
                memset("pool", ones_f, 1.0)
                xn_all = AR.alloc("xn", [128, 8, TH], BF16)
                xn_b = [P.buf("xnb%d" % i, xn_all[:, :, i * 512:(i + 1) * 512]) for i in range(TPH)]
                PH_BASE = AR.off
                w_in = Wt["w_in"][l]

                for half in range(NHALF):
                    phase_reset(PH_BASE)
                    xt_r = Rot([AR.alloc("xTt", [128, 8, 512], F32) for _ in range(2)])
                    sq8_r = Rot([AR.alloc("sq8", [128, 8, 512], BF16) for _ in range(1)])
                    for s in range(TPH):
                        t0 = half * TH + s * 512
                        xt_ = xt_r.next()
                        sq8 = sq8_r.next()
                        load(xt_, xT[l][:, :, t0:t0 + 512].rearrange("c p j -> p c j"))
                        tt("pool", sq8.ap, xt_.ap, xt_.ap, ALU.mult, [xt_], [sq8])
                        ps = pb[s % 2]
                        for c in range(8):
                            mm(ps, ps.ap, ones128.ap, sq8[:, c, :], c == 0, c == 7, [sq8, ones128])
                        rt = tmp("rt", [128, 512], F32, 2)
                        act(AF.Ln, rt.ap, ps.ap, [ps, eps_c], [rt], bias=eps_c.ap, scale=1.0 / DM)
                        rr = tmp("rr", [128, 512], F32, 2)
                        act(AF.Exp, rr.ap, rt.ap, [rt], [rr], scale=-0.5)
                        for c in range(8):
                            stt(xn_b[s][:, c, :], xt_[:, c, :], gcol[:, c:c + 1], rr.ap, ALU.mult, ALU.mult,
                                [xt_, rr, gcol], [xn_b[s]])
                    P.barrier()
                    new_pb()
                    phase_reset(PH_BASE)
                    wb_r = Rot([AR.alloc("wb", [128, 8, 128], BF16) for _ in range(4)])
                    wtb_r = Rot([AR.alloc("wtb", [128, 8, 512], BF16) for _ in range(2)])
                    stage_r = Rot([AR.alloc("stg", [128, 2048], BF16) for _ in range(3)])
                    main_rot = Rot(pb[0:3])
                    pss_rot = Rot(pb[3:5])
                    aux_rot = Rot(pb[5:8])

                    def load_w(segs, ncols):
                        wb = wb_r.next()
                        o = 0
                        for (c0, n) in segs:
                            dma(wb[:, :, o:o + n], w_in[:, c0:c0 + n].rearrange("(k p) n -> p k n", p=128),
                                writes=[wb], own=wb, grp=wb, q="pool")
                            o += n
                        return wb

                    def proj_fm(wb, ncols, s):
                        ps = main_rot.next()
                        for c in range(8):
                            mm(ps, ps[0:ncols, :], wb[:, c, 0:ncols], xn_b[s][:, c, :], c == 0, c == 7, [wb, xn_b[s]])
                        return ps

                    units = []

                    def fm_norm_unit(segs, group, gname, dests):
                        ncols = sum(n for _, n in segs)
                        units.append((lambda: load_w(segs, ncols),
                                      lambda wb: fm_norm_run(wb, ncols, group, gname, dests)))

                    def fm_norm_run(wb, ncols, group, gname, dests):
                        gb = col[gname]
                        pend = None
                        sg_new = None
                        for s in range(TPH + 1):
                            cur = None
                            if s < TPH:
                                if s % 4 == 0:
                                    sg_new = stage_r.next()
                                ps = proj_fm(wb, ncols, s)
                                qf, sq = rms1(ps, ncols)
                                cur = (qf, sq, s, sg_new)
                            if pend is not None:
                                qf_, sq_, s_, sg_ = pend
                                o = (s_ % 4) * 512
                                rms2([qf_], [sq_], ncols, group, [gb[0:ncols, :]], [gb],
                                     [(sg_, sg_[0:ncols, o:o + 512])], pss_rot.next())
                                if s_ % 4 == 3:
                                    t0 = half * TH + (s_ - 3) * 512
                                    for (dfn, r0, nr) in dests:
                                        store(dfn(t0, t0 + 2048), sg_, sg_[r0:r0 + nr, :])
                            pend = cur

                    def fm_act_unit(c0, func, bias_ap, bias_b, dst):
                        units.append((lambda: load_w([(c0, 128)], 128),
                                      lambda wb: fm_act_run(wb, func, bias_ap, bias_b, dst)))

                    def fm_act_run(wb, func, bias_ap, bias_b, dst):
                        sg_ = None
                        for s in range(TPH):
                            if s % 4 == 0:
                                sg_ = stage_r.next()
                            ps = proj_fm(wb, 128, s)
                            o = (s % 4) * 512
                            if bias_ap is None:
                                act(func, sg_[:, o:o + 512], ps.ap, [ps], [sg_])
                            else:
                                act(func, sg_[:, o:o + 512], ps.ap, [ps, bias_b], [sg_], bias=bias_ap)
                            if s % 4 == 3:
                                t0 = half * TH + (s - 3) * 512
                                store(dst[:, t0:t0 + 2048], sg_)

                    def load_wt(c0, ncols):
                        wtb = wtb_r.next()
                        dma(wtb[:, :, 0:ncols], w_in[:, c0:c0 + ncols].rearrange("(k p) n -> p k n", p=128),
                            writes=[wtb], own=wtb, q="pool")
                        return wtb

                    def tm_unit(c0, ncols, nh, aug, dst_fn):
                        units.append((lambda: load_wt(c0, ncols), lambda wtb: tm_run(wtb, ncols, nh, aug, dst_fn)))

                    def tm_run(wtb, ncols, nh, aug, dst_fn):
                        for s in range(TPH):
                            for k in range(4):
                                tok = half * TH + s * 512 + k * 128
                                ps = aux_rot.next()
                                for c in range(8):
                                    mm(ps, ps[:, 0:ncols], xn_b[s][:, c, k * 128:(k + 1) * 128], wtb[:, c, 0:ncols],
                                       c == 0, c == 7, [xn_b[s], wtb])
                                vcol = valid[:, tok // 128:tok // 128 + 1]
                                if not aug:
                                    vb = tmp("vb", [128, 512], BF16, 3)
                                    ts("dve", vb[:, 0:ncols], ps[:, 0:ncols], vcol, ALU.mult, [ps, valid], [vb])
                                    store(dst_fn(tok), vb, vb[:, 0:ncols])
                                else:
                                    va = tmp("va", [128, 8, 65], BF16, 3)
                                    ts("dve", va[:, 0:nh, 0:64], ps[:, 0:ncols].rearrange("p (h d) -> p h d", h=nh), vcol,
                                       ALU.mult, [ps, valid], [va])
                                    ts("dve", va[:, 0:nh, 64:65], ones_f[:, 0:nh].rearrange("p (h o) -> p h o", o=1), vcol,
                                       ALU.mult, [ones_f, valid], [va])
                                    store(dst_fn(tok), va, va[:, 0:nh, :])

                    for h in range(4):
                        fm_norm_unit([(OFF["a_q"] + h * 64, 64), (OFF["a_q"] + 256 + h * 64, 64)], 64, "a_qn",
                                     [(lambda a, b, h=h: aqT[h, :, a:b], 0, 128)])
                        fm_norm_unit([(OFF["a_k"] + h * 64, 64), (OFF["a_k"] + 256 + h * 64, 64)], 64, "a_kn",
                                     [(lambda a, b, h=h: akT[h, :, a:b], 0, 128)])
                    tm_unit(OFF["a_v"], 512, 0, False, lambda tok: avd[tok:tok + 128, :])
                    for b4 in range(4):
                        fm_norm_unit([(OFF["c_q"] + b4 * 128, 128)], 64, "c_qn",
                                     [(lambda a, b, hh=2 * b4: gqT[0][hh, :, a:b], 0, 64),
                                      (lambda a, b, hh=2 * b4 + 1: gqT[0][hh, :, a:b], 64, 64)])
                    fm_norm_unit([(OFF["c_k"], 128)], 64, "c_kn",
                                 [(lambda a, b: gkT[0][0, :, GH[0] + a:GH[0] + b], 0, 64),
                                  (lambda a, b: gkT[0][1, :, GH[0] + a:GH[0] + b], 64, 64)])
                    tm_unit(OFF["c_v"], 128, 2, True, lambda tok: gva[0][GH[0] + tok:GH[0] + tok + 128, :, :])
                    for g in range(3):
                        G = g + 1
                        for b4 in range(4):
                            fm_norm_unit([(OFF["d_q"] + g * 512 + b4 * 128, 128)], 64, "d_qn",
                                         [(lambda a, b, G=G, hh=2 * b4: gqT[G][hh, :, a:b], 0, 64),
                                          (lambda a, b, G=G, hh=2 * b4 + 1: gqT[G][hh, :, a:b], 64, 64)])
                            fm_norm_unit([(OFF["d_k"] + g * 512 + b4 * 128, 128)], 64, "d_kn",
                                         [(lambda a, b, G=G, hh=2 * b4: gkT[G][hh, :, GH[G] + a:GH[G] + b], 0, 64),
                                          (lambda a, b, G=G, hh=2 * b4 + 1: gkT[G][hh, :, GH[G] + a:GH[G] + b], 64, 64)])
                        tm_unit(OFF["d_v"] + g * 512, 512, 8, True,
                                lambda tok, G=G: gva[G][GH[G] + tok:GH[G] + tok + 128, :, :])
                    for h in range(4):
                        fm_norm_unit([(OFF["m_q"] + h * 128, 128)], 128, "m_qn", [(lambda a, b, h=h: mqT[h, :, a:b], 0, 128)])
                    for b20 in range(20):
                        fm_act_unit(OFF["z"] + b20 * 128, AF.Silu, None, None, zT[b20])
                    for b40 in range(40):
                        fm_act_unit(OFF["g"] + b40 * 128, AF.Sigmoid, bgate[:, b40 // 8, b40 % 8:b40 % 8 + 1], bgate, gT[b40])
                    wq = [units[0][0](), units[1][0]()]
                    for ui in range(len(units)):
                        if ui + 2 < len(units):
                            wq.append(units[ui + 2][0]())
                        units[ui][1](wq.pop(0))
                    wcq = [load_w([(OFF["b_cq"] + i * 128, 128)], 128) for i in range(2)]
                    for s in range(TPH):
                        t0 = half * TH + s * 512
                        qfs, sqs = [], []
                        for i in range(2):
                            ps = proj_fm(wcq[i], 128, s)
                            qf, sq = rms1(ps, 128)
                            qfs.append(qf)
                            sqs.append(sq)
                        cqn = tmp("cqn", [128, 2, 512], BF16, 2)
                        rms2(qfs, sqs, 128, 128, [b_cqn[:, 0:1], b_cqn[:, 1:2]], [b_cqn, b_cqn],
                             [(cqn, cqn[:, 0, :]), (cqn, cqn[:, 1, :])], pss_rot.next(), total=256)
                        for ob in range(3):
                            ps = aux_rot.next()
                            for i in range(2):
                                mm(ps, ps.ap, wqb[:, i, ob * 128:(ob + 1) * 128], cqn[:, i, :], i == 0, i == 1, [wqb, cqn])
                            qf, sq = rms1(ps, 128)
                            qo = tmp("qo", [128, 512], BF16, 3)
                            if ob < 2:
                                rms2([qf], [sq], 128, 64, [col["b_qn_n"].ap], [col["b_qn_n"]], [(qo, qo.ap)], pss_rot.next())
                                for hh in range(2):
                                    store(bqT[2 * ob + hh, 0:64, t0:t0 + 512], qo, qo[hh * 64:(hh + 1) * 64, :])
                            else:
                                rms2([qf], [sq], 128, 32, [col["b_qn_r"].ap], [col["b_qn_r"]], [(qo, qo.ap)], pss_rot.next())
                                rope(qo, 128, t0, aux_rot.next(), [(bqT, h, h * 32) for h in range(4)])
                    wckv = load_w([(OFF["b_ckv"], 128)], 128)
                    wkr = load_w([(OFF["b_kr"], 32)], 32)
                    for s in range(TPH):
                        t0 = half * TH + s * 512
                        ps = proj_fm(wckv, 128, s)
                        qf, sq = rms1(ps, 128)
                        ckvn = tmp("ckvn", [128, 512], BF16, 2)
                        rms2([qf], [sq], 128, 128, [col["b_ckvn"].ap], [col["b_ckvn"]], [(ckvn, ckvn.ap)], pss_rot.next())
                        for ob in range(2):
                            ps = aux_rot.next()
                            mm(ps, ps.ap, wkv[:, ob * 128:(ob + 1) * 128], ckvn.ap, True, True, [wkv, ckvn])
                            qf, sq = rms1(ps, 128)
                            ko = tmp("qo", [128, 512], BF16, 3)
                            rms2([qf], [sq], 128, 64, [col["b_kn_n"].ap], [col["b_kn_n"]], [(ko, ko.ap)], pss_rot.next())
                            for hh in range(2):
                                store(bkT[2 * ob + hh, 0:64, t0:t0 + 512], ko, ko[hh * 64:(hh + 1) * 64, :])
                        for k in range(4):
                            tok = t0 + k * 128
                            ps = aux_rot.next()
                            mm(ps, ps.ap, ckvn[:, k * 128:(k + 1) * 128], wkv[:, 256:768], True, True, [ckvn, wkv])
                            vb = tmp("vb", [128, 512], BF16, 3)
                            ts("dve", vb.ap, ps.ap, valid[:, tok // 128:tok // 128 + 1], ALU.mult, [ps, valid], [vb])
                            store(bvd[tok:tok + 128, :], vb)
                        ps = proj_fm(wkr, 32, s)
                        qf, sq = rms1(ps, 32)
                        ko = tmp("qo", [128, 512], BF16, 3)
                        rms2([qf], [sq], 32, 32, [col["b_kn_r"][0:32, :]], [col["b_kn_r"]], [(ko, ko[0:32, :])], pss_rot.next())
                        rope(ko, 32, t0, aux_rot.next(), [(bkT, h, 0) for h in range(4)])
                    P.barrier()
                    new_pb()

            def attn_core(maps, kts, vfn, vreads, onesfn, onesreads, bias, shift, po, pd, sps, dacc=None):
                def qk(kt):
                    res = []
                    for mi, m in enumerate(maps):
                        ps = sps[mi].next()
                        mm(ps, ps.ap, m["k"](kt), m["q"], True, True, m["reads"])
                        res.append(ps)
                    return res
                if isinstance(kts, int):
                    kts = list(range(kts))
                depth = max(1, min(len(sps[0].items) - 2, 2))
                qq = [qk(kts[i]) for i in range(min(depth, len(kts)))]
                for ki, kt in enumerate(kts):
                    if ki + depth < len(kts):
                        qq.append(qk(kts[ki + depth]))
                    cur = qq.pop(0)
                    for mi in range(len(maps)):
                        ps = cur[mi]
                        pT = tmp("pT", [128, 512], BF16, 4)
                        if bias is None:
                            act(AF.Exp, pT.ap, ps.ap, [ps], [pT], bias=-shift)
                        else:
                            e_ap, e_buf, const = bias(mi, kt)
                            p0 = tmp("p0", [128, 512], BF16, 4)
                            act(AF.Exp, p0.ap, ps.ap, [ps], [p0], bias=const - shift)
                            tt("dve", pT.ap, p0.ap, e_ap, ALU.mult, [p0, e_buf], [pT])
                        mm(po[mi], po[mi].ap, vfn(mi, kt), pT.ap, ki == 0, ki == len(kts) - 1, [pT] + vreads)
                        if dacc is None:
                            mm(pd[mi], pd[mi].ap, onesfn(kt), pT.ap, ki == 0, ki == len(kts) - 1, [pT] + onesreads)
                        elif ki % 3 == 2:
                            mm(pd[mi], pd[mi].ap, onesfn(kt), pT.ap, ki == 2, False, [pT] + onesreads)
                        elif ki == 0:
                            ts("dve", dacc.ap, pT.ap, valid[:, kt:kt + 1], ALU.mult, [pT, valid], [dacc])
                        else:
                            stt(dacc.ap, pT.ap, valid[:, kt:kt + 1], dacc.ap, ALU.mult, ALU.add, [pT, valid, dacc], [dacc])
                if dacc is not None:
                    assert len(kts) >= 3 and len(maps) == 1
                    mm(pd[0], pd[0].ap, ones_ff.ap, dacc.ap, False, True, [ones_ff, dacc])

            phase_reset(LAYER_END)
            onesv = AR.alloc("onesv", [128, NT, 128], BF16)
            ones_ff = AR.alloc("ones_ff", [128, 128], F32)
            memset("pool", ones_ff, 1.0)
            for t in range(NT):
                ts("pool", onesv[:, t, :], ones_ff.ap, valid[:, t:t + 1], ALU.mult, [ones_ff, valid], [onesv])
            P.barrier()
            new_pb()
            ATT_BASE = AR.off

            if "A" in phases:
                phase_reset(ATT_BASE)
                sets = Rot([(AR.alloc("aK", [128, S], BF16), AR.alloc("aQ", [128, S], BF16),
                             AR.alloc("aV", [128, NT, 128], BF16)) for _ in range(2 if S <= 4096 else 1)])
                ea_rot = Rot([AR.alloc("EAh", [128, 6, 512], BF16) for _ in range(2)])
                for h in range(4):
                    Kb, Qb, Vb = sets.next()
                    load(Kb, akT[h])
                    load(Qb, aqT[h])
                    load(Vb, avd[:, h * 128:(h + 1) * 128].rearrange("(t p) d -> p t d", p=128))
                    sl = SL4[h]
                    EAh = ea_rot.next()
                    load(EAh, C["EA"][h], q="pool")
                    for qt in range(NQ):
                        q0 = qt * 512
                        zt = tmp("zt", [128, 512], BF16, 2)
                        load(zt, zT[h][:, q0:q0 + 512])

                        def bias(mi, kt, q0=q0, sl=sl, EAh=EAh):
                            k0 = kt * 128
                            if k0 < q0:
                                return EAh[:, 0, :], EAh, -sl * (q0 - k0 - 127)
                            if k0 < q0 + 512:
                                return EAh[:, 2 + (k0 - q0) // 128, :], EAh, 0.0
                            return EAh[:, 1, :], EAh, -sl * (k0 - q0 - 511)

                        maps = [dict(k=lambda kt, m=m, Kb=Kb: Kb[64 * m:64 * m + 64, kt * 128:(kt + 1) * 128],
                                     q=Qb[64 * m:64 * m + 64, q0:q0 + 512], reads=[Kb, Qb]) for m in range(2)]
                        po, pd = [pb[4], pb[6]], [pb[5], pb[7]]
                        kts = [kt for kt in range(NT)
                               if sl * max(0, kt * 128 - (q0 + 511), q0 - (kt * 128 + 127)) < 140.0]
                        attn_core(maps, kts, lambda mi, kt, Vb=Vb: Vb[:, kt, :], [Vb], lambda kt: onesv[:, kt, :], [onesv],
                                  bias, SH_A, po, pd, [Rot([pb[0], pb[1]]), Rot([pb[2], pb[3]])])
                        tn = []
                        for m in range(2):
                            rd = tmp("rd", [128, 512], F32, 2)
                            srecip(rd, rd.ap, pd[m].ap, [pd[m]])
                            t_ = tmp("tn", [128, 512], F32, 2)
                            tt("dve", t_.ap, po[m].ap, rd.ap, ALU.mult, [po[m], rd], [t_])
                            tn.append(t_)
                        o_ = tmp("ao", [128, 512], F32, 2)
                        stt(o_.ap, tn[1].ap, neglam.ap, tn[0].ap, ALU.mult, ALU.add, [tn[0], tn[1], neglam], [o_])
                        sq = tmp("sq", [128, 512], BF16, 2)
                        tt("pool", sq.ap, o_.ap, o_.ap, ALU.mult, [o_], [sq])
                        on = tmp("on", [128, 512], F32, 2)
                        rms2([o_], [sq], 128, 128, [col["a_hn"].ap], [col["a_hn"]], [(on, on.ap)], pb[0])
                        u_ = tmp("u", [128, 512], BF16, 2)
                        tt("pool", u_.ap, on.ap, zt.ap, ALU.mult, [on, zt], [u_])
                        store(uT[0, h, :, q0:q0 + 512], u_)

            if "B" in phases:
                P.barrier()
                new_pb()
                phase_reset(ATT_BASE)
                sets = Rot([(AR.alloc("bK", [128, S], BF16), AR.alloc("bQ", [128, S], BF16),
                             AR.alloc("bV", [128, NT, 128], BF16)) for _ in range(2 if S <= 4096 else 1)])
                porot = Rot([(pb[4], pb[5]), (pb[6], pb[7])])
                for h in range(4):
                    Kb, Qb, Vb = sets.next()
                    load(Kb, bkT[h], sl=Kb[0:96, :])
                    load(Qb, bqT[h], sl=Qb[0:96, :])
                    load(Vb, bvd[:, h * 128:(h + 1) * 128].rearrange("(t p) d -> p t d", p=128))
                    for qt in range(NQ):
                        q0 = qt * 512
                        zt = tmp("zt", [128, 512], BF16, 2)
                        load(zt, zT[4 + h][:, q0:q0 + 512])
                        maps = [dict(k=lambda kt, Kb=Kb: Kb[0:96, kt * 128:(kt + 1) * 128], q=Qb[0:96, q0:q0 + 512],
                                     reads=[Kb, Qb])]
                        po_, pd_ = porot.next()
                        attn_core(maps, NT, lambda mi, kt, Vb=Vb: Vb[:, kt, :], [Vb], lambda kt: onesv[:, kt, :], [onesv],
                                  None, SH_B, [po_], [pd_], [Rot([pb[0], pb[1], pb[2], pb[3]])],
                                  dacc=tmp("dacc", [128, 512], F32, 2))
                        rd = tmp("rd", [128, 512], F32, 2)
                        srecip(rd, rd.ap, pd_.ap, [pd_])
                        t_ = tmp("tn", [128, 512], F32, 2)
                        tt("dve", t_.ap, po_.ap, rd.ap, ALU.mult, [po_, rd], [t_])
                        u_ = tmp("u", [128, 512], BF16, 2)
                        tt("pool", u_.ap, t_.ap, zt.ap, ALU.mult, [t_, zt], [u_])
                        store(uT[1, h, :, q0:q0 + 512], u_)

            if "M" in phases:
                P.barrier()
                new_pb()
                phase_reset(ATT_BASE)
                mt = AR.alloc("mt", [128, 2, DM], F32)
                load(mt, mem_in.rearrange("(k p) f -> p k f", p=128))
                memT = AR.alloc("memT", [128, 8, 256], F32)
                for c in range(8):
                    ps = pb[c % 4]
                    for k in range(2):
                        P.op("pe", lambda e, ps=ps, k=k, c=c: e.transpose(
                            out=ps[:, k * 128:(k + 1) * 128], in_=mt[:, k, c * 128:(c + 1) * 128], identity=ident.ap),
                            reads=[mt, ident], writes=[ps])
                    acopy(memT[:, c, :], ps[:, 0:256], [ps], [memT])
                sqm = AR.alloc("sqm", [128, 8, 256], BF16)
                tt("pool", sqm.ap, memT.ap, memT.ap, ALU.mult, [memT], [sqm])
                ps = pb[4]
                for c in range(8):
                    mm(ps, ps[:, 0:256], ones128.ap, sqm[:, c, :], c == 0, c == 7, [sqm, ones128])
                rtm = AR.alloc("rtm", [128, 256], F32)
                act(AF.Sqrt, rtm.ap, ps[:, 0:256], [ps, eps_c], [rtm], bias=eps_c.ap, scale=1.0 / DM)
                rrm = AR.alloc("rrm", [128, 256], F32)
                recip(rrm.ap, rtm.ap, [rtm], [rrm])
                mnT = AR.alloc("mnT", [128, 8, 256], BF16)
                for c in range(8):
                    stt(mnT[:, c, :], memT[:, c, :], mncol[:, c:c + 1], rrm.ap, ALU.mult, ALU.mult, [memT, rrm, mncol], [mnT])
                wmf = AR.alloc("wmf", [128, 8, 512], F32)
                wmk = AR.alloc("wmk", [128, 8, 512], BF16)
                wmv = AR.alloc("wmv", [128, 8, 512], BF16)
                load(wmf, Wt["w_mem_kv"][l][:, 0:512].rearrange("(k p) n -> p k n", p=128))
                vcopy(wmk.ap, wmf.ap, [wmf], [wmk], q="pool")
                load(wmf, Wt["w_mem_kv"][l][:, 512:1024].rearrange("(k p) n -> p k n", p=128))
                vcopy(wmv.ap, wmf.ap, [wmf], [wmv], q="pool")
                mkT = AR.alloc("mkT", [128, 4, 256], BF16)
                mv = AR.alloc("mv", [128, 2, 512], BF16)
                for h in range(4):
                    ps = pb[h % 2]
                    for c in range(8):
                        mm(ps, ps[:, 0:256], wmk[:, c, h * 128:(h + 1) * 128], mnT[:, c, :], c == 0, c == 7, [wmk, mnT])
                    qf, sq = rms1(ps, 128, w=256)
                    rms2([qf], [sq], 128, 128, [col["m_kn"].ap], [col["m_kn"]], [(mkT, mkT[:, h, :])], pb[2 + h % 2], w=256)
                for k in range(2):
                    ps = pb[4 + k]
                    for c in range(8):
                        mm(ps, ps.ap, mnT[:, c, k * 128:(k + 1) * 128], wmv[:, c, :], c == 0, c == 7, [mnT, wmv])
                    vcopy(mv[:, k, :], ps.ap, [ps], [mv])
                porot = Rot([(pb[4], pb[5]), (pb[6], pb[7])])
                srot = Rot([pb[0], pb[1], pb[2], pb[3]])
                for qt in range(NQ):
                    q0 = qt * 512
                    mq = tmp("mq", [128, 4, 512], BF16, 2)
                    load(mq, mqT[:, :, q0:q0 + 512].rearrange("h p j -> p h j"))
                    zt4 = tmp("zt4", [128, 4, 512], BF16, 2)
                    load(zt4, zT[16:20, :, q0:q0 + 512].rearrange("h p j -> p h j"))
                    for h in range(4):
                        maps = [dict(k=lambda kt, h=h: mkT[:, h, kt * 128:(kt + 1) * 128], q=mq[:, h, :], reads=[mkT, mq])]
                        po_, pd_ = porot.next()
                        attn_core(maps, 2, lambda mi, kt, h=h: mv[:, kt, h * 128:(h + 1) * 128], [mv],
                                  lambda kt: ones128.ap, [ones128], None, SH_M, [po_], [pd_], [srot])
                        rd = tmp("rd", [128, 512], F32, 2)
                        srecip(rd, rd.ap, pd_.ap, [pd_])
                        t_ = tmp("tn", [128, 512], F32, 2)
                        tt("dve", t_.ap, po_.ap, rd.ap, ALU.mult, [po_, rd], [t_])
                        u_ = tmp("u", [128, 512], BF16, 2)
                        tt("pool", u_.ap, t_.ap, zt4[:, h, :], ALU.mult, [t_, zt4], [u_])
                        store(uT[4, h, :, q0:q0 + 512], u_)

            def banded(groups, branch, use_sink, shift):
                P.barrier()
                new_pb()
                phase_reset(LAYER_END)
                UNIT = 2048
                accO = AR.alloc("accO", [64, 4, UNIT], F32)
                accD = AR.alloc("accD", [64, 4, UNIT], F32)
                Qb = AR.alloc("gQ", [64, 4, UNIT], BF16)
                maxspan = max(128 * GDIL[G] for G in groups)
                nk = 1 if groups == [0] else 4
                Kb = AR.alloc("gK", [64, nk, UNIT + 2 * maxspan], BF16)
                porot = Rot([(pb[4], pb[5]), (pb[6], pb[7])])
                srot = Rot([pb[0], pb[1], pb[2], pb[3]])
                eb_rot = Rot([AR.alloc("EBt", [128, 4, 384], BF16) for _ in range(2)])
                for u0 in range(0, S, UNIT):
                    for hh in range(2):
                        for gi, G in enumerate(groups):
                            dil, H = GDIL[G], GH[G]
                            span = 128 * dil
                            EBt = eb_rot.next()
                            load(EBt, C["EB"][G, hh * 4:hh * 4 + 4].rearrange("h p c -> p h c"), q="pool")
                            for j in range(4):
                                load(Qb, gqT[G][hh * 4 + j, :, u0:u0 + UNIT], sl=Qb[:, j, :], grp=Qb)
                            for j in range(nk):
                                kvh = hh if G == 0 else hh * 4 + j
                                load(Kb, gkT[G][kvh, :, H + u0 - span:H + u0 + UNIT + span],
                                     sl=Kb[:, j, 0:UNIT + 2 * span], grp=Kb)
                            NTL = 3 if G == 0 else 2
                            TW = NTL * 128
                            pend = []

                            def push(fn):
                                pend.append(fn)
                                if len(pend) > 3:
                                    pend.pop(0)()

                            for blk_i in range(UNIT // 128):
                                sp_i, r = blk_i // dil, blk_i % dil
                                qoff = sp_i * span + r
                                koffs = [sp_i * span + r + ((m * 128) if G == 0 else (64 + m * 128)) * dil for m in range(NTL)]
                                Vt = tmp("Vt", [128, 3, 4, 65], BF16, 3)
                                ov = tmp("ov", [128, 3, 64], BF16, 3)
                                for m in range(NTL):
                                    r0 = u0 + koffs[m]
                                    if G == 0:
                                        load(Vt, gva[G][r0:r0 + 127 * dil + 1:dil, hh:hh + 1, :], sl=Vt[:, m, 0:1, :], grp=Vt)
                                    else:
                                        load(Vt, gva[G][r0:r0 + 127 * dil + 1:dil, hh * 4:hh * 4 + 4, :], sl=Vt[:, m, :, :], grp=Vt)
                                vcopy(ov[:, 0:NTL, :], Vt[:, 0:NTL, 0, 64:65].to_broadcast([128, NTL, 64]), [Vt], [ov], q="pool")
                                po_, pd_ = porot.next()
                                for j in range(4):
                                    h = hh * 4 + j
                                    kj = 0 if G == 0 else j
                                    vj = 0 if G == 0 else j
                                    ps = srot.next()
                                    for m in range(NTL):
                                        mm(ps, ps[:, m * 128:(m + 1) * 128], Kb[:, kj, koffs[m]:koffs[m] + 127 * dil + 1:dil],
                                           Qb[:, j, qoff:qoff + 127 * dil + 1:dil], True, True, [Kb, Qb])
                                    p0 = tmp("p03", [128, 384], BF16, 5)
                                    act(AF.Exp, p0[:, 0:TW], ps[:, 0:TW], [ps], [p0], bias=-shift)
                                    pT = tmp("pT3", [128, 384], BF16, 5)
                                    tt("pool", pT[:, 0:TW], p0[:, 0:TW], EBt[:, j, 0:TW], ALU.mult, [p0, EBt], [pT])

                                    def stage2(j=j, vj=vj, pT=pT, Vt=Vt, ov=ov, po_=po_, pd_=pd_, qoff=qoff, NTL=NTL, gi=gi, dil=dil):
                                        for m in range(NTL):
                                            mm(po_, po_[0:64, j * 128:(j + 1) * 128], Vt[:, m, vj, 0:64], pT[:, m * 128:(m + 1) * 128],
                                               m == 0, m == NTL - 1, [Vt, pT])
                                        for m in range(NTL):
                                            mm(pd_, pd_[0:64, j * 128:(j + 1) * 128], ov[:, m, :], pT[:, m * 128:(m + 1) * 128],
                                               m == 0, m == NTL - 1, [ov, pT])
                                        if j == 3:
                                            aO = accO[:, :, qoff:qoff + 127 * dil + 1:dil]
                                            aD = accD[:, :, qoff:qoff + 127 * dil + 1:dil]
                                            pov = po_[0:64, :].rearrange("p (j q) -> p j q", j=4)
                                            pdv = pd_[0:64, :].rearrange("p (j q) -> p j q", j=4)
                                            if gi == 0:
                                                vcopy(aO, pov, [po_], [accO])
                                                acopy(aD, pdv, [pd_], [accD])
                                            else:
                                                tt("dve", aO, pov, aO, ALU.add, [po_, accO], [accO])
                                                tt("dve", aD, pdv, aD, ALU.add, [pd_, accD], [accD])
                                    push(stage2)
                            while pend:
                                pend.pop(0)()
                        if use_sink:
                            for j in range(4):
                                ts("dve", accD[:, j, :], accD[:, j, :], esink[0:64, hh * 4 + j:hh * 4 + j + 1], ALU.add,
                                   [accD, esink], [accD])
                        srecip(accD, accD.ap, accD.ap, [accD])
                        tt("dve", accO.ap, accO.ap, accD.ap, ALU.mult, [accO, accD], [accO])
                        for c0 in range(0, UNIT, 512):
                            zt = tmp("gz", [64, 4, 512], BF16, 2)
                            for j in range(4):
                                h = hh * 4 + j
                                load(zt, zT[branch * 4 + h // 2, (h % 2) * 64:(h % 2) * 64 + 64, u0 + c0:u0 + c0 + 512],
                                     sl=zt[:, j, :], grp=zt)
                            u_ = tmp("gu", [64, 4, 512], BF16, 2)
                            tt("pool", u_.ap, accO[:, :, c0:c0 + 512], zt.ap, ALU.mult, [accO, zt], [u_])
                            for j in range(4):
                                h = hh * 4 + j
                                store(uT[branch, h // 2, (h % 2) * 64:(h % 2) * 64 + 64, u0 + c0:u0 + c0 + 512], u_, u_[:, j, :])

            if "C" in phases:
                banded([0], 2, True, SH_C)
            if "D" in phases:
                banded([1, 2, 3], 3, False, SH_D)
            P.barrier()
            new_pb()

            if "G" in phases:
                phase_reset(LAYER_END)
                wbr = AR.alloc("wbr", [128, 5, 4, DM], BF16)
                wout = AR.alloc("wout", [128, 8, DM], BF16)
                WEND = AR.off
                wsf = AR.alloc("wsf", [128, 4, DM], F32)
                for i in range(5):
                    load(wsf, Wt["w_br"][l, i].rearrange("(c p) n -> p c n", p=128))
                    vcopy(wbr[:, i, :, :], wsf.ap, [wsf], [wbr], q="pool")
                for hf in range(2):
                    load(wsf, Wt["w_out"][l][hf * 512:(hf + 1) * 512, :].rearrange("(c p) n -> p c n", p=128))
                    vcopy(wout[:, hf * 4:(hf + 1) * 4, :], wsf.ap, [wsf], [wout], q="pool")
                P.barrier()
                new_pb()
                phase_reset(WEND)
                acc = AR.alloc("acc", [128, 8, 512], F32)
                accb = AR.alloc("accb", [128, 8, 512], BF16)
                mrot = Rot(pb[0:6])
                def ld_x(tq):
                    b_ = tmp("xTt", [128, 8, 512], F32, 2)
                    load(b_, xT[l][:, :, tq * 512:(tq + 1) * 512].rearrange("c p j -> p c j"))
                    return b_

                def ld_ug(k):
                    tq_, i_ = k // 5, k % 5
                    ut_ = tmp("ut", [128, 4, 512], BF16, 3)
                    load(ut_, uT[i_, :, :, tq_ * 512:(tq_ + 1) * 512].rearrange("c p j -> p c j"))
                    gt_ = tmp("gt", [128, 8, 512], BF16, 3)
                    load(gt_, gT[i_ * 8:(i_ + 1) * 8, :, tq_ * 512:(tq_ + 1) * 512].rearrange("b p j -> p b j"))
                    return ut_, gt_

                xq = [ld_x(0)]
                ugq = [ld_ug(0)]
                for tq in range(NQ):
                    t0 = tq * 512
                    xt_ = xq.pop(0)
                    for i in range(5):
                        if tq * 5 + i + 1 < NQ * 5:
                            ugq.append(ld_ug(tq * 5 + i + 1))
                        if i == 1 and tq + 1 < NQ:
                            xq.append(ld_x(tq + 1))
                        ut, gt = ugq.pop(0)
                        for j in range(8):
                            ps = mrot.next()
                            for c in range(4):
                                mm(ps, ps.ap, wbr[:, i, c, j * 128:(j + 1) * 128], ut[:, c, :], c == 0, c == 3, [wbr, ut])
                            if i == 0:
                                tt("dve", acc[:, j, :], ps.ap, gt[:, j, :], ALU.mult, [ps, gt], [acc])
                            else:
                                tm_ = tmp("tm", [128, 512], F32, 3)
                                tt("dve", tm_.ap, ps.ap, gt[:, j, :], ALU.mult, [ps, gt], [tm_])
                                tt("pool", acc[:, j, :], acc[:, j, :], tm_.ap, ALU.add, [acc, tm_], [acc])
                    acopy(accb.ap, acc.ap, [acc], [accb])
                    for j in range(8):
                        ps = mrot.next()
                        for c in range(8):
                            mm(ps, ps.ap, wout[:, c, j * 128:(j + 1) * 128], accb[:, c, :], c == 0, c == 7, [wout, accb])
                        tt("dve", xt_[:, j, :], ps.ap, xt_[:, j, :], ALU.add, [ps, xt_], [xt_])
                    if l < L - 1:
                        store(xT[l + 1][:, :, t0:t0 + 512].rearrange("c p j -> p c j"), xt_)
                    else:
                        for k in range(4):
                            yt = tmp("yt", [128, DM], F32, 2)
                            for hf in range(2):
                                ps = pb[6 + hf]
                                for c4 in range(4):
                                    c = hf * 4 + c4
                                    P.op("pe", lambda e, ps=ps, c=c, c4=c4, k=k, xt_=xt_: e.transpose(
                                        out=ps[:, c4 * 128:(c4 + 1) * 128], in_=xt_[:, c, k * 128:(k + 1) * 128],
                                        identity=ident.ap), reads=[xt_, ident], writes=[ps])
                                acopy(yt[:, hf * 512:(hf + 1) * 512], ps.ap, [ps], [yt])
                            store(y_out[t0 + k * 128:t0 + (k + 1) * 128, :], yt)
                P.barrier()
                new_pb()
        P.emit()
        print("ops", len(P.ops), "sems", P.nsem)
    return nc


_NC_CACHE = {}


def _core_inputs(xs, mems, valids, weights, S):
    consts = const_tables(S)
    maps = []
    for x, m, v in zip(xs, mems, valids):
        d = {"x": x, "mem": m, "valid": v}
        d.update(consts)
        d.update(weights)
        maps.append(d)
    return maps


def kernel(x_prompt, x_sample, mem_prompt, mem_sample, **w):
    S = 8192
    x_prompt = np.asarray(x_prompt, np.float32)
    x_sample = np.asarray(x_sample, np.float32)
    weights = {k: np.ascontiguousarray(np.asarray(v, np.float32)) for k, v in w.items()}
    xs, mems, valids = [], [], []
    for b in range(4):
        xp = np.zeros((S, DM), np.float32)
        xp[:x_prompt.shape[1]] = x_prompt[b]
        xs.append(xp)
        mems.append(np.ascontiguousarray(np.asarray(mem_prompt[b], np.float32)))
        v = np.zeros((S,), np.float32)
        v[:x_prompt.shape[1]] = 1.0
        valids.append(np.ascontiguousarray(v.reshape(S // 128, 128).T))
    for b in range(4):
        xs.append(np.ascontiguousarray(x_sample[b]))
        mems.append(np.ascontiguousarray(np.asarray(mem_sample[b], np.float32)))
        valids.append(np.ones((128, S // 128), np.float32))
    if S not in _NC_CACHE:
        _NC_CACHE[S] = build(S)
    nc = _NC_CACHE[S]
    in_maps = _core_inputs(xs, mems, valids, weights, S)
    res = run_bass_kernel_spmd(nc, in_maps, core_ids=list(range(8)))
    SP = x_prompt.shape[1]
    y_prompt = np.stack([np.asarray(res.results[b]["y"], np.float32)[:SP] for b in range(4)], axis=0)
    y_sample = np.stack([np.asarray(res.results[4 + b]["y"], np.float32) for b in range(4)], axis=0)
    return (y_prompt, y_sample)
```

```python
import contextlib
import math
import numpy as np
import concourse.bass as bass
import concourse.mybir as mybir
from concourse.bass_utils import run_bass_kernel_spmd

F32 = mybir.dt.float32
BF16 = mybir.dt.bfloat16
AF = mybir.ActivationFunctionType
ALU = mybir.AluOpType

COMPUTE = ("pe", "act", "dve", "pool")
QUEUES = ("pe", "act", "dve", "pool", "sp")

DM = 1024
N_IN = 15520
OFF = dict(a_q=0, a_k=512, a_v=1024, b_cq=1536, b_ckv=1792, b_kr=1920, c_q=1952, c_k=2464, c_v=2592,
           d_q=2720, d_k=4256, d_v=5792, m_q=7328, z=7840, g=10400)
EPS = 1e-6
BIG = 1.0e6
SH_A, SH_B, SH_C, SH_D, SH_M = 8.0, 10.0, 8.0, 8.0, 11.5
SL4 = [2.0 ** (-8.0 * (i + 1) / 4) for i in range(4)]
SL8 = [2.0 ** (-8.0 * (i + 1) / 8) for i in range(8)]
GDIL = (1, 1, 4, 16)
GNKV = (2, 8, 8, 8)
ARN = 44000


class Sem:
    __slots__ = ("h", "total", "kind")


class Buf:
    __slots__ = ("name", "ap", "last_w", "readers", "sem")

    def __init__(self, name, ap):
        self.name, self.ap = name, ap
        self.last_w, self.readers, self.sem = None, [], None

    def __getitem__(self, idx):
        return self.ap[idx]


class Op:
    __slots__ = ("q", "fn", "dma", "deps", "inc", "val", "dsem", "dval", "grp")


class Prog:
    def __init__(self, nc, stack):
        self.nc, self.stack = nc, stack
        self.ops, self.bufs, self.nsem = [], [], 0
        self.free_sems, self.live_sems = {"hw": [], "sw": []}, []

    def buf(self, name, ap):
        b = Buf(name, ap)
        self.bufs.append(b)
        return b

    def _newsem(self, name):
        self.nsem += 1
        return self.stack.enter_context(self.nc.semaphore(name))

    def _dsem(self, kind):
        if self.free_sems[kind]:
            s = self.free_sems[kind].pop()
        else:
            s = Sem()
            s.h, s.total, s.kind = self._newsem("d%s%d" % (kind, self.nsem)), 0, kind
        self.live_sems.append(s)
        return s

    def op(self, q, fn, reads=(), writes=(), dma=None, grp=None):
        o = Op()
        o.q, o.fn, o.dma, o.grp = q, fn, dma, grp
        o.inc, o.val, o.dsem, o.dval = False, None, None, None
        deps = set()
        me = "dma" if dma is not None else q
        for b in reads:
            w = b.last_w
            if w is not None:
                we = "dma" if w.dma is not None else w.q
                if not (we == me and me == "pe"):
                    deps.add(w)
        for b in writes:
            w = b.last_w
            if w is not None:
                we = "dma" if w.dma is not None else w.q
                if (we != me or me == "dma") and not (grp is not None and w.grp is grp):
                    deps.add(w)
            for r in b.readers:
                re_ = "dma" if r.dma is not None else r.q
                if re_ != me or me == "dma":
                    deps.add(r)
        o.deps = deps
        for d in deps:
            if d.dma is None:
                d.inc = True
        if dma is not None:
            kind = "sw" if q == "pool" else "hw"
            if dma.sem is None:
                dma.sem = {}
            if kind not in dma.sem:
                dma.sem[kind] = self._dsem(kind)
            sm = dma.sem[kind]
            sm.total += 16
            o.dsem, o.dval = sm, sm.total
        for b in reads:
            b.readers.append(o)
        for b in writes:
            b.last_w = o
            b.readers = []
        self.ops.append(o)
        return o

    def barrier(self):
        last = {}
        for o in self.ops:
            if o.dma is None and o.fn is not None:
                last[o.q] = o
        for q in QUEUES:
            o = Op()
            o.q, o.fn, o.dma, o.grp = q, None, None, None
            o.inc, o.val, o.dsem, o.dval = False, None, None, None
            o.deps = set(v for k, v in last.items() if k != q)
            for d in o.deps:
                d.inc = True
            o.deps |= set(("D", s, s.total) for s in self.live_sems)
            self.ops.append(o)
        for sm in self.live_sems:
            self.free_sems[sm.kind].append(sm)
        self.live_sems = []
        for b in self.bufs:
            b.last_w, b.readers, b.sem = None, [], None

    def emit(self):
        nc = self.nc
        esem = {q: self._newsem("e_" + q) for q in COMPUTE}
        cnt = {q: 0 for q in COMPUTE}
        for o in self.ops:
            if o.dma is None and o.inc:
                cnt[o.q] += 1
                o.val = cnt[o.q]
        byq = {q: [] for q in QUEUES}
        for o in self.ops:
            byq[o.q].append(o)
        allsems = self.free_sems["hw"] + self.free_sems["sw"] + self.live_sems
        block = self.stack.enter_context(nc.Block())
        handles = {"pe": block.tensor, "act": block.scalar, "dve": block.vector,
                   "pool": block.gpsimd, "sp": block.sync}

        def run_queue(q, e):
            known = {}
            for o in byq[q]:
                need = {}
                for d in o.deps:
                    if isinstance(d, tuple):
                        _, s, v = d
                        key, sem = ("D", id(s)), s.h
                    elif d.dma is not None:
                        key, sem, v = ("D", id(d.dsem)), d.dsem.h, d.dval
                    else:
                        key, sem, v = ("E", d.q), esem[d.q], d.val
                    if v > known.get(key, 0) and v > need.get(key, (None, 0))[1]:
                        need[key] = (sem, v)
                for key, (sem, v) in need.items():
                    e.wait_ge(sem, v)
                    known[key] = v
                if o.fn is None:
                    continue
                ins = o.fn(e)
                if o.dma is not None:
                    ins.then_inc(o.dsem.h, 16)
                elif o.inc:
                    ins.then_inc(esem[q], 1)
            if q == "sp":
                for s in allsems:
                    if s.total > known.get(("D", id(s)), 0):
                        e.wait_ge(s.h, s.total)

        for q in QUEUES:
            handles[q](lambda e, q=q: run_queue(q, e))


class Arena:
    def __init__(self, P, t, ncols):
        self.P, self.t, self.ncols, self.off, self.n = P, t, ncols, 0, 0

    def reset(self, off=0):
        self.off = off

    def alloc(self, name, shape, dt):
        per = 1
        for s in shape[1:]:
            per *= s
        nb = per * (2 if dt == BF16 else 4)
        n4 = (nb + 3) // 4
        assert self.off + n4 <= self.ncols, ("SBUF arena overflow", name, self.off, n4, self.ncols)
        ap = self.t[0:shape[0], self.off:self.off + n4]
        if dt == BF16:
            ap = ap.bitcast(BF16)[:, 0:per]
        if len(shape) == 3:
            ap = ap.rearrange("p (a b) -> p a b", a=shape[1])
        elif len(shape) == 4:
            ap = ap.rearrange("p (a b c) -> p a b c", a=shape[1], b=shape[2])
        self.off += n4
        self.n += 1
        return self.P.buf(name + "_" + str(self.n), ap)


class Rot:
    def __init__(self, items):
        self.items, self.i = items, 0

    def next(self):
        v = self.items[self.i % len(self.items)]
        self.i += 1
        return v


def const_tables(S):
    c = {}
    c["ident"] = np.eye(128, dtype=np.float32)
    c["ones128"] = np.ones((128, 128), np.float32)
    ii = np.arange(128)
    c["blk64"] = (ii[:, None] // 64 == ii[None, :] // 64).astype(np.float32)
    c["blk32"] = (ii[:, None] // 32 == ii[None, :] // 32).astype(np.float32)
    R = np.zeros((128, 128), np.float32)
    for m in range(128):
        if m % 32 < 16:
            R[m + 16, m] = -1.0
        else:
            R[m - 16, m] = 1.0
    c["rot"] = R
    i = np.arange(128, dtype=np.float32)[:, None]
    j = np.arange(512, dtype=np.float32)[None, :]
    c["tabA_L"] = (j - i).astype(np.float32)
    c["tabA_D"] = np.stack([np.abs(j - i - 128.0 * m) for m in range(4)], axis=1).astype(np.float32)
    q = np.arange(128, dtype=np.float32)[None, :]
    tabs = []
    for m in range(3):
        rel = (m - 1) * 128 + i - q
        tabs.append(np.where(np.abs(rel) <= 128, np.abs(rel), BIG))
    c["tabB128"] = np.concatenate(tabs, axis=1).astype(np.float32)
    tabs = []
    for m in range(3):
        rel = (m * 128 - 64) + i - q
        tabs.append(np.where(np.abs(rel) <= 64, np.abs(rel), BIG))
    c["tabB64"] = np.concatenate(tabs, axis=1).astype(np.float32)
    i64 = np.arange(128, dtype=np.float64)[:, None]
    j64 = np.arange(512, dtype=np.float64)[None, :]
    EA = np.zeros((4, 128, 6, 512), np.float32)
    for h in range(4):
        EA[h, :, 0, :] = np.exp(-SL4[h] * (j64 - i64 + 127.0))
        EA[h, :, 1, :] = np.exp(-SL4[h] * (i64 - j64 + 511.0))
        for m in range(4):
            EA[h, :, 2 + m, :] = np.exp(-SL4[h] * np.abs(j64 - i64 - 128.0 * m))
    c["EA"] = EA
    EB = np.zeros((4, 8, 128, 384), np.float32)
    for G in range(4):
        tab = c["tabB128"] if G == 0 else c["tabB64"]
        for h in range(8):
            EB[G, h] = np.where(tab < BIG, np.exp(-SL8[h] * GDIL[G] * tab.astype(np.float64)), 0.0)
    c["EB"] = EB
    half = 16
    inv = (np.float32(10000.0) ** (-np.arange(half, dtype=np.float32) / np.float32(half))).astype(np.float32)
    ang = (np.arange(S, dtype=np.float32)[:, None] * inv[None, :]).astype(np.float32)
    cos, sin = np.cos(ang).astype(np.float32), np.sin(ang).astype(np.float32)
    idx = np.arange(128) % 16
    c["cosT"] = np.ascontiguousarray(cos[:, idx].T)
    c["sinT"] = np.ascontiguousarray(sin[:, idx].T)
    return c


def const_shapes(S):
    return dict(ident=[128, 128], ones128=[128, 128], blk64=[128, 128], blk32=[128, 128], rot=[128, 128],
                tabA_L=[128, 512], tabA_D=[128, 4, 512], tabB128=[128, 384], tabB64=[128, 384],
                cosT=[128, S], sinT=[128, S], EA=[4, 128, 6, 512], EB=[4, 8, 128, 384])


WSHAPES = dict(norm_g=[2, 1024], w_in=[2, 1024, N_IN], a_qn=[2, 64], a_kn=[2, 64], a_lam=[2, 4, 64],
               a_hn=[2, 128], b_cqn=[2, 256], b_ckvn=[2, 128], w_qb=[2, 256, 384], w_kvb=[2, 128, 768],
               b_qn=[2, 96], b_kn=[2, 96], c_qn=[2, 64], c_kn=[2, 64], c_sink=[2, 8], d_qn=[2, 64],
               d_kn=[2, 64], m_norm=[2, 1024], w_mem_kv=[2, 1024, 1024], m_qn=[2, 128], m_kn=[2, 128],
               b_gate=[2, 5, 1024], w_br=[2, 5, 512, 1024], w_out=[2, 1024, 1024])


def build(S, L=2, dbg=(), phases="PABMCDG"):
    assert S % 2048 == 0
    NT, NQ = S // 128, S // 512
    TH = min(S, 4096)
    NHALF = S // TH
    TPH = TH // 512
    nc = bass.Bass("TRN2", target_bir_lowering=False)

    def din(name, shape, dt=F32):
        return nc.dram_tensor(name, list(shape), dt, kind="ExternalInput").ap()

    def dscr(name, shape, dt):
        kind = "ExternalOutput" if name in dbg else "Internal"
        return nc.dram_tensor(name, list(shape), dt, kind=kind).ap()

    x_in = din("x", [S, DM])
    mem_in = din("mem", [256, DM])
    valid_in = din("valid", [128, NT])
    C = {k: din(k, v) for k, v in const_shapes(S).items()}
    Wt = {k: din(k, v) for k, v in WSHAPES.items()}
    y_out = nc.dram_tensor("y", [S, DM], F32, kind="ExternalOutput").ap()

    xT = [dscr("xT0", [8, 128, S], F32), dscr("xT1", [8, 128, S], F32)]
    aqT = dscr("aqT", [4, 128, S], BF16)
    akT = dscr("akT", [4, 128, S], BF16)
    avd = dscr("avd", [S, 512], BF16)
    bqT = dscr("bqT", [4, 96, S], BF16)
    bkT = dscr("bkT", [4, 96, S], BF16)
    bvd = dscr("bvd", [S, 512], BF16)
    mqT = dscr("mqT", [4, 128, S], BF16)
    zT = dscr("zT", [20, 128, S], BF16)
    gT = dscr("gT", [40, 128, S], BF16)
    uT = dscr("uT", [5, 4, 128, S], BF16)
    GH = [128 * d for d in GDIL]
    gqT = [dscr("gqT%d" % g, [8, 64, S], BF16) for g in range(4)]
    gkT = [dscr("gkT%d" % g, [GNKV[g], 64, S + 2 * GH[g]], BF16) for g in range(4)]
    gva = [dscr("gva%d" % g, [S + 2 * GH[g], GNKV[g], 65], BF16) for g in range(4)]

    with contextlib.ExitStack() as st:
        P = Prog(nc, st)
        arena_t = st.enter_context(nc.sbuf_tensor("arena", [128, ARN], F32))
        AR = Arena(P, arena_t, ARN)
        pbt = [st.enter_context(nc.psum_tensor("pb%d" % i, [128, 512], F32)) for i in range(8)]
        pb = []

        def new_pb():
            pb[:] = [P.buf("pb%d" % i, pbt[i][:]) for i in range(8)]

        new_pb()

        def dma(out, in_, reads=(), writes=(), own=None, q="sp", grp=None, slow=False):
            if slow:
                P.op(q, lambda e: e.dma_start(out=out, in_=in_, allow_slow_non_contiguous=True),
                     reads=reads, writes=writes, dma=own, grp=grp)
            else:
                P.op(q, lambda e: e.dma_start(out=out, in_=in_), reads=reads, writes=writes, dma=own, grp=grp)

        def load(b, src, sl=None, q="sp", own=None, grp=None, slow=False):
            dma(b.ap if sl is None else sl, src, writes=[b], own=own or b, q=q, grp=grp, slow=slow)

        def store(dst, b, sl=None, q="sp"):
            dma(dst, b.ap if sl is None else sl, reads=[b], own=b, q=q)

        def mm(ps, out, lhsT, rhs, start, stop, reads):
            P.op("pe", lambda e: e.matmul(out, lhsT=lhsT, rhs=rhs, start=start, stop=stop), reads=reads, writes=[ps])

        def act(func, out, in_, reads, writes, bias=0.0, scale=1.0):
            P.op("act", lambda e: e.activation(out=out, in_=in_, func=func, bias=bias, scale=scale),
                 reads=reads, writes=writes)

        def acopy(out, in_, reads, writes):
            P.op("act", lambda e: e.copy(out=out, in_=in_), reads=reads, writes=writes)

        def vcopy(out, in_, reads, writes, q="dve"):
            P.op(q, lambda e: e.tensor_copy(out=out, in_=in_), reads=reads, writes=writes)

        def tt(q, out, in0, in1, op, reads, writes):
            P.op(q, lambda e: e.tensor_tensor(out=out, in0=in0, in1=in1, op=op), reads=reads, writes=writes)

        def ts(q, out, in0, s1, op0, reads, writes):
            P.op(q, lambda e: e.tensor_scalar(out=out, in0=in0, scalar1=s1, scalar2=None, op0=op0),
                 reads=reads, writes=writes)

        def stt(out, in0, scalar, in1, op0, op1, reads, writes):
            P.op("dve", lambda e: e.scalar_tensor_tensor(out=out, in0=in0, scalar=scalar, in1=in1, op0=op0, op1=op1),
                 reads=reads, writes=writes)

        def recip(out, in_, reads, writes):
            P.op("dve", lambda e: e.reciprocal(out=out, in_=in_), reads=reads, writes=writes)

        def srecip(out_b, out_ap, in_ap, in_bufs):
            ts("dve", out_ap, in_ap, 1e-18, ALU.max, in_bufs, [out_b])
            act(AF.Ln, out_ap, out_ap, [out_b], [out_b])
            act(AF.Exp, out_ap, out_ap, [out_b], [out_b], scale=-1.0)

        def memset(q, b, val, sl=None):
            ap = b.ap if sl is None else sl
            P.op(q, lambda e: e.memset(ap, val), writes=[b])

        tmp_rot = {}

        def tmp(name, shape, dt, n=2):
            key = (name, tuple(shape), dt)
            if key not in tmp_rot:
                tmp_rot[key] = Rot([AR.alloc(name, shape, dt) for _ in range(n)])
            return tmp_rot[key].next()

        def phase_reset(off):
            AR.reset(off)
            tmp_rot.clear()

        parsem = AR.alloc("parsem", [128, 1], F32)
        eps_c = AR.alloc("eps", [128, 1], F32)
        memset("dve", eps_c, EPS)
        ident = AR.alloc("ident", [128, 128], F32)
        load(ident, C["ident"], own=parsem)
        valid = AR.alloc("valid", [128, NT], F32)
        load(valid, valid_in, own=parsem)
        cb = {}
        stg0 = AR.off
        for nm in ("ones128", "blk64", "blk32", "rot"):
            cb[nm] = AR.alloc(nm, [128, 128], BF16)
        CONST_END = AR.off
        stgs = {}
        for nm in ("ones128", "blk64", "blk32", "rot"):
            stgs[nm] = AR.alloc(nm + "f", [128, 128], F32)
            load(stgs[nm], C[nm], own=parsem)
        P.barrier()
        new_pb()
        for nm in ("ones128", "blk64", "blk32", "rot"):
            vcopy(cb[nm].ap, stgs[nm].ap, [stgs[nm]], [cb[nm]])
        ones128, rotm = cb["ones128"], cb["rot"]
        blk = {128: cb["ones128"], 64: cb["blk64"], 32: cb["blk32"]}
        zero_b = AR.alloc("zero", [128, 2048], BF16)
        memset("pool", zero_b, 0.0)
        for g in range(4):
            H = GH[g]
            for kv in range(GNKV[g]):
                for base in (0, H + S):
                    for o in range(0, H, 2048):
                        n = min(2048, H - o)
                        store(gkT[g][kv, :, base + o:base + o + n], zero_b, zero_b[0:64, 0:n])
            for base in (0, H + S):
                for o in range(0, H, 128):
                    store(gva[g][base + o:base + o + 128].rearrange("p k c -> p (k c)"), zero_b,
                          zero_b[:, 0:GNKV[g] * 65])

        def transpose_in():
            xt_r = Rot([AR.alloc("xt", [128, 4, DM], F32) for _ in range(2)])
            xo_r = Rot([AR.alloc("xo", [128, 8, 512], F32) for _ in range(2)])
            pr = Rot(pb[0:4])
            for t0 in range(0, S, 512):
                xt = xt_r.next()
                load(xt, x_in[t0:t0 + 512, :].rearrange("(k p) f -> p k f", p=128))
                xo = xo_r.next()
                for c in range(8):
                    ps = pr.next()
                    for k in range(4):
                        P.op("pe", lambda e, ps=ps, k=k, c=c, xt=xt: e.transpose(
                            out=ps[:, k * 128:(k + 1) * 128], in_=xt[:, k, c * 128:(c + 1) * 128], identity=ident.ap),
                            reads=[xt, ident], writes=[ps])
                    acopy(xo[:, c, :], ps.ap, [ps], [xo])
                store(xT[0][:, :, t0:t0 + 512].rearrange("c p j -> p c j"), xo)

        transpose_in()
        P.barrier()
        new_pb()

        def rms1(ps, n, w=512):
            qf = tmp("qf", [128, 512], F32, 3)
            sq = tmp("sq", [128, 512], BF16, 3)
            vcopy(qf[0:n, 0:w], ps[0:n, 0:w], [ps], [qf])
            tt("pool", sq[0:n, 0:w], qf[0:n, 0:w], qf[0:n, 0:w], ALU.mult, [qf], [sq])
            return qf, sq

        def rms2(qfs, sqs, n, group, gains, gbufs, outs, ps2, w=512, total=None):
            for i, sq in enumerate(sqs):
                mm(ps2, ps2[0:n, 0:w], blk[group][0:n, 0:n], sq[0:n, 0:w], i == 0, i == len(sqs) - 1, [sq, blk[group]])
            rt = tmp("rt", [128, 512], F32, 2)
            act(AF.Ln, rt[0:n, 0:w], ps2[0:n, 0:w], [ps2, eps_c], [rt], bias=eps_c[0:n, :],
                scale=1.0 / (total if total else group))
            rr = tmp("rr", [128, 512], F32, 2)
            act(AF.Exp, rr[0:n, 0:w], rt[0:n, 0:w], [rt], [rr], scale=-0.5)
            for q_, g_, gb, (ob, oap) in zip(qfs, gains, gbufs, outs):
                stt(oap, q_[0:n, 0:w], g_, rr[0:n, 0:w], ALU.mult, ALU.mult, [q_, rr, gb], [ob])

        def rope(xb, n, t0, ps, dests):
            cs = tmp("cs", [128, 2, 512], F32, 2)
            load(cs, C["cosT"][0:n, t0:t0 + 512], sl=cs[0:n, 0, :], grp=cs)
            load(cs, C["sinT"][0:n, t0:t0 + 512], sl=cs[0:n, 1, :], grp=cs)
            mm(ps, ps[0:n, :], rotm[0:n, 0:n], xb[0:n, :], True, True, [rotm, xb])
            t1 = tmp("rp1", [128, 512], F32, 2)
            t2 = tmp("rp2", [128, 512], F32, 2)
            tt("dve", t1[0:n, :], xb[0:n, :], cs[0:n, 0, :], ALU.mult, [xb, cs], [t1])
            tt("dve", t2[0:n, :], ps[0:n, :], cs[0:n, 1, :], ALU.mult, [ps, cs], [t2])
            ro = tmp("ro", [128, 512], BF16, 2)
            tt("pool", ro[0:n, :], t1[0:n, :], t2[0:n, :], ALU.add, [t1, t2], [ro])
            for (dt_, h, r0) in dests:
                store(dt_[h, 64:96, t0:t0 + 512], ro, ro[r0:r0 + 32, :])

        for l in range(L):
            phase_reset(CONST_END)
            lam_init = 0.8 - 0.6 * math.exp(-0.3 * l)

            def colvec(name, vec_ap, n, reps, mul=None):
                b = AR.alloc(name, [128, 1], F32)
                for r in range(reps):
                    dma(b[r * n:(r + 1) * n, :], vec_ap.rearrange("(p o) -> p o", o=1), writes=[b], own=parsem, grp=b, slow=True)
                return (b, n * reps, mul)

            def colmat(name, vec_ap, ncol):
                b = AR.alloc(name, [128, ncol], F32)
                load(b, vec_ap.rearrange("(c p) -> p c", p=128), own=parsem, slow=True)
                return b

            cols = {}
            cols["a_qn"] = colvec("a_qn", Wt["a_qn"][l], 64, 2, 64 ** -0.5)
            cols["a_kn"] = colvec("a_kn", Wt["a_kn"][l], 64, 2)
            cols["a_hn"] = colvec("a_hn", Wt["a_hn"][l], 128, 1, 1.0 - lam_init)
            cols["b_ckvn"] = colvec("b_ckvn", Wt["b_ckvn"][l], 128, 1)
            cols["b_qn_n"] = colvec("b_qn_n", Wt["b_qn"][l, 0:64], 64, 2, 96 ** -0.5)
            cols["b_qn_r"] = colvec("b_qn_r", Wt["b_qn"][l, 64:96], 32, 4, 96 ** -0.5)
            cols["b_kn_n"] = colvec("b_kn_n", Wt["b_kn"][l, 0:64], 64, 2)
            cols["b_kn_r"] = colvec("b_kn_r", Wt["b_kn"][l, 64:96], 32, 1)
            cols["c_qn"] = colvec("c_qn", Wt["c_qn"][l], 64, 2, 64 ** -0.5)
            cols["c_kn"] = colvec("c_kn", Wt["c_kn"][l], 64, 2)
            cols["d_qn"] = colvec("d_qn", Wt["d_qn"][l], 64, 2, 64 ** -0.5)
            cols["d_kn"] = colvec("d_kn", Wt["d_kn"][l], 64, 2)
            cols["m_qn"] = colvec("m_qn", Wt["m_qn"][l], 128, 1, 128 ** -0.5)
            cols["m_kn"] = colvec("m_kn", Wt["m_kn"][l], 128, 1)
            gcol = colmat("gcol", Wt["norm_g"][l], 8)
            b_cqn = colmat("b_cqn", Wt["b_cqn"][l], 2)
            mncol = colmat("mncol", Wt["m_norm"][l], 8)
            bgate = AR.alloc("bgate", [128, 5, 8], F32)
            for i in range(5):
                dma(bgate[:, i, :], Wt["b_gate"][l, i].rearrange("(c p) -> p c", p=128), writes=[bgate], own=parsem, grp=bgate, slow=True)
            lamt = AR.alloc("lamt", [128, 4, 64], F32)
            load(lamt, Wt["a_lam"][l].rearrange("a d -> (a d)").partition_broadcast(128), sl=lamt.ap.rearrange("p a d -> p (a d)"),
                 own=parsem)
            sinkt = AR.alloc("sinkt", [128, 8], F32)
            load(sinkt, Wt["c_sink"][l].partition_broadcast(128), own=parsem)
            P.barrier()
            new_pb()
            for k, (b, n, mul) in cols.items():
                if mul is not None:
                    ts("pool", b[0:n, :], b[0:n, :], float(mul), ALU.mult, [b], [b])
            col = {k: v[0] for k, v in cols.items()}
            lp = AR.alloc("lp", [128, 2, 64], F32)
            tt("dve", lp[:, 0, :], lamt[:, 0, :], lamt[:, 1, :], ALU.mult, [lamt], [lp])
            tt("dve", lp[:, 1, :], lamt[:, 2, :], lamt[:, 3, :], ALU.mult, [lamt], [lp])
            ls = AR.alloc("ls", [128, 2], F32)
            P.op("dve", lambda e, ls=ls, lp=lp: e.reduce_sum(out=ls.ap, in_=lp.ap, axis=mybir.AxisListType.X),
                 reads=[lp], writes=[ls])
            le = AR.alloc("le", [128, 2], F32)
            act(AF.Exp, le.ap, ls.ap, [ls], [le])
            neglam = AR.alloc("neglam", [128, 1], F32)
            tt("dve", neglam.ap, le[:, 1:2], le[:, 0:1], ALU.subtract, [le], [neglam])
            ts("dve", neglam.ap, neglam.ap, -lam_init, ALU.add, [neglam], [neglam])
            esink = AR.alloc("esink", [128, 8], F32)
            act(AF.Exp, esink.ap, sinkt.ap, [sinkt], [esink], bias=-SH_C)
            LAYER_END = AR.off

            if "P" in phases:
                wqbf = AR.alloc("wqbf", [128, 2, 384], F32)
                for h in range(4):
                    dma(wqbf[:, :, h * 64:(h + 1) * 64],
                        Wt["w_qb"][l][:, h * 96:h * 96 + 64].rearrange("(k p) n -> p k n", p=128),
                        writes=[wqbf], own=wqbf, grp=wqbf)
                    dma(wqbf[:, :, 256 + h * 32:256 + (h + 1) * 32],
                        Wt["w_qb"][l][:, h * 96 + 64:h * 96 + 96].rearrange("(k p) n -> p k n", p=128),
                        writes=[wqbf], own=wqbf, grp=wqbf)
                wqb = AR.alloc("wqb", [128, 2, 384], BF16)
                vcopy(wqb.ap, wqbf.ap, [wqbf], [wqb], q="pool")
                wkvf = AR.alloc("wkvf", [128, 768], F32)
                for h in range(4):
                    dma(wkvf[:, h * 64:(h + 1) * 64], Wt["w_kvb"][l][:, h * 192:h * 192 + 64],
                        writes=[wkvf], own=wkvf, grp=wkvf)
                    dma(wkvf[:, 256 + h * 128:256 + (h + 1) * 128], Wt["w_kvb"][l][:, h * 192 + 64:h * 192 + 192],
                        writes=[wkvf], own=wkvf, grp=wkvf)
                wkv = AR.alloc("wkv", [128, 768], BF16)
                vcopy(wkv.ap, wkvf.ap, [wkvf], [wkv], q="pool")
                ones_f = AR.alloc("ones_f", [128, 8], F32)
                memset("pool", ones_f, 1.0)
                xn_all = AR.alloc("xn", [128, 8, TH], BF16)
                xn_b = [P.buf("xnb%d" % i, xn_all[:, :, i * 512:(i + 1) * 512]) for i in range(TPH)]
                PH_BASE = AR.off
                w_in = Wt["w_in"][l]

                for half in range(NHALF):
                    phase_reset(PH_BASE)
                    xt_r = Rot([AR.alloc("xTt", [128, 8, 512], F32) for _ in range(2)])
                    sq8_r = Rot([AR.alloc("sq8", [128, 8, 512], BF16) for _ in range(1)])
                    for s in range(TPH):
                        t0 = half * TH + s * 512
                        xt_ = xt_r.next()
                        sq8 = sq8_r.next()
                        load(xt_, xT[l][:, :, t0:t0 + 512].rearrange("c p j -> p c j"))
                        tt("pool", sq8.ap, xt_.ap, xt_.ap, ALU.mult, [xt_], [sq8])
                        ps = pb[s % 2]
                        for c in range(8):
                            mm(ps, ps.ap, ones128.ap, sq8[:, c, :], c == 0, c == 7, [sq8, ones128])
                        rt = tmp("rt", [128, 512], F32, 2)
                        act(AF.Ln, rt.ap, ps.ap, [ps, eps_c], [rt], bias=eps_c.ap, scale=1.0 / DM)
                        rr = tmp("rr", [128, 512], F32, 2)
                        act(AF.Exp, rr.ap, rt.ap, [rt], [rr], scale=-0.5)
                        for c in range(8):
                            stt(xn_b[s][:, c, :], xt_[:, c, :], gcol[:, c:c + 1], rr.ap, ALU.mult, ALU.mult,
                                [xt_, rr, gcol], [xn_b[s]])
                    P.barrier()
                    new_pb()
                    phase_reset(PH_BASE)
                    wb_r = Rot([AR.alloc("wb", [128, 8, 128], BF16) for _ in range(4)])
                    wtb_r = Rot([AR.alloc("wtb", [128, 8, 512], BF16) for _ in range(2)])
                    stage_r = Rot([AR.alloc("stg", [128, 2048], BF16) for _ in range(3)])
                    main_rot = Rot(pb[0:3])
                    pss_rot = Rot(pb[3:5])
                    aux_rot = Rot(pb[5:8])

                    def load_w(segs, ncols):
                        wb = wb_r.next()
                        o = 0
                        for (c0, n) in segs:
                            dma(wb[:, :, o:o + n], w_in[:, c0:c0 + n].rearrange("(k p) n -> p k n", p=128),
                                writes=[wb], own=wb, grp=wb, q="pool")
                            o += n
                        return wb

                    def proj_fm(wb, ncols, s):
                        ps = main_rot.next()
                        for c in range(8):
                            mm(ps, ps[0:ncols, :], wb[:, c, 0:ncols], xn_b[s][:, c, :], c == 0, c == 7, [wb, xn_b[s]])
                        return ps

                    units = []

                    def fm_norm_unit(segs, group, gname, dests):
                        ncols = sum(n for _, n in segs)
                        units.append((lambda: load_w(segs, ncols),
                                      lambda wb: fm_norm_run(wb, ncols, group, gname, dests)))

                    def fm_norm_run(wb, ncols, group, gname, dests):
                        gb = col[gname]
                        pend = None
                        sg_new = None
                        for s in range(TPH + 1):
                            cur = None
                            if s < TPH:
                                if s % 4 == 0:
                                    sg_new = stage_r.next()
                                ps = proj_fm(wb, ncols, s)
                                qf, sq = rms1(ps, ncols)
                                cur = (qf, sq, s, sg_new)
                            if pend is not None:
                                qf_, sq_, s_, sg_ = pend
                                o = (s_ % 4) * 512
                                rms2([qf_], [sq_], ncols, group, [gb[0:ncols, :]], [gb],
                                     [(sg_, sg_[0:ncols, o:o + 512])], pss_rot.next())
                                if s_ % 4 == 3:
                                    t0 = half * TH + (s_ - 3) * 512
                                    for (dfn, r0, nr) in dests:
                                        store(dfn(t0, t0 + 2048), sg_, sg_[r0:r0 + nr, :])
                            pend = cur

                    def fm_act_unit(c0, func, bias_ap, bias_b, dst):
                        units.append((lambda: load_w([(c0, 128)], 128),
                                      lambda wb: fm_act_run(wb, func, bias_ap, bias_b, dst)))

                    def fm_act_run(wb, func, bias_ap, bias_b, dst):
                        sg_ = None
                        for s in range(TPH):
                            if s % 4 == 0:
                                sg_ = stage_r.next()
                            ps = proj_fm(wb, 128, s)
                            o = (s % 4) * 512
                            if bias_ap is None:
                                act(func, sg_[:, o:o + 512], ps.ap, [ps], [sg_])
                            else:
                                act(func, sg_[:, o:o + 512], ps.ap, [ps, bias_b], [sg_], bias=bias_ap)
                            if s % 4 == 3:
                                t0 = half * TH + (s - 3) * 512
                                store(dst[:, t0:t0 + 2048], sg_)

                    def load_wt(c0, ncols):
                        wtb = wtb_r.next()
                        dma(wtb[:, :, 0:ncols], w_in[:, c0:c0 + ncols].rearrange("(k p) n -> p k n", p=128),
                            writes=[wtb], own=wtb, q="pool")
                        return wtb

                    def tm_unit(c0, ncols, nh, aug, dst_fn):
                        units.append((lambda: load_wt(c0, ncols), lambda wtb: tm_run(wtb, ncols, nh, aug, dst_fn)))

                    def tm_run(wtb, ncols, nh, aug, dst_fn):
                        for s in range(TPH):
                            for k in range(4):
                                tok = half * TH + s * 512 + k * 128
                                ps = aux_rot.next()
                                for c in range(8):
                                    mm(ps, ps[:, 0:ncols], xn_b[s][:, c, k * 128:(k + 1) * 128], wtb[:, c, 0:ncols],
                                       c == 0, c == 7, [xn_b[s], wtb])
                                vcol = valid[:, tok // 128:tok // 128 + 1]
                                if not aug:
                                    vb = tmp("vb", [128, 512], BF16, 3)
                                    ts("dve", vb[:, 0:ncols], ps[:, 0:ncols], vcol, ALU.mult, [ps, valid], [vb])
                                    store(dst_fn(tok), vb, vb[:, 0:ncols])
                                else:
                                    va = tmp("va", [128, 8, 65], BF16, 3)
                                    ts("dve", va[:, 0:nh, 0:64], ps[:, 0:ncols].rearrange("p (h d) -> p h d", h=nh), vcol,
                                       ALU.mult, [ps, valid], [va])
                                    ts("dve", va[:, 0:nh, 64:65], ones_f[:, 0:nh].rearrange("p (h o) -> p h o", o=1), vcol,
                                       ALU.mult, [ones_f, valid], [va])
                                    store(dst_fn(tok), va, va[:, 0:nh, :])

                    for h in range(4):
                        fm_norm_unit([(OFF["a_q"] + h * 64, 64), (OFF["a_q"] + 256 + h * 64, 64)], 64, "a_qn",
                                     [(lambda a, b, h=h: aqT[h, :, a:b], 0, 128)])
                        fm_norm_unit([(OFF["a_k"] + h * 64, 64), (OFF["a_k"] + 256 + h * 64, 64)], 64, "a_kn",
                                     [(lambda a, b, h=h: akT[h, :, a:b], 0, 128)])
                    tm_unit(OFF["a_v"], 512, 0, False, lambda tok: avd[tok:tok + 128, :])
                    for b4 in range(4):
                        fm_norm_unit([(OFF["c_q"] + b4 * 128, 128)], 64, "c_qn",
                                     [(lambda a, b, hh=2 * b4: gqT[0][hh, :, a:b], 0, 64),
                                      (lambda a, b, hh=2 * b4 + 1: gqT[0][hh, :, a:b], 64, 64)])
                    fm_norm_unit([(OFF["c_k"], 128)], 64, "c_kn",
                                 [(lambda a, b: gkT[0][0, :, GH[0] + a:GH[0] + b], 0, 64),
                                  (lambda a, b: gkT[0][1, :, GH[0] + a:GH[0] + b], 64, 64)])
                    tm_unit(OFF["c_v"], 128, 2, True, lambda tok: gva[0][GH[0] + tok:GH[0] + tok + 128, :, :])
                    for g in range(3):
                        G = g + 1
                        for b4 in range(4):
                            fm_norm_unit([(OFF["d_q"] + g * 512 + b4 * 128, 128)], 64, "d_qn",
                                         [(lambda a, b, G=G, hh=2 * b4: gqT[G][hh, :, a:b], 0, 64),
                                          (lambda a, b, G=G, hh=2 * b4 + 1: gqT[G][hh, :, a:b], 64, 64)])
                            fm_norm_unit([(OFF["d_k"] + g * 512 + b4 * 128, 128)], 64, "d_kn",
                                         [(lambda a, b, G=G, hh=2 * b4: gkT[G][hh, :, GH[G] + a:GH[G] + b], 0, 64),
                                          (lambda a, b, G=G, hh=2 * b4 + 1: gkT[G][hh, :, GH[G] + a:GH[G] + b], 64, 64)])
                        tm_unit(OFF["d_v"] + g * 512, 512, 8, True,
                                lambda tok, G=G: gva[G][GH[G] + tok:GH[G] + tok + 128, :, :])
                    for h in range(4):
                        fm_norm_unit([(OFF["m_q"] + h * 128, 128)], 128, "m_qn", [(lambda a, b, h=h: mqT[h, :, a:b], 0, 128)])
                    for b20 in range(20):
                        fm_act_unit(OFF["z"] + b20 * 128, AF.Silu, None, None, zT[b20])
                    for b40 in range(40):
                        fm_act_unit(OFF["g"] + b40 * 128, AF.Sigmoid, bgate[:, b40 // 8, b40 % 8:b40 % 8 + 1], bgate, gT[b40])
                    wq = [units[0][0](), units[1][0]()]
                    for ui in range(len(units)):
                        if ui + 2 < len(units):
                            wq.append(units[ui + 2][0]())
                        units[ui][1](wq.pop(0))
                    wcq = [load_w([(OFF["b_cq"] + i * 128, 128)], 128) for i in range(2)]
                    for s in range(TPH):
                        t0 = half * TH + s * 512
                        qfs, sqs = [], []
                        for i in range(2):
                            ps = proj_fm(wcq[i], 128, s)
                            qf, sq = rms1(ps, 128)
                            qfs.append(qf)
                            sqs.append(sq)
                        cqn = tmp("cqn", [128, 2, 512], BF16, 2)
                        rms2(qfs, sqs, 128, 128, [b_cqn[:, 0:1], b_cqn[:, 1:2]], [b_cqn, b_cqn],
                             [(cqn, cqn[:, 0, :]), (cqn, cqn[:, 1, :])], pss_rot.next(), total=256)
                        for ob in range(3):
                            ps = aux_rot.next()
                            for i in range(2):
                                mm(ps, ps.ap, wqb[:, i, ob * 128:(ob + 1) * 128], cqn[:, i, :], i == 0, i == 1, [wqb, cqn])
                            qf, sq = rms1(ps, 128)
                            qo = tmp("qo", [128, 512], BF16, 3)
                            if ob < 2:
                                rms2([qf], [sq], 128, 64, [col["b_qn_n"].ap], [col["b_qn_n"]], [(qo, qo.ap)], pss_rot.next())
                                for hh in range(2):
                                    store(bqT[2 * ob + hh, 0:64, t0:t0 + 512], qo, qo[hh * 64:(hh + 1) * 64, :])
                            else:
                                rms2([qf], [sq], 128, 32, [col["b_qn_r"].ap], [col["b_qn_r"]], [(qo, qo.ap)], pss_rot.next())
                                rope(qo, 128, t0, aux_rot.next(), [(bqT, h, h * 32) for h in range(4)])
                    wckv = load_w([(OFF["b_ckv"], 128)], 128)
                    wkr = load_w([(OFF["b_kr"], 32)], 32)
                    for s in range(TPH):
                        t0 = half * TH + s * 512
                        ps = proj_fm(wckv, 128, s)
                        qf, sq = rms1(ps, 128)
                        ckvn = tmp("ckvn", [128, 512], BF16, 2)
                        rms2([qf], [sq], 128, 128, [col["b_ckvn"].ap], [col["b_ckvn"]], [(ckvn, ckvn.ap)], pss_rot.next())
                        for ob in range(2):
                            ps = aux_rot.next()
                            mm(ps, ps.ap, wkv[:, ob * 128:(ob + 1) * 128], ckvn.ap, True, True, [wkv, ckvn])
                            qf, sq = rms1(ps, 128)
                            ko = tmp("qo", [128, 512], BF16, 3)
                            rms2([qf], [sq], 128, 64, [col["b_kn_n"].ap], [col["b_kn_n"]], [(ko, ko.ap)], pss_rot.next())
                            for hh in range(2):
                                store(bkT[2 * ob + hh, 0:64, t0:t0 + 512], ko, ko[hh * 64:(hh + 1) * 64, :])
                        for k in range(4):
                            tok = t0 + k * 128
                            ps = aux_rot.next()
                            mm(ps, ps.ap, ckvn[:, k * 128:(k + 1) * 128], wkv[:, 256:768], True, True, [ckvn, wkv])
                            vb = tmp("vb", [128, 512], BF16, 3)
                            ts("dve", vb.ap, ps.ap, valid[:, tok // 128:tok // 128 + 1], ALU.mult, [ps, valid], [vb])
                            store(bvd[tok:tok + 128, :], vb)
                        ps = proj_fm(wkr, 32, s)
                        qf, sq = rms1(ps, 32)
                        ko = tmp("qo", [128, 512], BF16, 3)
                        rms2([qf], [sq], 32, 32, [col["b_kn_r"][0:32, :]], [col["b_kn_r"]], [(ko, ko[0:32, :])], pss_rot.next())
                        rope(ko, 32, t0, aux_rot.next(), [(bkT, h, 0) for h in range(4)])
                    P.barrier()
                    new_pb()

            def attn_core(maps, kts, vfn, vreads, onesfn, onesreads, bias, shift, po, pd, sps, dacc=None):
                def qk(kt):
                    res = []
                    for mi, m in enumerate(maps):
                        ps = sps[mi].next()
                        mm(ps, ps.ap, m["k"](kt), m["q"], True, True, m["reads"])
                        res.append(ps)
                    return res
                if isinstance(kts, int):
                    kts = list(range(kts))
                depth = max(1, min(len(sps[0].items) - 2, 2))
                qq = [qk(kts[i]) for i in range(min(depth, len(kts)))]
                for ki, kt in enumerate(kts):
                    if ki + depth < len(kts):
                        qq.append(qk(kts[ki + depth]))
                    cur = qq.pop(0)
                    for mi in range(len(maps)):
                        ps = cur[mi]
                        pT = tmp("pT", [128, 512], BF16, 4)
                        if bias is None:
                            act(AF.Exp, pT.ap, ps.ap, [ps], [pT], bias=-shift)
                        else:
                            e_ap, e_buf, const = bias(mi, kt)
                            p0 = tmp("p0", [128, 512], BF16, 4)
                            act(AF.Exp, p0.ap, ps.ap, [ps], [p0], bias=const - shift)
                            tt("dve", pT.ap, p0.ap, e_ap, ALU.mult, [p0, e_buf], [pT])
                        mm(po[mi], po[mi].ap, vfn(mi, kt), pT.ap, ki == 0, ki == len(kts) - 1, [pT] + vreads)
                        if dacc is None:
                            mm(pd[mi], pd[mi].ap, onesfn(kt), pT.ap, ki == 0, ki == len(kts) - 1, [pT] + onesreads)
                        elif ki % 3 == 2:
                            mm(pd[mi], pd[mi].ap, onesfn(kt), pT.ap, ki == 2, False, [pT] + onesreads)
                        elif ki == 0:
                            ts("dve", dacc.ap, pT.ap, valid[:, kt:kt + 1], ALU.mult, [pT, valid], [dacc])
                        else:
                            stt(dacc.ap, pT.ap, valid[:, kt:kt + 1], dacc.ap, ALU.mult, ALU.add, [pT, valid, dacc], [dacc])
                if dacc is not None:
                    assert len(kts) >= 3 and len(maps) == 1
                    mm(pd[0], pd[0].ap, ones_ff.ap, dacc.ap, False, True, [ones_ff, dacc])

            phase_reset(LAYER_END)
            onesv = AR.alloc("onesv", [128, NT, 128], BF16)
            ones_ff = AR.alloc("ones_ff", [128, 128], F32)
            memset("pool", ones_ff, 1.0)
            for t in range(NT):
                ts("pool", onesv[:, t, :], ones_ff.ap, valid[:, t:t + 1], ALU.mult, [ones_ff, valid], [onesv])
            P.barrier()
            new_pb()
            ATT_BASE = AR.off

            if "A" in phases:
                phase_reset(ATT_BASE)
                sets = Rot([(AR.alloc("aK", [128, S], BF16), AR.alloc("aQ", [128, S], BF16),
                             AR.alloc("aV", [128, NT, 128], BF16)) for _ in range(2 if S <= 4096 else 1)])
                ea_rot = Rot([AR.alloc("EAh", [128, 6, 512], BF16) for _ in range(2)])
                for h in range(4):
                    Kb, Qb, Vb = sets.next()
                    load(Kb, akT[h])
                    load(Qb, aqT[h])
                    load(Vb, avd[:, h * 128:(h + 1) * 128].rearrange("(t p) d -> p t d", p=128))
                    sl = SL4[h]
                    EAh = ea_rot.next()
                    load(EAh, C["EA"][h], q="pool")
                    for qt in range(NQ):
                        q0 = qt * 512
                        zt = tmp("zt", [128, 512], BF16, 2)
                        load(zt, zT[h][:, q0:q0 + 512])

                        def bias(mi, kt, q0=q0, sl=sl, EAh=EAh):
                            k0 = kt * 128
                            if k0 < q0:
                                return EAh[:, 0, :], EAh, -sl * (q0 - k0 - 127)
                            if k0 < q0 + 512:
                                return EAh[:, 2 + (k0 - q0) // 128, :], EAh, 0.0
                            return EAh[:, 1, :], EAh, -sl * (k0 - q0 - 511)

                        maps = [dict(k=lambda kt, m=m, Kb=Kb: Kb[64 * m:64 * m + 64, kt * 128:(kt + 1) * 128],
                                     q=Qb[64 * m:64 * m + 64, q0:q0 + 512], reads=[Kb, Qb]) for m in range(2)]
                        po, pd = [pb[4], pb[6]], [pb[5], pb[7]]
                        kts = [kt for kt in range(NT)
                               if sl * max(0, kt * 128 - (q0 + 511), q0 - (kt * 128 + 127)) < 140.0]
                        attn_core(maps, kts, lambda mi, kt, Vb=Vb: Vb[:, kt, :], [Vb], lambda kt: onesv[:, kt, :], [onesv],
                                  bias, SH_A, po, pd, [Rot([pb[0], pb[1]]), Rot([pb[2], pb[3]])])
                        tn = []
                        for m in range(2):
                            rd = tmp("rd", [128, 512], F32, 2)
                            srecip(rd, rd.ap, pd[m].ap, [pd[m]])
                            t_ = tmp("tn", [128, 512], F32, 2)
                            tt("dve", t_.ap, po[m].ap, rd.ap, ALU.mult, [po[m], rd], [t_])
                            tn.append(t_)
                        o_ = tmp("ao", [128, 512], F32, 2)
                        stt(o_.ap, tn[1].ap, neglam.ap, tn[0].ap, ALU.mult, ALU.add, [tn[0], tn[1], neglam], [o_])
                        sq = tmp("sq", [128, 512], BF16, 2)
                        tt("pool", sq.ap, o_.ap, o_.ap, ALU.mult, [o_], [sq])
                        on = tmp("on", [128, 512], F32, 2)
                        rms2([o_], [sq], 128, 128, [col["a_hn"].ap], [col["a_hn"]], [(on, on.ap)], pb[0])
                        u_ = tmp("u", [128, 512], BF16, 2)
                        tt("pool", u_.ap, on.ap, zt.ap, ALU.mult, [on, zt], [u_])
                        store(uT[0, h, :, q0:q0 + 512], u_)

            if "B" in phases:
                P.barrier()
                new_pb()
                phase_reset(ATT_BASE)
                sets = Rot([(AR.alloc("bK", [128, S], BF16), AR.alloc("bQ", [128, S], BF16),
                             AR.alloc("bV", [128, NT, 128], BF16)) for _ in range(2 if S <= 4096 else 1)])
                porot = Rot([(pb[4], pb[5]), (pb[6], pb[7])])
                for h in range(4):
                    Kb, Qb, Vb = sets.next()
                    load(Kb, bkT[h], sl=Kb[0:96, :])
                    load(Qb, bqT[h], sl=Qb[0:96, :])
                    load(Vb, bvd[:, h * 128:(h + 1) * 128].rearrange("(t p) d -> p t d", p=128))
                    for qt in range(NQ):
                        q0 = qt * 512
                        zt = tmp("zt", [128, 512], BF16, 2)
                        load(zt, zT[4 + h][:, q0:q0 + 512])
                        maps = [dict(k=lambda kt, Kb=Kb: Kb[0:96, kt * 128:(kt + 1) * 128], q=Qb[0:96, q0:q0 + 512],
                                     reads=[Kb, Qb])]
                        po_, pd_ = porot.next()
                        attn_core(maps, NT, lambda mi, kt, Vb=Vb: Vb[:, kt, :], [Vb], lambda kt: onesv[:, kt, :], [onesv],
                                  None, SH_B, [po_], [pd_], [Rot([pb[0], pb[1], pb[2], pb[3]])],
                                  dacc=tmp("dacc", [128, 512], F32, 2))
                        rd = tmp("rd", [128, 512], F32, 2)
                        srecip(rd, rd.ap, pd_.ap, [pd_])
                        t_ = tmp("tn", [128, 512], F32, 2)
                        tt("dve", t_.ap, po_.ap, rd.ap, ALU.mult, [po_, rd], [t_])
                        u_ = tmp("u", [128, 512], BF16, 2)
                        tt("pool", u_.ap, t_.ap, zt.ap, ALU.mult, [t_, zt], [u_])
                        store(uT[1, h, :, q0:q0 + 512], u_)

            if "M" in phases:
                P.barrier()
                new_pb()
                phase_reset(ATT_BASE)
                mt = AR.alloc("mt", [128, 2, DM], F32)
                load(mt, mem_in.rearrange("(k p) f -> p k f", p=128))
                memT = AR.alloc("memT", [128, 8, 256], F32)
                for c in range(8):
                    ps = pb[c % 4]
                    for k in range(2):
                        P.op("pe", lambda e, ps=ps, k=k, c=c: e.transpose(
                            out=ps[:, k * 128:(k + 1) * 128], in_=mt[:, k, c * 128:(c + 1) * 128], identity=ident.ap),
                            reads=[mt, ident], writes=[ps])
                    acopy(memT[:, c, :], ps[:, 0:256], [ps], [memT])
                sqm = AR.alloc("sqm", [128, 8, 256], BF16)
                tt("pool", sqm.ap, memT.ap, memT.ap, ALU.mult, [memT], [sqm])
                ps = pb[4]
                for c in range(8):
                    mm(ps, ps[:, 0:256], ones128.ap, sqm[:, c, :], c == 0, c == 7, [sqm, ones128])
                rtm = AR.alloc("rtm", [128, 256], F32)
                act(AF.Sqrt, rtm.ap, ps[:, 0:256], [ps, eps_c], [rtm], bias=eps_c.ap, scale=1.0 / DM)
                rrm = AR.alloc("rrm", [128, 256], F32)
                recip(rrm.ap, rtm.ap, [rtm], [rrm])
                mnT = AR.alloc("mnT", [128, 8, 256], BF16)
                for c in range(8):
                    stt(mnT[:, c, :], memT[:, c, :], mncol[:, c:c + 1], rrm.ap, ALU.mult, ALU.mult, [memT, rrm, mncol], [mnT])
                wmf = AR.alloc("wmf", [128, 8, 512], F32)
                wmk = AR.alloc("wmk", [128, 8, 512], BF16)
                wmv = AR.alloc("wmv", [128, 8, 512], BF16)
                load(wmf, Wt["w_mem_kv"][l][:, 0:512].rearrange("(k p) n -> p k n", p=128))
                vcopy(wmk.ap, wmf.ap, [wmf], [wmk], q="pool")
                load(wmf, Wt["w_mem_kv"][l][:, 512:1024].rearrange("(k p) n -> p k n", p=128))
                vcopy(wmv.ap, wmf.ap, [wmf], [wmv], q="pool")
                mkT = AR.alloc("mkT", [128, 4, 256], BF16)
                mv = AR.alloc("mv", [128, 2, 512], BF16)
                for h in range(4):
                    ps = pb[h % 2]
                    for c in range(8):
                        mm(ps, ps[:, 0:256], wmk[:, c, h * 128:(h + 1) * 128], mnT[:, c, :], c == 0, c == 7, [wmk, mnT])
                    qf, sq = rms1(ps, 128, w=256)
                    rms2([qf], [sq], 128, 128, [col["m_kn"].ap], [col["m_kn"]], [(mkT, mkT[:, h, :])], pb[2 + h % 2], w=256)
                for k in range(2):
                    ps = pb[4 + k]
                    for c in range(8):
                        mm(ps, ps.ap, mnT[:, c, k * 128:(k + 1) * 128], wmv[:, c, :], c == 0, c == 7, [mnT, wmv])
                    vcopy(mv[:, k, :], ps.ap, [ps], [mv])
                porot = Rot([(pb[4], pb[5]), (pb[6], pb[7])])
                srot = Rot([pb[0], pb[1], pb[2], pb[3]])
                for qt in range(NQ):
                    q0 = qt * 512
                    mq = tmp("mq", [128, 4, 512], BF16, 2)
                    load(mq, mqT[:, :, q0:q0 + 512].rearrange("h p j -> p h j"))
                    zt4 = tmp("zt4", [128, 4, 512], BF16, 2)
                    load(zt4, zT[16:20, :, q0:q0 + 512].rearrange("h p j -> p h j"))
                    for h in range(4):
                        maps = [dict(k=lambda kt, h=h: mkT[:, h, kt * 128:(kt + 1) * 128], q=mq[:, h, :], reads=[mkT, mq])]
                        po_, pd_ = porot.next()
                        attn_core(maps, 2, lambda mi, kt, h=h: mv[:, kt, h * 128:(h + 1) * 128], [mv],
                                  lambda kt: ones128.ap, [ones128], None, SH_M, [po_], [pd_], [srot])
                        rd = tmp("rd", [128, 512], F32, 2)
                        srecip(rd, rd.ap, pd_.ap, [pd_])
                        t_ = tmp("tn", [128, 512], F32, 2)
                        tt("dve", t_.ap, po_.ap, rd.ap, ALU.mult, [po_, rd], [t_])
                        u_ = tmp("u", [128, 512], BF16, 2)
                        tt("pool", u_.ap, t_.ap, zt4[:, h, :], ALU.mult, [t_, zt4], [u_])
                        store(uT[4, h, :, q0:q0 + 512], u_)

            def banded(groups, branch, use_sink, shift):
                P.barrier()
                new_pb()
                phase_reset(LAYER_END)
                UNIT = 2048
                accO = AR.alloc("accO", [64, 4, UNIT], F32)
                accD = AR.alloc("accD", [64, 4, UNIT], F32)
                Qb = AR.alloc("gQ", [64, 4, UNIT], BF16)
                maxspan = max(128 * GDIL[G] for G in groups)
                nk = 1 if groups == [0] else 4
                Kb = AR.alloc("gK", [64, nk, UNIT + 2 * maxspan], BF16)
                porot = Rot([(pb[4], pb[5]), (pb[6], pb[7])])
                srot = Rot([pb[0], pb[1], pb[2], pb[3]])
                eb_rot = Rot([AR.alloc("EBt", [128, 4, 384], BF16) for _ in range(2)])
                for u0 in range(0, S, UNIT):
                    for hh in range(2):
                        for gi, G in enumerate(groups):
                            dil, H = GDIL[G], GH[G]
                            span = 128 * dil
                            EBt = eb_rot.next()
                            load(EBt, C["EB"][G, hh * 4:hh * 4 + 4].rearrange("h p c -> p h c"), q="pool")
                            for j in range(4):
                                load(Qb, gqT[G][hh * 4 + j, :, u0:u0 + UNIT], sl=Qb[:, j, :], grp=Qb)
                            for j in range(nk):
                                kvh = hh if G == 0 else hh * 4 + j
                                load(Kb, gkT[G][kvh, :, H + u0 - span:H + u0 + UNIT + span],
                                     sl=Kb[:, j, 0:UNIT + 2 * span], grp=Kb)
                            NTL = 3 if G == 0 else 2
                            TW = NTL * 128
                            pend = []

                            def push(fn):
                                pend.append(fn)
                                if len(pend) > 3:
                                    pend.pop(0)()

                            for blk_i in range(UNIT // 128):
                                sp_i, r = blk_i // dil, blk_i % dil
                                qoff = sp_i * span + r
                                koffs = [sp_i * span + r + ((m * 128) if G == 0 else (64 + m * 128)) * dil for m in range(NTL)]
                                Vt = tmp("Vt", [128, 3, 4, 65], BF16, 3)
                                ov = tmp("ov", [128, 3, 64], BF16, 3)
                                for m in range(NTL):
                                    r0 = u0 + koffs[m]
                                    if G == 0:
                                        load(Vt, gva[G][r0:r0 + 127 * dil + 1:dil, hh:hh + 1, :], sl=Vt[:, m, 0:1, :], grp=Vt)
                                    else:
                                        load(Vt, gva[G][r0:r0 + 127 * dil + 1:dil, hh * 4:hh * 4 + 4, :], sl=Vt[:, m, :, :], grp=Vt)
                                vcopy(ov[:, 0:NTL, :], Vt[:, 0:NTL, 0, 64:65].to_broadcast([128, NTL, 64]), [Vt], [ov], q="pool")
                                po_, pd_ = porot.next()
                                for j in range(4):
                                    h = hh * 4 + j
                                    kj = 0 if G == 0 else j
                                    vj = 0 if G == 0 else j
                                    ps = srot.next()
                                    for m in range(NTL):
                                        mm(ps, ps[:, m * 128:(m + 1) * 128], Kb[:, kj, koffs[m]:koffs[m] + 127 * dil + 1:dil],
                                           Qb[:, j, qoff:qoff + 127 * dil + 1:dil], True, True, [Kb, Qb])
                                    p0 = tmp("p03", [128, 384], BF16, 5)
                                    act(AF.Exp, p0[:, 0:TW], ps[:, 0:TW], [ps], [p0], bias=-shift)
                                    pT = tmp("pT3", [128, 384], BF16, 5)
                                    tt("pool", pT[:, 0:TW], p0[:, 0:TW], EBt[:, j, 0:TW], ALU.mult, [p0, EBt], [pT])

                                    def stage2(j=j, vj=vj, pT=pT, Vt=Vt, ov=ov, po_=po_, pd_=pd_, qoff=qoff, NTL=NTL, gi=gi, dil=dil):
                                        for m in range(NTL):
                                            mm(po_, po_[0:64, j * 128:(j + 1) * 128], Vt[:, m, vj, 0:64], pT[:, m * 128:(m + 1) * 128],
                                               m == 0, m == NTL - 1, [Vt, pT])
                                        for m in range(NTL):
                                            mm(pd_, pd_[0:64, j * 128:(j + 1) * 128], ov[:, m, :], pT[:, m * 128:(m + 1) * 128],
                                               m == 0, m == NTL - 1, [ov, pT])
                                        if j == 3:
                                            aO = accO[:, :, qoff:qoff + 127 * dil + 1:dil]
                                            aD = accD[:, :, qoff:qoff + 127 * dil + 1:dil]
                                            pov = po_[0:64, :].rearrange("p (j q) -> p j q", j=4)
                                            pdv = pd_[0:64, :].rearrange("p (j q) -> p j q", j=4)
                                            if gi == 0:
                                                vcopy(aO, pov, [po_], [accO])
                                                acopy(aD, pdv, [pd_], [accD])
                                            else:
                                                tt("dve", aO, pov, aO, ALU.add, [po_, accO], [accO])
                                                tt("dve", aD, pdv, aD, ALU.add, [pd_, accD], [accD])
                                    push(stage2)
                            while pend:
                                pend.pop(0)()
                        if use_sink:
                            for j in range(4):
                                ts("dve", accD[:, j, :], accD[:, j, :], esink[0:64, hh * 4 + j:hh * 4 + j + 1], ALU.add,
                                   [accD, esink], [accD])
                        srecip(accD, accD.ap, accD.ap, [accD])
                        tt("dve", accO.ap, accO.ap, accD.ap, ALU.mult, [accO, accD], [accO])
                        for c0 in range(0, UNIT, 512):
                            zt = tmp("gz", [64, 4, 512], BF16, 2)
                            for j in range(4):
                                h = hh * 4 + j
                                load(zt, zT[branch * 4 + h // 2, (h % 2) * 64:(h % 2) * 64 + 64, u0 + c0:u0 + c0 + 512],
                                     sl=zt[:, j, :], grp=zt)
                            u_ = tmp("gu", [64, 4, 512], BF16, 2)
                            tt("pool", u_.ap, accO[:, :, c0:c0 + 512], zt.ap, ALU.mult, [accO, zt], [u_])
                            for j in range(4):
                                h = hh * 4 + j
                                store(uT[branch, h // 2, (h % 2) * 64:(h % 2) * 64 + 64, u0 + c0:u0 + c0 + 512], u_, u_[:, j, :])

            if "C" in phases:
                banded([0], 2, True, SH_C)
            if "D" in phases:
                banded([1, 2, 3], 3, False, SH_D)
            P.barrier()
            new_pb()

            if "G" in phases:
                phase_reset(LAYER_END)
                wbr = AR.alloc("wbr", [128, 5, 4, DM], BF16)
                wout = AR.alloc("wout", [128, 8, DM], BF16)
                WEND = AR.off
                wsf = AR.alloc("wsf", [128, 4, DM], F32)
                for i in range(5):
                    load(wsf, Wt["w_br"][l, i].rearrange("(c p) n -> p c n", p=128))
                    vcopy(wbr[:, i, :, :], wsf.ap, [wsf], [wbr], q="pool")
                for hf in range(2):
                    load(wsf, Wt["w_out"][l][hf * 512:(hf + 1) * 512, :].rearrange("(c p) n -> p c n", p=128))
                    vcopy(wout[:, hf * 4:(hf + 1) * 4, :], wsf.ap, [wsf], [wout], q="pool")
                P.barrier()
                new_pb()
                phase_reset(WEND)
                acc = AR.alloc("acc", [128, 8, 512], F32)
                accb = AR.alloc("accb", [128, 8, 512], BF16)
                mrot = Rot(pb[0:6])
                def ld_x(tq):
                    b_ = tmp("xTt", [128, 8, 512], F32, 2)
                    load(b_, xT[l][:, :, tq * 512:(tq + 1) * 512].rearrange("c p j -> p c j"))
                    return b_

                def ld_ug(k):
                    tq_, i_ = k // 5, k % 5
                    ut_ = tmp("ut", [128, 4, 512], BF16, 3)
                    load(ut_, uT[i_, :, :, tq_ * 512:(tq_ + 1) * 512].rearrange("c p j -> p c j"))
                    gt_ = tmp("gt", [128, 8, 512], BF16, 3)
                    load(gt_, gT[i_ * 8:(i_ + 1) * 8, :, tq_ * 512:(tq_ + 1) * 512].rearrange("b p j -> p b j"))
                    return ut_, gt_

                xq = [ld_x(0)]
                ugq = [ld_ug(0)]
                for tq in range(NQ):
                    t0 = tq * 512
                    xt_ = xq.pop(0)
                    for i in range(5):
                        if tq * 5 + i + 1 < NQ * 5:
                            ugq.append(ld_ug(tq * 5 + i + 1))
                        if i == 1 and tq + 1 < NQ:
                            xq.append(ld_x(tq + 1))
                        ut, gt = ugq.pop(0)
                        for j in range(8):
                            ps = mrot.next()
                            for c in range(4):
                                mm(ps, ps.ap, wbr[:, i, c, j * 128:(j + 1) * 128], ut[:, c, :], c == 0, c == 3, [wbr, ut])
                            if i == 0:
                                tt("dve", acc[:, j, :], ps.ap, gt[:, j, :], ALU.mult, [ps, gt], [acc])
                            else:
                                tm_ = tmp("tm", [128, 512], F32, 3)
                                tt("dve", tm_.ap, ps.ap, gt[:, j, :], ALU.mult, [ps, gt], [tm_])
                                tt("pool", acc[:, j, :], acc[:, j, :], tm_.ap, ALU.add, [acc, tm_], [acc])
                    acopy(accb.ap, acc.ap, [acc], [accb])
                    for j in range(8):
                        ps = mrot.next()
                        for c in range(8):
                            mm(ps, ps.ap, wout[:, c, j * 128:(j + 1) * 128], accb[:, c, :], c == 0, c == 7, [wout, accb])
                        tt("dve", xt_[:, j, :], ps.ap, xt_[:, j, :], ALU.add, [ps, xt_], [xt_])
                    if l < L - 1:
                        store(xT[l + 1][:, :, t0:t0 + 512].rearrange("c p j -> p c j"), xt_)
                    else:
                        for k in range(4):
                            yt = tmp("yt", [128, DM], F32, 2)
                            for hf in range(2):
                                ps = pb[6 + hf]
                                for c4 in range(4):
                                    c = hf * 4 + c4
                                    P.op("pe", lambda e, ps=ps, c=c, c4=c4, k=k, xt_=xt_: e.transpose(
                                        out=ps[:, c4 * 128:(c4 + 1) * 128], in_=xt_[:, c, k * 128:(k + 1) * 128],
                                        identity=ident.ap), reads=[xt_, ident], writes=[ps])
                                acopy(yt[:, hf * 512:(hf + 1) * 512], ps.ap, [ps], [yt])
                            store(y_out[t0 + k * 128:t0 + (k + 1) * 128, :], yt)
                P.barrier()
                new_pb()
        P.emit()
        print("ops", len(P.ops), "sems", P.nsem)
    return nc


_NC_CACHE = {}


def _core_inputs(xs, mems, valids, weights, S):
    consts = const_tables(S)
    maps = []
    for x, m, v in zip(xs, mems, valids):
        d = {"x": x, "mem": m, "valid": v}
        d.update(consts)
        d.update(weights)
        maps.append(d)
    return maps


def kernel(x_prompt, x_sample, mem_prompt, mem_sample, **w):
    S = 8192
    x_prompt = np.asarray(x_prompt, np.float32)
    x_sample = np.asarray(x_sample, np.float32)
    weights = {k: np.ascontiguousarray(np.asarray(v, np.float32)) for k, v in w.items()}
    xs, mems, valids = [], [], []
    for b in range(4):
        xp = np.zeros((S, DM), np.float32)
        xp[:x_prompt.shape[1]] = x_prompt[b]
        xs.append(xp)
        mems.append(np.ascontiguousarray(np.asarray(mem_prompt[b], np.float32)))
        v = np.zeros((S,), np.float32)
        v[:x_prompt.shape[1]] = 1.0
        valids.append(np.ascontiguousarray(v.reshape(S // 128, 128).T))
    for b in range(4):
        xs.append(np.ascontiguousarray(x_sample[b]))
        mems.append(np.ascontiguousarray(np.asarray(mem_sample[b], np.float32)))
        valids.append(np.ones((128, S // 128), np.float32))
    if S not in _NC_CACHE:
        _NC_CACHE[S] = build(S)
    nc = _NC_CACHE[S]
    in_maps = _core_inputs(xs, mems, valids, weights, S)
    res = run_bass_kernel_spmd(nc, in_maps, core_ids=list(range(8)))
    SP = x_prompt.shape[1]
    y_prompt = np.stack([np.asarray(res.results[b]["y"], np.float32)[:SP] for b in range(4)], axis=0)
    y_sample = np.stack([np.asarray(res.results[4 + b]["y"], np.float32) for b in range(4)], axis=0)
    return (y_prompt, y_sample)
```
